# Optimizing a Trainium2 kernel written in Bass

```python
import jax, jax.numpy as jnp
from jax import lax
import numpy as np

D_MODEL = 1024
BATCH = 8
SEQ = 4096
DEPTH = 2

N_MIXERS = 2
N_MEM = 256
MEM_HEADS = 4
MEM_HD = 64
SB_HEADS = 12
SB_HD = 64
SB_BLOCK = 128
HG_HEADS = 6
HG_DK = 128
HG_DV = 128
HG_CHUNK = 64
D_FF = 4 * D_MODEL
D_MIX = SB_HEADS * SB_HD + MEM_HEADS * MEM_HD
SB_PROJ = 3 * SB_HEADS * SB_HD + MEM_HEADS * MEM_HD
HG_PROJ = 2 * HG_HEADS * HG_DK + 2 * HG_HEADS * HG_DV + MEM_HEADS * MEM_HD
N_LAYERS_A = (DEPTH + N_MIXERS - 1) // N_MIXERS
N_LAYERS_B = (DEPTH + N_MIXERS - 2) // N_MIXERS
DN_ALPHA = (2 * DEPTH) ** 0.25
DN_BETA = (8 * DEPTH) ** -0.25
LN_EPS = 1e-5
RMS_EPS = 1e-6

kernel_name = "hybrid_stickbreak_hgrn2_memxattn_deepnorm"


def layer_norm(x, g, b):
    xf = x.astype(jnp.float32)
    mu = jnp.mean(xf, axis=-1, keepdims=True)
    var = jnp.mean(jnp.square(xf - mu), axis=-1, keepdims=True)
    return ((xf - mu) * lax.rsqrt(var + LN_EPS) * g + b).astype(x.dtype)


def split_heads(t, h):
    b, s, _ = t.shape
    return t.reshape(b, s, h, -1).transpose(0, 2, 1, 3)


def merge_heads(t):
    b, h, s, d = t.shape
    return t.transpose(0, 2, 1, 3).reshape(b, s, h * d)


def memory_attention(q_mem, mem, w_mem_kv):
    k, v = jnp.split(mem @ w_mem_kv, 2, axis=-1)
    q = split_heads(q_mem, MEM_HEADS)
    k = split_heads(k, MEM_HEADS)
    v = split_heads(v, MEM_HEADS)
    s = jnp.einsum('bhtd,bhmd->bhtm', q, k).astype(jnp.float32) * (MEM_HD ** -0.5)
    p = jax.nn.softmax(s, axis=-1).astype(v.dtype)
    return merge_heads(jnp.einsum('bhtm,bhmd->bhtd', p, v))


def stick_breaking_attention(q, k, v):
    seq = q.shape[2]
    scale = SB_HD ** -0.5
    outs = []
    for blk in range(seq // SB_BLOCK):
        t0 = blk * SB_BLOCK
        t1 = t0 + SB_BLOCK
        qb = q[:, :, t0:t1]
        kb = k[:, :, :t1]
        vb = v[:, :, :t1]
        z = jnp.einsum('bhtd,bhsd->bhts', qb, kb).astype(jnp.float32) * scale
        strict = jnp.arange(t1)[None, :] < jnp.arange(t0, t1)[:, None]
        log_stay = jnp.where(strict, jax.nn.log_sigmoid(-z), 0.0)
        later = lax.cumsum(log_stay, axis=3, reverse=True) - log_stay
        w = jnp.where(strict, jnp.exp(jax.nn.log_sigmoid(z) + later), 0.0)
        outs.append(jnp.einsum('bhts,bhsd->bhtd', w.astype(vb.dtype), vb))
    return jnp.concatenate(outs, axis=2)


def hgrn2_recurrence(q, k, v, log_f):
    b, h, s, dk = q.shape
    dv = v.shape[-1]
    nc = s // HG_CHUNK

    def chunks(t):
        return t.reshape(b, h, nc, HG_CHUNK, t.shape[-1]).transpose(2, 0, 1, 3, 4)

    causal = jnp.tril(jnp.ones((HG_CHUNK, HG_CHUNK), dtype=bool))[:, :, None]

    def step(state, inp):
        qc, kc, vc, gc = inp
        G = jnp.cumsum(gc.astype(jnp.float32), axis=2)
        G_last = G[:, :, -1:]
        o_inter = jnp.einsum('bhck,bhkv->bhcv', qc * jnp.exp(G), state)
        decay = jnp.exp(jnp.where(causal, G[:, :, :, None, :] - G[:, :, None, :, :], -jnp.inf))
        scores = jnp.einsum('bhtk,bhsk,bhtsk->bhts', qc, kc, decay)
        o_intra = jnp.einsum('bhts,bhsv->bhtv', scores, vc)
        k_dec = kc * jnp.exp(G_last - G)
        new_state = jnp.exp(G_last[:, :, 0])[..., None] * state + jnp.einsum('bhck,bhcv->bhkv', k_dec, vc)
        return new_state, o_inter + o_intra

    state0 = jnp.zeros((b, h, dk, dv), jnp.float32)
    _, o = lax.scan(step, state0, (chunks(q), chunks(k), chunks(v), chunks(log_f)))
    return o.transpose(1, 2, 0, 3, 4).reshape(b, h, s, dv)


def hgrn2_mixer(proj, lb, gnorm_g):
    wk = HG_HEADS * HG_DK
    wv = HG_HEADS * HG_DV
    q = proj[..., :wk]
    f_pre = proj[..., wk:2 * wk].astype(jnp.float32)
    i = proj[..., 2 * wk:2 * wk + wv]
    gate = proj[..., 2 * wk + wv:]
    log_f = jnp.logaddexp(jnp.log(lb), jnp.log1p(-lb) + jax.nn.log_sigmoid(f_pre))
    k = ((1.0 - lb) * jax.nn.sigmoid(-f_pre)).astype(q.dtype)
    o = hgrn2_recurrence(split_heads(q, HG_HEADS), split_heads(k, HG_HEADS),
                         split_heads(i, HG_HEADS), split_heads(log_f, HG_HEADS))
    of = o.astype(jnp.float32)
    of = of * lax.rsqrt(jnp.mean(jnp.square(of), axis=-1, keepdims=True) + RMS_EPS)
    o = merge_heads(of) * gnorm_g
    return (o * jax.nn.silu(gate.astype(jnp.float32))).astype(proj.dtype)


def hgrn2_lower_bounds(lower_bounds):
    p = jax.nn.softmax(lower_bounds.astype(jnp.float32), axis=0)
    return jnp.cumsum(p, axis=0) - p[0:1]


def setup_inputs(seed: int = 0) -> dict:
    key = jax.random.key(seed)
    ks = jax.random.split(key, 16)
    d_hg = HG_HEADS * HG_DK
    nrm = jax.random.normal
    return {
        "x": nrm(ks[0], (BATCH, SEQ, D_MODEL), jnp.float32),
        "mem": nrm(ks[1], (BATCH, N_MEM, D_MODEL), jnp.float32),
        "w_in_sb": nrm(ks[2], (N_LAYERS_A, D_MODEL, SB_PROJ), jnp.float32) * D_MODEL ** -0.5,
        "w_in_hg": nrm(ks[3], (N_LAYERS_B, D_MODEL, HG_PROJ), jnp.float32) * D_MODEL ** -0.5,
        "w_mem_kv": nrm(ks[4], (DEPTH, D_MODEL, 2 * MEM_HEADS * MEM_HD), jnp.float32) * D_MODEL ** -0.5,
        "lower_bounds": 1.0 + 0.1 * nrm(ks[5], (DEPTH, d_hg), jnp.float32),
        "hg_norm_g": 1.0 + 0.05 * nrm(ks[6], (N_LAYERS_B, HG_HEADS * HG_DV), jnp.float32),
        "w_out": nrm(ks[7], (DEPTH, D_MIX, D_MODEL), jnp.float32) * (D_MIX ** -0.5 * DN_BETA),
        "ln_mix_g": 1.0 + 0.05 * nrm(ks[8], (DEPTH, D_MODEL), jnp.float32),
        "ln_mix_b": 0.02 * nrm(ks[9], (DEPTH, D_MODEL), jnp.float32),
        "w_up": nrm(ks[10], (DEPTH, D_MODEL, D_FF), jnp.float32) * D_MODEL ** -0.5,
        "w_down": nrm(ks[11], (DEPTH, D_FF, D_MODEL), jnp.float32) * (D_FF ** -0.5 * DN_BETA),
        "ln_ffn_g": 1.0 + 0.05 * nrm(ks[12], (DEPTH, D_MODEL), jnp.float32),
        "ln_ffn_b": 0.02 * nrm(ks[13], (DEPTH, D_MODEL), jnp.float32),
    }


def reference(x, mem, w_in_sb, w_in_hg, w_mem_kv, lower_bounds, hg_norm_g, w_out,
              ln_mix_g, ln_mix_b, w_up, w_down, ln_ffn_g, ln_ffn_b):
    lbs = hgrn2_lower_bounds(lower_bounds)
    w_sb = SB_HEADS * SB_HD
    for layer in range(DEPTH):
        slot = layer // N_MIXERS
        if layer % N_MIXERS == 0:
            proj = x @ w_in_sb[slot]
            q = split_heads(proj[..., :w_sb], SB_HEADS)
            k = split_heads(proj[..., w_sb:2 * w_sb], SB_HEADS)
            v = split_heads(proj[..., 2 * w_sb:3 * w_sb], SB_HEADS)
            q_mem = proj[..., 3 * w_sb:]
            mix = merge_heads(stick_breaking_attention(q, k, v))
        else:
            proj = x @ w_in_hg[slot]
            split = 2 * HG_HEADS * HG_DK + 2 * HG_HEADS * HG_DV
            q_mem = proj[..., split:]
            mix = hgrn2_mixer(proj[..., :split], lbs[layer], hg_norm_g[slot])
        mem_out = memory_attention(q_mem, mem, w_mem_kv[layer])
        y = jnp.concatenate([mix, mem_out.astype(mix.dtype)], axis=-1) @ w_out[layer]
        x = layer_norm(DN_ALPHA * x + y, ln_mix_g[layer], ln_mix_b[layer])
        h = jnp.square(jax.nn.relu(x @ w_up[layer]))
        x = layer_norm(DN_ALPHA * x + h @ w_down[layer], ln_ffn_g[layer], ln_ffn_b[layer])
    return x
```

```python
from contextlib import ExitStack
import numpy as np
import concourse.bass as bass
import concourse.mybir as mybir
from concourse.bass_utils import run_bass_kernel_spmd

F32 = mybir.dt.float32
BF16 = mybir.dt.bfloat16
AF = mybir.ActivationFunctionType
ALU = mybir.AluOpType
AX = mybir.AxisListType

COMPUTE = ("pe", "act", "dve", "pool", "sp")
DMAQ = ("sp", "pool", "act")
NDMA_SEMS = 12


class Buf:
    def __init__(self, t, name, psum=False):
        self.t = t
        self.name = name
        self.psum = psum
        self.state = {}

    def __getitem__(self, idx):
        return self.t[idx]


class Scope:
    def __init__(self, prog):
        self.prog = prog
        self.bufs = []

    def __enter__(self):
        return self

    def __exit__(self, *a):
        self.close()
        return False

    def close(self):
        for b in self.bufs:
            self.prog.free(b)
        self.bufs = []


class Op:
    __slots__ = ("idx", "eng", "fn", "deps", "is_dma", "sig", "dma_slot", "dma_val", "eidx", "vc", "dmaknown")

    def __init__(self, idx, eng, fn, is_dma):
        self.idx = idx
        self.eng = eng
        self.fn = fn
        self.deps = set()
        self.is_dma = is_dma
        self.sig = 0
        self.eidx = 0
        self.vc = None
        self.dmaknown = None


class Prog:
    def __init__(self, nc, es):
        self.nc = nc
        self.es = es
        self.ops = []
        self.bufs = []
        self.engs = {"pe": nc.tensor, "act": nc.scalar, "dve": nc.vector, "pool": nc.gpsimd, "sp": nc.sync}
        self.sems = {e: es.enter_context(nc.semaphore("s_" + e)) for e in COMPUTE}
        self.dsems = {q: [es.enter_context(nc.semaphore("d_%s%d" % (q, i))) for i in range(NDMA_SEMS)]
                      for q in DMAQ}
        self.barrier_deps = {e: set() for e in self.engs}
        self.emitted = 0
        self.phase_start = 0
        NE = len(COMPUTE)
        self.cur_vc = {e: [0] * NE for e in self.engs}
        self.cur_dma = {e: set() for e in self.engs}
        self.cnt = {e: 0 for e in COMPUTE}
        self.scount = {e: 0 for e in COMPUTE}
        self.dcount = {q: 0 for q in DMAQ}
        self.n_ins = 0

    ARENA_BYTES = 212736

    def _arena_init(self):
        self.arena = self.es.enter_context(self.nc.sbuf_tensor("arena", [128, self.ARENA_BYTES // 2], BF16))
        self.free_list = [(0, self.ARENA_BYTES)]

    def scope(self):
        return Scope(self)

    def sbuf(self, name, shape, dt, scope=None):
        if not hasattr(self, "arena"):
            self._arena_init()
        esz = 4 if dt == F32 else 2
        n = 1
        for d in shape[1:]:
            n *= d
        nbytes = (n * esz + 63) // 64 * 64
        for i, (off, sz) in enumerate(self.free_list):
            if sz >= nbytes:
                if sz == nbytes:
                    self.free_list.pop(i)
                else:
                    self.free_list[i] = (off + nbytes, sz - nbytes)
                break
        else:
            raise AssertionError("arena out of SBUF for %s %s; free=%s" % (name, shape, self.free_list))
        ap = self.arena[0:shape[0], off // 2:off // 2 + n * esz // 2]
        if dt == F32:
            ap = ap.bitcast(F32)
        if len(shape) == 3:
            ap = ap.rearrange("p (a b) -> p a b", b=shape[2])
        elif len(shape) != 2:
            raise AssertionError("2-D / 3-D only")
        b = Buf(ap, name)
        b.region = (off, nbytes)
        self.bufs.append(b)
        if scope is not None:
            scope.bufs.append(b)
        return b

    def free(self, b):
        off, nbytes = b.region
        b.region = None
        fl = sorted(self.free_list + [(off, nbytes)])
        merged = []
        for o, sz in fl:
            if merged and merged[-1][0] + merged[-1][1] == o:
                merged[-1] = (merged[-1][0], merged[-1][1] + sz)
            else:
                merged.append((o, sz))
        self.free_list = merged

    def psum(self, name, shape, dt):
        t = self.es.enter_context(self.nc.psum_tensor(name, list(shape), dt))
        b = Buf(t, name, psum=True)
        self.bufs.append(b)
        return b

    def dram(self, name, shape, dt, kind="Internal"):
        t = self.nc.dram_tensor(name, list(shape), dt, kind=kind)
        b = Buf(t, name)
        self.bufs.append(b)
        return b

    @staticmethod
    def _conf(k1, k2):
        return k1 is None or k2 is None or k1 == k2

    def _rec(self, eng, fn, reads, writes, is_dma):
        op = Op(len(self.ops), eng, fn, is_dma)
        self.ops.append(op)
        for r in reads:
            b, key = r if isinstance(r, tuple) else (r, None)
            for k2, st in b.state.items():
                if st[0] is not None and (self._conf(key, k2) or (b.psum and self.ops[st[0]].eng != eng)):
                    op.deps.add(st[0])
                if b.psum:
                    op.deps.update(r2 for r2 in st[1] if self.ops[r2].eng != eng)
        for w in writes:
            b, key = w if isinstance(w, tuple) else (w, None)
            for k2, st in b.state.items():
                if self._conf(key, k2):
                    if st[0] is not None:
                        op.deps.add(st[0])
                    op.deps.update(st[1])
                elif b.psum:
                    if st[0] is not None and self.ops[st[0]].eng != eng:
                        op.deps.add(st[0])
                    op.deps.update(r2 for r2 in st[1] if self.ops[r2].eng != eng)
        op.deps |= self.barrier_deps[eng]
        self.barrier_deps[eng] = set()
        for r in reads:
            b, key = r if isinstance(r, tuple) else (r, None)
            st = b.state.setdefault(key, [None, []])
            st[1].append(op.idx)
        for w in writes:
            b, key = w if isinstance(w, tuple) else (w, None)
            if key is None:
                b.state = {None: [op.idx, []]}
            else:
                b.state[key] = [op.idx, []]
        op.deps.discard(op.idx)
        return op

    def op(self, eng, fn, reads=(), writes=()):
        return self._rec(eng, fn, reads, writes, False)

    def I(self, eng, meth, *args, reads=(), writes=(), **kw):
        return self._rec(eng, lambda e: getattr(e, meth)(*args, **kw), reads, writes, False)

    def dma(self, q, out, in_, reads=(), writes=()):
        return self._rec(q, lambda e: e.dma_start(out=out, in_=in_), reads, writes, True)

    def barrier(self):
        tails = set()
        last = {}
        for op in self.ops[self.phase_start:]:
            if op.is_dma:
                tails.add(op.idx)
            else:
                last[op.eng] = op.idx
        tails |= set(last.values())
        bop = Op(len(self.ops), "sp", lambda e: e.nop(), False)
        bop.deps = tails | self.barrier_deps["sp"]
        self.ops.append(bop)
        bop.sig = 1
        self.flush()
        for e in self.engs:
            self.barrier_deps[e] = {bop.idx}
        for b in self.bufs:
            b.state = {}
        self.phase_start = len(self.ops)

    def flush(self):
        ops = self.ops
        NE = len(COMPUTE)
        eid = {e: i for i, e in enumerate(COMPUTE)}
        new = ops[self.emitted:]
        for op in new:
            if not op.is_dma:
                self.cnt[op.eng] += 1
                op.eidx = self.cnt[op.eng]
        needed = []
        for op in new:
            vc = self.cur_vc[op.eng]
            dk = self.cur_dma[op.eng]
            real = []
            for d in sorted(op.deps, reverse=True):
                dop = ops[d]
                if dop.is_dma:
                    if d in dk:
                        continue
                    real.append(d)
                    dk.add(d)
                else:
                    if dop.eng == "pe" and op.eng == "pe":
                        continue
                    j = eid[dop.eng]
                    if vc[j] >= dop.eidx:
                        continue
                    real.append(d)
                    vc[j] = dop.eidx
                dk |= dop.dmaknown
                dvc = dop.vc
                for i in range(NE):
                    if dvc[i] > vc[i]:
                        vc[i] = dvc[i]
            op.vc = list(vc)
            op.dmaknown = set(dk)
            needed.append(real)
            for d in real:
                if not ops[d].sig:
                    assert d >= self.emitted, "dependency on an already emitted non-signalling op"
                    ops[d].sig = 1
        for op in new:
            if op.is_dma:
                n = self.dcount[op.eng]
                self.dcount[op.eng] += 1
                op.dma_slot = n % NDMA_SEMS
                op.dma_val = 16 * (n // NDMA_SEMS + 1)
            elif op.sig:
                self.scount[op.eng] += 1
                op.sig = self.scount[op.eng]
        for op, real in zip(new, needed):
            e = self.engs[op.eng]
            if op.is_dma and op.dma_val > 16:
                e.wait_ge(self.dsems[op.eng][op.dma_slot], op.dma_val - 16)
                self.n_ins += 1
            for d in real:
                dop = ops[d]
                if dop.is_dma:
                    e.wait_ge(self.dsems[dop.eng][dop.dma_slot], dop.dma_val)
                else:
                    e.wait_ge(self.sems[dop.eng], dop.sig)
                self.n_ins += 1
            ins = op.fn(e)
            self.n_ins += 1
            if op.is_dma:
                ins.then_inc(self.dsems[op.eng][op.dma_slot], 16)
            elif op.sig:
                ins.then_inc(self.sems[op.eng], 1)
            op.fn = None
        self.emitted = len(ops)

    def finish(self):
        self.barrier()


S = 4096
D = 1024
NT = S // 128
DFF = 4096
SB_H = 12
HG_H = 6
NMEM = 256
ALPHA = float(4 ** 0.25)
LN_EPS = 1e-5
RMS_EPS = 1e-6
NEG = -30000.0


class K:
    pass


def _copy(P, k, idx, out, in_, reads, writes, scale=None):
    if idx % 2 == 0:
        if scale is None:
            P.op("act", lambda e: e.activation(out=out, in_=in_, func=AF.Copy), reads=reads, writes=writes)
        else:
            P.op("act", lambda e: e.activation(out=out, in_=in_, func=AF.Copy, scale=scale), reads=reads, writes=writes)
    else:
        if scale is None:
            P.op("dve", lambda e: e.tensor_copy(out=out, in_=in_), reads=reads, writes=writes)
        else:
            P.op("dve", lambda e: e.tensor_single_scalar(out=out, in_=in_, scalar=scale, op=ALU.mult),
                 reads=reads, writes=writes)


def build_consts(P, k):
    k.ident = P.sbuf("ident", [128, 128], BF16)
    k.negtri = P.sbuf("negtri", [128, 128], BF16)
    k.negones = P.sbuf("negones", [128, 128], BF16)
    k.maskneg = P.sbuf("maskneg", [128, 128], BF16)
    k.blkmask = P.sbuf("blkmask", [128, 128], F32)
    k.scanmask = P.sbuf("scanmask", [128, 512], F32)
    tmp = P.sbuf("ctmp", [128, 128], F32)
    tmp1 = P.sbuf("ctmp1", [128, 128], F32)
    P.op("dve", lambda e: e.memset(tmp[:], 0.0), writes=[tmp])
    P.op("pool", lambda e: e.affine_select(out=tmp[:], in_=tmp[:], pattern=[[-1, 128]], compare_op=ALU.not_equal,
                                            fill=1.0, base=0, channel_multiplier=1), reads=[tmp], writes=[tmp])
    P.op("dve", lambda e: e.tensor_copy(out=k.ident[:], in_=tmp[:]), reads=[tmp], writes=[k.ident])
    P.op("dve", lambda e: e.memset(tmp1[:], 0.0), writes=[tmp1])
    P.op("pool", lambda e: e.affine_select(out=tmp1[:], in_=tmp1[:], pattern=[[1, 128]], compare_op=ALU.is_gt,
                                            fill=-1.0, base=0, channel_multiplier=-1), reads=[tmp1], writes=[tmp1])
    P.op("dve", lambda e: e.tensor_copy(out=k.negtri[:], in_=tmp1[:]), reads=[tmp1], writes=[k.negtri])
    P.op("dve", lambda e: e.tensor_single_scalar(out=k.maskneg[:], in_=tmp1[:], scalar=-NEG, op=ALU.mult),
         reads=[tmp1], writes=[k.maskneg])
    P.op("dve", lambda e: e.memset(k.negones[:], -1.0), writes=[k.negones])
    P.op("dve", lambda e: e.memset(k.blkmask[:], 1.0), writes=[k.blkmask])
    P.op("pool", lambda e: e.affine_select(out=k.blkmask[:], in_=k.blkmask[:], pattern=[[1, 128]], compare_op=ALU.is_ge,
                                            fill=0.0, base=0, channel_multiplier=-1), reads=[k.blkmask], writes=[k.blkmask])
    P.op("dve", lambda e: e.memset(k.blkmask[0:64, 64:128], 0.0), reads=[k.blkmask], writes=[k.blkmask])
    P.op("dve", lambda e: e.memset(k.scanmask[:], 1.0), writes=[k.scanmask])
    P.op("dve", lambda e: e.memset(k.scanmask[:].rearrange("p (c j) -> p c j", j=64)[:, :, 0:1], 0.0),
         reads=[k.scanmask], writes=[k.scanmask])
    k.epsln = P.sbuf("epsln", [128, 1], F32)
    P.op("dve", lambda e: e.memset(k.epsln[:], LN_EPS), writes=[k.epsln])
    k.pspair = [P.es.enter_context(P.nc.psum_tensor("pspair%d" % i, [128, 1024], F32)) for i in range(4)]
    k.ps = []
    for i in range(8):
        b = Buf(k.pspair[i // 2][:, (i % 2) * 512:(i % 2 + 1) * 512], "psb%d" % i, psum=True)
        P.bufs.append(b)
        k.ps.append(b)


def transpose_tile(P, k, src, src_ap_fn, nblk, psbank, dst, dst_ap, cidx, extra_reads=(), dkey=None):
    psb = psbank[:].bitcast(BF16)
    for b in range(nblk):
        P.I("pe", "transpose", out=psb[:, b * 128:(b + 1) * 128], in_=src_ap_fn(b), identity=k.ident[:],
            reads=[src, k.ident] + list(extra_reads), writes=[(psbank, b)])
    _copy(P, k, cidx, dst_ap, psb[:, 0:nblk * 128] if dst_ap.ndim == 2 else
          psb[:, 0:nblk * 128].rearrange("p (c t) -> p c t", t=128), reads=[psbank], writes=[(dst, dkey)])


def phase_A(P, k):
    with P.scope() as ls:
        W0 = P.sbuf("W0", [128, 8, 2560], BF16, ls)
        wsrc = k.w_in_sb[0].rearrange("(c p) n -> p c n", p=128)
        for c in range(8):
            P.dma("pool", W0[:, c, :], wsrc[:, c, :], reads=[], writes=[(W0, c)])
        xb = [P.sbuf("A_xb%d" % i, [128, 4, 1024], BF16, ls) for i in range(2)]
        xT = [P.sbuf("A_xT%d" % i, [128, 8, 512], BF16, ls) for i in range(2)]
        st = [P.sbuf("A_st%d" % i, [128, 512], BF16, ls) for i in range(4)]
        vst = [P.sbuf("A_vst%d" % i, [128, 4, 768], BF16, ls) for i in range(2)]
        cp = 0
        sti = 0
        rot = 0
        for g in range(8):
            xbg = xb[g % 2]
            xTg = xT[g % 2]
            P.dma("pool", xbg[:], k.x[g * 512:(g + 1) * 512, :].rearrange("(j p) d -> p j d", p=128),
                  reads=[], writes=[xbg])
            for j in range(4):
                transpose_tile(P, k, xbg, (lambda xbg, j: lambda b: xbg[:, j, b * 128:(b + 1) * 128])(xbg, j), 8, k.ps[j % 2],
                               xTg, xTg[:, :, j * 128:(j + 1) * 128], cp, dkey=j)
                cp += 1
            for fc in list(range(12)) + [18, 19]:
                bank = k.ps[2 + rot % 6]
                rot += 1
                for c in range(8):
                    P.I("pe", "matmul", bank[:], lhsT=W0[:, c, fc * 128:(fc + 1) * 128], rhs=xTg[:, c, :],
                        start=(c == 0), stop=(c == 7), reads=[W0, xTg], writes=[bank])
                s_ = st[sti % 4]
                sti += 1
                _copy(P, k, cp, s_[:], bank[:], reads=[bank], writes=[s_], scale=(0.125 if fc < 6 else None))
                cp += 1
                if fc < 6:
                    dst = k.qT0[fc * 128:(fc + 1) * 128, g * 512:(g + 1) * 512]
                    dbuf = k.qT0
                elif fc < 12:
                    dst = k.kT0[(fc - 6) * 128:(fc - 5) * 128, g * 512:(g + 1) * 512]
                    dbuf = k.kT0
                else:
                    dst = k.qmT[(fc - 18) * 128:(fc - 17) * 128, g * 512:(g + 1) * 512]
                    dbuf = k.qmT
                P.dma("sp", dst, s_[:], reads=[s_], writes=[(dbuf, (fc, g))])
            vs = vst[g % 2]
            for j in range(4):
                for (c0, n) in ((0, 512), (512, 256)):
                    bank = k.ps[2 + rot % 6]
                    rot += 1
                    for c in range(8):
                        P.I("pe", "matmul", bank[:, 0:n], lhsT=xTg[:, c, j * 128:(j + 1) * 128],
                            rhs=W0[:, c, 1536 + c0:1536 + c0 + n], start=(c == 0), stop=(c == 7),
                            reads=[W0, xTg], writes=[bank])
                    _copy(P, k, cp, vs[:, j, c0:c0 + n], bank[:, 0:n], reads=[bank], writes=[(vs, (j, c0))])
                    cp += 1
            P.dma("sp", k.v0[g * 512:(g + 1) * 512, :].rearrange("(j p) f -> p j f", p=128), vs[:],
                  reads=[vs], writes=[(k.v0, g)])
        P.barrier()


def phase_B(P, k):
    with P.scope() as ls:
        kTs = [P.sbuf("B_kT%d" % i, [128, S], BF16, ls) for i in range(2)]
        qTs = [P.sbuf("B_qT%d" % i, [128, S], BF16, ls) for i in range(2)]
        vvs = [P.sbuf("B_v%d" % i, [128, NT, 128], BF16, ls) for i in range(2)]
        ebuf = [P.sbuf("B_e%d" % i, [128, 2, 512], F32, ls) for i in range(2)]
        spb = [P.sbuf("B_sp%d" % i, [128, 2, 512], BF16, ls) for i in range(3)]
        wb = [P.sbuf("B_w%d" % i, [128, 2, 512], BF16, ls) for i in range(3)]
        Rf = [P.sbuf("B_R%d" % i, [128, 2, 512], F32, ls) for i in range(2)]
        Rb = [P.sbuf("B_Rb%d" % i, [128, 2, 512], BF16, ls) for i in range(2)]
        ost = [P.sbuf("B_ost%d" % i, [128, 512], BF16, ls) for i in range(2)]

        def pair(pi):
            return k.pspair[pi][:].rearrange("p (b c) -> p b c", c=512), [k.ps[2 * pi], k.ps[2 * pi + 1]]

        psz = pair(0)
        pse = [pair(1), pair(2)]
        pso = [k.ps[6], k.ps[7]]
        units = []
        for hp in range(6):
            for g in range(8):
                for n in range(4 * g + 4):
                    i = 4 * g + 3 - n
                    col0 = max(0, (i - 4 * g) * 128)
                    units.append(dict(hp=hp, g=g, n=n, i=i, col0=col0, diag=(i >= 4 * g), first=(n == 0), last=(i == 0),
                                      gi=hp * 8 + g))
        NU = len(units)
        loaded = [-1]

        def load_pair(hp):
            if loaded[0] >= hp or hp >= 6:
                return
            loaded[0] = hp
            kT, qT, vv = kTs[hp % 2], qTs[hp % 2], vvs[hp % 2]
            P.dma("sp", kT[:], k.kT0[hp * 128:(hp + 1) * 128, :], reads=[k.kT0], writes=[kT])
            P.dma("sp", qT[:], k.qT0[hp * 128:(hp + 1) * 128, :], reads=[k.qT0], writes=[qT])
            P.dma("sp", vv[:], k.v0[:, hp * 128:(hp + 1) * 128].rearrange("(i p) f -> p i f", p=128),
                  reads=[k.v0], writes=[vv])

        def qk(u, pz, stop):
            ap, bufs = pz
            c0, q0, i = u["col0"], u["g"] * 512, u["i"]
            kT, qT = kTs[u["hp"] % 2], qTs[u["hp"] % 2]
            for hh in range(2):
                b0 = 64 * hh
                P.I("pe", "matmul", ap[:, hh, c0:512], lhsT=kT[b0:b0 + 64, i * 128:(i + 1) * 128],
                    rhs=qT[b0:b0 + 64, q0 + c0:q0 + 512], start=True, stop=stop, reads=[kT, qT], writes=[bufs[hh]])

        def maskmm(u, pz):
            ap, bufs = pz
            c0 = u["col0"]
            for hh in range(2):
                P.I("pe", "matmul", ap[:, hh, c0:c0 + 128], lhsT=k.ident[:], rhs=k.maskneg[:], start=False, stop=True,
                    reads=[k.ident, k.maskneg], writes=[bufs[hh]])

        def st_z(idx):
            u = units[idx]
            load_pair(u["hp"])
            load_pair(u["hp"] + 1)
            qk(u, psz, not u["diag"])
            if u["diag"]:
                maskmm(u, psz)

        def st_E(idx):
            u = units[idx]
            c0 = u["col0"]
            eb = ebuf[idx % 2]
            P.I("act", "activation", out=eb[:, :, c0:512], in_=psz[0][:, :, c0:512], func=AF.Exp, reads=psz[1], writes=[eb])

        def st_SP(idx):
            u = units[idx]
            c0 = u["col0"]
            eb = ebuf[idx % 2]
            sp = spb[idx % 3]
            P.I("act", "activation", out=sp[:, :, c0:512], in_=eb[:, :, c0:512], func=AF.Ln, bias=1.0, reads=[eb], writes=[sp])

        def st_R(idx):
            u = units[idx]
            if u["last"]:
                return
            c0 = u["col0"]
            sp = spb[idx % 3]
            R = Rf[u["gi"] % 2]
            if u["first"]:
                P.I("pool", "memset", R[:], 0.0, writes=[R])
            P.I("dve", "tensor_tensor", out=R[:, :, c0:512], in0=R[:, :, c0:512], in1=sp[:, :, c0:512], op=ALU.add,
                reads=[R, sp], writes=[R])
            rb = Rb[idx % 2]
            P.I("dve", "tensor_copy", out=rb[:], in_=R[:], reads=[R], writes=[rb])

        def st_eg(idx):
            u = units[idx]
            c0 = u["col0"]
            pz = pse[idx % 2]
            ap, bufs = pz
            sp = spb[idx % 3]
            qk(u, pz, False)
            for hh in range(2):
                P.I("pe", "matmul", ap[:, hh, c0:512], lhsT=k.negtri[:], rhs=sp[:, hh, c0:512], start=False,
                    stop=(u["first"] and not u["diag"]), reads=[k.negtri, sp], writes=[bufs[hh]])
            if not u["first"]:
                rb = Rb[(idx - 1) % 2]
                for hh in range(2):
                    P.I("pe", "matmul", ap[:, hh, c0:512], lhsT=k.negones[:], rhs=rb[:, hh, c0:512], start=False,
                        stop=(not u["diag"]), reads=[k.negones, rb], writes=[bufs[hh]])
            if u["diag"]:
                maskmm(u, pz)

        def st_W(idx):
            u = units[idx]
            c0 = u["col0"]
            ap, bufs = pse[idx % 2]
            w = wb[idx % 3]
            P.I("act", "activation", out=w[:, :, c0:512], in_=ap[:, :, c0:512], func=AF.Exp, reads=bufs, writes=[w])

        def st_pv(idx):
            u = units[idx]
            c0, i = u["col0"], u["i"]
            w = wb[idx % 3]
            bank = pso[u["gi"] % 2]
            vv = vvs[u["hp"] % 2]
            for hh in range(2):
                P.I("pe", "matmul", bank[64 * hh:64 * hh + 64, c0:512], lhsT=vv[:, i, hh * 64:(hh + 1) * 64], rhs=w[:, hh, c0:512],
                    start=u["first"], stop=u["last"], reads=[vv, w], writes=[bank])
            if u["last"]:
                o = ost[u["gi"] % 2]
                hp, g = u["hp"], u["g"]
                P.I("dve", "tensor_copy", out=o[:], in_=bank[:], reads=[bank], writes=[o])
                P.dma("pool", k.mixT[hp * 128:(hp + 1) * 128, g * 512:(g + 1) * 512], o[:], reads=[o], writes=[(k.mixT, (hp, g))])

        st_z(0)
        for n in range(NU + 3):
            if 1 <= n <= NU:
                st_eg(n - 1)
            if 3 <= n:
                st_pv(n - 3)
            if n < NU:
                st_E(n)
            if n + 1 < NU:
                st_z(n + 1)
            if 2 <= n and n - 2 < NU:
                st_W(n - 2)
            if n < NU:
                st_SP(n)
                st_R(n)
        P.barrier()


class LN:
    def __init__(self, P, k, r, dst, gam, bet, bufs):
        self.P, self.k, self.r, self.dst, self.gam, self.bet = P, k, r, dst, gam, bet
        self.st, self.mv, self.rs, self.nmr = bufs

    def stats(self):
        P, r, st, mv = self.P, self.r, self.st, self.mv
        P.I("dve", "bn_stats", out=st[:, 0, :], in_=r[:, 0:512], reads=[r], writes=[(st, 0)])
        P.I("dve", "bn_stats", out=st[:, 1, :], in_=r[:, 512:1024], reads=[r], writes=[(st, 1)])
        P.I("dve", "bn_aggr", out=mv[:], in_=st[:].rearrange("p a b -> p (a b)"), reads=[st], writes=[mv])

    def rstd(self):
        P, k, mv, rs = self.P, self.k, self.mv, self.rs
        P.I("act", "activation", out=rs[:], in_=mv[:, 1:2], func=AF.Ln, bias=k.epsln[:, 0:1], reads=[mv, k.epsln], writes=[rs])
        P.I("act", "activation", out=rs[:], in_=rs[:], func=AF.Exp, scale=-0.5, reads=[rs], writes=[rs])

    def nmr_(self):
        P, mv, rs, nmr = self.P, self.mv, self.rs, self.nmr
        P.I("pool", "tensor_tensor", out=nmr[:], in0=mv[:, 0:1], in1=rs[:], op=ALU.mult, reads=[mv, rs], writes=[nmr])
        P.I("pool", "tensor_single_scalar", out=nmr[:], in_=nmr[:], scalar=-1.0, op=ALU.mult, reads=[nmr], writes=[nmr])

    def norm(self):
        P, r, rs, nmr, dst = self.P, self.r, self.rs, self.nmr, self.dst
        P.I("act", "activation", out=dst[:], in_=r[:], func=AF.Identity, scale=rs[:, 0:1], bias=nmr[:, 0:1],
            reads=[r, rs, nmr], writes=[dst])

    def affine(self):
        P, dst, gam, bet = self.P, self.dst, self.gam, self.bet
        P.I("pool", "tensor_tensor", out=dst[:], in0=dst[:], in1=gam[:], op=ALU.mult, reads=[dst, gam], writes=[dst])
        P.I("pool", "tensor_tensor", out=dst[:], in0=dst[:], in1=bet[:], op=ALU.add, reads=[dst, bet], writes=[dst])


def lnbufs(P, name, ls):
    return (P.sbuf(name + "st", [128, 2, 6], F32, ls), P.sbuf(name + "mv", [128, 2], F32, ls),
            P.sbuf(name + "rs", [128, 1], F32, ls), P.sbuf(name + "nm", [128, 1], F32, ls))


def load_C_weights(P, k, L, sc):
    w = K()
    w.wout = P.sbuf("C_wout", [128, 8, 1024], BF16, sc)
    wsrc = k.w_out[L].rearrange("(c p) n -> p c n", p=128)
    for c in range(0, 8, 2):
        P.dma("pool", w.wout[:, c:c + 2, :], wsrc[:, c:c + 2, :], writes=[(w.wout, c)])
    w.wkv = P.sbuf("C_wkv", [128, 8, 512], BF16, sc)
    P.dma("pool", w.wkv[:], k.w_mem_kv[L].rearrange("(c p) n -> p c n", p=128), writes=[w.wkv])
    w.memb = P.sbuf("C_memb", [128, 2, 1024], BF16, sc)
    P.dma("pool", w.memb[:], k.mem[:, :].rearrange("(j p) d -> p j d", p=128), writes=[w.memb])
    w.gam = P.sbuf("C_gam", [128, 1024], F32, sc)
    w.bet = P.sbuf("C_bet", [128, 1024], F32, sc)
    P.dma("sp", w.gam[:], k.ln_mix_g[L:L + 1, :].partition_broadcast(128), writes=[w.gam])
    P.dma("sp", w.bet[:], k.ln_mix_b[L:L + 1, :].partition_broadcast(128), writes=[w.bet])
    return w


def alloc_D_weights(P, k, L, sc):
    w = K()
    w.L = L
    w.WUP = P.sbuf("D_wup", [128, 8, DFF], BF16, sc)
    w.WDN = [P.sbuf("D_wdn0", [128, 16, 1024], BF16, sc), None]
    usrc = k.w_up[L].rearrange("(c p) f -> p c f", p=128)
    dsrc = k.w_down[L].rearrange("(c p) n -> p c n", p=128)
    w.pending = []
    for c in range(8):
        w.pending.append((w.WUP[:, c, :], usrc[:, c, :], (w.WUP, c)))
    for c in range(0, 16, 4):
        w.pending.append((w.WDN[0][:, c:c + 4, :], dsrc[:, c:c + 4, :], (w.WDN[0], c)))
    return w


def issue_pending(P, w, n):
    for _ in range(n):
        if w.pending:
            dst, src, wr = w.pending.pop(0)
            P.dma("pool", dst, src, writes=[wr])


def phase_C(P, k, L, xin, xout, w, dw=None):
    with P.scope() as ls:
        wout, wkv, memb, gam, bet = w.wout, w.wkv, w.memb, w.gam, w.bet
        memT = P.sbuf("C_memT", [128, 8, 256], BF16, ls)
        kmT = P.sbuf("C_kmT", [128, 2, 256], BF16, ls)
        vm = P.sbuf("C_vm", [128, 2, 256], BF16, ls)
        for j in range(2):
            transpose_tile(P, k, memb, (lambda j: lambda b: memb[:, j, b * 128:(b + 1) * 128])(j), 8, k.ps[j],
                           memT, memT[:, :, j * 128:(j + 1) * 128], j, dkey=j)
        for fc in range(2):
            bank = k.ps[2 + fc]
            for c in range(8):
                P.I("pe", "matmul", bank[:, 0:256], lhsT=wkv[:, c, fc * 128:(fc + 1) * 128], rhs=memT[:, c, :],
                    start=(c == 0), stop=(c == 7), reads=[wkv, memT], writes=[bank])
            _copy(P, k, fc, kmT[:, fc, :], bank[:, 0:256], reads=[bank], writes=[(kmT, fc)])
        for mc in range(2):
            bank = k.ps[4 + mc]
            for c in range(8):
                P.I("pe", "matmul", bank[:, 0:256], lhsT=memT[:, c, mc * 128:(mc + 1) * 128], rhs=wkv[:, c, 256:512],
                    start=(c == 0), stop=(c == 7), reads=[wkv, memT], writes=[bank])
            _copy(P, k, mc + 1, vm[:, mc, :], bank[:, 0:256], reads=[bank], writes=[(vm, mc)])

        qm = [P.sbuf("C_qm%d" % i, [128, 2, 128], BF16, ls) for i in range(3)]
        mx = [P.sbuf("C_mx%d" % i, [128, 6, 128], BF16, ls) for i in range(3)]
        xr = [P.sbuf("C_xr%d" % i, [128, 1024], F32, ls) for i in range(4)]
        E = [P.sbuf("C_E%d" % i, [128, 4, 256], F32, ls) for i in range(2)]
        Pb = [P.sbuf("C_P%d" % i, [128, 4, 256], BF16, ls) for i in range(2)]
        PT = [P.sbuf("C_PT%d" % i, [128, 8, 128], BF16, ls) for i in range(2)]
        mmT = [P.sbuf("C_mmT%d" % i, [128, 2, 128], BF16, ls) for i in range(2)]
        xo = [P.sbuf("C_xo%d" % i, [128, 1024], F32, ls) for i in range(2)]
        mxv = [P.sbuf("C_mxv%d" % i, [128, 4], F32, ls) for i in range(2)]
        nb_ = [P.sbuf("C_nb%d" % i, [128, 4], F32, ls) for i in range(2)]
        ssum = [P.sbuf("C_ss%d" % i, [128, 4], F32, ls) for i in range(2)]
        rsum = [P.sbuf("C_rs%d" % i, [128, 4], F32, ls) for i in range(2)]
        lnb = [lnbufs(P, "C_ln%d" % i, ls) for i in range(2)]
        psS = [[k.ps[0], k.ps[1]], [k.ps[2], k.ps[3]]]
        psT = k.ps[4]
        psMO = k.ps[5]
        psY = [k.ps[6], k.ps[7]]
        psTb = psT[:].bitcast(BF16)
        lns = {}

        def A0(t):
            P.dma("sp", qm[t % 3][:], k.qmT[:, t * 128:(t + 1) * 128].rearrange("(c p) t -> p c t", p=128), reads=[k.qmT],
                  writes=[qm[t % 3]])

        def A1(t):
            for h in range(4):
                bank = psS[t % 2][h % 2]
                p0 = 64 * (h % 2)
                P.I("pe", "matmul", bank[:, (h // 2) * 256:(h // 2 + 1) * 256], lhsT=qm[t % 3][p0:p0 + 64, h // 2, :],
                    rhs=kmT[p0:p0 + 64, h // 2, :], start=True, stop=True, reads=[qm[t % 3], kmT], writes=[(bank, h // 2)])

        def A2(t):
            b2 = t % 2
            for hb in range(2):
                P.I("dve", "tensor_reduce", out=mxv[b2][:, 2 * hb:2 * hb + 2], in_=psS[b2][hb][:].rearrange("p (h m) -> p h m", m=256),
                    axis=AX.X, op=ALU.max, reads=[psS[b2][hb]], writes=[(mxv[b2], hb)])
            P.I("dve", "tensor_single_scalar", out=nb_[b2][:], in_=mxv[b2][:], scalar=-0.125, op=ALU.mult,
                reads=[mxv[b2]], writes=[nb_[b2]])
            for h in range(4):
                q = (h % 2) * 2 + h // 2
                P.I("act", "activation", out=E[b2][:, h, :], in_=psS[b2][h % 2][:, (h // 2) * 256:(h // 2 + 1) * 256], func=AF.Exp,
                    scale=0.125, bias=nb_[b2][:, q:q + 1], accum_out=ssum[b2][:, h:h + 1],
                    reads=[psS[b2][h % 2], nb_[b2]], writes=[(E[b2], h), (ssum[b2], h)])

        def A3(t):
            b2 = t % 2
            P.I("dve", "reciprocal", out=rsum[b2][:], in_=ssum[b2][:], reads=[ssum[b2]], writes=[rsum[b2]])
            P.I("dve", "tensor_tensor", out=Pb[b2][:], in0=E[b2][:], in1=rsum[b2][:].unsqueeze(2).to_broadcast([128, 4, 256]),
                op=ALU.mult, reads=[E[b2], rsum[b2]], writes=[Pb[b2]])

        def A4(t):
            b2 = t % 2
            P.dma("sp", mx[t % 3][:], k.mixT[0:768, t * 128:(t + 1) * 128].rearrange("(c p) t -> p c t", p=128), reads=[k.mixT],
                  writes=[mx[t % 3]])
            for blk in range(8):
                P.I("pe", "transpose", out=psTb[:, blk * 128:(blk + 1) * 128], in_=Pb[b2][:, blk // 2, (blk % 2) * 128:(blk % 2 + 1) * 128],
                    identity=k.ident[:], reads=[Pb[b2], k.ident], writes=[(psT, blk)])
            P.I("act", "activation", out=PT[b2][:], in_=psTb[:, 0:1024].rearrange("p (c t) -> p c t", t=128), func=AF.Copy,
                reads=[psT], writes=[PT[b2]])

        def A5(t):
            b2 = t % 2
            P.dma("sp", xr[t % 4][:], xin[t * 128:(t + 1) * 128, :], reads=[xin], writes=[xr[t % 4]])
            for h in range(4):
                p0 = 64 * (h % 2)
                for mc in range(2):
                    P.I("pe", "matmul", psMO[p0:p0 + 64, (h // 2) * 128:(h // 2 + 1) * 128], lhsT=vm[:, mc, h * 64:(h + 1) * 64],
                        rhs=PT[b2][:, h * 2 + mc, :], start=(mc == 0), stop=(mc == 1), reads=[vm, PT[b2]], writes=[(psMO, h)])
            P.I("dve", "tensor_copy", out=mmT[b2][:], in_=psMO[:, 0:256].rearrange("p (c t) -> p c t", t=128), reads=[psMO],
                writes=[mmT[b2]])

        def A6(t):
            for nh in range(2):
                for c in range(8):
                    lhsT = mx[t % 3][:, c, :] if c < 6 else mmT[t % 2][:, c - 6, :]
                    P.I("pe", "matmul", psY[nh][:], lhsT=lhsT, rhs=wout[:, c, nh * 512:(nh + 1) * 512], start=(c == 0), stop=(c == 7),
                        reads=[mx[t % 3], mmT[t % 2], wout], writes=[psY[nh]])

        def A7(t):
            x_ = xr[t % 4]
            for nh in range(2):
                P.I("dve", "scalar_tensor_tensor", out=x_[:, nh * 512:(nh + 1) * 512], in0=x_[:, nh * 512:(nh + 1) * 512],
                    scalar=ALPHA, in1=psY[nh][:], op0=ALU.mult, op1=ALU.add, reads=[x_, psY[nh]], writes=[(x_, nh)])
            lns[t] = LN(P, k, x_, xo[t % 2], gam, bet, lnb[t % 2])
            lns[t].stats()
            lns[t].rstd()
            lns[t].nmr_()

        def A8(t):
            lns[t].norm()
            lns[t].affine()
            P.dma("sp", xout[t * 128:(t + 1) * 128, :], xo[t % 2][:], reads=[xo[t % 2]], writes=[(xout, t)])
            del lns[t]

        stages_ = [A1, A2, A3, A4, A5, A6, A7, A8]
        A0(0)
        for it in range(NT + len(stages_) - 1):
            if dw is not None and it % 2 == 0:
                issue_pending(P, dw, 1)
            for si in range(len(stages_) - 1, -1, -1):
                t = it - si
                if 0 <= t < NT:
                    stages_[si](t)
            if it + 1 < NT:
                A0(it + 1)
        if dw is not None:
            issue_pending(P, dw, 100)
        P.barrier()


def phase_D(P, k, L, xin, xout, w):
    with P.scope() as ls:
        WUP = w.WUP
        w.WDN[1] = P.sbuf("D_wdn1", [128, 16, 1024], BF16, ls)
        dsrc = k.w_down[L].rearrange("(c p) n -> p c n", p=128)
        issue_pending(P, w, 100)
        for c in range(0, 16, 4):
            P.dma("pool", w.WDN[1][:, c:c + 4, :], dsrc[:, 16 + c:20 + c, :], writes=[(w.WDN[1], c)])
        gam = P.sbuf("D_gam", [128, 1024], F32, ls)
        bet = P.sbuf("D_bet", [128, 1024], F32, ls)
        P.dma("sp", gam[:], k.ln_ffn_g[L:L + 1, :].partition_broadcast(128), writes=[gam])
        P.dma("sp", bet[:], k.ln_ffn_b[L:L + 1, :].partition_broadcast(128), writes=[bet])
        GT = 4
        NG = NT // GT
        NW = GT * 128
        xb = P.sbuf("D_xb", [128, GT, 1024], BF16, ls)
        xT = P.sbuf("D_xT", [128, 8, NW], BF16, ls)
        hT = P.sbuf("D_hT", [128, 32, NW], BF16, ls)
        rl = [P.sbuf("D_rl%d" % i, [128, NW], F32, ls) for i in range(2)]
        xr = [P.sbuf("D_xr%d" % i, [128, 1024], F32, ls) for i in range(2)]
        lnb = [lnbufs(P, "D_ln%d" % i, ls) for i in range(2)]
        psH = [k.ps[2], k.ps[3]]
        psY = [[k.ps[4], k.ps[5]], [k.ps[6], k.ps[7]]]
        st_ = dict(cp=0)

        def load_x(g):
            P.dma("pool", xb[:], xin[g * NW:(g + 1) * NW, :].rearrange("(j p) d -> p j d", p=128), reads=[xin], writes=[xb])

        def transposes(g):
            for j in range(GT):
                transpose_tile(P, k, xb, (lambda j: lambda blk: xb[:, j, blk * 128:(blk + 1) * 128])(j), 8, k.ps[j % 2],
                               xT, xT[:, :, j * 128:(j + 1) * 128], st_["cp"], dkey=j)
                st_["cp"] += 1

        pend = []

        def tail2():
            while pend:
                ln, tt, x_ = pend.pop(0)
                ln.norm()
                ln.affine()
                P.dma("sp", xout[tt * 128:(tt + 1) * 128, :], x_[:], reads=[x_], writes=[(xout, tt)])

        load_x(0)
        transposes(0)
        tl = 0
        for g in range(NG):
            if g + 1 < NG:
                load_x(g + 1)
            for fc in range(32):
                bank = psH[fc % 2]
                for c in range(8):
                    P.I("pe", "matmul", bank[:], lhsT=WUP[:, c, fc * 128:(fc + 1) * 128], rhs=xT[:, c, :], start=(c == 0), stop=(c == 7),
                        reads=[(WUP, c), xT], writes=[bank])
                r_ = rl[fc % 2]
                P.I("act", "activation", out=r_[:], in_=bank[:], func=AF.Relu, reads=[bank], writes=[r_])
                P.I("dve", "tensor_tensor", out=hT[:, fc, :], in0=r_[:], in1=r_[:], op=ALU.mult, reads=[r_], writes=[(hT, fc)])
            if g + 1 < NG:
                transposes(g + 1)
            for j in range(GT):
                tt = g * GT + j
                b = tl % 2
                tl += 1
                tail2()
                x_ = xr[b]
                P.dma("sp", x_[:], xin[tt * 128:(tt + 1) * 128, :], reads=[xin], writes=[x_])
                py = psY[b]
                for nh in range(2):
                    for fc in range(32):
                        wd = w.WDN[fc // 16]
                        P.I("pe", "matmul", py[nh][:], lhsT=hT[:, fc, j * 128:(j + 1) * 128], rhs=wd[:, fc % 16, nh * 512:(nh + 1) * 512],
                            start=(fc == 0), stop=(fc == 31), reads=[hT, (wd, (fc % 16) // 4 * 4)], writes=[py[nh]])
                for nh in range(2):
                    P.I("dve", "scalar_tensor_tensor", out=x_[:, nh * 512:(nh + 1) * 512], in0=x_[:, nh * 512:(nh + 1) * 512],
                        scalar=ALPHA, in1=py[nh][:], op0=ALU.mult, op1=ALU.add, reads=[x_, py[nh]], writes=[(x_, nh)])
                ln = LN(P, k, x_, x_, gam, bet, lnb[b])
                ln.stats()
                ln.rstd()
                ln.nmr_()
                pend.append((ln, tt, x_))
        tail2()
        P.barrier()


def phase_E(P, k):
    with P.scope() as ls:
        W1 = P.sbuf("E_W1", [128, 8, 3328], BF16, ls)
        wsrc = k.w_in_hg[0].rearrange("(c p) n -> p c n", p=128)
        for c in range(8):
            P.dma("pool", W1[:, c, :], wsrc[:, c, :], writes=[(W1, c)])
        l0 = P.sbuf("E_l0", [128, 6], F32, ls)
        l1 = P.sbuf("E_l1", [128, 6], F32, ls)
        for h in range(6):
            P.dma("sp", l0[:, h:h + 1], k.lower_bounds[0:1, h * 128:(h + 1) * 128].rearrange("o p -> p o"), writes=[(l0, h)])
            P.dma("sp", l1[:, h:h + 1], k.lower_bounds[1:2, h * 128:(h + 1) * 128].rearrange("o p -> p o"), writes=[(l1, h)])
        lb = P.sbuf("E_lb", [128, 6], F32, ls)
        omlb = P.sbuf("E_omlb", [128, 6], F32, ls)
        nomlb = P.sbuf("E_nomlb", [128, 6], F32, ls)
        ltmp = P.sbuf("E_ltmp", [128, 6], F32, ls)
        P.I("dve", "tensor_tensor", out=ltmp[:], in0=l0[:], in1=l1[:], op=ALU.subtract, reads=[l0, l1], writes=[ltmp])
        P.I("act", "activation", out=ltmp[:], in_=ltmp[:], func=AF.Exp, reads=[ltmp], writes=[ltmp])
        P.I("dve", "tensor_single_scalar", out=lb[:], in_=ltmp[:], scalar=1.0, op=ALU.add, reads=[ltmp], writes=[lb])
        P.I("dve", "reciprocal", out=lb[:], in_=lb[:], reads=[lb], writes=[lb])
        P.I("dve", "tensor_tensor", out=omlb[:], in0=ltmp[:], in1=lb[:], op=ALU.mult, reads=[ltmp, lb], writes=[omlb])
        P.I("dve", "tensor_single_scalar", out=nomlb[:], in_=omlb[:], scalar=-1.0, op=ALU.mult, reads=[omlb], writes=[nomlb])
        gn = P.sbuf("E_gn", [128, 768], F32, ls)
        P.dma("sp", gn[:], k.hg_norm_g[0:1, :].partition_broadcast(128), writes=[gn])
        epsr = P.sbuf("E_epsr", [128, 1], F32, ls)
        P.I("dve", "memset", epsr[:], RMS_EPS, writes=[epsr])
        state = P.sbuf("E_state", [128, 6, 128], F32, ls)
        sbf = [P.sbuf("E_sbf%d" % i, [128, 6, 128], BF16, ls) for i in range(2)]
        P.I("dve", "memset", state[:], 0.0, writes=[state])
        P.I("dve", "memset", sbf[0][:], 0.0, writes=[sbf[0]])
        xb = [P.sbuf("E_xb%d" % i, [128, 4, 1024], BF16, ls) for i in range(1)]
        xT = [P.sbuf("E_xT%d" % i, [128, 8, 512], BF16, ls) for i in range(1)]
        vbf = [P.sbuf("E_v%d" % i, [128, 4, 768], BF16, ls) for i in range(1)]
        sg = [P.sbuf("E_sg%d" % i, [128, 4, 768], F32, ls) for i in range(1)]
        qst = [P.sbuf("E_qst%d" % i, [128, 512], BF16, ls) for i in range(1)]
        A = [[P.sbuf("E_A%d_%d" % (s_, i), [128, 512], F32, ls) for i in range(5)] for s_ in range(3)]
        QT = [P.sbuf("E_QT%d" % i, [128, 6, 512], BF16, ls) for i in range(1)]
        KT = [P.sbuf("E_KT%d" % i, [128, 6, 512], BF16, ls) for i in range(1)]
        KD = [P.sbuf("E_KD%d" % i, [128, 6, 512], BF16, ls) for i in range(1)]
        EGL = [P.sbuf("E_EGL%d" % i, [128, 6, 8], F32, ls) for i in range(1)]
        KDA = [P.sbuf("E_KDA%d" % i, [128, 6, 128], BF16, ls) for i in range(2)]
        KDB = [P.sbuf("E_KDB%d" % i, [128, 6, 128], BF16, ls) for i in range(2)]
        for i in range(2):
            P.I("pool", "memset", KDA[i][:], 0.0, writes=[KDA[i]])
            P.I("pool", "memset", KDB[i][:], 0.0, writes=[KDB[i]])
        SCM = [P.sbuf("E_SCM%d" % i, [128, 6, 128], BF16, ls) for i in range(2)]
        junk = P.sbuf("E_junk", [128, 128], F32, ls)
        o32 = [P.sbuf("E_o32_%d" % i, [128, 768], F32, ls) for i in range(2)]
        ss = [P.sbuf("E_ss%d" % i, [128, 6], F32, ls) for i in range(2)]
        rstd = [P.sbuf("E_rstd%d" % i, [128, 6], F32, ls) for i in range(2)]
        t1 = [P.sbuf("E_t1_%d" % i, [128, 768], F32, ls) for i in range(1)]
        mixb = [P.sbuf("E_mixb%d" % i, [128, 768], BF16, ls) for i in range(1)]
        mst = [P.sbuf("E_mst%d" % i, [128, 6, 128], BF16, ls) for i in range(1)]
        psSC = [k.ps[0], k.ps[1]]
        psO = [k.ps[2], k.ps[3]]
        psKV = [k.ps[4], k.ps[5]]
        rotb = [k.ps[6], k.ps[7]]
        st_ = dict(rot=0, cp=0, hs=0, sb=0, tj=0)

        def nbank():
            st_["rot"] += 1
            return rotb[st_["rot"] % 2]

        pend_tail = []

        def flush_tail():
            while pend_tail:
                pend_tail.pop(0)()

        def hsl(bank_list, hd):
            return bank_list[hd // 4][:, (hd % 4) * 128:(hd % 4 + 1) * 128], bank_list[hd // 4]

        CUTE = 9.0
        for g in range(8 if CUTE >= 9 else 1):
            if CUTE < 2:
                break
            gb = g % 2
            xbg, xTg, vg, sgg = xb[0], xT[0], vbf[0], sg[0]
            P.dma("pool", xbg[:], k.xmid[g * 512:(g + 1) * 512, :].rearrange("(j p) d -> p j d", p=128), reads=[k.xmid], writes=[xbg])
            for j in range(4):
                transpose_tile(P, k, xbg, (lambda xbg, j: lambda b: xbg[:, j, b * 128:(b + 1) * 128])(xbg, j), 8, nbank(),
                               xTg, xTg[:, :, j * 128:(j + 1) * 128], st_["cp"], dkey=j)
                st_["cp"] += 1
            if CUTE < 2.2:
                continue
            for j in range(4):
                for n in range(3):
                    bank = nbank()
                    for c in range(8):
                        P.I("pe", "matmul", bank[:], lhsT=xTg[:, c, j * 128:(j + 1) * 128], rhs=W1[:, c, 1536 + 512 * n:2048 + 512 * n],
                            start=(c == 0), stop=(c == 7), reads=[W1, xTg], writes=[bank])
                    if n == 0:
                        _copy(P, k, 0, vg[:, j, 0:512], bank[:], reads=[bank], writes=[(vg, (j, 0))])
                    elif n == 1:
                        _copy(P, k, 0, vg[:, j, 512:768], bank[:, 0:256], reads=[bank], writes=[(vg, (j, 1))])
                        _copy(P, k, 1, sgg[:, j, 0:256], bank[:, 256:512], reads=[bank], writes=[(sgg, (j, 0))])
                    else:
                        _copy(P, k, 1, sgg[:, j, 256:768], bank[:], reads=[bank], writes=[(sgg, (j, 1))])
            if CUTE < 2.3:
                continue
            for fc in (24, 25):
                bank = nbank()
                for c in range(8):
                    P.I("pe", "matmul", bank[:], lhsT=W1[:, c, fc * 128:(fc + 1) * 128], rhs=xTg[:, c, :], start=(c == 0), stop=(c == 7),
                        reads=[W1, xTg], writes=[bank])
                s_ = qst[0]
                _copy(P, k, 1, s_[:], bank[:], reads=[bank], writes=[s_])
                P.dma("sp", k.qmT[(fc - 24) * 128:(fc - 23) * 128, g * 512:(g + 1) * 512], s_[:], reads=[s_], writes=[(k.qmT, (fc, g))])
            if CUTE < 2.4:
                continue
            P.I("act", "activation", out=sgg[:], in_=sgg[:], func=AF.Silu, reads=[sgg], writes=[sgg])
            P.I("pool", "tensor_tensor", out=sgg[:], in0=sgg[:], in1=gn[:].unsqueeze(1).to_broadcast([128, 4, 768]), op=ALU.mult,
                reads=[sgg, gn], writes=[sgg])
            if CUTE < 3:
                continue
            for trio in range(2):
                hds = [trio * 3 + i for i in range(3)]
                Fb = [k.ps[i] for i in range(3)]
                Qb = [k.ps[3 + i] for i in range(3)]
                for i, hd in enumerate(hds):
                    for c in range(8):
                        P.I("pe", "matmul", Fb[i][:], lhsT=W1[:, c, 768 + hd * 128:768 + (hd + 1) * 128], rhs=xTg[:, c, :], start=(c == 0),
                            stop=(c == 7), reads=[W1, xTg], writes=[Fb[i]])
                for i, hd in enumerate(hds):
                    a = A[i]
                    P.I("act", "activation", out=a[0][:], in_=Fb[i][:], func=AF.Exp, scale=-1.0, reads=[Fb[i]], writes=[a[0]])
                for i, hd in enumerate(hds):
                    for c in range(8):
                        P.I("pe", "matmul", Qb[i][:], lhsT=W1[:, c, hd * 128:(hd + 1) * 128], rhs=xTg[:, c, :], start=(c == 0), stop=(c == 7),
                            reads=[W1, xTg], writes=[Qb[i]])
                for i, hd in enumerate(hds):
                    a = A[i]
                    P.I("act", "activation", out=a[0][:], in_=a[0][:], func=AF.Ln, bias=1.0, reads=[a[0]], writes=[a[0]])
                for i, hd in enumerate(hds):
                    a = A[i]
                    P.I("act", "activation", out=a[1][:], in_=a[0][:], func=AF.Exp, scale=-1.0, reads=[a[0]], writes=[a[1]])
                for i, hd in enumerate(hds):
                    a = A[i]
                    P.I("act", "activation", out=a[2][:], in_=a[1][:], func=AF.Ln, scale=omlb[:, hd:hd + 1], bias=lb[:, hd:hd + 1],
                        reads=[a[1], omlb, lb], writes=[a[2]])
                    P.I("dve", "tensor_scalar", out=a[3][:], in0=a[1][:], scalar1=nomlb[:, hd:hd + 1], scalar2=omlb[:, hd:hd + 1],
                        op0=ALU.mult, op1=ALU.add, reads=[a[1], nomlb, omlb], writes=[a[3]])
                for i, hd in enumerate(hds):
                    a = A[i]
                    P.I("dve", "tensor_tensor_scan", out=a[4][:], data0=k.scanmask[:], data1=a[2][:], initial=0.0, op0=ALU.mult,
                        op1=ALU.add, reads=[k.scanmask, a[2]], writes=[a[4]])
                for i, hd in enumerate(hds):
                    a = A[i]
                    G3 = a[4][:].rearrange("p (c j) -> p c j", j=64)
                    P.I("act", "activation", out=a[0][:], in_=a[4][:], func=AF.Exp, reads=[a[4]], writes=[a[0]])
                    P.I("act", "activation", out=a[1][:], in_=a[4][:], func=AF.Exp, scale=-1.0, reads=[a[4]], writes=[a[1]])
                    P.I("dve", "tensor_tensor", out=a[2][:].rearrange("p (c j) -> p c j", j=64), in0=G3,
                        in1=G3[:, :, 63:64].to_broadcast([128, 8, 64]), op=ALU.subtract, reads=[a[4]], writes=[a[2]])
                for i, hd in enumerate(hds):
                    a = A[i]
                    G3 = a[4][:].rearrange("p (c j) -> p c j", j=64)
                    P.I("dve", "tensor_tensor", out=QT[0][:, hd, :], in0=Qb[i][:], in1=a[0][:], op=ALU.mult, reads=[Qb[i], a[0]],
                        writes=[(QT[0], hd)])
                    P.I("dve", "tensor_tensor", out=KT[0][:, hd, :], in0=a[3][:], in1=a[1][:], op=ALU.mult, reads=[a[3], a[1]],
                        writes=[(KT[0], hd)])
                    P.I("act", "activation", out=a[2][:], in_=a[2][:], func=AF.Exp, scale=-1.0, reads=[a[2]], writes=[a[2]])
                    P.I("act", "activation", out=EGL[0][:, hd, :], in_=G3[:, :, 63], func=AF.Exp, reads=[a[4]], writes=[(EGL[0], hd)])
                for i, hd in enumerate(hds):
                    a = A[i]
                    P.I("dve", "tensor_tensor", out=KD[0][:, hd, :], in0=a[3][:], in1=a[2][:], op=ALU.mult, reads=[a[3], a[2]],
                        writes=[(KD[0], hd)])
            if CUTE < 4:
                continue
            for j in range(4):
                tj = st_["tj"]
                st_["tj"] += 1
                jb = tj % 2
                tt = g * 4 + j
                jc = slice(j * 128, (j + 1) * 128)
                bT = nbank()
                bTb = bT[:].bitcast(BF16)
                for hd in range(6):
                    P.I("pe", "transpose", out=bTb[:, hd * 128:(hd + 1) * 128], in_=KD[0][:, hd, jc], identity=k.ident[:],
                        reads=[(KD[0], hd), k.ident], writes=[(bT, hd)])
                P.I("dve", "tensor_copy", out=KDA[jb][0:64, :, :], in_=bTb[0:64, 0:768].rearrange("p (h k) -> p h k", k=128),
                    reads=[bT], writes=[KDA[jb]])
                P.I("dve", "tensor_copy", out=KDB[jb][64:128, :, :], in_=bTb[64:128, 0:768].rearrange("p (h k) -> p h k", k=128),
                    reads=[bT], writes=[KDB[jb]])
                for hd in range(6):
                    osl, ob = hsl(psSC, hd)
                    P.I("pe", "matmul", osl, lhsT=KT[0][:, hd, jc], rhs=QT[0][:, hd, jc], start=True, stop=True,
                        reads=[(KT[0], hd), (QT[0], hd)], writes=[(ob, hd)])
                P.I("dve", "tensor_tensor", out=SCM[jb][:, 0:4, :], in0=psSC[0][:].rearrange("p (h t) -> p h t", t=128),
                    in1=k.blkmask[:].unsqueeze(1).to_broadcast([128, 4, 128]), op=ALU.mult, reads=[psSC[0], k.blkmask],
                    writes=[(SCM[jb], 0)])
                P.I("dve", "tensor_tensor", out=SCM[jb][:, 4:6, :], in0=psSC[1][:, 0:256].rearrange("p (h t) -> p h t", t=128),
                    in1=k.blkmask[:].unsqueeze(1).to_broadcast([128, 2, 128]), op=ALU.mult, reads=[psSC[1], k.blkmask],
                    writes=[(SCM[jb], 1)])
                if CUTE < 5:
                    continue
                sA, sB = sbf[0], sbf[1]
                for hd in range(6):
                    osl, ob = hsl(psO, hd)
                    vs_ = vg[:, j, hd * 128:(hd + 1) * 128]
                    P.I("pe", "matmul", osl, lhsT=SCM[jb][:, hd, :], rhs=vs_, start=(hd % 4 == 0), stop=False,
                        reads=[SCM[jb], vg], writes=[(ob, hd)])
                for hd in range(6):
                    osl, ob = hsl(psO, hd)
                    P.I("pe", "matmul", osl[0:64, :], lhsT=QT[0][:, hd, j * 128:j * 128 + 64], rhs=sA[:, hd, :], start=False, stop=False,
                        reads=[(QT[0], hd), (sA, hd)], writes=[(ob, hd)])
                for half, (KDx, s_src, s_dst) in enumerate(((KDA[jb], sA, sB), (KDB[jb], sB, sA))):
                    ch = j * 2 + half
                    for hd in range(6):
                        ksl, kb = hsl(psKV, hd)
                        P.I("pe", "matmul", ksl, lhsT=KDx[:, hd, :], rhs=vg[:, j, hd * 128:(hd + 1) * 128], start=True, stop=True,
                            reads=[KDx, vg], writes=[(kb, hd)])
                    if half == 0:
                        flush_tail()
                    for hd in range(6):
                        ksl, kb = hsl(psKV, hd)
                        P.I("dve", "scalar_tensor_tensor", out=state[:, hd, :], in0=state[:, hd, :], scalar=EGL[0][:, hd, ch:ch + 1],
                            in1=ksl, op0=ALU.mult, op1=ALU.add, reads=[(state, hd), (EGL[0], hd), (kb, hd)], writes=[(state, hd)])
                        if hd % 3 != 2:
                            P.I("dve", "tensor_copy", out=s_dst[:, hd, :], in_=state[:, hd, :], reads=[(state, hd)], writes=[(s_dst, hd)])
                        else:
                            P.I("pool", "tensor_copy", out=s_dst[:, hd, :], in_=state[:, hd, :], reads=[(state, hd)], writes=[(s_dst, hd)])
                    if half == 0:
                        for hd in range(6):
                            osl, ob = hsl(psO, hd)
                            P.I("pe", "matmul", osl[64:128, :], lhsT=QT[0][:, hd, j * 128 + 64:(j + 1) * 128], rhs=sB[:, hd, :],
                                start=False, stop=True, reads=[(QT[0], hd), (sB, hd)], writes=[(ob, hd)])
                ob_ = o32[tj % 2]
                P.I("act", "activation", out=ob_[:, 0:512], in_=psO[0][:], func=AF.Copy, reads=[psO[0]], writes=[(ob_, 0)])
                P.I("act", "activation", out=ob_[:, 512:768], in_=psO[1][:, 0:256], func=AF.Copy, reads=[psO[1]], writes=[(ob_, 1)])

                def tail(jb=jb, j=j, tt=tt, ob_=ob_, sgg=sgg):
                    for hd in range(6):
                        P.I("act", "activation", out=junk[:], in_=ob_[:, hd * 128:(hd + 1) * 128], func=AF.Square,
                            accum_out=ss[jb][:, hd:hd + 1], reads=[ob_], writes=[junk, (ss[jb], hd)])
                    P.I("act", "activation", out=rstd[jb][:], in_=ss[jb][:], func=AF.Ln, scale=1.0 / 128.0, bias=epsr[:, 0:1],
                        reads=[ss[jb], epsr], writes=[rstd[jb]])
                    P.I("act", "activation", out=rstd[jb][:], in_=rstd[jb][:], func=AF.Exp, scale=-0.5, reads=[rstd[jb]], writes=[rstd[jb]])
                    P.I("dve", "tensor_tensor", out=t1[0][:].rearrange("p (h v) -> p h v", v=128),
                        in0=ob_[:].rearrange("p (h v) -> p h v", v=128), in1=rstd[jb][:].unsqueeze(2).to_broadcast([128, 6, 128]),
                        op=ALU.mult, reads=[ob_, rstd[jb]], writes=[t1[0]])
                    P.I("dve", "tensor_tensor", out=mixb[0][:], in0=t1[0][:], in1=sgg[:, j, :], op=ALU.mult, reads=[t1[0], sgg],
                        writes=[mixb[0]])
                    transpose_tile(P, k, mixb[0], lambda blk: mixb[0][:, blk * 128:(blk + 1) * 128], 6, nbank(), mst[0], mst[0][:], 1)
                    P.dma("sp", k.mixT[0:768, tt * 128:(tt + 1) * 128].rearrange("(c p) t -> p c t", p=128), mst[0][:], reads=[mst[0]],
                          writes=[(k.mixT, tt)])
                pend_tail.append(tail)
            flush_tail()
        P.barrier()


WSHAPES = dict(w_in_sb=[1, 1024, 2560], w_in_hg=[1, 1024, 3328], w_mem_kv=[2, 1024, 512], lower_bounds=[2, 768],
               hg_norm_g=[1, 768], w_out=[2, 1024, 1024], ln_mix_g=[2, 1024], ln_mix_b=[2, 1024],
               w_up=[2, 1024, 4096], w_down=[2, 4096, 1024], ln_ffn_g=[2, 1024], ln_ffn_b=[2, 1024])


def build(stages="ABCDE", debug=False):
    nc = bass.Bass("TRN2", target_bir_lowering=False)
    with ExitStack() as es:
        P = Prog(nc, es)
        k = K()
        k.x = P.dram("x", [S, D], F32, kind="ExternalInput")
        k.mem = P.dram("mem", [NMEM, D], F32, kind="ExternalInput")
        for name, shp in WSHAPES.items():
            setattr(k, name, P.dram(name, shp, F32, kind="ExternalInput"))
        skind = "ExternalOutput" if debug else "Internal"
        k.qT0 = P.dram("qT0", [768, S], BF16, kind=skind)
        k.kT0 = P.dram("kT0", [768, S], BF16, kind=skind)
        k.v0 = P.dram("v0", [S, 768], BF16, kind=skind)
        k.qmT = P.dram("qmT", [256, S], BF16, kind=skind)
        k.mixT = P.dram("mixT", [768, S], BF16, kind=skind)
        k.x1 = P.dram("x1", [S, D], F32, kind=skind)
        k.xmid = P.dram("xmid", [S, D], F32, kind=skind)
        k.y = P.dram("y", [S, D], F32, kind="ExternalOutput")
        build_consts(P, k)
        P.barrier()
        two = "E" in stages
        if "A" in stages:
            phase_A(P, k)
        scC = P.scope()
        cw = load_C_weights(P, k, 0, scC)
        if "B" in stages:
            phase_B(P, k)
        scD = P.scope()
        dw = alloc_D_weights(P, k, 0, scD)
        if "C" in stages:
            phase_C(P, k, 0, k.x, k.x1, cw, dw)
        scC.close()
        if "D" in stages:
            phase_D(P, k, 0, k.x1, k.xmid if two else k.y, dw)
        scD.close()
        if two:
            scC = P.scope()
            cw = load_C_weights(P, k, 1, scC)
            phase_E(P, k)
            scD = P.scope()
            dw = alloc_D_weights(P, k, 1, scD)
            phase_C(P, k, 1, k.xmid, k.x1, cw, dw)
            scC.close()
            phase_D(P, k, 1, k.x1, k.y, dw)
            scD.close()
        if "e" in stages:
            phase_E(P, k)
        P.finish()
        k.n_ins = P.n_ins
    return nc, k


_CACHE = {}


def kernel(**inputs):
    if "nc" not in _CACHE:
        _CACHE["nc"] = build("ABCDE")[0]
    nc = _CACHE["nc"]
    B = inputs["x"].shape[0]
    shared = {n: np.ascontiguousarray(inputs[n], dtype=np.float32) for n in WSHAPES}
    in_maps = []
    for b in range(B):
        m = dict(shared)
        m["x"] = np.ascontiguousarray(inputs["x"][b], dtype=np.float32)
        m["mem"] = np.ascontiguousarray(inputs["mem"][b], dtype=np.float32)
        in_maps.append(m)
    res = run_bass_kernel_spmd(nc, in_maps, core_ids=list(range(B)))
    return np.stack([np.asarray(r["y"], dtype=np.float32) for r in res.results], axis=0)
```

```python
from contextlib import ExitStack
import numpy as np
import concourse.bass as bass
import concourse.mybir as mybir
from concourse.bass_utils import run_bass_kernel_spmd

F32 = mybir.dt.float32
BF16 = mybir.dt.bfloat16
AF = mybir.ActivationFunctionType
ALU = mybir.AluOpType
AX = mybir.AxisListType

COMPUTE = ("pe", "act", "dve", "pool", "sp")
DMAQ = ("sp", "pool", "act")
NDMA_SEMS = 12


class Buf:
    def __init__(self, t, name, psum=False):
        self.t = t
        self.name = name
        self.psum = psum
        self.state = {}

    def __getitem__(self, idx):
        return self.t[idx]


class Scope:
    def __init__(self, prog):
        self.prog = prog
        self.bufs = []

    def __enter__(self):
        return self

    def __exit__(self, *a):
        self.close()
        return False

    def close(self):
        for b in self.bufs:
            self.prog.free(b)
        self.bufs = []


class Op:
    __slots__ = ("idx", "eng", "fn", "deps", "is_dma", "sig", "dma_slot", "dma_val", "eidx", "vc", "dmaknown")

    def __init__(self, idx, eng, fn, is_dma):
        self.idx = idx
        self.eng = eng
        self.fn = fn
        self.deps = set()
        self.is_dma = is_dma
        self.sig = 0
        self.eidx = 0
        self.vc = None
        self.dmaknown = None


class Prog:
    def __init__(self, nc, es):
        self.nc = nc
        self.es = es
        self.ops = []
        self.bufs = []
        self.engs = {"pe": nc.tensor, "act": nc.scalar, "dve": nc.vector, "pool": nc.gpsimd, "sp": nc.sync}
        self.sems = {e: es.enter_context(nc.semaphore("s_" + e)) for e in COMPUTE}
        self.dsems = {q: [es.enter_context(nc.semaphore("d_%s%d" % (q, i))) for i in range(NDMA_SEMS)]
                      for q in DMAQ}
        self.barrier_deps = {e: set() for e in self.engs}
        self.emitted = 0
        self.phase_start = 0
        NE = len(COMPUTE)
        self.cur_vc = {e: [0] * NE for e in self.engs}
        self.cur_dma = {e: set() for e in self.engs}
        self.cnt = {e: 0 for e in COMPUTE}
        self.scount = {e: 0 for e in COMPUTE}
        self.dcount = {q: 0 for q in DMAQ}
        self.n_ins = 0

    ARENA_BYTES = 212736

    def _arena_init(self):
        self.arena = self.es.enter_context(self.nc.sbuf_tensor("arena", [128, self.ARENA_BYTES // 2], BF16))
        self.free_list = [(0, self.ARENA_BYTES)]

    def scope(self):
        return Scope(self)

    def sbuf(self, name, shape, dt, scope=None):
        if not hasattr(self, "arena"):
            self._arena_init()
        esz = 4 if dt == F32 else 2
        n = 1
        for d in shape[1:]:
            n *= d
        nbytes = (n * esz + 63) // 64 * 64
        for i, (off, sz) in enumerate(self.free_list):
            if sz >= nbytes:
                if sz == nbytes:
                    self.free_list.pop(i)
                else:
                    self.free_list[i] = (off + nbytes, sz - nbytes)
                break
        else:
            raise AssertionError("arena out of SBUF for %s %s; free=%s" % (name, shape, self.free_list))
        ap = self.arena[0:shape[0], off // 2:off // 2 + n * esz // 2]
        if dt == F32:
            ap = ap.bitcast(F32)
        if len(shape) == 3:
            ap = ap.rearrange("p (a b) -> p a b", b=shape[2])
        elif len(shape) != 2:
            raise AssertionError("2-D / 3-D only")
        b = Buf(ap, name)
        b.region = (off, nbytes)
        self.bufs.append(b)
        if scope is not None:
            scope.bufs.append(b)
        return b

    def free(self, b):
        off, nbytes = b.region
        b.region = None
        fl = sorted(self.free_list + [(off, nbytes)])
        merged = []
        for o, sz in fl:
            if merged and merged[-1][0] + merged[-1][1] == o:
                merged[-1] = (merged[-1][0], merged[-1][1] + sz)
            else:
                merged.append((o, sz))
        self.free_list = merged

    def psum(self, name, shape, dt):
        t = self.es.enter_context(self.nc.psum_tensor(name, list(shape), dt))
        b = Buf(t, name, psum=True)
        self.bufs.append(b)
        return b

    def dram(self, name, shape, dt, kind="Internal"):
        t = self.nc.dram_tensor(name, list(shape), dt, kind=kind)
        b = Buf(t, name)
        self.bufs.append(b)
        return b

    @staticmethod
    def _conf(k1, k2):
        return k1 is None or k2 is None or k1 == k2

    def _rec(self, eng, fn, reads, writes, is_dma):
        op = Op(len(self.ops), eng, fn, is_dma)
        self.ops.append(op)
        for r in reads:
            b, key = r if isinstance(r, tuple) else (r, None)
            for k2, st in b.state.items():
                if st[0] is not None and (self._conf(key, k2) or (b.psum and self.ops[st[0]].eng != eng)):
                    op.deps.add(st[0])
                if b.psum:
                    op.deps.update(r2 for r2 in st[1] if self.ops[r2].eng != eng)
        for w in writes:
            b, key = w if isinstance(w, tuple) else (w, None)
            for k2, st in b.state.items():
                if self._conf(key, k2):
                    if st[0] is not None:
                        op.deps.add(st[0])
                    op.deps.update(st[1])
                elif b.psum:
                    if st[0] is not None and self.ops[st[0]].eng != eng:
                        op.deps.add(st[0])
                    op.deps.update(r2 for r2 in st[1] if self.ops[r2].eng != eng)
        op.deps |= self.barrier_deps[eng]
        self.barrier_deps[eng] = set()
        for r in reads:
            b, key = r if isinstance(r, tuple) else (r, None)
            st = b.state.setdefault(key, [None, []])
            st[1].append(op.idx)
        for w in writes:
            b, key = w if isinstance(w, tuple) else (w, None)
            if key is None:
                b.state = {None: [op.idx, []]}
            else:
                b.state[key] = [op.idx, []]
        op.deps.discard(op.idx)
        return op

    def op(self, eng, fn, reads=(), writes=()):
        return self._rec(eng, fn, reads, writes, False)

    def I(self, eng, meth, *args, reads=(), writes=(), **kw):
        return self._rec(eng, lambda e: getattr(e, meth)(*args, **kw), reads, writes, False)

    def dma(self, q, out, in_, reads=(), writes=()):
        return self._rec(q, lambda e: e.dma_start(out=out, in_=in_), reads, writes, True)

    def barrier(self):
        tails = set()
        last = {}
        for op in self.ops[self.phase_start:]:
            if op.is_dma:
                tails.add(op.idx)
            else:
                last[op.eng] = op.idx
        tails |= set(last.values())
        bop = Op(len(self.ops), "sp", lambda e: e.nop(), False)
        bop.deps = tails | self.barrier_deps["sp"]
        self.ops.append(bop)
        bop.sig = 1
        self.flush()
        for e in self.engs:
            self.barrier_deps[e] = {bop.idx}
        for b in self.bufs:
            b.state = {}
        self.phase_start = len(self.ops)

    def flush(self):
        ops = self.ops
        NE = len(COMPUTE)
        eid = {e: i for i, e in enumerate(COMPUTE)}
        new = ops[self.emitted:]
        for op in new:
            if not op.is_dma:
                self.cnt[op.eng] += 1
                op.eidx = self.cnt[op.eng]
        needed = []
        for op in new:
            vc = self.cur_vc[op.eng]
            dk = self.cur_dma[op.eng]
            real = []
            for d in sorted(op.deps, reverse=True):
                dop = ops[d]
                if dop.is_dma:
                    if d in dk:
                        continue
                    real.append(d)
                    dk.add(d)
                else:
                    if dop.eng == "pe" and op.eng == "pe":
                        continue
                    j = eid[dop.eng]
                    if vc[j] >= dop.eidx:
                        continue
                    real.append(d)
                    vc[j] = dop.eidx
                dk |= dop.dmaknown
                dvc = dop.vc
                for i in range(NE):
                    if dvc[i] > vc[i]:
                        vc[i] = dvc[i]
            op.vc = list(vc)
            op.dmaknown = set(dk)
            needed.append(real)
            for d in real:
                if not ops[d].sig:
                    assert d >= self.emitted, "dependency on an already emitted non-signalling op"
                    ops[d].sig = 1
        for op in new:
            if op.is_dma:
                n = self.dcount[op.eng]
                self.dcount[op.eng] += 1
                op.dma_slot = n % NDMA_SEMS
                op.dma_val = 16 * (n // NDMA_SEMS + 1)
            elif op.sig:
                self.scount[op.eng] += 1
                op.sig = self.scount[op.eng]
        for op, real in zip(new, needed):
            e = self.engs[op.eng]
            if op.is_dma and op.dma_val > 16:
                e.wait_ge(self.dsems[op.eng][op.dma_slot], op.dma_val - 16)
                self.n_ins += 1
            for d in real:
                dop = ops[d]
                if dop.is_dma:
                    e.wait_ge(self.dsems[dop.eng][dop.dma_slot], dop.dma_val)
                else:
                    e.wait_ge(self.sems[dop.eng], dop.sig)
                self.n_ins += 1
            ins = op.fn(e)
            self.n_ins += 1
            if op.is_dma:
                ins.then_inc(self.dsems[op.eng][op.dma_slot], 16)
            elif op.sig:
                ins.then_inc(self.sems[op.eng], 1)
            op.fn = None
        self.emitted = len(ops)

    def finish(self):
        self.barrier()


S = 4096
D = 1024
NT = S // 128
DFF = 4096
SB_H = 12
HG_H = 6
NMEM = 256
ALPHA = float(4 ** 0.25)
LN_EPS = 1e-5
RMS_EPS = 1e-6
NEG = -30000.0


class K:
    pass


def _copy(P, k, idx, out, in_, reads, writes, scale=None):
    if idx % 2 == 0:
        if scale is None:
            P.op("act", lambda e: e.activation(out=out, in_=in_, func=AF.Copy), reads=reads, writes=writes)
        else:
            P.op("act", lambda e: e.activation(out=out, in_=in_, func=AF.Copy, scale=scale), reads=reads, writes=writes)
    else:
        if scale is None:
            P.op("dve", lambda e: e.tensor_copy(out=out, in_=in_), reads=reads, writes=writes)
        else:
            P.op("dve", lambda e: e.tensor_single_scalar(out=out, in_=in_, scalar=scale, op=ALU.mult),
                 reads=reads, writes=writes)


def build_consts(P, k):
    k.ident = P.sbuf("ident", [128, 128], BF16)
    k.negtri = P.sbuf("negtri", [128, 128], BF16)
    k.negones = P.sbuf("negones", [128, 128], BF16)
    k.maskneg = P.sbuf("maskneg", [128, 128], BF16)
    k.blkmask = P.sbuf("blkmask", [128, 128], F32)
    k.scanmask = P.sbuf("scanmask", [128, 512], F32)
    tmp = P.sbuf("ctmp", [128, 128], F32)
    tmp1 = P.sbuf("ctmp1", [128, 128], F32)
    P.op("dve", lambda e: e.memset(tmp[:], 0.0), writes=[tmp])
    P.op("pool", lambda e: e.affine_select(out=tmp[:], in_=tmp[:], pattern=[[-1, 128]], compare_op=ALU.not_equal,
                                            fill=1.0, base=0, channel_multiplier=1), reads=[tmp], writes=[tmp])
    P.op("dve", lambda e: e.tensor_copy(out=k.ident[:], in_=tmp[:]), reads=[tmp], writes=[k.ident])
    P.op("dve", lambda e: e.memset(tmp1[:], 0.0), writes=[tmp1])
    P.op("pool", lambda e: e.affine_select(out=tmp1[:], in_=tmp1[:], pattern=[[1, 128]], compare_op=ALU.is_gt,
                                            fill=-1.0, base=0, channel_multiplier=-1), reads=[tmp1], writes=[tmp1])
    P.op("dve", lambda e: e.tensor_copy(out=k.negtri[:], in_=tmp1[:]), reads=[tmp1], writes=[k.negtri])
    P.op("dve", lambda e: e.tensor_single_scalar(out=k.maskneg[:], in_=tmp1[:], scalar=-NEG, op=ALU.mult),
         reads=[tmp1], writes=[k.maskneg])
    P.op("dve", lambda e: e.memset(k.negones[:], -1.0), writes=[k.negones])
    P.op("dve", lambda e: e.memset(k.blkmask[:], 1.0), writes=[k.blkmask])
    P.op("pool", lambda e: e.affine_select(out=k.blkmask[:], in_=k.blkmask[:], pattern=[[1, 128]], compare_op=ALU.is_ge,
                                            fill=0.0, base=0, channel_multiplier=-1), reads=[k.blkmask], writes=[k.blkmask])
    P.op("dve", lambda e: e.memset(k.blkmask[0:64, 64:128], 0.0), reads=[k.blkmask], writes=[k.blkmask])
    P.op("dve", lambda e: e.memset(k.scanmask[:], 1.0), writes=[k.scanmask])
    P.op("dve", lambda e: e.memset(k.scanmask[:].rearrange("p (c j) -> p c j", j=64)[:, :, 0:1], 0.0),
         reads=[k.scanmask], writes=[k.scanmask])
    k.epsln = P.sbuf("epsln", [128, 1], F32)
    P.op("dve", lambda e: e.memset(k.epsln[:], LN_EPS), writes=[k.epsln])
    k.pspair = [P.es.enter_context(P.nc.psum_tensor("pspair%d" % i, [128, 1024], F32)) for i in range(4)]
    k.ps = []
    for i in range(8):
        b = Buf(k.pspair[i // 2][:, (i % 2) * 512:(i % 2 + 1) * 512], "psb%d" % i, psum=True)
        P.bufs.append(b)
        k.ps.append(b)


def transpose_tile(P, k, src, src_ap_fn, nblk, psbank, dst, dst_ap, cidx, extra_reads=(), dkey=None):
    psb = psbank[:].bitcast(BF16)
    for b in range(nblk):
        P.I("pe", "transpose", out=psb[:, b * 128:(b + 1) * 128], in_=src_ap_fn(b), identity=k.ident[:],
            reads=[src, k.ident] + list(extra_reads), writes=[(psbank, b)])
    _copy(P, k, cidx, dst_ap, psb[:, 0:nblk * 128] if dst_ap.ndim == 2 else
          psb[:, 0:nblk * 128].rearrange("p (c t) -> p c t", t=128), reads=[psbank], writes=[(dst, dkey)])


def phase_A(P, k):
    with P.scope() as ls:
        W0 = P.sbuf("W0", [128, 8, 2560], BF16, ls)
        wsrc = k.w_in_sb[0].rearrange("(c p) n -> p c n", p=128)
        for c in range(8):
            P.dma("pool", W0[:, c, :], wsrc[:, c, :], reads=[], writes=[(W0, c)])
        xb = [P.sbuf("A_xb%d" % i, [128, 4, 1024], BF16, ls) for i in range(2)]
        xT = [P.sbuf("A_xT%d" % i, [128, 8, 512], BF16, ls) for i in range(2)]
        st = [P.sbuf("A_st%d" % i, [128, 512], BF16, ls) for i in range(4)]
        vst = [P.sbuf("A_vst%d" % i, [128, 4, 768], BF16, ls) for i in range(2)]
        cp = 0
        sti = 0
        rot = 0
        for g in range(8):
            xbg = xb[g % 2]
            xTg = xT[g % 2]
            P.dma("pool", xbg[:], k.x[g * 512:(g + 1) * 512, :].rearrange("(j p) d -> p j d", p=128),
                  reads=[], writes=[xbg])
            for j in range(4):
                transpose_tile(P, k, xbg, (lambda xbg, j: lambda b: xbg[:, j, b * 128:(b + 1) * 128])(xbg, j), 8, k.ps[j % 2],
                               xTg, xTg[:, :, j * 128:(j + 1) * 128], cp, dkey=j)
                cp += 1
            for fc in list(range(12)) + [18, 19]:
                bank = k.ps[2 + rot % 6]
                rot += 1
                for c in range(8):
                    P.I("pe", "matmul", bank[:], lhsT=W0[:, c, fc * 128:(fc + 1) * 128], rhs=xTg[:, c, :],
                        start=(c == 0), stop=(c == 7), reads=[W0, xTg], writes=[bank])
                s_ = st[sti % 4]
                sti += 1
                _copy(P, k, cp, s_[:], bank[:], reads=[bank], writes=[s_], scale=(0.125 if fc < 6 else None))
                cp += 1
                if fc < 6:
                    dst = k.qT0[fc * 128:(fc + 1) * 128, g * 512:(g + 1) * 512]
                    dbuf = k.qT0
                elif fc < 12:
                    dst = k.kT0[(fc - 6) * 128:(fc - 5) * 128, g * 512:(g + 1) * 512]
                    dbuf = k.kT0
                else:
                    dst = k.qmT[(fc - 18) * 128:(fc - 17) * 128, g * 512:(g + 1) * 512]
                    dbuf = k.qmT
                P.dma("sp", dst, s_[:], reads=[s_], writes=[(dbuf, (fc, g))])
            vs = vst[g % 2]
            for j in range(4):
                for (c0, n) in ((0, 512), (512, 256)):
                    bank = k.ps[2 + rot % 6]
                    rot += 1
                    for c in range(8):
                        P.I("pe", "matmul", bank[:, 0:n], lhsT=xTg[:, c, j * 128:(j + 1) * 128],
                            rhs=W0[:, c, 1536 + c0:1536 + c0 + n], start=(c == 0), stop=(c == 7),
                            reads=[W0, xTg], writes=[bank])
                    _copy(P, k, cp, vs[:, j, c0:c0 + n], bank[:, 0:n], reads=[bank], writes=[(vs, (j, c0))])
                    cp += 1
            P.dma("sp", k.v0[g * 512:(g + 1) * 512, :].rearrange("(j p) f -> p j f", p=128), vs[:],
                  reads=[vs], writes=[(k.v0, g)])
        P.barrier()


def phase_B(P, k):
    with P.scope() as ls:
        kTs = [P.sbuf("B_kT%d" % i, [128, S], BF16, ls) for i in range(2)]
        qTs = [P.sbuf("B_qT%d" % i, [128, S], BF16, ls) for i in range(2)]
        vvs = [P.sbuf("B_v%d" % i, [128, NT, 128], BF16, ls) for i in range(2)]
        ebuf = [P.sbuf("B_e%d" % i, [128, 2, 512], F32, ls) for i in range(2)]
        spb = [P.sbuf("B_sp%d" % i, [128, 2, 512], BF16, ls) for i in range(3)]
        wb = [P.sbuf("B_w%d" % i, [128, 2, 512], BF16, ls) for i in range(3)]
        Rf = [P.sbuf("B_R%d" % i, [128, 2, 512], F32, ls) for i in range(2)]
        Rb = [P.sbuf("B_Rb%d" % i, [128, 2, 512], BF16, ls) for i in range(2)]
        ost = [P.sbuf("B_ost%d" % i, [128, 512], BF16, ls) for i in range(2)]

        def pair(pi):
            return k.pspair[pi][:].rearrange("p (b c) -> p b c", c=512), [k.ps[2 * pi], k.ps[2 * pi + 1]]

        psz = pair(0)
        pse = [pair(1), pair(2)]
        pso = [k.ps[6], k.ps[7]]
        units = []
        for hp in range(6):
            for g in range(8):
                for n in range(4 * g + 4):
                    i = 4 * g + 3 - n
                    col0 = max(0, (i - 4 * g) * 128)
                    units.append(dict(hp=hp, g=g, n=n, i=i, col0=col0, diag=(i >= 4 * g), first=(n == 0), last=(i == 0),
                                      gi=hp * 8 + g))
        NU = len(units)
        loaded = [-1]

        def load_pair(hp):
            if loaded[0] >= hp or hp >= 6:
                return
            loaded[0] = hp
            kT, qT, vv = kTs[hp % 2], qTs[hp % 2], vvs[hp % 2]
            P.dma("sp", kT[:], k.kT0[hp * 128:(hp + 1) * 128, :], reads=[k.kT0], writes=[kT])
            P.dma("sp", qT[:], k.qT0[hp * 128:(hp + 1) * 128, :], reads=[k.qT0], writes=[qT])
            P.dma("sp", vv[:], k.v0[:, hp * 128:(hp + 1) * 128].rearrange("(i p) f -> p i f", p=128),
                  reads=[k.v0], writes=[vv])

        def qk(u, pz, stop):
            ap, bufs = pz
            c0, q0, i = u["col0"], u["g"] * 512, u["i"]
            kT, qT = kTs[u["hp"] % 2], qTs[u["hp"] % 2]
            for hh in range(2):
                b0 = 64 * hh
                P.I("pe", "matmul", ap[:, hh, c0:512], lhsT=kT[b0:b0 + 64, i * 128:(i + 1) * 128],
                    rhs=qT[b0:b0 + 64, q0 + c0:q0 + 512], start=True, stop=stop, reads=[kT, qT], writes=[bufs[hh]])

        def maskmm(u, pz):
            ap, bufs = pz
            c0 = u["col0"]
            for hh in range(2):
                P.I("pe", "matmul", ap[:, hh, c0:c0 + 128], lhsT=k.ident[:], rhs=k.maskneg[:], start=False, stop=True,
                    reads=[k.ident, k.maskneg], writes=[bufs[hh]])

        def st_z(idx):
            u = units[idx]
            load_pair(u["hp"])
            load_pair(u["hp"] + 1)
            qk(u, psz, not u["diag"])
            if u["diag"]:
                maskmm(u, psz)

        def st_E(idx):
            u = units[idx]
            c0 = u["col0"]
            eb = ebuf[idx % 2]
            P.I("act", "activation", out=eb[:, :, c0:512], in_=psz[0][:, :, c0:512], func=AF.Exp, reads=psz[1], writes=[eb])

        def st_SP(idx):
            u = units[idx]
            c0 = u["col0"]
            eb = ebuf[idx % 2]
            sp = spb[idx % 3]
            P.I("act", "activation", out=sp[:, :, c0:512], in_=eb[:, :, c0:512], func=AF.Ln, bias=1.0, reads=[eb], writes=[sp])

        def st_R(idx):
            u = units[idx]
            if u["last"]:
                return
            c0 = u["col0"]
            sp = spb[idx % 3]
            R = Rf[u["gi"] % 2]
            if u["first"]:
                P.I("pool", "memset", R[:], 0.0, writes=[R])
            P.I("dve", "tensor_tensor", out=R[:, :, c0:512], in0=R[:, :, c0:512], in1=sp[:, :, c0:512], op=ALU.add,
                reads=[R, sp], writes=[R])
            rb = Rb[idx % 2]
            P.I("dve", "tensor_copy", out=rb[:], in_=R[:], reads=[R], writes=[rb])

        def st_eg(idx):
            u = units[idx]
            c0 = u["col0"]
            pz = pse[idx % 2]
            ap, bufs = pz
            sp = spb[idx % 3]
            qk(u, pz, False)
            for hh in range(2):
                P.I("pe", "matmul", ap[:, hh, c0:512], lhsT=k.negtri[:], rhs=sp[:, hh, c0:512], start=False,
                    stop=(u["first"] and not u["diag"]), reads=[k.negtri, sp], writes=[bufs[hh]])
            if not u["first"]:
                rb = Rb[(idx - 1) % 2]
                for hh in range(2):
                    P.I("pe", "matmul", ap[:, hh, c0:512], lhsT=k.negones[:], rhs=rb[:, hh, c0:512], start=False,
                        stop=(not u["diag"]), reads=[k.negones, rb], writes=[bufs[hh]])
            if u["diag"]:
                maskmm(u, pz)

        def st_W(idx):
            u = units[idx]
            c0 = u["col0"]
            ap, bufs = pse[idx % 2]
            w = wb[idx % 3]
            P.I("act", "activation", out=w[:, :, c0:512], in_=ap[:, :, c0:512], func=AF.Exp, reads=bufs, writes=[w])

        def st_pv(idx):
            u = units[idx]
            c0, i = u["col0"], u["i"]
            w = wb[idx % 3]
            bank = pso[u["gi"] % 2]
            vv = vvs[u["hp"] % 2]
            for hh in range(2):
                P.I("pe", "matmul", bank[64 * hh:64 * hh + 64, c0:512], lhsT=vv[:, i, hh * 64:(hh + 1) * 64], rhs=w[:, hh, c0:512],
                    start=u["first"], stop=u["last"], reads=[vv, w], writes=[bank])
            if u["last"]:
                o = ost[u["gi"] % 2]
                hp, g = u["hp"], u["g"]
                P.I("dve", "tensor_copy", out=o[:], in_=bank[:], reads=[bank], writes=[o])
                P.dma("pool", k.mixT[hp * 128:(hp + 1) * 128, g * 512:(g + 1) * 512], o[:], reads=[o], writes=[(k.mixT, (hp, g))])

        st_z(0)
        for n in range(NU + 3):
            if 1 <= n <= NU:
                st_eg(n - 1)
            if 3 <= n:
                st_pv(n - 3)
            if n < NU:
                st_E(n)
            if n + 1 < NU:
                st_z(n + 1)
            if 2 <= n and n - 2 < NU:
                st_W(n - 2)
            if n < NU:
                st_SP(n)
                st_R(n)
        P.barrier()


class LN:
    def __init__(self, P, k, r, dst, gam, bet, bufs):
        self.P, self.k, self.r, self.dst, self.gam, self.bet = P, k, r, dst, gam, bet
        self.st, self.mv, self.rs, self.nmr = bufs

    def stats(self):
        P, r, st, mv = self.P, self.r, self.st, self.mv
        P.I("dve", "bn_stats", out=st[:, 0, :], in_=r[:, 0:512], reads=[r], writes=[(st, 0)])
        P.I("dve", "bn_stats", out=st[:, 1, :], in_=r[:, 512:1024], reads=[r], writes=[(st, 1)])
        P.I("dve", "bn_aggr", out=mv[:], in_=st[:].rearrange("p a b -> p (a b)"), reads=[st], writes=[mv])

    def rstd(self):
        P, k, mv, rs = self.P, self.k, self.mv, self.rs
        P.I("act", "activation", out=rs[:], in_=mv[:, 1:2], func=AF.Ln, bias=k.epsln[:, 0:1], reads=[mv, k.epsln], writes=[rs])
        P.I("act", "activation", out=rs[:], in_=rs[:], func=AF.Exp, scale=-0.5, reads=[rs], writes=[rs])

    def nmr_(self):
        P, mv, rs, nmr = self.P, self.mv, self.rs, self.nmr
        P.I("pool", "tensor_tensor", out=nmr[:], in0=mv[:, 0:1], in1=rs[:], op=ALU.mult, reads=[mv, rs], writes=[nmr])
        P.I("pool", "tensor_single_scalar", out=nmr[:], in_=nmr[:], scalar=-1.0, op=ALU.mult, reads=[nmr], writes=[nmr])

    def norm(self):
        P, r, rs, nmr, dst = self.P, self.r, self.rs, self.nmr, self.dst
        P.I("act", "activation", out=dst[:], in_=r[:], func=AF.Identity, scale=rs[:, 0:1], bias=nmr[:, 0:1],
            reads=[r, rs, nmr], writes=[dst])

    def affine(self):
        P, dst, gam, bet = self.P, self.dst, self.gam, self.bet
        P.I("pool", "tensor_tensor", out=dst[:], in0=dst[:], in1=gam[:], op=ALU.mult, reads=[dst, gam], writes=[dst])
        P.I("pool", "tensor_tensor", out=dst[:], in0=dst[:], in1=bet[:], op=ALU.add, reads=[dst, bet], writes=[dst])


def lnbufs(P, name, ls):
    return (P.sbuf(name + "st", [128, 2, 6], F32, ls), P.sbuf(name + "mv", [128, 2], F32, ls),
            P.sbuf(name + "rs", [128, 1], F32, ls), P.sbuf(name + "nm", [128, 1], F32, ls))


def load_C_weights(P, k, L, sc):
    w = K()
    w.wout = P.sbuf("C_wout", [128, 8, 1024], BF16, sc)
    wsrc = k.w_out[L].rearrange("(c p) n -> p c n", p=128)
    for c in range(0, 8, 2):
        P.dma("pool", w.wout[:, c:c + 2, :], wsrc[:, c:c + 2, :], writes=[(w.wout, c)])
    w.wkv = P.sbuf("C_wkv", [128, 8, 512], BF16, sc)
    P.dma("pool", w.wkv[:], k.w_mem_kv[L].rearrange("(c p) n -> p c n", p=128), writes=[w.wkv])
    w.memb = P.sbuf("C_memb", [128, 2, 1024], BF16, sc)
    P.dma("pool", w.memb[:], k.mem[:, :].rearrange("(j p) d -> p j d", p=128), writes=[w.memb])
    w.gam = P.sbuf("C_gam", [128, 1024], F32, sc)
    w.bet = P.sbuf("C_bet", [128, 1024], F32, sc)
    P.dma("sp", w.gam[:], k.ln_mix_g[L:L + 1, :].partition_broadcast(128), writes=[w.gam])
    P.dma("sp", w.bet[:], k.ln_mix_b[L:L + 1, :].partition_broadcast(128), writes=[w.bet])
    return w


def alloc_D_weights(P, k, L, sc):
    w = K()
    w.L = L
    w.WUP = P.sbuf("D_wup", [128, 8, DFF], BF16, sc)
    w.WDN = [P.sbuf("D_wdn0", [128, 16, 1024], BF16, sc), None]
    usrc = k.w_up[L].rearrange("(c p) f -> p c f", p=128)
    dsrc = k.w_down[L].rearrange("(c p) n -> p c n", p=128)
    w.pending = []
    for c in range(8):
        w.pending.append((w.WUP[:, c, :], usrc[:, c, :], (w.WUP, c)))
    for c in range(0, 16, 4):
        w.pending.append((w.WDN[0][:, c:c + 4, :], dsrc[:, c:c + 4, :], (w.WDN[0], c)))
    return w


def issue_pending(P, w, n):
    for _ in range(n):
        if w.pending:
            dst, src, wr = w.pending.pop(0)
            P.dma("pool", dst, src, writes=[wr])


def phase_C(P, k, L, xin, xout, w, dw=None):
    with P.scope() as ls:
        wout, wkv, memb, gam, bet = w.wout, w.wkv, w.memb, w.gam, w.bet
        memT = P.sbuf("C_memT", [128, 8, 256], BF16, ls)
        kmT = P.sbuf("C_kmT", [128, 2, 256], BF16, ls)
        vm = P.sbuf("C_vm", [128, 2, 256], BF16, ls)
        for j in range(2):
            transpose_tile(P, k, memb, (lambda j: lambda b: memb[:, j, b * 128:(b + 1) * 128])(j), 8, k.ps[j],
                           memT, memT[:, :, j * 128:(j + 1) * 128], j, dkey=j)
        for fc in range(2):
            bank = k.ps[2 + fc]
            for c in range(8):
                P.I("pe", "matmul", bank[:, 0:256], lhsT=wkv[:, c, fc * 128:(fc + 1) * 128], rhs=memT[:, c, :],
                    start=(c == 0), stop=(c == 7), reads=[wkv, memT], writes=[bank])
            _copy(P, k, fc, kmT[:, fc, :], bank[:, 0:256], reads=[bank], writes=[(kmT, fc)])
        for mc in range(2):
            bank = k.ps[4 + mc]
            for c in range(8):
                P.I("pe", "matmul", bank[:, 0:256], lhsT=memT[:, c, mc * 128:(mc + 1) * 128], rhs=wkv[:, c, 256:512],
                    start=(c == 0), stop=(c == 7), reads=[wkv, memT], writes=[bank])
            _copy(P, k, mc + 1, vm[:, mc, :], bank[:, 0:256], reads=[bank], writes=[(vm, mc)])

        qm = [P.sbuf("C_qm%d" % i, [128, 2, 128], BF16, ls) for i in range(3)]
        mx = [P.sbuf("C_mx%d" % i, [128, 6, 128], BF16, ls) for i in range(3)]
        xr = [P.sbuf("C_xr%d" % i, [128, 1024], F32, ls) for i in range(4)]
        E = [P.sbuf("C_E%d" % i, [128, 4, 256], F32, ls) for i in range(2)]
        Pb = [P.sbuf("C_P%d" % i, [128, 4, 256], BF16, ls) for i in range(2)]
        PT = [P.sbuf("C_PT%d" % i, [128, 8, 128], BF16, ls) for i in range(2)]
        mmT = [P.sbuf("C_mmT%d" % i, [128, 2, 128], BF16, ls) for i in range(2)]
        xo = [P.sbuf("C_xo%d" % i, [128, 1024], F32, ls) for i in range(2)]
        mxv = [P.sbuf("C_mxv%d" % i, [128, 4], F32, ls) for i in range(2)]
        nb_ = [P.sbuf("C_nb%d" % i, [128, 4], F32, ls) for i in range(2)]
        ssum = [P.sbuf("C_ss%d" % i, [128, 4], F32, ls) for i in range(2)]
        rsum = [P.sbuf("C_rs%d" % i, [128, 4], F32, ls) for i in range(2)]
        lnb = [lnbufs(P, "C_ln%d" % i, ls) for i in range(2)]
        psS = [[k.ps[0], k.ps[1]], [k.ps[2], k.ps[3]]]
        psT = k.ps[4]
        psMO = k.ps[5]
        psY = [k.ps[6], k.ps[7]]
        psTb = psT[:].bitcast(BF16)
        lns = {}

        def A0(t):
            P.dma("sp", qm[t % 3][:], k.qmT[:, t * 128:(t + 1) * 128].rearrange("(c p) t -> p c t", p=128), reads=[k.qmT],
                  writes=[qm[t % 3]])

        def A1(t):
            for h in range(4):
                bank = psS[t % 2][h % 2]
                p0 = 64 * (h % 2)
                P.I("pe", "matmul", bank[:, (h // 2) * 256:(h // 2 + 1) * 256], lhsT=qm[t % 3][p0:p0 + 64, h // 2, :],
                    rhs=kmT[p0:p0 + 64, h // 2, :], start=True, stop=True, reads=[qm[t % 3], kmT], writes=[(bank, h // 2)])

        def A2(t):
            b2 = t % 2
            for hb in range(2):
                P.I("dve", "tensor_reduce", out=mxv[b2][:, 2 * hb:2 * hb + 2], in_=psS[b2][hb][:].rearrange("p (h m) -> p h m", m=256),
                    axis=AX.X, op=ALU.max, reads=[psS[b2][hb]], writes=[(mxv[b2], hb)])
            P.I("dve", "tensor_single_scalar", out=nb_[b2][:], in_=mxv[b2][:], scalar=-0.125, op=ALU.mult,
                reads=[mxv[b2]], writes=[nb_[b2]])
            for h in range(4):
                q = (h % 2) * 2 + h // 2
                P.I("act", "activation", out=E[b2][:, h, :], in_=psS[b2][h % 2][:, (h // 2) * 256:(h // 2 + 1) * 256], func=AF.Exp,
                    scale=0.125, bias=nb_[b2][:, q:q + 1], accum_out=ssum[b2][:, h:h + 1],
                    reads=[psS[b2][h % 2], nb_[b2]], writes=[(E[b2], h), (ssum[b2], h)])

        def A3(t):
            b2 = t % 2
            P.I("dve", "reciprocal", out=rsum[b2][:], in_=ssum[b2][:], reads=[ssum[b2]], writes=[rsum[b2]])
            P.I("dve", "tensor_tensor", out=Pb[b2][:], in0=E[b2][:], in1=rsum[b2][:].unsqueeze(2).to_broadcast([128, 4, 256]),
                op=ALU.mult, reads=[E[b2], rsum[b2]], writes=[Pb[b2]])

        def A4(t):
            b2 = t % 2
            P.dma("sp", mx[t % 3][:], k.mixT[0:768, t * 128:(t + 1) * 128].rearrange("(c p) t -> p c t", p=128), reads=[k.mixT],
                  writes=[mx[t % 3]])
            for blk in range(8):
                P.I("pe", "transpose", out=psTb[:, blk * 128:(blk + 1) * 128], in_=Pb[b2][:, blk // 2, (blk % 2) * 128:(blk % 2 + 1) * 128],
                    identity=k.ident[:], reads=[Pb[b2], k.ident], writes=[(psT, blk)])
            P.I("act", "activation", out=PT[b2][:], in_=psTb[:, 0:1024].rearrange("p (c t) -> p c t", t=128), func=AF.Copy,
                reads=[psT], writes=[PT[b2]])

        def A5(t):
            b2 = t % 2
            P.dma("sp", xr[t % 4][:], xin[t * 128:(t + 1) * 128, :], reads=[xin], writes=[xr[t % 4]])
            for h in range(4):
                p0 = 64 * (h % 2)
                for mc in range(2):
                    P.I("pe", "matmul", psMO[p0:p0 + 64, (h // 2) * 128:(h // 2 + 1) * 128], lhsT=vm[:, mc, h * 64:(h + 1) * 64],
                        rhs=PT[b2][:, h * 2 + mc, :], start=(mc == 0), stop=(mc == 1), reads=[vm, PT[b2]], writes=[(psMO, h)])
            P.I("dve", "tensor_copy", out=mmT[b2][:], in_=psMO[:, 0:256].rearrange("p (c t) -> p c t", t=128), reads=[psMO],
                writes=[mmT[b2]])

        def A6(t):
            for nh in range(2):
                for c in range(8):
                    lhsT = mx[t % 3][:, c, :] if c < 6 else mmT[t % 2][:, c - 6, :]
                    P.I("pe", "matmul", psY[nh][:], lhsT=lhsT, rhs=wout[:, c, nh * 512:(nh + 1) * 512], start=(c == 0), stop=(c == 7),
                        reads=[mx[t % 3], mmT[t % 2], wout], writes=[psY[nh]])

        def A7(t):
            x_ = xr[t % 4]
            for nh in range(2):
                P.I("dve", "scalar_tensor_tensor", out=x_[:, nh * 512:(nh + 1) * 512], in0=x_[:, nh * 512:(nh + 1) * 512],
                    scalar=ALPHA, in1=psY[nh][:], op0=ALU.mult, op1=ALU.add, reads=[x_, psY[nh]], writes=[(x_, nh)])
            lns[t] = LN(P, k, x_, xo[t % 2], gam, bet, lnb[t % 2])
            lns[t].stats()
            lns[t].rstd()
            lns[t].nmr_()

        def A8(t):
            lns[t].norm()
            lns[t].affine()
            P.dma("sp", xout[t * 128:(t + 1) * 128, :], xo[t % 2][:], reads=[xo[t % 2]], writes=[(xout, t)])
            del lns[t]

        stages_ = [A1, A2, A3, A4, A5, A6, A7, A8]
        A0(0)
        for it in range(NT + len(stages_) - 1):
            if dw is not None and it % 2 == 0:
                issue_pending(P, dw, 1)
            for si in range(len(stages_) - 1, -1, -1):
                t = it - si
                if 0 <= t < NT:
                    stages_[si](t)
            if it + 1 < NT:
                A0(it + 1)
        if dw is not None:
            issue_pending(P, dw, 100)
        P.barrier()


def phase_D(P, k, L, xin, xout, w):
    with P.scope() as ls:
        WUP = w.WUP
        w.WDN[1] = P.sbuf("D_wdn1", [128, 16, 1024], BF16, ls)
        dsrc = k.w_down[L].rearrange("(c p) n -> p c n", p=128)
        issue_pending(P, w, 100)
        for c in range(0, 16, 4):
            P.dma("pool", w.WDN[1][:, c:c + 4, :], dsrc[:, 16 + c:20 + c, :], writes=[(w.WDN[1], c)])
        gam = P.sbuf("D_gam", [128, 1024], F32, ls)
        bet = P.sbuf("D_bet", [128, 1024], F32, ls)
        P.dma("sp", gam[:], k.ln_ffn_g[L:L + 1, :].partition_broadcast(128), writes=[gam])
        P.dma("sp", bet[:], k.ln_ffn_b[L:L + 1, :].partition_broadcast(128), writes=[bet])
        GT = 4
        NG = NT // GT
        NW = GT * 128
        xb = P.sbuf("D_xb", [128, GT, 1024], BF16, ls)
        xT = P.sbuf("D_xT", [128, 8, NW], BF16, ls)
        hT = P.sbuf("D_hT", [128, 32, NW], BF16, ls)
        rl = [P.sbuf("D_rl%d" % i, [128, NW], F32, ls) for i in range(2)]
        xr = [P.sbuf("D_xr%d" % i, [128, 1024], F32, ls) for i in range(2)]
        lnb = [lnbufs(P, "D_ln%d" % i, ls) for i in range(2)]
        psH = [k.ps[2], k.ps[3]]
        psY = [[k.ps[4], k.ps[5]], [k.ps[6], k.ps[7]]]
        st_ = dict(cp=0)

        def load_x(g):
            P.dma("pool", xb[:], xin[g * NW:(g + 1) * NW, :].rearrange("(j p) d -> p j d", p=128), reads=[xin], writes=[xb])

        def transposes(g):
            for j in range(GT):
                transpose_tile(P, k, xb, (lambda j: lambda blk: xb[:, j, blk * 128:(blk + 1) * 128])(j), 8, k.ps[j % 2],
                               xT, xT[:, :, j * 128:(j + 1) * 128], st_["cp"], dkey=j)
                st_["cp"] += 1

        pend = []

        def tail2():
            while pend:
                ln, tt, x_ = pend.pop(0)
                ln.norm()
                ln.affine()
                P.dma("sp", xout[tt * 128:(tt + 1) * 128, :], x_[:], reads=[x_], writes=[(xout, tt)])

        load_x(0)
        transposes(0)
        tl = 0
        for g in range(NG):
            if g + 1 < NG:
                load_x(g + 1)
            for fc in range(32):
                bank = psH[fc % 2]
                for c in range(8):
                    P.I("pe", "matmul", bank[:], lhsT=WUP[:, c, fc * 128:(fc + 1) * 128], rhs=xT[:, c, :], start=(c == 0), stop=(c == 7),
                        reads=[(WUP, c), xT], writes=[bank])
                r_ = rl[fc % 2]
                P.I("act", "activation", out=r_[:], in_=bank[:], func=AF.Relu, reads=[bank], writes=[r_])
                P.I("dve", "tensor_tensor", out=hT[:, fc, :], in0=r_[:], in1=r_[:], op=ALU.mult, reads=[r_], writes=[(hT, fc)])
            if g + 1 < NG:
                transposes(g + 1)
            for j in range(GT):
                tt = g * GT + j
                b = tl % 2
                tl += 1
                tail2()
                x_ = xr[b]
                P.dma("sp", x_[:], xin[tt * 128:(tt + 1) * 128, :], reads=[xin], writes=[x_])
                py = psY[b]
                for nh in range(2):
                    for fc in range(32):
                        wd = w.WDN[fc // 16]
                        P.I("pe", "matmul", py[nh][:], lhsT=hT[:, fc, j * 128:(j + 1) * 128], rhs=wd[:, fc % 16, nh * 512:(nh + 1) * 512],
                            start=(fc == 0), stop=(fc == 31), reads=[hT, (wd, (fc % 16) // 4 * 4)], writes=[py[nh]])
                for nh in range(2):
                    P.I("dve", "scalar_tensor_tensor", out=x_[:, nh * 512:(nh + 1) * 512], in0=x_[:, nh * 512:(nh + 1) * 512],
                        scalar=ALPHA, in1=py[nh][:], op0=ALU.mult, op1=ALU.add, reads=[x_, py[nh]], writes=[(x_, nh)])
                ln = LN(P, k, x_, x_, gam, bet, lnb[b])
                ln.stats()
                ln.rstd()
                ln.nmr_()
                pend.append((ln, tt, x_))
        tail2()
        P.barrier()


def phase_E(P, k):
    with P.scope() as ls:
        W1 = P.sbuf("E_W1", [128, 8, 3328], BF16, ls)
        wsrc = k.w_in_hg[0].rearrange("(c p) n -> p c n", p=128)
        for c in range(8):
            P.dma("pool", W1[:, c, :], wsrc[:, c, :], writes=[(W1, c)])
        l0 = P.sbuf("E_l0", [128, 6], F32, ls)
        l1 = P.sbuf("E_l1", [128, 6], F32, ls)
        for h in range(6):
            P.dma("sp", l0[:, h:h + 1], k.lower_bounds[0:1, h * 128:(h + 1) * 128].rearrange("o p -> p o"), writes=[(l0, h)])
            P.dma("sp", l1[:, h:h + 1], k.lower_bounds[1:2, h * 128:(h + 1) * 128].rearrange("o p -> p o"), writes=[(l1, h)])
        lb = P.sbuf("E_lb", [128, 6], F32, ls)
        omlb = P.sbuf("E_omlb", [128, 6], F32, ls)
        nomlb = P.sbuf("E_nomlb", [128, 6], F32, ls)
        ltmp = P.sbuf("E_ltmp", [128, 6], F32, ls)
        P.I("dve", "tensor_tensor", out=ltmp[:], in0=l0[:], in1=l1[:], op=ALU.subtract, reads=[l0, l1], writes=[ltmp])
        P.I("act", "activation", out=ltmp[:], in_=ltmp[:], func=AF.Exp, reads=[ltmp], writes=[ltmp])
        P.I("dve", "tensor_single_scalar", out=lb[:], in_=ltmp[:], scalar=1.0, op=ALU.add, reads=[ltmp], writes=[lb])
        P.I("dve", "reciprocal", out=lb[:], in_=lb[:], reads=[lb], writes=[lb])
        P.I("dve", "tensor_tensor", out=omlb[:], in0=ltmp[:], in1=lb[:], op=ALU.mult, reads=[ltmp, lb], writes=[omlb])
        P.I("dve", "tensor_single_scalar", out=nomlb[:], in_=omlb[:], scalar=-1.0, op=ALU.mult, reads=[omlb], writes=[nomlb])
        gn = P.sbuf("E_gn", [128, 768], F32, ls)
        P.dma("sp", gn[:], k.hg_norm_g[0:1, :].partition_broadcast(128), writes=[gn])
        epsr = P.sbuf("E_epsr", [128, 1], F32, ls)
        P.I("dve", "memset", epsr[:], RMS_EPS, writes=[epsr])
        state = P.sbuf("E_state", [128, 6, 128], F32, ls)
        sbf = [P.sbuf("E_sbf%d" % i, [128, 6, 128], BF16, ls) for i in range(2)]
        P.I("dve", "memset", state[:], 0.0, writes=[state])
        P.I("dve", "memset", sbf[0][:], 0.0, writes=[sbf[0]])
        xb = [P.sbuf("E_xb%d" % i, [128, 4, 1024], BF16, ls) for i in range(1)]
        xT = [P.sbuf("E_xT%d" % i, [128, 8, 512], BF16, ls) for i in range(1)]
        vbf = [P.sbuf("E_v%d" % i, [128, 4, 768], BF16, ls) for i in range(2)]
        sg = [P.sbuf("E_sg%d" % i, [128, 4, 768], F32, ls) for i in range(2)]
        qst = [P.sbuf("E_qst%d" % i, [128, 512], BF16, ls) for i in range(1)]
        A = [[P.sbuf("E_A%d_%d" % (s_, i), [128, 512], F32, ls) for i in range(5)] for s_ in range(3)]
        QT = [P.sbuf("E_QT%d" % i, [128, 6, 512], BF16, ls) for i in range(1)]
        KT = [P.sbuf("E_KT%d" % i, [128, 6, 512], BF16, ls) for i in range(1)]
        KD = [P.sbuf("E_KD%d" % i, [128, 6, 512], BF16, ls) for i in range(1)]
        EGL = [P.sbuf("E_EGL%d" % i, [128, 6, 8], F32, ls) for i in range(1)]
        KDA = [P.sbuf("E_KDA%d" % i, [128, 6, 128], BF16, ls) for i in range(4)]
        KDB = [P.sbuf("E_KDB%d" % i, [128, 6, 128], BF16, ls) for i in range(4)]
        for i in range(4):
            P.I("pool", "memset", KDA[i][:], 0.0, writes=[KDA[i]])
            P.I("pool", "memset", KDB[i][:], 0.0, writes=[KDB[i]])
        SCM = [P.sbuf("E_SCM%d" % i, [128, 6, 128], BF16, ls) for i in range(4)]
        junk = P.sbuf("E_junk", [128, 128], F32, ls)
        o32 = [P.sbuf("E_o32_%d" % i, [128, 768], F32, ls) for i in range(2)]
        ss = [P.sbuf("E_ss%d" % i, [128, 6], F32, ls) for i in range(2)]
        rstd = [P.sbuf("E_rstd%d" % i, [128, 6], F32, ls) for i in range(2)]
        t1 = [P.sbuf("E_t1_%d" % i, [128, 768], F32, ls) for i in range(1)]
        mixb = [P.sbuf("E_mixb%d" % i, [128, 768], BF16, ls) for i in range(1)]
        mst = [P.sbuf("E_mst%d" % i, [128, 6, 128], BF16, ls) for i in range(1)]
        psSC = [k.ps[0], k.ps[1]]
        psO = [k.ps[2], k.ps[3]]
        psKV = [k.ps[4], k.ps[5]]
        rotb = [k.ps[6], k.ps[7]]
        st_ = dict(rot=0, cp=0, hs=0, sb=0, tj=0)

        def nbank():
            st_["rot"] += 1
            return rotb[st_["rot"] % 2]

        pend_tail = []

        def flush_tail():
            while pend_tail:
                pend_tail.pop(0)()

        def hsl(bank_list, hd):
            return bank_list[hd // 4][:, (hd % 4) * 128:(hd % 4 + 1) * 128], bank_list[hd // 4]

        CUTE = 9.0
        for g in range(8 if CUTE >= 9 else 1):
            if CUTE < 2:
                break
            gb = g % 2
            xbg, xTg, vg, sgg = xb[0], xT[0], vbf[gb], sg[gb]

            def x_load_T(gn_):
                P.dma("pool", xbg[:], k.xmid[gn_ * 512:(gn_ + 1) * 512, :].rearrange("(j p) d -> p j d", p=128), reads=[k.xmid], writes=[xbg])
                for j in range(4):
                    transpose_tile(P, k, xbg, (lambda j: lambda b: xbg[:, j, b * 128:(b + 1) * 128])(j), 8, nbank(),
                                   xTg, xTg[:, :, j * 128:(j + 1) * 128], st_["cp"], dkey=j)
                    st_["cp"] += 1

            def tok_steps(gn_):
                vn, sn = vbf[gn_ % 2], sg[gn_ % 2]
                steps = []
                for j in range(4):
                    for n in range(3):
                        def st(j=j, n=n):
                            bank = nbank()
                            for c in range(8):
                                P.I("pe", "matmul", bank[:], lhsT=xTg[:, c, j * 128:(j + 1) * 128], rhs=W1[:, c, 1536 + 512 * n:2048 + 512 * n],
                                    start=(c == 0), stop=(c == 7), reads=[W1, xTg], writes=[bank])
                            if n == 0:
                                _copy(P, k, 0, vn[:, j, 0:512], bank[:], reads=[bank], writes=[(vn, (j, 0))])
                            elif n == 1:
                                _copy(P, k, 0, vn[:, j, 512:768], bank[:, 0:256], reads=[bank], writes=[(vn, (j, 1))])
                                _copy(P, k, 0, sn[:, j, 0:256], bank[:, 256:512], reads=[bank], writes=[(sn, (j, 0))])
                            else:
                                _copy(P, k, 0, sn[:, j, 256:768], bank[:], reads=[bank], writes=[(sn, (j, 1))])
                        steps.append(st)
                for fc in (24, 25):
                    def sq(fc=fc):
                        bank = nbank()
                        for c in range(8):
                            P.I("pe", "matmul", bank[:], lhsT=W1[:, c, fc * 128:(fc + 1) * 128], rhs=xTg[:, c, :], start=(c == 0), stop=(c == 7),
                                reads=[W1, xTg], writes=[bank])
                        s_ = qst[0]
                        _copy(P, k, 0, s_[:], bank[:], reads=[bank], writes=[s_])
                        P.dma("sp", k.qmT[(fc - 24) * 128:(fc - 23) * 128, gn_ * 512:(gn_ + 1) * 512], s_[:], reads=[s_],
                              writes=[(k.qmT, (fc, gn_))])
                    steps.append(sq)

                def ssilu():
                    P.I("act", "activation", out=sn[:], in_=sn[:], func=AF.Silu, reads=[sn], writes=[sn])
                    P.I("pool", "tensor_tensor", out=sn[:], in0=sn[:], in1=gn[:].unsqueeze(1).to_broadcast([128, 4, 768]), op=ALU.mult,
                        reads=[sn, gn], writes=[sn])
                steps.append(ssilu)
                return steps

            if g == 0:
                x_load_T(0)
                for st in tok_steps(0):
                    st()
            if CUTE < 3:
                continue
            for trio in range(2):
                hds = [trio * 3 + i for i in range(3)]
                Fb = [k.ps[i] for i in range(3)]
                Qb = [k.ps[3 + i] for i in range(3)]
                for i, hd in enumerate(hds):
                    for c in range(8):
                        P.I("pe", "matmul", Fb[i][:], lhsT=W1[:, c, 768 + hd * 128:768 + (hd + 1) * 128], rhs=xTg[:, c, :], start=(c == 0),
                            stop=(c == 7), reads=[W1, xTg], writes=[Fb[i]])
                for i, hd in enumerate(hds):
                    a = A[i]
                    P.I("act", "activation", out=a[0][:], in_=Fb[i][:], func=AF.Exp, scale=-1.0, reads=[Fb[i]], writes=[a[0]])
                for i, hd in enumerate(hds):
                    for c in range(8):
                        P.I("pe", "matmul", Qb[i][:], lhsT=W1[:, c, hd * 128:(hd + 1) * 128], rhs=xTg[:, c, :], start=(c == 0), stop=(c == 7),
                            reads=[W1, xTg], writes=[Qb[i]])
                for i, hd in enumerate(hds):
                    a = A[i]
                    P.I("act", "activation", out=a[0][:], in_=a[0][:], func=AF.Ln, bias=1.0, reads=[a[0]], writes=[a[0]])
                for i, hd in enumerate(hds):
                    a = A[i]
                    P.I("act", "activation", out=a[1][:], in_=a[0][:], func=AF.Exp, scale=-1.0, reads=[a[0]], writes=[a[1]])
                for i, hd in enumerate(hds):
                    a = A[i]
                    P.I("act", "activation", out=a[2][:], in_=a[1][:], func=AF.Ln, scale=omlb[:, hd:hd + 1], bias=lb[:, hd:hd + 1],
                        reads=[a[1], omlb, lb], writes=[a[2]])
                    P.I("dve", "tensor_scalar", out=a[3][:], in0=a[1][:], scalar1=nomlb[:, hd:hd + 1], scalar2=omlb[:, hd:hd + 1],
                        op0=ALU.mult, op1=ALU.add, reads=[a[1], nomlb, omlb], writes=[a[3]])
                for i, hd in enumerate(hds):
                    a = A[i]
                    P.I("dve", "tensor_tensor_scan", out=a[4][:], data0=k.scanmask[:], data1=a[2][:], initial=0.0, op0=ALU.mult,
                        op1=ALU.add, reads=[k.scanmask, a[2]], writes=[a[4]])
                for i, hd in enumerate(hds):
                    a = A[i]
                    G3 = a[4][:].rearrange("p (c j) -> p c j", j=64)
                    P.I("act", "activation", out=a[0][:], in_=a[4][:], func=AF.Exp, reads=[a[4]], writes=[a[0]])
                    P.I("act", "activation", out=a[1][:], in_=a[4][:], func=AF.Exp, scale=-1.0, reads=[a[4]], writes=[a[1]])
                    P.I("dve", "tensor_tensor", out=a[2][:].rearrange("p (c j) -> p c j", j=64), in0=G3,
                        in1=G3[:, :, 63:64].to_broadcast([128, 8, 64]), op=ALU.subtract, reads=[a[4]], writes=[a[2]])
                for i, hd in enumerate(hds):
                    a = A[i]
                    G3 = a[4][:].rearrange("p (c j) -> p c j", j=64)
                    P.I("dve", "tensor_tensor", out=QT[0][:, hd, :], in0=Qb[i][:], in1=a[0][:], op=ALU.mult, reads=[Qb[i], a[0]],
                        writes=[(QT[0], hd)])
                    P.I("dve", "tensor_tensor", out=KT[0][:, hd, :], in0=a[3][:], in1=a[1][:], op=ALU.mult, reads=[a[3], a[1]],
                        writes=[(KT[0], hd)])
                    P.I("act", "activation", out=a[2][:], in_=a[2][:], func=AF.Exp, scale=-1.0, reads=[a[2]], writes=[a[2]])
                    P.I("act", "activation", out=EGL[0][:, hd, :], in_=G3[:, :, 63], func=AF.Exp, reads=[a[4]], writes=[(EGL[0], hd)])
                for i, hd in enumerate(hds):
                    a = A[i]
                    P.I("dve", "tensor_tensor", out=KD[0][:, hd, :], in0=a[3][:], in1=a[2][:], op=ALU.mult, reads=[a[3], a[2]],
                        writes=[(KD[0], hd)])
            if CUTE < 4:
                continue
            for j in range(4):
                jc = slice(j * 128, (j + 1) * 128)
                bT = nbank()
                bTb = bT[:].bitcast(BF16)
                for hd in range(6):
                    P.I("pe", "transpose", out=bTb[:, hd * 128:(hd + 1) * 128], in_=KD[0][:, hd, jc], identity=k.ident[:],
                        reads=[(KD[0], hd), k.ident], writes=[(bT, hd)])
                P.I("dve", "tensor_copy", out=KDA[j][0:64, :, :], in_=bTb[0:64, 0:768].rearrange("p (h k) -> p h k", k=128),
                    reads=[bT], writes=[KDA[j]])
                P.I("dve", "tensor_copy", out=KDB[j][64:128, :, :], in_=bTb[64:128, 0:768].rearrange("p (h k) -> p h k", k=128),
                    reads=[bT], writes=[KDB[j]])
                for hd in range(6):
                    osl, ob = hsl(psSC, hd)
                    P.I("pe", "matmul", osl, lhsT=KT[0][:, hd, jc], rhs=QT[0][:, hd, jc], start=True, stop=True,
                        reads=[(KT[0], hd), (QT[0], hd)], writes=[(ob, hd)])
                P.I("dve", "tensor_tensor", out=SCM[j][:, 0:4, :], in0=psSC[0][:].rearrange("p (h t) -> p h t", t=128),
                    in1=k.blkmask[:].unsqueeze(1).to_broadcast([128, 4, 128]), op=ALU.mult, reads=[psSC[0], k.blkmask],
                    writes=[(SCM[j], 0)])
                P.I("dve", "tensor_tensor", out=SCM[j][:, 4:6, :], in0=psSC[1][:, 0:256].rearrange("p (h t) -> p h t", t=128),
                    in1=k.blkmask[:].unsqueeze(1).to_broadcast([128, 2, 128]), op=ALU.mult, reads=[psSC[1], k.blkmask],
                    writes=[(SCM[j], 1)])
            tokq = []
            if g + 1 < 8:
                x_load_T(g + 1)
                tokq = tok_steps(g + 1)

            def tok_drain(n, tokq=tokq):
                for _ in range(n):
                    if tokq:
                        tokq.pop(0)()
            for j in range(4):
                tj = st_["tj"]
                st_["tj"] += 1
                jb = tj % 2
                tt = g * 4 + j
                jc = slice(j * 128, (j + 1) * 128)
                if CUTE < 5:
                    continue
                sA, sB = sbf[0], sbf[1]
                for hd in range(6):
                    osl, ob = hsl(psO, hd)
                    vs_ = vg[:, j, hd * 128:(hd + 1) * 128]
                    P.I("pe", "matmul", osl, lhsT=SCM[j][:, hd, :], rhs=vs_, start=(hd % 4 == 0), stop=False,
                        reads=[SCM[j], vg], writes=[(ob, hd)])
                for hd in range(6):
                    osl, ob = hsl(psO, hd)
                    P.I("pe", "matmul", osl[0:64, :], lhsT=QT[0][:, hd, j * 128:j * 128 + 64], rhs=sA[:, hd, :], start=False, stop=False,
                        reads=[(QT[0], hd), (sA, hd)], writes=[(ob, hd)])
                for half, (KDx, s_src, s_dst) in enumerate(((KDA[j], sA, sB), (KDB[j], sB, sA))):
                    ch = j * 2 + half
                    for hd in range(6):
                        ksl, kb = hsl(psKV, hd)
                        P.I("pe", "matmul", ksl, lhsT=KDx[:, hd, :], rhs=vg[:, j, hd * 128:(hd + 1) * 128], start=True, stop=True,
                            reads=[KDx, vg], writes=[(kb, hd)])
                    if half == 0:
                        flush_tail()
                    tok_drain(2 if half == 0 else 1)
                    for hd in range(6):
                        ksl, kb = hsl(psKV, hd)
                        P.I("dve", "scalar_tensor_tensor", out=state[:, hd, :], in0=state[:, hd, :], scalar=EGL[0][:, hd, ch:ch + 1],
                            in1=ksl, op0=ALU.mult, op1=ALU.add, reads=[(state, hd), (EGL[0], hd), (kb, hd)], writes=[(state, hd)])
                        if hd % 3 != 2:
                            P.I("dve", "tensor_copy", out=s_dst[:, hd, :], in_=state[:, hd, :], reads=[(state, hd)], writes=[(s_dst, hd)])
                        else:
                            P.I("pool", "tensor_copy", out=s_dst[:, hd, :], in_=state[:, hd, :], reads=[(state, hd)], writes=[(s_dst, hd)])
                    if half == 0:
                        for hd in range(6):
                            osl, ob = hsl(psO, hd)
                            P.I("pe", "matmul", osl[64:128, :], lhsT=QT[0][:, hd, j * 128 + 64:(j + 1) * 128], rhs=sB[:, hd, :],
                                start=False, stop=True, reads=[(QT[0], hd), (sB, hd)], writes=[(ob, hd)])
                ob_ = o32[tj % 2]
                P.I("act", "activation", out=ob_[:, 0:512], in_=psO[0][:], func=AF.Copy, reads=[psO[0]], writes=[(ob_, 0)])
                P.I("act", "activation", out=ob_[:, 512:768], in_=psO[1][:, 0:256], func=AF.Copy, reads=[psO[1]], writes=[(ob_, 1)])

                def tail(jb=jb, j=j, tt=tt, ob_=ob_, sgg=sgg):
                    for hd in range(6):
                        P.I("act", "activation", out=junk[:], in_=ob_[:, hd * 128:(hd + 1) * 128], func=AF.Square,
                            accum_out=ss[jb][:, hd:hd + 1], reads=[ob_], writes=[junk, (ss[jb], hd)])
                    P.I("act", "activation", out=rstd[jb][:], in_=ss[jb][:], func=AF.Ln, scale=1.0 / 128.0, bias=epsr[:, 0:1],
                        reads=[ss[jb], epsr], writes=[rstd[jb]])
                    P.I("act", "activation", out=rstd[jb][:], in_=rstd[jb][:], func=AF.Exp, scale=-0.5, reads=[rstd[jb]], writes=[rstd[jb]])
                    P.I("dve", "tensor_tensor", out=t1[0][:].rearrange("p (h v) -> p h v", v=128),
                        in0=ob_[:].rearrange("p (h v) -> p h v", v=128), in1=rstd[jb][:].unsqueeze(2).to_broadcast([128, 6, 128]),
                        op=ALU.mult, reads=[ob_, rstd[jb]], writes=[t1[0]])
                    P.I("dve", "tensor_tensor", out=mixb[0][:], in0=t1[0][:], in1=sgg[:, j, :], op=ALU.mult, reads=[t1[0], sgg],
                        writes=[mixb[0]])
                    transpose_tile(P, k, mixb[0], lambda blk: mixb[0][:, blk * 128:(blk + 1) * 128], 6, nbank(), mst[0], mst[0][:], 1)
                    P.dma("sp", k.mixT[0:768, tt * 128:(tt + 1) * 128].rearrange("(c p) t -> p c t", p=128), mst[0][:], reads=[mst[0]],
                          writes=[(k.mixT, tt)])
                pend_tail.append(tail)
            tok_drain(100)
            flush_tail()
        P.barrier()


WSHAPES = dict(w_in_sb=[1, 1024, 2560], w_in_hg=[1, 1024, 3328], w_mem_kv=[2, 1024, 512], lower_bounds=[2, 768],
               hg_norm_g=[1, 768], w_out=[2, 1024, 1024], ln_mix_g=[2, 1024], ln_mix_b=[2, 1024],
               w_up=[2, 1024, 4096], w_down=[2, 4096, 1024], ln_ffn_g=[2, 1024], ln_ffn_b=[2, 1024])


def build(stages="ABCDE", debug=False):
    nc = bass.Bass("TRN2", target_bir_lowering=False)
    with ExitStack() as es:
        P = Prog(nc, es)
        k = K()
        k.x = P.dram("x", [S, D], F32, kind="ExternalInput")
        k.mem = P.dram("mem", [NMEM, D], F32, kind="ExternalInput")
        for name, shp in WSHAPES.items():
            setattr(k, name, P.dram(name, shp, F32, kind="ExternalInput"))
        skind = "ExternalOutput" if debug else "Internal"
        k.qT0 = P.dram("qT0", [768, S], BF16, kind=skind)
        k.kT0 = P.dram("kT0", [768, S], BF16, kind=skind)
        k.v0 = P.dram("v0", [S, 768], BF16, kind=skind)
        k.qmT = P.dram("qmT", [256, S], BF16, kind=skind)
        k.mixT = P.dram("mixT", [768, S], BF16, kind=skind)
        k.x1 = P.dram("x1", [S, D], F32, kind=skind)
        k.xmid = P.dram("xmid", [S, D], F32, kind=skind)
        k.y = P.dram("y", [S, D], F32, kind="ExternalOutput")
        build_consts(P, k)
        P.barrier()
        two = "E" in stages
        if "A" in stages:
            phase_A(P, k)
        scC = P.scope()
        cw = load_C_weights(P, k, 0, scC)
        if "B" in stages:
            phase_B(P, k)
        scD = P.scope()
        dw = alloc_D_weights(P, k, 0, scD)
        if "C" in stages:
            phase_C(P, k, 0, k.x, k.x1, cw, dw)
        scC.close()
        if "D" in stages:
            phase_D(P, k, 0, k.x1, k.xmid if two else k.y, dw)
        scD.close()
        if two:
            phase_E(P, k)
            scC = P.scope()
            cw = load_C_weights(P, k, 1, scC)
            scD = P.scope()
            dw = alloc_D_weights(P, k, 1, scD)
            phase_C(P, k, 1, k.xmid, k.x1, cw, dw)
            scC.close()
            phase_D(P, k, 1, k.x1, k.y, dw)
            scD.close()
        if "e" in stages:
            phase_E(P, k)
        P.finish()
        k.n_ins = P.n_ins
    return nc, k


_CACHE = {}


def kernel(**inputs):
    if "nc" not in _CACHE:
        _CACHE["nc"] = build("ABCDE")[0]
    nc = _CACHE["nc"]
    B = inputs["x"].shape[0]
    shared = {n: np.ascontiguousarray(inputs[n], dtype=np.float32) for n in WSHAPES}
    in_maps = []
    for b in range(B):
        m = dict(shared)
        m["x"] = np.ascontiguousarray(inputs["x"][b], dtype=np.float32)
        m["mem"] = np.ascontiguousarray(inputs["mem"][b], dtype=np.float32)
        in_maps.append(m)
    res = run_bass_kernel_spmd(nc, in_maps, core_ids=list(range(B)))
    return np.stack([np.asarray(r["y"], dtype=np.float32) for r in res.results], axis=0)
```

```python
from contextlib import ExitStack
import numpy as np
import concourse.bass as bass
import concourse.mybir as mybir
from concourse.bass_utils import run_bass_kernel_spmd

F32 = mybir.dt.float32
BF16 = mybir.dt.bfloat16
AF = mybir.ActivationFunctionType
ALU = mybir.AluOpType
AX = mybir.AxisListType

COMPUTE = ("pe", "act", "dve", "pool", "sp")
DMAQ = ("sp", "pool", "act")
NDMA_SEMS = 12


class Buf:
    def __init__(self, t, name, psum=False):
        self.t = t
        self.name = name
        self.psum = psum
        self.state = {}

    def __getitem__(self, idx):
        return self.t[idx]


class Scope:
    def __init__(self, prog):
        self.prog = prog
        self.bufs = []

    def __enter__(self):
        return self

    def __exit__(self, *a):
        self.close()
        return False

    def close(self):
        for b in self.bufs:
            self.prog.free(b)
        self.bufs = []


class Op:
    __slots__ = ("idx", "eng", "fn", "deps", "is_dma", "sig", "dma_slot", "dma_val", "eidx", "vc", "dmaknown")

    def __init__(self, idx, eng, fn, is_dma):
        self.idx = idx
        self.eng = eng
        self.fn = fn
        self.deps = set()
        self.is_dma = is_dma
        self.sig = 0
        self.eidx = 0
        self.vc = None
        self.dmaknown = None


class Prog:
    def __init__(self, nc, es):
        self.nc = nc
        self.es = es
        self.ops = []
        self.bufs = []
        self.engs = {"pe": nc.tensor, "act": nc.scalar, "dve": nc.vector, "pool": nc.gpsimd, "sp": nc.sync}
        self.sems = {e: es.enter_context(nc.semaphore("s_" + e)) for e in COMPUTE}
        self.dsems = {q: [es.enter_context(nc.semaphore("d_%s%d" % (q, i))) for i in range(NDMA_SEMS)]
                      for q in DMAQ}
        self.barrier_deps = {e: set() for e in self.engs}
        self.emitted = 0
        self.phase_start = 0
        NE = len(COMPUTE)
        self.cur_vc = {e: [0] * NE for e in self.engs}
        self.cur_dma = {e: set() for e in self.engs}
        self.cnt = {e: 0 for e in COMPUTE}
        self.scount = {e: 0 for e in COMPUTE}
        self.dcount = {q: 0 for q in DMAQ}
        self.n_ins = 0

    ARENA_BYTES = 212736

    def _arena_init(self):
        self.arena = self.es.enter_context(self.nc.sbuf_tensor("arena", [128, self.ARENA_BYTES // 2], BF16))
        self.free_list = [(0, self.ARENA_BYTES)]

    def scope(self):
        return Scope(self)

    def sbuf(self, name, shape, dt, scope=None):
        if not hasattr(self, "arena"):
            self._arena_init()
        esz = 4 if dt == F32 else 2
        n = 1
        for d in shape[1:]:
            n *= d
        nbytes = (n * esz + 63) // 64 * 64
        for i, (off, sz) in enumerate(self.free_list):
            if sz >= nbytes:
                if sz == nbytes:
                    self.free_list.pop(i)
                else:
                    self.free_list[i] = (off + nbytes, sz - nbytes)
                break
        else:
            raise AssertionError("arena out of SBUF for %s %s; free=%s" % (name, shape, self.free_list))
        ap = self.arena[0:shape[0], off // 2:off // 2 + n * esz // 2]
        if dt == F32:
            ap = ap.bitcast(F32)
        if len(shape) == 3:
            ap = ap.rearrange("p (a b) -> p a b", b=shape[2])
        elif len(shape) != 2:
            raise AssertionError("2-D / 3-D only")
        b = Buf(ap, name)
        b.region = (off, nbytes)
        self.bufs.append(b)
        if scope is not None:
            scope.bufs.append(b)
        return b

    def free(self, b):
        off, nbytes = b.region
        b.region = None
        fl = sorted(self.free_list + [(off, nbytes)])
        merged = []
        for o, sz in fl:
            if merged and merged[-1][0] + merged[-1][1] == o:
                merged[-1] = (merged[-1][0], merged[-1][1] + sz)
            else:
                merged.append((o, sz))
        self.free_list = merged

    def psum(self, name, shape, dt):
        t = self.es.enter_context(self.nc.psum_tensor(name, list(shape), dt))
        b = Buf(t, name, psum=True)
        self.bufs.append(b)
        return b

    def dram(self, name, shape, dt, kind="Internal"):
        t = self.nc.dram_tensor(name, list(shape), dt, kind=kind)
        b = Buf(t, name)
        self.bufs.append(b)
        return b

    @staticmethod
    def _conf(k1, k2):
        return k1 is None or k2 is None or k1 == k2

    def _rec(self, eng, fn, reads, writes, is_dma):
        op = Op(len(self.ops), eng, fn, is_dma)
        self.ops.append(op)
        for r in reads:
            b, key = r if isinstance(r, tuple) else (r, None)
            for k2, st in b.state.items():
                if st[0] is not None and (self._conf(key, k2) or (b.psum and self.ops[st[0]].eng != eng)):
                    op.deps.add(st[0])
                if b.psum:
                    op.deps.update(r2 for r2 in st[1] if self.ops[r2].eng != eng)
        for w in writes:
            b, key = w if isinstance(w, tuple) else (w, None)
            for k2, st in b.state.items():
                if self._conf(key, k2):
                    if st[0] is not None:
                        op.deps.add(st[0])
                    op.deps.update(st[1])
                elif b.psum:
                    if st[0] is not None and self.ops[st[0]].eng != eng:
                        op.deps.add(st[0])
                    op.deps.update(r2 for r2 in st[1] if self.ops[r2].eng != eng)
        op.deps |= self.barrier_deps[eng]
        self.barrier_deps[eng] = set()
        for r in reads:
            b, key = r if isinstance(r, tuple) else (r, None)
            st = b.state.setdefault(key, [None, []])
            st[1].append(op.idx)
        for w in writes:
            b, key = w if isinstance(w, tuple) else (w, None)
            if key is None:
                b.state = {None: [op.idx, []]}
            else:
                b.state[key] = [op.idx, []]
        op.deps.discard(op.idx)
        return op

    def op(self, eng, fn, reads=(), writes=()):
        return self._rec(eng, fn, reads, writes, False)

    def I(self, eng, meth, *args, reads=(), writes=(), **kw):
        return self._rec(eng, lambda e: getattr(e, meth)(*args, **kw), reads, writes, False)

    def dma(self, q, out, in_, reads=(), writes=()):
        return self._rec(q, lambda e: e.dma_start(out=out, in_=in_), reads, writes, True)

    def barrier(self):
        tails = set()
        last = {}
        for op in self.ops[self.phase_start:]:
            if op.is_dma:
                tails.add(op.idx)
            else:
                last[op.eng] = op.idx
        tails |= set(last.values())
        bop = Op(len(self.ops), "sp", lambda e: e.nop(), False)
        bop.deps = tails | self.barrier_deps["sp"]
        self.ops.append(bop)
        bop.sig = 1
        self.flush()
        for e in self.engs:
            self.barrier_deps[e] = {bop.idx}
        for b in self.bufs:
            b.state = {}
        self.phase_start = len(self.ops)

    def flush(self):
        ops = self.ops
        NE = len(COMPUTE)
        eid = {e: i for i, e in enumerate(COMPUTE)}
        new = ops[self.emitted:]
        for op in new:
            if not op.is_dma:
                self.cnt[op.eng] += 1
                op.eidx = self.cnt[op.eng]
        needed = []
        for op in new:
            vc = self.cur_vc[op.eng]
            dk = self.cur_dma[op.eng]
            real = []
            for d in sorted(op.deps, reverse=True):
                dop = ops[d]
                if dop.is_dma:
                    if d in dk:
                        continue
                    real.append(d)
                    dk.add(d)
                else:
                    if dop.eng == "pe" and op.eng == "pe":
                        continue
                    j = eid[dop.eng]
                    if vc[j] >= dop.eidx:
                        continue
                    real.append(d)
                    vc[j] = dop.eidx
                dk |= dop.dmaknown
                dvc = dop.vc
                for i in range(NE):
                    if dvc[i] > vc[i]:
                        vc[i] = dvc[i]
            op.vc = list(vc)
            op.dmaknown = set(dk)
            needed.append(real)
            for d in real:
                if not ops[d].sig:
                    assert d >= self.emitted, "dependency on an already emitted non-signalling op"
                    ops[d].sig = 1
        for op in new:
            if op.is_dma:
                n = self.dcount[op.eng]
                self.dcount[op.eng] += 1
                op.dma_slot = n % NDMA_SEMS
                op.dma_val = 16 * (n // NDMA_SEMS + 1)
            elif op.sig:
                self.scount[op.eng] += 1
                op.sig = self.scount[op.eng]
        for op, real in zip(new, needed):
            e = self.engs[op.eng]
            if op.is_dma and op.dma_val > 16:
                e.wait_ge(self.dsems[op.eng][op.dma_slot], op.dma_val - 16)
                self.n_ins += 1
            for d in real:
                dop = ops[d]
                if dop.is_dma:
                    e.wait_ge(self.dsems[dop.eng][dop.dma_slot], dop.dma_val)
                else:
                    e.wait_ge(self.sems[dop.eng], dop.sig)
                self.n_ins += 1
            ins = op.fn(e)
            self.n_ins += 1
            if op.is_dma:
                ins.then_inc(self.dsems[op.eng][op.dma_slot], 16)
            elif op.sig:
                ins.then_inc(self.sems[op.eng], 1)
            op.fn = None
        self.emitted = len(ops)

    def finish(self):
        self.barrier()


S = 4096
D = 1024
NT = S // 128
DFF = 4096
SB_H = 12
HG_H = 6
NMEM = 256
ALPHA = float(4 ** 0.25)
LN_EPS = 1e-5
RMS_EPS = 1e-6
NEG = -30000.0


class K:
    pass


def _copy(P, k, idx, out, in_, reads, writes, scale=None):
    if idx % 2 == 0:
        if scale is None:
            P.op("act", lambda e: e.activation(out=out, in_=in_, func=AF.Copy), reads=reads, writes=writes)
        else:
            P.op("act", lambda e: e.activation(out=out, in_=in_, func=AF.Copy, scale=scale), reads=reads, writes=writes)
    else:
        if scale is None:
            P.op("dve", lambda e: e.tensor_copy(out=out, in_=in_), reads=reads, writes=writes)
        else:
            P.op("dve", lambda e: e.tensor_single_scalar(out=out, in_=in_, scalar=scale, op=ALU.mult),
                 reads=reads, writes=writes)


def build_consts(P, k):
    k.ident = P.sbuf("ident", [128, 128], BF16)
    k.negtri = P.sbuf("negtri", [128, 128], BF16)
    k.negones = P.sbuf("negones", [128, 128], BF16)
    k.maskneg = P.sbuf("maskneg", [128, 128], BF16)
    k.blkmask = P.sbuf("blkmask", [128, 128], F32)
    k.scanmask = P.sbuf("scanmask", [128, 512], F32)
    tmp = P.sbuf("ctmp", [128, 128], F32)
    tmp1 = P.sbuf("ctmp1", [128, 128], F32)
    P.op("dve", lambda e: e.memset(tmp[:], 0.0), writes=[tmp])
    P.op("pool", lambda e: e.affine_select(out=tmp[:], in_=tmp[:], pattern=[[-1, 128]], compare_op=ALU.not_equal,
                                            fill=1.0, base=0, channel_multiplier=1), reads=[tmp], writes=[tmp])
    P.op("dve", lambda e: e.tensor_copy(out=k.ident[:], in_=tmp[:]), reads=[tmp], writes=[k.ident])
    P.op("dve", lambda e: e.memset(tmp1[:], 0.0), writes=[tmp1])
    P.op("pool", lambda e: e.affine_select(out=tmp1[:], in_=tmp1[:], pattern=[[1, 128]], compare_op=ALU.is_gt,
                                            fill=-1.0, base=0, channel_multiplier=-1), reads=[tmp1], writes=[tmp1])
    P.op("dve", lambda e: e.tensor_copy(out=k.negtri[:], in_=tmp1[:]), reads=[tmp1], writes=[k.negtri])
    P.op("dve", lambda e: e.tensor_single_scalar(out=k.maskneg[:], in_=tmp1[:], scalar=-NEG, op=ALU.mult),
         reads=[tmp1], writes=[k.maskneg])
    P.op("dve", lambda e: e.memset(k.negones[:], -1.0), writes=[k.negones])
    P.op("dve", lambda e: e.memset(k.blkmask[:], 1.0), writes=[k.blkmask])
    P.op("pool", lambda e: e.affine_select(out=k.blkmask[:], in_=k.blkmask[:], pattern=[[1, 128]], compare_op=ALU.is_ge,
                                            fill=0.0, base=0, channel_multiplier=-1), reads=[k.blkmask], writes=[k.blkmask])
    P.op("dve", lambda e: e.memset(k.blkmask[0:64, 64:128], 0.0), reads=[k.blkmask], writes=[k.blkmask])
    P.op("dve", lambda e: e.memset(k.scanmask[:], 1.0), writes=[k.scanmask])
    P.op("dve", lambda e: e.memset(k.scanmask[:].rearrange("p (c j) -> p c j", j=64)[:, :, 0:1], 0.0),
         reads=[k.scanmask], writes=[k.scanmask])
    k.epsln = P.sbuf("epsln", [128, 1], F32)
    P.op("dve", lambda e: e.memset(k.epsln[:], LN_EPS), writes=[k.epsln])
    k.pspair = [P.es.enter_context(P.nc.psum_tensor("pspair%d" % i, [128, 1024], F32)) for i in range(4)]
    k.ps = []
    for i in range(8):
        b = Buf(k.pspair[i // 2][:, (i % 2) * 512:(i % 2 + 1) * 512], "psb%d" % i, psum=True)
        P.bufs.append(b)
        k.ps.append(b)


def transpose_tile(P, k, src, src_ap_fn, nblk, psbank, dst, dst_ap, cidx, extra_reads=(), dkey=None):
    psb = psbank[:].bitcast(BF16)
    for b in range(nblk):
        P.I("pe", "transpose", out=psb[:, b * 128:(b + 1) * 128], in_=src_ap_fn(b), identity=k.ident[:],
            reads=[src, k.ident] + list(extra_reads), writes=[(psbank, b)])
    _copy(P, k, cidx, dst_ap, psb[:, 0:nblk * 128] if dst_ap.ndim == 2 else
          psb[:, 0:nblk * 128].rearrange("p (c t) -> p c t", t=128), reads=[psbank], writes=[(dst, dkey)])


def phase_A(P, k):
    with P.scope() as ls:
        W0 = P.sbuf("W0", [128, 8, 2560], BF16, ls)
        wsrc = k.w_in_sb[0].rearrange("(c p) n -> p c n", p=128)
        for c in range(8):
            P.dma("pool", W0[:, c, :], wsrc[:, c, :], reads=[], writes=[(W0, c)])
        xb = [P.sbuf("A_xb%d" % i, [128, 4, 1024], BF16, ls) for i in range(2)]
        xT = [P.sbuf("A_xT%d" % i, [128, 8, 512], BF16, ls) for i in range(2)]
        st = [P.sbuf("A_st%d" % i, [128, 512], BF16, ls) for i in range(4)]
        vst = [P.sbuf("A_vst%d" % i, [128, 4, 768], BF16, ls) for i in range(2)]
        cp = 0
        sti = 0
        rot = 0
        for g in range(8):
            xbg = xb[g % 2]
            xTg = xT[g % 2]
            P.dma("pool", xbg[:], k.x[g * 512:(g + 1) * 512, :].rearrange("(j p) d -> p j d", p=128),
                  reads=[], writes=[xbg])
            for j in range(4):
                transpose_tile(P, k, xbg, (lambda xbg, j: lambda b: xbg[:, j, b * 128:(b + 1) * 128])(xbg, j), 8, k.ps[j % 2],
                               xTg, xTg[:, :, j * 128:(j + 1) * 128], cp, dkey=j)
                cp += 1
            for fc in list(range(12)) + [18, 19]:
                bank = k.ps[2 + rot % 6]
                rot += 1
                for c in range(8):
                    P.I("pe", "matmul", bank[:], lhsT=W0[:, c, fc * 128:(fc + 1) * 128], rhs=xTg[:, c, :],
                        start=(c == 0), stop=(c == 7), reads=[W0, xTg], writes=[bank])
                s_ = st[sti % 4]
                sti += 1
                _copy(P, k, cp, s_[:], bank[:], reads=[bank], writes=[s_], scale=(0.125 if fc < 6 else None))
                cp += 1
                if fc < 6:
                    dst = k.qT0[fc * 128:(fc + 1) * 128, g * 512:(g + 1) * 512]
                    dbuf = k.qT0
                elif fc < 12:
                    dst = k.kT0[(fc - 6) * 128:(fc - 5) * 128, g * 512:(g + 1) * 512]
                    dbuf = k.kT0
                else:
                    dst = k.qmT[(fc - 18) * 128:(fc - 17) * 128, g * 512:(g + 1) * 512]
                    dbuf = k.qmT
                P.dma("sp", dst, s_[:], reads=[s_], writes=[(dbuf, (fc, g))])
            vs = vst[g % 2]
            for j in range(4):
                for (c0, n) in ((0, 512), (512, 256)):
                    bank = k.ps[2 + rot % 6]
                    rot += 1
                    for c in range(8):
                        P.I("pe", "matmul", bank[:, 0:n], lhsT=xTg[:, c, j * 128:(j + 1) * 128],
                            rhs=W0[:, c, 1536 + c0:1536 + c0 + n], start=(c == 0), stop=(c == 7),
                            reads=[W0, xTg], writes=[bank])
                    _copy(P, k, cp, vs[:, j, c0:c0 + n], bank[:, 0:n], reads=[bank], writes=[(vs, (j, c0))])
                    cp += 1
            P.dma("sp", k.v0[g * 512:(g + 1) * 512, :].rearrange("(j p) f -> p j f", p=128), vs[:],
                  reads=[vs], writes=[(k.v0, g)])
        P.barrier()


KB = 16


def phase_B(P, k):
    with P.scope() as ls:
        kTs = [P.sbuf("B_kT%d" % i, [128, S], BF16, ls) for i in range(2)]
        qTs = [P.sbuf("B_qT%d" % i, [128, S], BF16, ls) for i in range(2)]
        vvs = [P.sbuf("B_v%d" % i, [128, NT, 128], BF16, ls) for i in range(2)]
        NSP = KB + 2
        spb = [P.sbuf("B_sp%d" % i, [128, 2, 512], BF16, ls) for i in range(NSP)]
        Rb = [P.sbuf("B_Rb%d" % i, [128, 2, 512], BF16, ls) for i in range(NSP)]
        wb = [P.sbuf("B_w%d" % i, [128, 2, 512], BF16, ls) for i in range(NSP)]
        Rf = [P.sbuf("B_R%d" % i, [128, 2, 512], F32, ls) for i in range(2)]
        ost = [P.sbuf("B_ost%d" % i, [128, 512], BF16, ls) for i in range(2)]

        def pair(pi):
            return k.pspair[pi][:].rearrange("p (b c) -> p b c", c=512), [k.ps[2 * pi], k.ps[2 * pi + 1]]

        slots = [pair(0), pair(1), pair(2)]
        pso = [k.ps[6], k.ps[7]]
        units = []
        for hp in range(6):
            for g in range(8):
                for n in range(4 * g + 4):
                    i = 4 * g + 3 - n
                    col0 = max(0, (i - 4 * g) * 128)
                    units.append(dict(hp=hp, g=g, n=n, i=i, col0=col0, diag=(i >= 4 * g), first=(n == 0), last=(i == 0),
                                      gi=hp * 8 + g))
        NU = len(units)
        loaded = [-1]
        slot_ctr = [0]
        zslot = {}
        eslot = {}

        def next_slot():
            slot_ctr[0] += 1
            return slots[slot_ctr[0] % 3]

        def load_pair(hp):
            if loaded[0] >= hp or hp >= 6:
                return
            loaded[0] = hp
            kT, qT, vv = kTs[hp % 2], qTs[hp % 2], vvs[hp % 2]
            P.dma("sp", kT[:], k.kT0[hp * 128:(hp + 1) * 128, :], reads=[k.kT0], writes=[kT])
            P.dma("sp", qT[:], k.qT0[hp * 128:(hp + 1) * 128, :], reads=[k.qT0], writes=[qT])
            P.dma("sp", vv[:], k.v0[:, hp * 128:(hp + 1) * 128].rearrange("(i p) f -> p i f", p=128),
                  reads=[k.v0], writes=[vv])

        def qk(u, pz, stop):
            ap, bufs = pz
            c0, q0, i = u["col0"], u["g"] * 512, u["i"]
            kT, qT = kTs[u["hp"] % 2], qTs[u["hp"] % 2]
            for hh in range(2):
                b0 = 64 * hh
                P.I("pe", "matmul", ap[:, hh, c0:512], lhsT=kT[b0:b0 + 64, i * 128:(i + 1) * 128],
                    rhs=qT[b0:b0 + 64, q0 + c0:q0 + 512], start=True, stop=stop, reads=[kT, qT], writes=[bufs[hh]])

        def maskmm(u, pz):
            ap, bufs = pz
            c0 = u["col0"]
            for hh in range(2):
                P.I("pe", "matmul", ap[:, hh, c0:c0 + 128], lhsT=k.ident[:], rhs=k.maskneg[:], start=False, stop=True,
                    reads=[k.ident, k.maskneg], writes=[bufs[hh]])

        def st_z(idx):
            u = units[idx]
            load_pair(u["hp"])
            load_pair(u["hp"] + 1)
            pz = next_slot()
            zslot[idx] = pz
            qk(u, pz, not u["diag"])
            if u["diag"]:
                maskmm(u, pz)

        def st_SP(idx):
            u = units[idx]
            c0 = u["col0"]
            ap, bufs = zslot.pop(idx)
            sp = spb[idx % NSP]
            P.I("act", "activation", out=sp[:, :, c0:512], in_=ap[:, :, c0:512], func=AF.Softplus, reads=bufs, writes=[sp])

        def st_R(idx):
            u = units[idx]
            if u["last"]:
                return
            c0 = u["col0"]
            sp = spb[idx % NSP]
            R = Rf[u["gi"] % 2]
            if u["first"]:
                P.I("pool", "memset", R[:], 0.0, writes=[R])
            P.I("dve", "tensor_tensor", out=R[:, :, c0:512], in0=R[:, :, c0:512], in1=sp[:, :, c0:512], op=ALU.add,
                reads=[R, sp], writes=[R])
            rb = Rb[idx % NSP]
            P.I("dve", "tensor_copy", out=rb[:], in_=R[:], reads=[R], writes=[rb])

        def st_eg(idx):
            u = units[idx]
            c0 = u["col0"]
            pz = next_slot()
            eslot[idx] = pz
            ap, bufs = pz
            sp = spb[idx % NSP]
            qk(u, pz, False)
            for hh in range(2):
                P.I("pe", "matmul", ap[:, hh, c0:512], lhsT=k.negtri[:], rhs=sp[:, hh, c0:512], start=False,
                    stop=(u["first"] and not u["diag"]), reads=[k.negtri, sp], writes=[bufs[hh]])
            if not u["first"]:
                rb = Rb[(idx - 1) % NSP]
                for hh in range(2):
                    P.I("pe", "matmul", ap[:, hh, c0:512], lhsT=k.negones[:], rhs=rb[:, hh, c0:512], start=False,
                        stop=(not u["diag"]), reads=[k.negones, rb], writes=[bufs[hh]])
            if u["diag"]:
                maskmm(u, pz)

        def st_W(idx):
            u = units[idx]
            c0 = u["col0"]
            ap, bufs = eslot.pop(idx)
            w = wb[idx % NSP]
            P.I("act", "activation", out=w[:, :, c0:512], in_=ap[:, :, c0:512], func=AF.Exp, reads=bufs, writes=[w])

        def st_pv(idx):
            u = units[idx]
            c0, i = u["col0"], u["i"]
            w = wb[idx % NSP]
            bank = pso[u["gi"] % 2]
            vv = vvs[u["hp"] % 2]
            for hh in range(2):
                P.I("pe", "matmul", bank[64 * hh:64 * hh + 64, c0:512], lhsT=vv[:, i, hh * 64:(hh + 1) * 64], rhs=w[:, hh, c0:512],
                    start=u["first"], stop=u["last"], reads=[vv, w], writes=[bank])
            if u["last"]:
                o = ost[u["gi"] % 2]
                hp, g = u["hp"], u["g"]
                P.I("dve", "tensor_copy", out=o[:], in_=bank[:], reads=[bank], writes=[o])
                P.dma("pool", k.mixT[hp * 128:(hp + 1) * 128, g * 512:(g + 1) * 512], o[:], reads=[o], writes=[(k.mixT, (hp, g))])

        st_z(0)
        prev = []
        for b0 in range(0, NU, KB):
            ids = list(range(b0, min(b0 + KB, NU)))
            for n_, idx in enumerate(ids):
                if n_ + 1 < len(ids):
                    st_z(ids[n_ + 1])
                st_SP(idx)
                st_R(idx)
                if prev:
                    st_pv(prev.pop(0))
            while prev:
                st_pv(prev.pop(0))
            st_eg(ids[0])
            for n_, idx in enumerate(ids):
                if n_ + 1 < len(ids):
                    st_eg(ids[n_ + 1])
                elif ids[-1] + 1 < NU:
                    st_z(ids[-1] + 1)
                st_W(idx)
            prev = list(ids)
        while prev:
            st_pv(prev.pop(0))
        P.barrier()


class LN:
    def __init__(self, P, k, r, dst, gam, bet, bufs):
        self.P, self.k, self.r, self.dst, self.gam, self.bet = P, k, r, dst, gam, bet
        self.st, self.mv, self.rs, self.nmr = bufs

    def stats(self):
        P, r, st, mv = self.P, self.r, self.st, self.mv
        P.I("dve", "bn_stats", out=st[:, 0, :], in_=r[:, 0:512], reads=[r], writes=[(st, 0)])
        P.I("dve", "bn_stats", out=st[:, 1, :], in_=r[:, 512:1024], reads=[r], writes=[(st, 1)])
        P.I("dve", "bn_aggr", out=mv[:], in_=st[:].rearrange("p a b -> p (a b)"), reads=[st], writes=[mv])

    def rstd(self):
        P, k, mv, rs = self.P, self.k, self.mv, self.rs
        P.I("act", "activation", out=rs[:], in_=mv[:, 1:2], func=AF.Ln, bias=k.epsln[:, 0:1], reads=[mv, k.epsln], writes=[rs])
        P.I("act", "activation", out=rs[:], in_=rs[:], func=AF.Exp, scale=-0.5, reads=[rs], writes=[rs])

    def nmr_(self):
        P, mv, rs, nmr = self.P, self.mv, self.rs, self.nmr
        P.I("pool", "tensor_tensor", out=nmr[:], in0=mv[:, 0:1], in1=rs[:], op=ALU.mult, reads=[mv, rs], writes=[nmr])
        P.I("pool", "tensor_single_scalar", out=nmr[:], in_=nmr[:], scalar=-1.0, op=ALU.mult, reads=[nmr], writes=[nmr])

    def norm(self):
        P, r, rs, nmr, dst = self.P, self.r, self.rs, self.nmr, self.dst
        P.I("act", "activation", out=dst[:], in_=r[:], func=AF.Identity, scale=rs[:, 0:1], bias=nmr[:, 0:1],
            reads=[r, rs, nmr], writes=[dst])

    def affine(self):
        P, dst, gam, bet = self.P, self.dst, self.gam, self.bet
        P.I("pool", "tensor_tensor", out=dst[:], in0=dst[:], in1=gam[:], op=ALU.mult, reads=[dst, gam], writes=[dst])
        P.I("pool", "tensor_tensor", out=dst[:], in0=dst[:], in1=bet[:], op=ALU.add, reads=[dst, bet], writes=[dst])


def lnbufs(P, name, ls):
    return (P.sbuf(name + "st", [128, 2, 6], F32, ls), P.sbuf(name + "mv", [128, 2], F32, ls),
            P.sbuf(name + "rs", [128, 1], F32, ls), P.sbuf(name + "nm", [128, 1], F32, ls))


def load_C_weights(P, k, L, sc):
    w = K()
    w.wout = P.sbuf("C_wout", [128, 8, 1024], BF16, sc)
    wsrc = k.w_out[L].rearrange("(c p) n -> p c n", p=128)
    for c in range(0, 8, 2):
        P.dma("pool", w.wout[:, c:c + 2, :], wsrc[:, c:c + 2, :], writes=[(w.wout, c)])
    w.wkv = P.sbuf("C_wkv", [128, 8, 512], BF16, sc)
    P.dma("pool", w.wkv[:], k.w_mem_kv[L].rearrange("(c p) n -> p c n", p=128), writes=[w.wkv])
    w.memb = P.sbuf("C_memb", [128, 2, 1024], BF16, sc)
    P.dma("pool", w.memb[:], k.mem[:, :].rearrange("(j p) d -> p j d", p=128), writes=[w.memb])
    w.gam = P.sbuf("C_gam", [128, 1024], F32, sc)
    w.bet = P.sbuf("C_bet", [128, 1024], F32, sc)
    P.dma("sp", w.gam[:], k.ln_mix_g[L:L + 1, :].partition_broadcast(128), writes=[w.gam])
    P.dma("sp", w.bet[:], k.ln_mix_b[L:L + 1, :].partition_broadcast(128), writes=[w.bet])
    return w


def alloc_D_weights(P, k, L, sc):
    w = K()
    w.L = L
    w.WUP = P.sbuf("D_wup", [128, 8, DFF], BF16, sc)
    w.WDN = [P.sbuf("D_wdn0", [128, 16, 1024], BF16, sc), None]
    usrc = k.w_up[L].rearrange("(c p) f -> p c f", p=128)
    dsrc = k.w_down[L].rearrange("(c p) n -> p c n", p=128)
    w.pending = []
    for c in range(8):
        w.pending.append((w.WUP[:, c, :], usrc[:, c, :], (w.WUP, c)))
    for c in range(0, 16, 4):
        w.pending.append((w.WDN[0][:, c:c + 4, :], dsrc[:, c:c + 4, :], (w.WDN[0], c)))
    return w


def issue_pending(P, w, n):
    for _ in range(n):
        if w.pending:
            dst, src, wr = w.pending.pop(0)
            P.dma("pool", dst, src, writes=[wr])


def phase_C(P, k, L, xin, xout, w, dw=None):
    with P.scope() as ls:
        wout, wkv, memb, gam, bet = w.wout, w.wkv, w.memb, w.gam, w.bet
        memT = P.sbuf("C_memT", [128, 8, 256], BF16, ls)
        kmT = P.sbuf("C_kmT", [128, 2, 256], BF16, ls)
        vm = P.sbuf("C_vm", [128, 2, 256], BF16, ls)
        for j in range(2):
            transpose_tile(P, k, memb, (lambda j: lambda b: memb[:, j, b * 128:(b + 1) * 128])(j), 8, k.ps[j],
                           memT, memT[:, :, j * 128:(j + 1) * 128], j, dkey=j)
        for fc in range(2):
            bank = k.ps[2 + fc]
            for c in range(8):
                P.I("pe", "matmul", bank[:, 0:256], lhsT=wkv[:, c, fc * 128:(fc + 1) * 128], rhs=memT[:, c, :],
                    start=(c == 0), stop=(c == 7), reads=[wkv, memT], writes=[bank])
            _copy(P, k, fc, kmT[:, fc, :], bank[:, 0:256], reads=[bank], writes=[(kmT, fc)])
        for mc in range(2):
            bank = k.ps[4 + mc]
            for c in range(8):
                P.I("pe", "matmul", bank[:, 0:256], lhsT=memT[:, c, mc * 128:(mc + 1) * 128], rhs=wkv[:, c, 256:512],
                    start=(c == 0), stop=(c == 7), reads=[wkv, memT], writes=[bank])
            _copy(P, k, mc + 1, vm[:, mc, :], bank[:, 0:256], reads=[bank], writes=[(vm, mc)])

        qm = [P.sbuf("C_qm%d" % i, [128, 2, 128], BF16, ls) for i in range(3)]
        mx = [P.sbuf("C_mx%d" % i, [128, 6, 128], BF16, ls) for i in range(3)]
        xr = [P.sbuf("C_xr%d" % i, [128, 1024], F32, ls) for i in range(4)]
        E = [P.sbuf("C_E%d" % i, [128, 4, 256], F32, ls) for i in range(2)]
        Pb = [P.sbuf("C_P%d" % i, [128, 4, 256], BF16, ls) for i in range(2)]
        PT = [P.sbuf("C_PT%d" % i, [128, 8, 128], BF16, ls) for i in range(2)]
        mmT = [P.sbuf("C_mmT%d" % i, [128, 2, 128], BF16, ls) for i in range(2)]
        xo = [P.sbuf("C_xo%d" % i, [128, 1024], F32, ls) for i in range(2)]
        mxv = [P.sbuf("C_mxv%d" % i, [128, 4], F32, ls) for i in range(2)]
        nb_ = [P.sbuf("C_nb%d" % i, [128, 4], F32, ls) for i in range(2)]
        ssum = [P.sbuf("C_ss%d" % i, [128, 4], F32, ls) for i in range(2)]
        rsum = [P.sbuf("C_rs%d" % i, [128, 4], F32, ls) for i in range(2)]
        lnb = [lnbufs(P, "C_ln%d" % i, ls) for i in range(2)]
        psS = [[k.ps[0], k.ps[1]], [k.ps[2], k.ps[3]]]
        psT = k.ps[4]
        psMO = k.ps[5]
        psY = [k.ps[6], k.ps[7]]
        psTb = psT[:].bitcast(BF16)
        lns = {}

        def A0(t):
            P.dma("sp", qm[t % 3][:], k.qmT[:, t * 128:(t + 1) * 128].rearrange("(c p) t -> p c t", p=128), reads=[k.qmT],
                  writes=[qm[t % 3]])

        def A1(t):
            for h in range(4):
                bank = psS[t % 2][h % 2]
                p0 = 64 * (h % 2)
                P.I("pe", "matmul", bank[:, (h // 2) * 256:(h // 2 + 1) * 256], lhsT=qm[t % 3][p0:p0 + 64, h // 2, :],
                    rhs=kmT[p0:p0 + 64, h // 2, :], start=True, stop=True, reads=[qm[t % 3], kmT], writes=[(bank, h // 2)])

        def A2(t):
            b2 = t % 2
            for hb in range(2):
                P.I("dve", "tensor_reduce", out=mxv[b2][:, 2 * hb:2 * hb + 2], in_=psS[b2][hb][:].rearrange("p (h m) -> p h m", m=256),
                    axis=AX.X, op=ALU.max, reads=[psS[b2][hb]], writes=[(mxv[b2], hb)])
            P.I("dve", "tensor_single_scalar", out=nb_[b2][:], in_=mxv[b2][:], scalar=-0.125, op=ALU.mult,
                reads=[mxv[b2]], writes=[nb_[b2]])
            for h in range(4):
                q = (h % 2) * 2 + h // 2
                P.I("act", "activation", out=E[b2][:, h, :], in_=psS[b2][h % 2][:, (h // 2) * 256:(h // 2 + 1) * 256], func=AF.Exp,
                    scale=0.125, bias=nb_[b2][:, q:q + 1], accum_out=ssum[b2][:, h:h + 1],
                    reads=[psS[b2][h % 2], nb_[b2]], writes=[(E[b2], h), (ssum[b2], h)])

        def A3(t):
            b2 = t % 2
            P.I("dve", "reciprocal", out=rsum[b2][:], in_=ssum[b2][:], reads=[ssum[b2]], writes=[rsum[b2]])
            P.I("dve", "tensor_tensor", out=Pb[b2][:], in0=E[b2][:], in1=rsum[b2][:].unsqueeze(2).to_broadcast([128, 4, 256]),
                op=ALU.mult, reads=[E[b2], rsum[b2]], writes=[Pb[b2]])

        def A4(t):
            b2 = t % 2
            P.dma("sp", mx[t % 3][:], k.mixT[0:768, t * 128:(t + 1) * 128].rearrange("(c p) t -> p c t", p=128), reads=[k.mixT],
                  writes=[mx[t % 3]])
            for blk in range(8):
                P.I("pe", "transpose", out=psTb[:, blk * 128:(blk + 1) * 128], in_=Pb[b2][:, blk // 2, (blk % 2) * 128:(blk % 2 + 1) * 128],
                    identity=k.ident[:], reads=[Pb[b2], k.ident], writes=[(psT, blk)])
            P.I("act", "activation", out=PT[b2][:], in_=psTb[:, 0:1024].rearrange("p (c t) -> p c t", t=128), func=AF.Copy,
                reads=[psT], writes=[PT[b2]])

        def A5(t):
            b2 = t % 2
            P.dma("sp", xr[t % 4][:], xin[t * 128:(t + 1) * 128, :], reads=[xin], writes=[xr[t % 4]])
            for h in range(4):
                p0 = 64 * (h % 2)
                for mc in range(2):
                    P.I("pe", "matmul", psMO[p0:p0 + 64, (h // 2) * 128:(h // 2 + 1) * 128], lhsT=vm[:, mc, h * 64:(h + 1) * 64],
                        rhs=PT[b2][:, h * 2 + mc, :], start=(mc == 0), stop=(mc == 1), reads=[vm, PT[b2]], writes=[(psMO, h)])
            P.I("dve", "tensor_copy", out=mmT[b2][:], in_=psMO[:, 0:256].rearrange("p (c t) -> p c t", t=128), reads=[psMO],
                writes=[mmT[b2]])

        def A6(t):
            for nh in range(2):
                for c in range(8):
                    lhsT = mx[t % 3][:, c, :] if c < 6 else mmT[t % 2][:, c - 6, :]
                    P.I("pe", "matmul", psY[nh][:], lhsT=lhsT, rhs=wout[:, c, nh * 512:(nh + 1) * 512], start=(c == 0), stop=(c == 7),
                        reads=[mx[t % 3], mmT[t % 2], wout], writes=[psY[nh]])

        def A7(t):
            x_ = xr[t % 4]
            for nh in range(2):
                P.I("dve", "scalar_tensor_tensor", out=x_[:, nh * 512:(nh + 1) * 512], in0=x_[:, nh * 512:(nh + 1) * 512],
                    scalar=ALPHA, in1=psY[nh][:], op0=ALU.mult, op1=ALU.add, reads=[x_, psY[nh]], writes=[(x_, nh)])
            lns[t] = LN(P, k, x_, xo[t % 2], gam, bet, lnb[t % 2])
            lns[t].stats()
            lns[t].rstd()
            lns[t].nmr_()

        def A8(t):
            lns[t].norm()
            lns[t].affine()
            P.dma("sp", xout[t * 128:(t + 1) * 128, :], xo[t % 2][:], reads=[xo[t % 2]], writes=[(xout, t)])
            del lns[t]

        stages_ = [A1, A2, A3, A4, A5, A6, A7, A8]
        A0(0)
        for it in range(NT + len(stages_) - 1):
            if dw is not None and it % 2 == 0:
                issue_pending(P, dw, 1)
            for si in range(len(stages_) - 1, -1, -1):
                t = it - si
                if 0 <= t < NT:
                    stages_[si](t)
            if it + 1 < NT:
                A0(it + 1)
        if dw is not None:
            issue_pending(P, dw, 100)
        P.barrier()


def phase_D(P, k, L, xin, xout, w):
    with P.scope() as ls:
        WUP = w.WUP
        w.WDN[1] = P.sbuf("D_wdn1", [128, 16, 1024], BF16, ls)
        dsrc = k.w_down[L].rearrange("(c p) n -> p c n", p=128)
        issue_pending(P, w, 100)
        for c in range(0, 16, 4):
            P.dma("pool", w.WDN[1][:, c:c + 4, :], dsrc[:, 16 + c:20 + c, :], writes=[(w.WDN[1], c)])
        gam = P.sbuf("D_gam", [128, 1024], F32, ls)
        bet = P.sbuf("D_bet", [128, 1024], F32, ls)
        P.dma("sp", gam[:], k.ln_ffn_g[L:L + 1, :].partition_broadcast(128), writes=[gam])
        P.dma("sp", bet[:], k.ln_ffn_b[L:L + 1, :].partition_broadcast(128), writes=[bet])
        GT = 4
        NG = NT // GT
        NW = GT * 128
        xb = P.sbuf("D_xb", [128, GT, 1024], BF16, ls)
        xT = P.sbuf("D_xT", [128, 8, NW], BF16, ls)
        hT = P.sbuf("D_hT", [128, 32, NW], BF16, ls)
        rl = [P.sbuf("D_rl%d" % i, [128, NW], F32, ls) for i in range(2)]
        xr = [P.sbuf("D_xr%d" % i, [128, 1024], F32, ls) for i in range(2)]
        lnb = [lnbufs(P, "D_ln%d" % i, ls) for i in range(2)]
        psH = [k.ps[2], k.ps[3]]
        psY = [[k.ps[4], k.ps[5]], [k.ps[6], k.ps[7]]]
        st_ = dict(cp=0)

        def load_x(g):
            P.dma("pool", xb[:], xin[g * NW:(g + 1) * NW, :].rearrange("(j p) d -> p j d", p=128), reads=[xin], writes=[xb])

        def transposes(g):
            for j in range(GT):
                transpose_tile(P, k, xb, (lambda j: lambda blk: xb[:, j, blk * 128:(blk + 1) * 128])(j), 8, k.ps[j % 2],
                               xT, xT[:, :, j * 128:(j + 1) * 128], st_["cp"], dkey=j)
                st_["cp"] += 1

        pend = []

        def tail2():
            while pend:
                ln, tt, x_ = pend.pop(0)
                ln.norm()
                ln.affine()
                P.dma("sp", xout[tt * 128:(tt + 1) * 128, :], x_[:], reads=[x_], writes=[(xout, tt)])

        load_x(0)
        transposes(0)
        tl = 0
        for g in range(NG):
            if g + 1 < NG:
                load_x(g + 1)
            for fc in range(32):
                bank = psH[fc % 2]
                for c in range(8):
                    P.I("pe", "matmul", bank[:], lhsT=WUP[:, c, fc * 128:(fc + 1) * 128], rhs=xT[:, c, :], start=(c == 0), stop=(c == 7),
                        reads=[(WUP, c), xT], writes=[bank])
                r_ = rl[fc % 2]
                P.I("act", "activation", out=r_[:], in_=bank[:], func=AF.Relu, reads=[bank], writes=[r_])
                P.I("dve", "tensor_tensor", out=hT[:, fc, :], in0=r_[:], in1=r_[:], op=ALU.mult, reads=[r_], writes=[(hT, fc)])
            if g + 1 < NG:
                transposes(g + 1)
            for j in range(GT):
                tt = g * GT + j
                b = tl % 2
                tl += 1
                tail2()
                x_ = xr[b]
                P.dma("sp", x_[:], xin[tt * 128:(tt + 1) * 128, :], reads=[xin], writes=[x_])
                py = psY[b]
                for nh in range(2):
                    for fc in range(32):
                        wd = w.WDN[fc // 16]
                        P.I("pe", "matmul", py[nh][:], lhsT=hT[:, fc, j * 128:(j + 1) * 128], rhs=wd[:, fc % 16, nh * 512:(nh + 1) * 512],
                            start=(fc == 0), stop=(fc == 31), reads=[hT, (wd, (fc % 16) // 4 * 4)], writes=[py[nh]])
                for nh in range(2):
                    P.I("dve", "scalar_tensor_tensor", out=x_[:, nh * 512:(nh + 1) * 512], in0=x_[:, nh * 512:(nh + 1) * 512],
                        scalar=ALPHA, in1=py[nh][:], op0=ALU.mult, op1=ALU.add, reads=[x_, py[nh]], writes=[(x_, nh)])
                ln = LN(P, k, x_, x_, gam, bet, lnb[b])
                ln.stats()
                ln.rstd()
                ln.nmr_()
                pend.append((ln, tt, x_))
        tail2()
        P.barrier()


def phase_E(P, k):
    with P.scope() as ls:
        W1 = P.sbuf("E_W1", [128, 8, 3328], BF16, ls)
        wsrc = k.w_in_hg[0].rearrange("(c p) n -> p c n", p=128)
        for c in range(8):
            P.dma("pool", W1[:, c, :], wsrc[:, c, :], writes=[(W1, c)])
        l0 = P.sbuf("E_l0", [128, 6], F32, ls)
        l1 = P.sbuf("E_l1", [128, 6], F32, ls)
        for h in range(6):
            P.dma("sp", l0[:, h:h + 1], k.lower_bounds[0:1, h * 128:(h + 1) * 128].rearrange("o p -> p o"), writes=[(l0, h)])
            P.dma("sp", l1[:, h:h + 1], k.lower_bounds[1:2, h * 128:(h + 1) * 128].rearrange("o p -> p o"), writes=[(l1, h)])
        lb = P.sbuf("E_lb", [128, 6], F32, ls)
        omlb = P.sbuf("E_omlb", [128, 6], F32, ls)
        nomlb = P.sbuf("E_nomlb", [128, 6], F32, ls)
        ltmp = P.sbuf("E_ltmp", [128, 6], F32, ls)
        P.I("dve", "tensor_tensor", out=ltmp[:], in0=l0[:], in1=l1[:], op=ALU.subtract, reads=[l0, l1], writes=[ltmp])
        P.I("act", "activation", out=ltmp[:], in_=ltmp[:], func=AF.Exp, reads=[ltmp], writes=[ltmp])
        P.I("dve", "tensor_single_scalar", out=lb[:], in_=ltmp[:], scalar=1.0, op=ALU.add, reads=[ltmp], writes=[lb])
        P.I("dve", "reciprocal", out=lb[:], in_=lb[:], reads=[lb], writes=[lb])
        P.I("dve", "tensor_tensor", out=omlb[:], in0=ltmp[:], in1=lb[:], op=ALU.mult, reads=[ltmp, lb], writes=[omlb])
        P.I("dve", "tensor_single_scalar", out=nomlb[:], in_=omlb[:], scalar=-1.0, op=ALU.mult, reads=[omlb], writes=[nomlb])
        gn = P.sbuf("E_gn", [128, 768], F32, ls)
        P.dma("sp", gn[:], k.hg_norm_g[0:1, :].partition_broadcast(128), writes=[gn])
        epsr = P.sbuf("E_epsr", [128, 1], F32, ls)
        P.I("dve", "memset", epsr[:], RMS_EPS, writes=[epsr])
        state = P.sbuf("E_state", [128, 6, 128], F32, ls)
        sbf = [P.sbuf("E_sbf%d" % i, [128, 6, 128], BF16, ls) for i in range(2)]
        P.I("dve", "memset", state[:], 0.0, writes=[state])
        P.I("dve", "memset", sbf[0][:], 0.0, writes=[sbf[0]])
        xb = [P.sbuf("E_xb%d" % i, [128, 4, 1024], BF16, ls) for i in range(1)]
        xT = [P.sbuf("E_xT%d" % i, [128, 8, 512], BF16, ls) for i in range(1)]
        vbf = [P.sbuf("E_v%d" % i, [128, 4, 768], BF16, ls) for i in range(2)]
        sg = [P.sbuf("E_sg%d" % i, [128, 4, 768], F32, ls) for i in range(2)]
        qst = [P.sbuf("E_qst%d" % i, [128, 512], BF16, ls) for i in range(1)]
        A = [[P.sbuf("E_A%d_%d" % (s_, i), [128, 512], F32, ls) for i in range(5)] for s_ in range(3)]
        QT = [P.sbuf("E_QT%d" % i, [128, 6, 512], BF16, ls) for i in range(1)]
        KT = [P.sbuf("E_KT%d" % i, [128, 6, 512], BF16, ls) for i in range(1)]
        KD = [P.sbuf("E_KD%d" % i, [128, 6, 512], BF16, ls) for i in range(1)]
        EGL = [P.sbuf("E_EGL%d" % i, [128, 6, 8], F32, ls) for i in range(1)]
        KDA = [P.sbuf("E_KDA%d" % i, [128, 6, 128], BF16, ls) for i in range(4)]
        KDB = [P.sbuf("E_KDB%d" % i, [128, 6, 128], BF16, ls) for i in range(4)]
        for i in range(4):
            P.I("pool", "memset", KDA[i][:], 0.0, writes=[KDA[i]])
            P.I("pool", "memset", KDB[i][:], 0.0, writes=[KDB[i]])
        SCM = [P.sbuf("E_SCM%d" % i, [128, 6, 128], BF16, ls) for i in range(4)]
        junk = P.sbuf("E_junk", [128, 128], F32, ls)
        o32 = [P.sbuf("E_o32_%d" % i, [128, 768], F32, ls) for i in range(2)]
        ss = [P.sbuf("E_ss%d" % i, [128, 6], F32, ls) for i in range(2)]
        rstd = [P.sbuf("E_rstd%d" % i, [128, 6], F32, ls) for i in range(2)]
        t1 = [P.sbuf("E_t1_%d" % i, [128, 768], F32, ls) for i in range(1)]
        mixb = [P.sbuf("E_mixb%d" % i, [128, 768], BF16, ls) for i in range(1)]
        mst = [P.sbuf("E_mst%d" % i, [128, 6, 128], BF16, ls) for i in range(1)]
        psSC = [k.ps[0], k.ps[1]]
        psO = [k.ps[2], k.ps[3]]
        psKV = [k.ps[4], k.ps[5]]
        rotb = [k.ps[6], k.ps[7]]
        st_ = dict(rot=0, cp=0, hs=0, sb=0, tj=0)

        def nbank():
            st_["rot"] += 1
            return rotb[st_["rot"] % 2]

        pend_tail = []

        def flush_tail():
            while pend_tail:
                pend_tail.pop(0)()

        def hsl(bank_list, hd):
            return bank_list[hd // 4][:, (hd % 4) * 128:(hd % 4 + 1) * 128], bank_list[hd // 4]

        CUTE = 9.0
        for g in range(8 if CUTE >= 9 else 1):
            if CUTE < 2:
                break
            gb = g % 2
            xbg, xTg, vg, sgg = xb[0], xT[0], vbf[gb], sg[gb]

            def x_load_T(gn_):
                P.dma("pool", xbg[:], k.xmid[gn_ * 512:(gn_ + 1) * 512, :].rearrange("(j p) d -> p j d", p=128), reads=[k.xmid], writes=[xbg])
                for j in range(4):
                    transpose_tile(P, k, xbg, (lambda j: lambda b: xbg[:, j, b * 128:(b + 1) * 128])(j), 8, nbank(),
                                   xTg, xTg[:, :, j * 128:(j + 1) * 128], st_["cp"], dkey=j)
                    st_["cp"] += 1

            def tok_steps(gn_):
                vn, sn = vbf[gn_ % 2], sg[gn_ % 2]
                steps = []
                for j in range(4):
                    for n in range(3):
                        def st(j=j, n=n):
                            bank = nbank()
                            for c in range(8):
                                P.I("pe", "matmul", bank[:], lhsT=xTg[:, c, j * 128:(j + 1) * 128], rhs=W1[:, c, 1536 + 512 * n:2048 + 512 * n],
                                    start=(c == 0), stop=(c == 7), reads=[W1, xTg], writes=[bank])
                            if n == 0:
                                _copy(P, k, 0, vn[:, j, 0:512], bank[:], reads=[bank], writes=[(vn, (j, 0))])
                            elif n == 1:
                                _copy(P, k, 0, vn[:, j, 512:768], bank[:, 0:256], reads=[bank], writes=[(vn, (j, 1))])
                                _copy(P, k, 0, sn[:, j, 0:256], bank[:, 256:512], reads=[bank], writes=[(sn, (j, 0))])
                            else:
                                _copy(P, k, 0, sn[:, j, 256:768], bank[:], reads=[bank], writes=[(sn, (j, 1))])
                        steps.append(st)
                for fc in (24, 25):
                    def sq(fc=fc):
                        bank = nbank()
                        for c in range(8):
                            P.I("pe", "matmul", bank[:], lhsT=W1[:, c, fc * 128:(fc + 1) * 128], rhs=xTg[:, c, :], start=(c == 0), stop=(c == 7),
                                reads=[W1, xTg], writes=[bank])
                        s_ = qst[0]
                        _copy(P, k, 0, s_[:], bank[:], reads=[bank], writes=[s_])
                        P.dma("sp", k.qmT[(fc - 24) * 128:(fc - 23) * 128, gn_ * 512:(gn_ + 1) * 512], s_[:], reads=[s_],
                              writes=[(k.qmT, (fc, gn_))])
                    steps.append(sq)

                def ssilu():
                    P.I("act", "activation", out=sn[:], in_=sn[:], func=AF.Silu, reads=[sn], writes=[sn])
                    P.I("pool", "tensor_tensor", out=sn[:], in0=sn[:], in1=gn[:].unsqueeze(1).to_broadcast([128, 4, 768]), op=ALU.mult,
                        reads=[sn, gn], writes=[sn])
                steps.append(ssilu)
                return steps

            if g == 0:
                x_load_T(0)
                for st in tok_steps(0):
                    st()
            if CUTE < 3:
                continue
            for trio in range(2):
                hds = [trio * 3 + i for i in range(3)]
                Fb = [k.ps[i] for i in range(3)]
                Qb = [k.ps[3 + i] for i in range(3)]
                for i, hd in enumerate(hds):
                    for c in range(8):
                        P.I("pe", "matmul", Fb[i][:], lhsT=W1[:, c, 768 + hd * 128:768 + (hd + 1) * 128], rhs=xTg[:, c, :], start=(c == 0),
                            stop=(c == 7), reads=[W1, xTg], writes=[Fb[i]])
                for i, hd in enumerate(hds):
                    a = A[i]
                    P.I("act", "activation", out=a[0][:], in_=Fb[i][:], func=AF.Exp, scale=-1.0, reads=[Fb[i]], writes=[a[0]])
                for i, hd in enumerate(hds):
                    for c in range(8):
                        P.I("pe", "matmul", Qb[i][:], lhsT=W1[:, c, hd * 128:(hd + 1) * 128], rhs=xTg[:, c, :], start=(c == 0), stop=(c == 7),
                            reads=[W1, xTg], writes=[Qb[i]])
                for i, hd in enumerate(hds):
                    a = A[i]
                    P.I("act", "activation", out=a[0][:], in_=a[0][:], func=AF.Ln, bias=1.0, reads=[a[0]], writes=[a[0]])
                for i, hd in enumerate(hds):
                    a = A[i]
                    P.I("act", "activation", out=a[1][:], in_=a[0][:], func=AF.Exp, scale=-1.0, reads=[a[0]], writes=[a[1]])
                for i, hd in enumerate(hds):
                    a = A[i]
                    P.I("act", "activation", out=a[2][:], in_=a[1][:], func=AF.Ln, scale=omlb[:, hd:hd + 1], bias=lb[:, hd:hd + 1],
                        reads=[a[1], omlb, lb], writes=[a[2]])
                    P.I("dve", "tensor_scalar", out=a[3][:], in0=a[1][:], scalar1=nomlb[:, hd:hd + 1], scalar2=omlb[:, hd:hd + 1],
                        op0=ALU.mult, op1=ALU.add, reads=[a[1], nomlb, omlb], writes=[a[3]])
                for i, hd in enumerate(hds):
                    a = A[i]
                    P.I("dve", "tensor_tensor_scan", out=a[4][:], data0=k.scanmask[:], data1=a[2][:], initial=0.0, op0=ALU.mult,
                        op1=ALU.add, reads=[k.scanmask, a[2]], writes=[a[4]])
                for i, hd in enumerate(hds):
                    a = A[i]
                    G3 = a[4][:].rearrange("p (c j) -> p c j", j=64)
                    P.I("act", "activation", out=a[0][:], in_=a[4][:], func=AF.Exp, reads=[a[4]], writes=[a[0]])
                    P.I("act", "activation", out=a[1][:], in_=a[4][:], func=AF.Exp, scale=-1.0, reads=[a[4]], writes=[a[1]])
                    P.I("dve", "tensor_tensor", out=a[2][:].rearrange("p (c j) -> p c j", j=64), in0=G3,
                        in1=G3[:, :, 63:64].to_broadcast([128, 8, 64]), op=ALU.subtract, reads=[a[4]], writes=[a[2]])
                for i, hd in enumerate(hds):
                    a = A[i]
                    G3 = a[4][:].rearrange("p (c j) -> p c j", j=64)
                    P.I("dve", "tensor_tensor", out=QT[0][:, hd, :], in0=Qb[i][:], in1=a[0][:], op=ALU.mult, reads=[Qb[i], a[0]],
                        writes=[(QT[0], hd)])
                    P.I("dve", "tensor_tensor", out=KT[0][:, hd, :], in0=a[3][:], in1=a[1][:], op=ALU.mult, reads=[a[3], a[1]],
                        writes=[(KT[0], hd)])
                    P.I("act", "activation", out=a[2][:], in_=a[2][:], func=AF.Exp, scale=-1.0, reads=[a[2]], writes=[a[2]])
                    P.I("act", "activation", out=EGL[0][:, hd, :], in_=G3[:, :, 63], func=AF.Exp, reads=[a[4]], writes=[(EGL[0], hd)])
                for i, hd in enumerate(hds):
                    a = A[i]
                    P.I("dve", "tensor_tensor", out=KD[0][:, hd, :], in0=a[3][:], in1=a[2][:], op=ALU.mult, reads=[a[3], a[2]],
                        writes=[(KD[0], hd)])
            if CUTE < 4:
                continue
            for j in range(4):
                jc = slice(j * 128, (j + 1) * 128)
                bT = nbank()
                bTb = bT[:].bitcast(BF16)
                for hd in range(6):
                    P.I("pe", "transpose", out=bTb[:, hd * 128:(hd + 1) * 128], in_=KD[0][:, hd, jc], identity=k.ident[:],
                        reads=[(KD[0], hd), k.ident], writes=[(bT, hd)])
                P.I("dve", "tensor_copy", out=KDA[j][0:64, :, :], in_=bTb[0:64, 0:768].rearrange("p (h k) -> p h k", k=128),
                    reads=[bT], writes=[KDA[j]])
                P.I("dve", "tensor_copy", out=KDB[j][64:128, :, :], in_=bTb[64:128, 0:768].rearrange("p (h k) -> p h k", k=128),
                    reads=[bT], writes=[KDB[j]])
                for hd in range(6):
                    osl, ob = hsl(psSC, hd)
                    P.I("pe", "matmul", osl, lhsT=KT[0][:, hd, jc], rhs=QT[0][:, hd, jc], start=True, stop=True,
                        reads=[(KT[0], hd), (QT[0], hd)], writes=[(ob, hd)])
                P.I("dve", "tensor_tensor", out=SCM[j][:, 0:4, :], in0=psSC[0][:].rearrange("p (h t) -> p h t", t=128),
                    in1=k.blkmask[:].unsqueeze(1).to_broadcast([128, 4, 128]), op=ALU.mult, reads=[psSC[0], k.blkmask],
                    writes=[(SCM[j], 0)])
                P.I("dve", "tensor_tensor", out=SCM[j][:, 4:6, :], in0=psSC[1][:, 0:256].rearrange("p (h t) -> p h t", t=128),
                    in1=k.blkmask[:].unsqueeze(1).to_broadcast([128, 2, 128]), op=ALU.mult, reads=[psSC[1], k.blkmask],
                    writes=[(SCM[j], 1)])
            tokq = []
            if g + 1 < 8:
                x_load_T(g + 1)
                tokq = tok_steps(g + 1)

            def tok_drain(n, tokq=tokq):
                for _ in range(n):
                    if tokq:
                        tokq.pop(0)()
            for j in range(4):
                tj = st_["tj"]
                st_["tj"] += 1
                jb = tj % 2
                tt = g * 4 + j
                jc = slice(j * 128, (j + 1) * 128)
                if CUTE < 5:
                    continue
                sA, sB = sbf[0], sbf[1]
                for hd in range(6):
                    osl, ob = hsl(psO, hd)
                    vs_ = vg[:, j, hd * 128:(hd + 1) * 128]
                    P.I("pe", "matmul", osl, lhsT=SCM[j][:, hd, :], rhs=vs_, start=(hd % 4 == 0), stop=False,
                        reads=[SCM[j], vg], writes=[(ob, hd)])
                for hd in range(6):
                    osl, ob = hsl(psO, hd)
                    P.I("pe", "matmul", osl[0:64, :], lhsT=QT[0][:, hd, j * 128:j * 128 + 64], rhs=sA[:, hd, :], start=False, stop=False,
                        reads=[(QT[0], hd), (sA, hd)], writes=[(ob, hd)])
                for half, (KDx, s_src, s_dst) in enumerate(((KDA[j], sA, sB), (KDB[j], sB, sA))):
                    ch = j * 2 + half
                    for hd in range(6):
                        ksl, kb = hsl(psKV, hd)
                        P.I("pe", "matmul", ksl, lhsT=KDx[:, hd, :], rhs=vg[:, j, hd * 128:(hd + 1) * 128], start=True, stop=True,
                            reads=[KDx, vg], writes=[(kb, hd)])
                    if half == 0:
                        flush_tail()
                    tok_drain(2 if half == 0 else 1)
                    for hd in range(6):
                        ksl, kb = hsl(psKV, hd)
                        P.I("dve", "scalar_tensor_tensor", out=state[:, hd, :], in0=state[:, hd, :], scalar=EGL[0][:, hd, ch:ch + 1],
                            in1=ksl, op0=ALU.mult, op1=ALU.add, reads=[(state, hd), (EGL[0], hd), (kb, hd)], writes=[(state, hd)])
                        if hd % 3 != 2:
                            P.I("dve", "tensor_copy", out=s_dst[:, hd, :], in_=state[:, hd, :], reads=[(state, hd)], writes=[(s_dst, hd)])
                        else:
                            P.I("pool", "tensor_copy", out=s_dst[:, hd, :], in_=state[:, hd, :], reads=[(state, hd)], writes=[(s_dst, hd)])
                    if half == 0:
                        for hd in range(6):
                            osl, ob = hsl(psO, hd)
                            P.I("pe", "matmul", osl[64:128, :], lhsT=QT[0][:, hd, j * 128 + 64:(j + 1) * 128], rhs=sB[:, hd, :],
                                start=False, stop=True, reads=[(QT[0], hd), (sB, hd)], writes=[(ob, hd)])
                ob_ = o32[tj % 2]
                P.I("act", "activation", out=ob_[:, 0:512], in_=psO[0][:], func=AF.Copy, reads=[psO[0]], writes=[(ob_, 0)])
                P.I("act", "activation", out=ob_[:, 512:768], in_=psO[1][:, 0:256], func=AF.Copy, reads=[psO[1]], writes=[(ob_, 1)])

                def tail(jb=jb, j=j, tt=tt, ob_=ob_, sgg=sgg):
                    for hd in range(6):
                        P.I("act", "activation", out=junk[:], in_=ob_[:, hd * 128:(hd + 1) * 128], func=AF.Square,
                            accum_out=ss[jb][:, hd:hd + 1], reads=[ob_], writes=[junk, (ss[jb], hd)])
                    P.I("act", "activation", out=rstd[jb][:], in_=ss[jb][:], func=AF.Ln, scale=1.0 / 128.0, bias=epsr[:, 0:1],
                        reads=[ss[jb], epsr], writes=[rstd[jb]])
                    P.I("act", "activation", out=rstd[jb][:], in_=rstd[jb][:], func=AF.Exp, scale=-0.5, reads=[rstd[jb]], writes=[rstd[jb]])
                    P.I("dve", "tensor_tensor", out=t1[0][:].rearrange("p (h v) -> p h v", v=128),
                        in0=ob_[:].rearrange("p (h v) -> p h v", v=128), in1=rstd[jb][:].unsqueeze(2).to_broadcast([128, 6, 128]),
                        op=ALU.mult, reads=[ob_, rstd[jb]], writes=[t1[0]])
                    P.I("dve", "tensor_tensor", out=mixb[0][:], in0=t1[0][:], in1=sgg[:, j, :], op=ALU.mult, reads=[t1[0], sgg],
                        writes=[mixb[0]])
                    transpose_tile(P, k, mixb[0], lambda blk: mixb[0][:, blk * 128:(blk + 1) * 128], 6, nbank(), mst[0], mst[0][:], 1)
                    P.dma("sp", k.mixT[0:768, tt * 128:(tt + 1) * 128].rearrange("(c p) t -> p c t", p=128), mst[0][:], reads=[mst[0]],
                          writes=[(k.mixT, tt)])
                pend_tail.append(tail)
            tok_drain(100)
            flush_tail()
        P.barrier()


WSHAPES = dict(w_in_sb=[1, 1024, 2560], w_in_hg=[1, 1024, 3328], w_mem_kv=[2, 1024, 512], lower_bounds=[2, 768],
               hg_norm_g=[1, 768], w_out=[2, 1024, 1024], ln_mix_g=[2, 1024], ln_mix_b=[2, 1024],
               w_up=[2, 1024, 4096], w_down=[2, 4096, 1024], ln_ffn_g=[2, 1024], ln_ffn_b=[2, 1024])


def build(stages="ABCDE", debug=False):
    nc = bass.Bass("TRN2", target_bir_lowering=False)
    with ExitStack() as es:
        P = Prog(nc, es)
        k = K()
        k.x = P.dram("x", [S, D], F32, kind="ExternalInput")
        k.mem = P.dram("mem", [NMEM, D], F32, kind="ExternalInput")
        for name, shp in WSHAPES.items():
            setattr(k, name, P.dram(name, shp, F32, kind="ExternalInput"))
        skind = "ExternalOutput" if debug else "Internal"
        k.qT0 = P.dram("qT0", [768, S], BF16, kind=skind)
        k.kT0 = P.dram("kT0", [768, S], BF16, kind=skind)
        k.v0 = P.dram("v0", [S, 768], BF16, kind=skind)
        k.qmT = P.dram("qmT", [256, S], BF16, kind=skind)
        k.mixT = P.dram("mixT", [768, S], BF16, kind=skind)
        k.x1 = P.dram("x1", [S, D], F32, kind=skind)
        k.xmid = P.dram("xmid", [S, D], F32, kind=skind)
        k.y = P.dram("y", [S, D], F32, kind="ExternalOutput")
        build_consts(P, k)
        P.barrier()
        two = "E" in stages
        if "A" in stages:
            phase_A(P, k)
        scC = P.scope()
        cw = load_C_weights(P, k, 0, scC)
        if "B" in stages:
            phase_B(P, k)
        scD = P.scope()
        dw = alloc_D_weights(P, k, 0, scD)
        if "C" in stages:
            phase_C(P, k, 0, k.x, k.x1, cw, dw)
        scC.close()
        if "D" in stages:
            phase_D(P, k, 0, k.x1, k.xmid if two else k.y, dw)
        scD.close()
        if two:
            phase_E(P, k)
            scC = P.scope()
            cw = load_C_weights(P, k, 1, scC)
            scD = P.scope()
            dw = alloc_D_weights(P, k, 1, scD)
            phase_C(P, k, 1, k.xmid, k.x1, cw, dw)
            scC.close()
            phase_D(P, k, 1, k.x1, k.y, dw)
            scD.close()
        if "e" in stages:
            phase_E(P, k)
        P.finish()
        k.n_ins = P.n_ins
    return nc, k


_CACHE = {}


def kernel(**inputs):
    if "nc" not in _CACHE:
        _CACHE["nc"] = build("ABCDE")[0]
    nc = _CACHE["nc"]
    B = inputs["x"].shape[0]
    shared = {n: np.ascontiguousarray(inputs[n], dtype=np.float32) for n in WSHAPES}
    in_maps = []
    for b in range(B):
        m = dict(shared)
        m["x"] = np.ascontiguousarray(inputs["x"][b], dtype=np.float32)
        m["mem"] = np.ascontiguousarray(inputs["mem"][b], dtype=np.float32)
        in_maps.append(m)
    res = run_bass_kernel_spmd(nc, in_maps, core_ids=list(range(B)))
    return np.stack([np.asarray(r["y"], dtype=np.float32) for r in res.results], axis=0)
```

```python
from contextlib import ExitStack
import numpy as np
import concourse.bass as bass
import concourse.mybir as mybir
from concourse.bass_utils import run_bass_kernel_spmd

F32 = mybir.dt.float32
BF16 = mybir.dt.bfloat16
AF = mybir.ActivationFunctionType
ALU = mybir.AluOpType
AX = mybir.AxisListType

COMPUTE = ("pe", "act", "dve", "pool", "sp")
DMAQ = ("sp", "pool", "act")
NDMA_SEMS = 12


class Buf:
    def __init__(self, t, name, psum=False):
        self.t = t
        self.name = name
        self.psum = psum
        self.state = {}

    def __getitem__(self, idx):
        return self.t[idx]


class Scope:
    def __init__(self, prog):
        self.prog = prog
        self.bufs = []

    def __enter__(self):
        return self

    def __exit__(self, *a):
        self.close()
        return False

    def close(self):
        for b in self.bufs:
            self.prog.free(b)
        self.bufs = []


class Op:
    __slots__ = ("idx", "eng", "fn", "deps", "is_dma", "sig", "dma_slot", "dma_val", "eidx", "vc", "dmaknown")

    def __init__(self, idx, eng, fn, is_dma):
        self.idx = idx
        self.eng = eng
        self.fn = fn
        self.deps = set()
        self.is_dma = is_dma
        self.sig = 0
        self.eidx = 0
        self.vc = None
        self.dmaknown = None


class Prog:
    def __init__(self, nc, es):
        self.nc = nc
        self.es = es
        self.ops = []
        self.bufs = []
        self.engs = {"pe": nc.tensor, "act": nc.scalar, "dve": nc.vector, "pool": nc.gpsimd, "sp": nc.sync}
        self.sems = {e: es.enter_context(nc.semaphore("s_" + e)) for e in COMPUTE}
        self.dsems = {q: [es.enter_context(nc.semaphore("d_%s%d" % (q, i))) for i in range(NDMA_SEMS)]
                      for q in DMAQ}
        self.barrier_deps = {e: set() for e in self.engs}
        self.emitted = 0
        self.phase_start = 0
        NE = len(COMPUTE)
        self.cur_vc = {e: [0] * NE for e in self.engs}
        self.cur_dma = {e: set() for e in self.engs}
        self.cnt = {e: 0 for e in COMPUTE}
        self.scount = {e: 0 for e in COMPUTE}
        self.dcount = {q: 0 for q in DMAQ}
        self.n_ins = 0

    ARENA_BYTES = 212736

    def _arena_init(self):
        self.arena = self.es.enter_context(self.nc.sbuf_tensor("arena", [128, self.ARENA_BYTES // 2], BF16))
        self.free_list = [(0, self.ARENA_BYTES)]

    def scope(self):
        return Scope(self)

    def sbuf(self, name, shape, dt, scope=None):
        if not hasattr(self, "arena"):
            self._arena_init()
        esz = 4 if dt == F32 else 2
        n = 1
        for d in shape[1:]:
            n *= d
        nbytes = (n * esz + 63) // 64 * 64
        for i, (off, sz) in enumerate(self.free_list):
            if sz >= nbytes:
                if sz == nbytes:
                    self.free_list.pop(i)
                else:
                    self.free_list[i] = (off + nbytes, sz - nbytes)
                break
        else:
            raise AssertionError("arena out of SBUF for %s %s; free=%s" % (name, shape, self.free_list))
        ap = self.arena[0:shape[0], off // 2:off // 2 + n * esz // 2]
        if dt == F32:
            ap = ap.bitcast(F32)
        if len(shape) == 3:
            ap = ap.rearrange("p (a b) -> p a b", b=shape[2])
        elif len(shape) != 2:
            raise AssertionError("2-D / 3-D only")
        b = Buf(ap, name)
        b.region = (off, nbytes)
        self.bufs.append(b)
        if scope is not None:
            scope.bufs.append(b)
        return b

    def free(self, b):
        off, nbytes = b.region
        b.region = None
        fl = sorted(self.free_list + [(off, nbytes)])
        merged = []
        for o, sz in fl:
            if merged and merged[-1][0] + merged[-1][1] == o:
                merged[-1] = (merged[-1][0], merged[-1][1] + sz)
            else:
                merged.append((o, sz))
        self.free_list = merged

    def psum(self, name, shape, dt):
        t = self.es.enter_context(self.nc.psum_tensor(name, list(shape), dt))
        b = Buf(t, name, psum=True)
        self.bufs.append(b)
        return b

    def dram(self, name, shape, dt, kind="Internal"):
        t = self.nc.dram_tensor(name, list(shape), dt, kind=kind)
        b = Buf(t, name)
        self.bufs.append(b)
        return b

    @staticmethod
    def _conf(k1, k2):
        return k1 is None or k2 is None or k1 == k2

    def _rec(self, eng, fn, reads, writes, is_dma):
        op = Op(len(self.ops), eng, fn, is_dma)
        self.ops.append(op)
        for r in reads:
            b, key = r if isinstance(r, tuple) else (r, None)
            for k2, st in b.state.items():
                if st[0] is not None and (self._conf(key, k2) or (b.psum and self.ops[st[0]].eng != eng)):
                    op.deps.add(st[0])
                if b.psum:
                    op.deps.update(r2 for r2 in st[1] if self.ops[r2].eng != eng)
        for w in writes:
            b, key = w if isinstance(w, tuple) else (w, None)
            for k2, st in b.state.items():
                if self._conf(key, k2):
                    if st[0] is not None:
                        op.deps.add(st[0])
                    op.deps.update(st[1])
                elif b.psum:
                    if st[0] is not None and self.ops[st[0]].eng != eng:
                        op.deps.add(st[0])
                    op.deps.update(r2 for r2 in st[1] if self.ops[r2].eng != eng)
        op.deps |= self.barrier_deps[eng]
        self.barrier_deps[eng] = set()
        for r in reads:
            b, key = r if isinstance(r, tuple) else (r, None)
            st = b.state.setdefault(key, [None, []])
            st[1].append(op.idx)
        for w in writes:
            b, key = w if isinstance(w, tuple) else (w, None)
            if key is None:
                b.state = {None: [op.idx, []]}
            else:
                b.state[key] = [op.idx, []]
        op.deps.discard(op.idx)
        return op

    def op(self, eng, fn, reads=(), writes=()):
        return self._rec(eng, fn, reads, writes, False)

    def I(self, eng, meth, *args, reads=(), writes=(), **kw):
        return self._rec(eng, lambda e: getattr(e, meth)(*args, **kw), reads, writes, False)

    def dma(self, q, out, in_, reads=(), writes=()):
        return self._rec(q, lambda e: e.dma_start(out=out, in_=in_), reads, writes, True)

    def barrier(self):
        tails = set()
        last = {}
        for op in self.ops[self.phase_start:]:
            if op.is_dma:
                tails.add(op.idx)
            else:
                last[op.eng] = op.idx
        tails |= set(last.values())
        bop = Op(len(self.ops), "sp", lambda e: e.nop(), False)
        bop.deps = tails | self.barrier_deps["sp"]
        self.ops.append(bop)
        bop.sig = 1
        self.flush()
        for e in self.engs:
            self.barrier_deps[e] = {bop.idx}
        for b in self.bufs:
            b.state = {}
        self.phase_start = len(self.ops)

    def flush(self):
        ops = self.ops
        NE = len(COMPUTE)
        eid = {e: i for i, e in enumerate(COMPUTE)}
        new = ops[self.emitted:]
        for op in new:
            if not op.is_dma:
                self.cnt[op.eng] += 1
                op.eidx = self.cnt[op.eng]
        needed = []
        for op in new:
            vc = self.cur_vc[op.eng]
            dk = self.cur_dma[op.eng]
            real = []
            for d in sorted(op.deps, reverse=True):
                dop = ops[d]
                if dop.is_dma:
                    if d in dk:
                        continue
                    real.append(d)
                    dk.add(d)
                else:
                    if dop.eng == "pe" and op.eng == "pe":
                        continue
                    j = eid[dop.eng]
                    if vc[j] >= dop.eidx:
                        continue
                    real.append(d)
                    vc[j] = dop.eidx
                dk |= dop.dmaknown
                dvc = dop.vc
                for i in range(NE):
                    if dvc[i] > vc[i]:
                        vc[i] = dvc[i]
            op.vc = list(vc)
            op.dmaknown = set(dk)
            needed.append(real)
            for d in real:
                if not ops[d].sig:
                    assert d >= self.emitted, "dependency on an already emitted non-signalling op"
                    ops[d].sig = 1
        for op in new:
            if op.is_dma:
                n = self.dcount[op.eng]
                self.dcount[op.eng] += 1
                op.dma_slot = n % NDMA_SEMS
                op.dma_val = 16 * (n // NDMA_SEMS + 1)
            elif op.sig:
                self.scount[op.eng] += 1
                op.sig = self.scount[op.eng]
        for op, real in zip(new, needed):
            e = self.engs[op.eng]
            if op.is_dma and op.dma_val > 16:
                e.wait_ge(self.dsems[op.eng][op.dma_slot], op.dma_val - 16)
                self.n_ins += 1
            for d in real:
                dop = ops[d]
                if dop.is_dma:
                    e.wait_ge(self.dsems[dop.eng][dop.dma_slot], dop.dma_val)
                else:
                    e.wait_ge(self.sems[dop.eng], dop.sig)
                self.n_ins += 1
            ins = op.fn(e)
            self.n_ins += 1
            if op.is_dma:
                ins.then_inc(self.dsems[op.eng][op.dma_slot], 16)
            elif op.sig:
                ins.then_inc(self.sems[op.eng], 1)
            op.fn = None
        self.emitted = len(ops)

    def finish(self):
        self.barrier()


S = 4096
D = 1024
NT = S // 128
DFF = 4096
SB_H = 12
HG_H = 6
NMEM = 256
ALPHA = float(4 ** 0.25)
LN_EPS = 1e-5
RMS_EPS = 1e-6
NEG = -30000.0


class K:
    pass


def _copy(P, k, idx, out, in_, reads, writes, scale=None):
    if idx % 2 == 0:
        if scale is None:
            P.op("act", lambda e: e.activation(out=out, in_=in_, func=AF.Copy), reads=reads, writes=writes)
        else:
            P.op("act", lambda e: e.activation(out=out, in_=in_, func=AF.Copy, scale=scale), reads=reads, writes=writes)
    else:
        if scale is None:
            P.op("dve", lambda e: e.tensor_copy(out=out, in_=in_), reads=reads, writes=writes)
        else:
            P.op("dve", lambda e: e.tensor_single_scalar(out=out, in_=in_, scalar=scale, op=ALU.mult),
                 reads=reads, writes=writes)


def build_consts(P, k):
    k.ident = P.sbuf("ident", [128, 128], BF16)
    k.negtri = P.sbuf("negtri", [128, 128], BF16)
    k.negones = P.sbuf("negones", [128, 128], BF16)
    k.maskneg = P.sbuf("maskneg", [128, 128], BF16)
    k.blkmask = P.sbuf("blkmask", [128, 128], F32)
    k.scanmask = P.sbuf("scanmask", [128, 512], F32)
    tmp = P.sbuf("ctmp", [128, 128], F32)
    tmp1 = P.sbuf("ctmp1", [128, 128], F32)
    P.op("dve", lambda e: e.memset(tmp[:], 0.0), writes=[tmp])
    P.op("pool", lambda e: e.affine_select(out=tmp[:], in_=tmp[:], pattern=[[-1, 128]], compare_op=ALU.not_equal,
                                            fill=1.0, base=0, channel_multiplier=1), reads=[tmp], writes=[tmp])
    P.op("dve", lambda e: e.tensor_copy(out=k.ident[:], in_=tmp[:]), reads=[tmp], writes=[k.ident])
    P.op("dve", lambda e: e.memset(tmp1[:], 0.0), writes=[tmp1])
    P.op("pool", lambda e: e.affine_select(out=tmp1[:], in_=tmp1[:], pattern=[[1, 128]], compare_op=ALU.is_gt,
                                            fill=-1.0, base=0, channel_multiplier=-1), reads=[tmp1], writes=[tmp1])
    P.op("dve", lambda e: e.tensor_copy(out=k.negtri[:], in_=tmp1[:]), reads=[tmp1], writes=[k.negtri])
    P.op("dve", lambda e: e.tensor_single_scalar(out=k.maskneg[:], in_=tmp1[:], scalar=-NEG, op=ALU.mult),
         reads=[tmp1], writes=[k.maskneg])
    P.op("dve", lambda e: e.memset(k.negones[:], -1.0), writes=[k.negones])
    P.op("dve", lambda e: e.memset(k.blkmask[:], 1.0), writes=[k.blkmask])
    P.op("pool", lambda e: e.affine_select(out=k.blkmask[:], in_=k.blkmask[:], pattern=[[1, 128]], compare_op=ALU.is_ge,
                                            fill=0.0, base=0, channel_multiplier=-1), reads=[k.blkmask], writes=[k.blkmask])
    P.op("dve", lambda e: e.memset(k.blkmask[0:64, 64:128], 0.0), reads=[k.blkmask], writes=[k.blkmask])
    P.op("dve", lambda e: e.memset(k.scanmask[:], 1.0), writes=[k.scanmask])
    P.op("dve", lambda e: e.memset(k.scanmask[:].rearrange("p (c j) -> p c j", j=64)[:, :, 0:1], 0.0),
         reads=[k.scanmask], writes=[k.scanmask])
    k.epsln = P.sbuf("epsln", [128, 1], F32)
    P.op("dve", lambda e: e.memset(k.epsln[:], LN_EPS), writes=[k.epsln])
    k.pspair = [P.es.enter_context(P.nc.psum_tensor("pspair%d" % i, [128, 1024], F32)) for i in range(4)]
    k.ps = []
    for i in range(8):
        b = Buf(k.pspair[i // 2][:, (i % 2) * 512:(i % 2 + 1) * 512], "psb%d" % i, psum=True)
        P.bufs.append(b)
        k.ps.append(b)


def transpose_tile(P, k, src, src_ap_fn, nblk, psbank, dst, dst_ap, cidx, extra_reads=(), dkey=None):
    psb = psbank[:].bitcast(BF16)
    for b in range(nblk):
        P.I("pe", "transpose", out=psb[:, b * 128:(b + 1) * 128], in_=src_ap_fn(b), identity=k.ident[:],
            reads=[src, k.ident] + list(extra_reads), writes=[(psbank, b)])
    _copy(P, k, cidx, dst_ap, psb[:, 0:nblk * 128] if dst_ap.ndim == 2 else
          psb[:, 0:nblk * 128].rearrange("p (c t) -> p c t", t=128), reads=[psbank], writes=[(dst, dkey)])


def phase_A(P, k):
    with P.scope() as ls:
        W0 = P.sbuf("W0", [128, 8, 2560], BF16, ls)
        wsrc = k.w_in_sb[0].rearrange("(c p) n -> p c n", p=128)
        for c in range(8):
            P.dma("pool", W0[:, c, :], wsrc[:, c, :], reads=[], writes=[(W0, c)])
        xb = [P.sbuf("A_xb%d" % i, [128, 4, 1024], BF16, ls) for i in range(2)]
        xT = [P.sbuf("A_xT%d" % i, [128, 8, 512], BF16, ls) for i in range(2)]
        st = [P.sbuf("A_st%d" % i, [128, 512], BF16, ls) for i in range(4)]
        vst = [P.sbuf("A_vst%d" % i, [128, 4, 768], BF16, ls) for i in range(2)]
        cp = 0
        sti = 0
        rot = 0
        for g in range(8):
            xbg = xb[g % 2]
            xTg = xT[g % 2]
            P.dma("pool", xbg[:], k.x[g * 512:(g + 1) * 512, :].rearrange("(j p) d -> p j d", p=128),
                  reads=[], writes=[xbg])
            for j in range(4):
                transpose_tile(P, k, xbg, (lambda xbg, j: lambda b: xbg[:, j, b * 128:(b + 1) * 128])(xbg, j), 8, k.ps[j % 2],
                               xTg, xTg[:, :, j * 128:(j + 1) * 128], cp, dkey=j)
                cp += 1
            for fc in list(range(12)) + [18, 19]:
                bank = k.ps[2 + rot % 6]
                rot += 1
                for c in range(8):
                    P.I("pe", "matmul", bank[:], lhsT=W0[:, c, fc * 128:(fc + 1) * 128], rhs=xTg[:, c, :],
                        start=(c == 0), stop=(c == 7), reads=[W0, xTg], writes=[bank])
                s_ = st[sti % 4]
                sti += 1
                _copy(P, k, cp, s_[:], bank[:], reads=[bank], writes=[s_], scale=(0.125 if fc < 6 else None))
                cp += 1
                if fc < 6:
                    dst = k.qT0[fc * 128:(fc + 1) * 128, g * 512:(g + 1) * 512]
                    dbuf = k.qT0
                elif fc < 12:
                    dst = k.kT0[(fc - 6) * 128:(fc - 5) * 128, g * 512:(g + 1) * 512]
                    dbuf = k.kT0
                else:
                    dst = k.qmT[(fc - 18) * 128:(fc - 17) * 128, g * 512:(g + 1) * 512]
                    dbuf = k.qmT
                P.dma("sp", dst, s_[:], reads=[s_], writes=[(dbuf, (fc, g))])
            vs = vst[g % 2]
            for j in range(4):
                for (c0, n) in ((0, 512), (512, 256)):
                    bank = k.ps[2 + rot % 6]
                    rot += 1
                    for c in range(8):
                        P.I("pe", "matmul", bank[:, 0:n], lhsT=xTg[:, c, j * 128:(j + 1) * 128],
                            rhs=W0[:, c, 1536 + c0:1536 + c0 + n], start=(c == 0), stop=(c == 7),
                            reads=[W0, xTg], writes=[bank])
                    _copy(P, k, cp, vs[:, j, c0:c0 + n], bank[:, 0:n], reads=[bank], writes=[(vs, (j, c0))])
                    cp += 1
            P.dma("sp", k.v0[g * 512:(g + 1) * 512, :].rearrange("(j p) f -> p j f", p=128), vs[:],
                  reads=[vs], writes=[(k.v0, g)])
        P.barrier()


KB = 16


def phase_B(P, k):
    with P.scope() as ls:
        kTs = [P.sbuf("B_kT%d" % i, [128, S], BF16, ls) for i in range(2)]
        qTs = [P.sbuf("B_qT%d" % i, [128, S], BF16, ls) for i in range(2)]
        vvs = [P.sbuf("B_v%d" % i, [128, NT, 128], BF16, ls) for i in range(2)]
        NSP = KB + 2
        spb = [P.sbuf("B_sp%d" % i, [128, 2, 512], BF16, ls) for i in range(NSP)]
        Rb = [P.sbuf("B_Rb%d" % i, [128, 2, 512], BF16, ls) for i in range(NSP)]
        wb = [P.sbuf("B_w%d" % i, [128, 2, 512], BF16, ls) for i in range(NSP)]
        Rf = [P.sbuf("B_R%d" % i, [128, 2, 512], F32, ls) for i in range(2)]
        ost = [P.sbuf("B_ost%d" % i, [128, 512], BF16, ls) for i in range(2)]

        def pair(pi):
            return k.pspair[pi][:].rearrange("p (b c) -> p b c", c=512), [k.ps[2 * pi], k.ps[2 * pi + 1]]

        slots = [pair(0), pair(1), pair(2)]
        pso = [k.ps[6], k.ps[7]]
        units = []
        for hp in range(6):
            for g in range(8):
                for n in range(4 * g + 4):
                    i = 4 * g + 3 - n
                    col0 = max(0, (i - 4 * g) * 128)
                    units.append(dict(hp=hp, g=g, n=n, i=i, col0=col0, diag=(i >= 4 * g), first=(n == 0), last=(i == 0),
                                      gi=hp * 8 + g))
        NU = len(units)
        loaded = [-1]
        slot_ctr = [0]
        zslot = {}
        eslot = {}

        def next_slot():
            slot_ctr[0] += 1
            return slots[slot_ctr[0] % 3]

        def load_pair(hp):
            if loaded[0] >= hp or hp >= 6:
                return
            loaded[0] = hp
            kT, qT, vv = kTs[hp % 2], qTs[hp % 2], vvs[hp % 2]
            P.dma("sp", kT[:], k.kT0[hp * 128:(hp + 1) * 128, :], reads=[k.kT0], writes=[kT])
            P.dma("sp", qT[:], k.qT0[hp * 128:(hp + 1) * 128, :], reads=[k.qT0], writes=[qT])
            P.dma("sp", vv[:], k.v0[:, hp * 128:(hp + 1) * 128].rearrange("(i p) f -> p i f", p=128),
                  reads=[k.v0], writes=[vv])

        def qk(u, pz, stop):
            ap, bufs = pz
            c0, q0, i = u["col0"], u["g"] * 512, u["i"]
            kT, qT = kTs[u["hp"] % 2], qTs[u["hp"] % 2]
            for hh in range(2):
                b0 = 64 * hh
                P.I("pe", "matmul", ap[:, hh, c0:512], lhsT=kT[b0:b0 + 64, i * 128:(i + 1) * 128],
                    rhs=qT[b0:b0 + 64, q0 + c0:q0 + 512], start=True, stop=stop, reads=[kT, qT], writes=[bufs[hh]])

        def maskmm(u, pz):
            ap, bufs = pz
            c0 = u["col0"]
            for hh in range(2):
                P.I("pe", "matmul", ap[:, hh, c0:c0 + 128], lhsT=k.ident[:], rhs=k.maskneg[:], start=False, stop=True,
                    reads=[k.ident, k.maskneg], writes=[bufs[hh]])

        def st_z(idx):
            u = units[idx]
            load_pair(u["hp"])
            if idx - u["hp"] * 144 >= 2 * KB:
                load_pair(u["hp"] + 1)
            pz = next_slot()
            zslot[idx] = pz
            qk(u, pz, not u["diag"])
            if u["diag"]:
                maskmm(u, pz)

        def st_SP(idx):
            u = units[idx]
            c0 = u["col0"]
            ap, bufs = zslot.pop(idx)
            sp = spb[idx % NSP]
            P.I("act", "activation", out=sp[:, :, c0:512], in_=ap[:, :, c0:512], func=AF.Softplus, reads=bufs, writes=[sp])

        def st_R(idx):
            u = units[idx]
            if u["last"]:
                return
            c0 = u["col0"]
            sp = spb[idx % NSP]
            R = Rf[u["gi"] % 2]
            if u["first"]:
                P.I("pool", "memset", R[:], 0.0, writes=[R])
            P.I("dve", "tensor_tensor", out=R[:, :, c0:512], in0=R[:, :, c0:512], in1=sp[:, :, c0:512], op=ALU.add,
                reads=[R, sp], writes=[R])
            rb = Rb[idx % NSP]
            P.I("dve", "tensor_copy", out=rb[:], in_=R[:], reads=[R], writes=[rb])

        def st_eg(idx):
            u = units[idx]
            c0 = u["col0"]
            pz = next_slot()
            eslot[idx] = pz
            ap, bufs = pz
            sp = spb[idx % NSP]
            qk(u, pz, False)
            for hh in range(2):
                P.I("pe", "matmul", ap[:, hh, c0:512], lhsT=k.negtri[:], rhs=sp[:, hh, c0:512], start=False,
                    stop=(u["first"] and not u["diag"]), reads=[k.negtri, sp], writes=[bufs[hh]])
            if not u["first"]:
                rb = Rb[(idx - 1) % NSP]
                for hh in range(2):
                    P.I("pe", "matmul", ap[:, hh, c0:512], lhsT=k.negones[:], rhs=rb[:, hh, c0:512], start=False,
                        stop=(not u["diag"]), reads=[k.negones, rb], writes=[bufs[hh]])
            if u["diag"]:
                maskmm(u, pz)

        def st_W(idx):
            u = units[idx]
            c0 = u["col0"]
            ap, bufs = eslot.pop(idx)
            w = wb[idx % NSP]
            P.I("act", "activation", out=w[:, :, c0:512], in_=ap[:, :, c0:512], func=AF.Exp, reads=bufs, writes=[w])

        def st_pv(idx):
            u = units[idx]
            c0, i = u["col0"], u["i"]
            w = wb[idx % NSP]
            bank = pso[u["gi"] % 2]
            vv = vvs[u["hp"] % 2]
            for hh in range(2):
                P.I("pe", "matmul", bank[64 * hh:64 * hh + 64, c0:512], lhsT=vv[:, i, hh * 64:(hh + 1) * 64], rhs=w[:, hh, c0:512],
                    start=u["first"], stop=u["last"], reads=[vv, w], writes=[bank])
            if u["last"]:
                o = ost[u["gi"] % 2]
                hp, g = u["hp"], u["g"]
                P.I("dve", "tensor_copy", out=o[:], in_=bank[:], reads=[bank], writes=[o])
                P.dma("pool", k.mixT[hp * 128:(hp + 1) * 128, g * 512:(g + 1) * 512], o[:], reads=[o], writes=[(k.mixT, (hp, g))])

        st_z(0)
        prev = []
        for b0 in range(0, NU, KB):
            ids = list(range(b0, min(b0 + KB, NU)))
            for n_, idx in enumerate(ids):
                if n_ + 1 < len(ids):
                    st_z(ids[n_ + 1])
                st_SP(idx)
                st_R(idx)
                if prev:
                    st_pv(prev.pop(0))
            while prev:
                st_pv(prev.pop(0))
            st_eg(ids[0])
            for n_, idx in enumerate(ids):
                if n_ + 1 < len(ids):
                    st_eg(ids[n_ + 1])
                elif ids[-1] + 1 < NU:
                    st_z(ids[-1] + 1)
                st_W(idx)
            prev = list(ids)
        while prev:
            st_pv(prev.pop(0))
        P.barrier()


class LN:
    def __init__(self, P, k, r, dst, gam, bet, bufs):
        self.P, self.k, self.r, self.dst, self.gam, self.bet = P, k, r, dst, gam, bet
        self.st, self.mv, self.rs, self.nmr = bufs

    def stats(self):
        P, r, st, mv = self.P, self.r, self.st, self.mv
        P.I("dve", "bn_stats", out=st[:, 0, :], in_=r[:, 0:512], reads=[r], writes=[(st, 0)])
        P.I("dve", "bn_stats", out=st[:, 1, :], in_=r[:, 512:1024], reads=[r], writes=[(st, 1)])
        P.I("dve", "bn_aggr", out=mv[:], in_=st[:].rearrange("p a b -> p (a b)"), reads=[st], writes=[mv])

    def rstd(self):
        P, k, mv, rs = self.P, self.k, self.mv, self.rs
        P.I("act", "activation", out=rs[:], in_=mv[:, 1:2], func=AF.Ln, bias=k.epsln[:, 0:1], reads=[mv, k.epsln], writes=[rs])
        P.I("act", "activation", out=rs[:], in_=rs[:], func=AF.Exp, scale=-0.5, reads=[rs], writes=[rs])

    def nmr_(self):
        P, mv, rs, nmr = self.P, self.mv, self.rs, self.nmr
        P.I("pool", "tensor_tensor", out=nmr[:], in0=mv[:, 0:1], in1=rs[:], op=ALU.mult, reads=[mv, rs], writes=[nmr])
        P.I("pool", "tensor_single_scalar", out=nmr[:], in_=nmr[:], scalar=-1.0, op=ALU.mult, reads=[nmr], writes=[nmr])

    def norm(self):
        P, r, rs, nmr, dst = self.P, self.r, self.rs, self.nmr, self.dst
        P.I("act", "activation", out=dst[:], in_=r[:], func=AF.Identity, scale=rs[:, 0:1], bias=nmr[:, 0:1],
            reads=[r, rs, nmr], writes=[dst])

    def affine(self):
        P, dst, gam, bet = self.P, self.dst, self.gam, self.bet
        P.I("pool", "tensor_tensor", out=dst[:], in0=dst[:], in1=gam[:], op=ALU.mult, reads=[dst, gam], writes=[dst])
        P.I("pool", "tensor_tensor", out=dst[:], in0=dst[:], in1=bet[:], op=ALU.add, reads=[dst, bet], writes=[dst])


def lnbufs(P, name, ls):
    return (P.sbuf(name + "st", [128, 2, 6], F32, ls), P.sbuf(name + "mv", [128, 2], F32, ls),
            P.sbuf(name + "rs", [128, 1], F32, ls), P.sbuf(name + "nm", [128, 1], F32, ls))


def load_C_weights(P, k, L, sc):
    w = K()
    w.wout = P.sbuf("C_wout", [128, 8, 1024], BF16, sc)
    wsrc = k.w_out[L].rearrange("(c p) n -> p c n", p=128)
    for c in range(0, 8, 2):
        P.dma("pool", w.wout[:, c:c + 2, :], wsrc[:, c:c + 2, :], writes=[(w.wout, c)])
    w.wkv = P.sbuf("C_wkv", [128, 8, 512], BF16, sc)
    P.dma("pool", w.wkv[:], k.w_mem_kv[L].rearrange("(c p) n -> p c n", p=128), writes=[w.wkv])
    w.memb = P.sbuf("C_memb", [128, 2, 1024], BF16, sc)
    P.dma("pool", w.memb[:], k.mem[:, :].rearrange("(j p) d -> p j d", p=128), writes=[w.memb])
    w.gam = P.sbuf("C_gam", [128, 1024], F32, sc)
    w.bet = P.sbuf("C_bet", [128, 1024], F32, sc)
    P.dma("sp", w.gam[:], k.ln_mix_g[L:L + 1, :].partition_broadcast(128), writes=[w.gam])
    P.dma("sp", w.bet[:], k.ln_mix_b[L:L + 1, :].partition_broadcast(128), writes=[w.bet])
    return w


def alloc_D_weights(P, k, L, sc):
    w = K()
    w.L = L
    w.WUP = P.sbuf("D_wup", [128, 8, DFF], BF16, sc)
    w.WDN = [P.sbuf("D_wdn0", [128, 16, 1024], BF16, sc), None]
    usrc = k.w_up[L].rearrange("(c p) f -> p c f", p=128)
    dsrc = k.w_down[L].rearrange("(c p) n -> p c n", p=128)
    w.pending = []
    for c in range(8):
        w.pending.append((w.WUP[:, c, :], usrc[:, c, :], (w.WUP, c)))
    for c in range(0, 16, 4):
        w.pending.append((w.WDN[0][:, c:c + 4, :], dsrc[:, c:c + 4, :], (w.WDN[0], c)))
    return w


def issue_pending(P, w, n):
    for _ in range(n):
        if w.pending:
            dst, src, wr = w.pending.pop(0)
            P.dma("pool", dst, src, writes=[wr])


def phase_C(P, k, L, xin, xout, w, dw=None):
    with P.scope() as ls:
        wout, wkv, memb, gam, bet = w.wout, w.wkv, w.memb, w.gam, w.bet
        memT = P.sbuf("C_memT", [128, 8, 256], BF16, ls)
        kmT = P.sbuf("C_kmT", [128, 2, 256], BF16, ls)
        vm = P.sbuf("C_vm", [128, 2, 256], BF16, ls)
        for j in range(2):
            transpose_tile(P, k, memb, (lambda j: lambda b: memb[:, j, b * 128:(b + 1) * 128])(j), 8, k.ps[j],
                           memT, memT[:, :, j * 128:(j + 1) * 128], j, dkey=j)
        for fc in range(2):
            bank = k.ps[2 + fc]
            for c in range(8):
                P.I("pe", "matmul", bank[:, 0:256], lhsT=wkv[:, c, fc * 128:(fc + 1) * 128], rhs=memT[:, c, :],
                    start=(c == 0), stop=(c == 7), reads=[wkv, memT], writes=[bank])
            _copy(P, k, fc, kmT[:, fc, :], bank[:, 0:256], reads=[bank], writes=[(kmT, fc)])
        for mc in range(2):
            bank = k.ps[4 + mc]
            for c in range(8):
                P.I("pe", "matmul", bank[:, 0:256], lhsT=memT[:, c, mc * 128:(mc + 1) * 128], rhs=wkv[:, c, 256:512],
                    start=(c == 0), stop=(c == 7), reads=[wkv, memT], writes=[bank])
            _copy(P, k, mc + 1, vm[:, mc, :], bank[:, 0:256], reads=[bank], writes=[(vm, mc)])

        qm = [P.sbuf("C_qm%d" % i, [128, 2, 128], BF16, ls) for i in range(3)]
        mx = [P.sbuf("C_mx%d" % i, [128, 6, 128], BF16, ls) for i in range(3)]
        xr = [P.sbuf("C_xr%d" % i, [128, 1024], F32, ls) for i in range(4)]
        E = [P.sbuf("C_E%d" % i, [128, 4, 256], F32, ls) for i in range(2)]
        Pb = [P.sbuf("C_P%d" % i, [128, 4, 256], BF16, ls) for i in range(2)]
        PT = [P.sbuf("C_PT%d" % i, [128, 8, 128], BF16, ls) for i in range(2)]
        mmT = [P.sbuf("C_mmT%d" % i, [128, 2, 128], BF16, ls) for i in range(2)]
        xo = [P.sbuf("C_xo%d" % i, [128, 1024], F32, ls) for i in range(2)]
        mxv = [P.sbuf("C_mxv%d" % i, [128, 4], F32, ls) for i in range(2)]
        nb_ = [P.sbuf("C_nb%d" % i, [128, 4], F32, ls) for i in range(2)]
        ssum = [P.sbuf("C_ss%d" % i, [128, 4], F32, ls) for i in range(2)]
        rsum = [P.sbuf("C_rs%d" % i, [128, 4], F32, ls) for i in range(2)]
        lnb = [lnbufs(P, "C_ln%d" % i, ls) for i in range(2)]
        psS = [[k.ps[0], k.ps[1]], [k.ps[2], k.ps[3]]]
        psT = k.ps[4]
        psMO = k.ps[5]
        psY = [k.ps[6], k.ps[7]]
        psTb = psT[:].bitcast(BF16)
        lns = {}

        def A0(t):
            P.dma("sp", qm[t % 3][:], k.qmT[:, t * 128:(t + 1) * 128].rearrange("(c p) t -> p c t", p=128), reads=[k.qmT],
                  writes=[qm[t % 3]])

        def A1(t):
            for h in range(4):
                bank = psS[t % 2][h % 2]
                p0 = 64 * (h % 2)
                P.I("pe", "matmul", bank[:, (h // 2) * 256:(h // 2 + 1) * 256], lhsT=qm[t % 3][p0:p0 + 64, h // 2, :],
                    rhs=kmT[p0:p0 + 64, h // 2, :], start=True, stop=True, reads=[qm[t % 3], kmT], writes=[(bank, h // 2)])

        def A2(t):
            b2 = t % 2
            for hb in range(2):
                P.I("dve", "tensor_reduce", out=mxv[b2][:, 2 * hb:2 * hb + 2], in_=psS[b2][hb][:].rearrange("p (h m) -> p h m", m=256),
                    axis=AX.X, op=ALU.max, reads=[psS[b2][hb]], writes=[(mxv[b2], hb)])
            P.I("dve", "tensor_single_scalar", out=nb_[b2][:], in_=mxv[b2][:], scalar=-0.125, op=ALU.mult,
                reads=[mxv[b2]], writes=[nb_[b2]])
            for h in range(4):
                q = (h % 2) * 2 + h // 2
                P.I("act", "activation", out=E[b2][:, h, :], in_=psS[b2][h % 2][:, (h // 2) * 256:(h // 2 + 1) * 256], func=AF.Exp,
                    scale=0.125, bias=nb_[b2][:, q:q + 1], accum_out=ssum[b2][:, h:h + 1],
                    reads=[psS[b2][h % 2], nb_[b2]], writes=[(E[b2], h), (ssum[b2], h)])

        def A3(t):
            b2 = t % 2
            P.I("dve", "reciprocal", out=rsum[b2][:], in_=ssum[b2][:], reads=[ssum[b2]], writes=[rsum[b2]])
            P.I("dve", "tensor_tensor", out=Pb[b2][:], in0=E[b2][:], in1=rsum[b2][:].unsqueeze(2).to_broadcast([128, 4, 256]),
                op=ALU.mult, reads=[E[b2], rsum[b2]], writes=[Pb[b2]])

        def A4(t):
            b2 = t % 2
            P.dma("sp", mx[t % 3][:], k.mixT[0:768, t * 128:(t + 1) * 128].rearrange("(c p) t -> p c t", p=128), reads=[k.mixT],
                  writes=[mx[t % 3]])
            for blk in range(8):
                P.I("pe", "transpose", out=psTb[:, blk * 128:(blk + 1) * 128], in_=Pb[b2][:, blk // 2, (blk % 2) * 128:(blk % 2 + 1) * 128],
                    identity=k.ident[:], reads=[Pb[b2], k.ident], writes=[(psT, blk)])
            P.I("act", "activation", out=PT[b2][:], in_=psTb[:, 0:1024].rearrange("p (c t) -> p c t", t=128), func=AF.Copy,
                reads=[psT], writes=[PT[b2]])

        def A5(t):
            b2 = t % 2
            P.dma("sp", xr[t % 4][:], xin[t * 128:(t + 1) * 128, :], reads=[xin], writes=[xr[t % 4]])
            for h in range(4):
                p0 = 64 * (h % 2)
                for mc in range(2):
                    P.I("pe", "matmul", psMO[p0:p0 + 64, (h // 2) * 128:(h // 2 + 1) * 128], lhsT=vm[:, mc, h * 64:(h + 1) * 64],
                        rhs=PT[b2][:, h * 2 + mc, :], start=(mc == 0), stop=(mc == 1), reads=[vm, PT[b2]], writes=[(psMO, h)])
            P.I("dve", "tensor_copy", out=mmT[b2][:], in_=psMO[:, 0:256].rearrange("p (c t) -> p c t", t=128), reads=[psMO],
                writes=[mmT[b2]])

        def A6(t):
            for nh in range(2):
                for c in range(8):
                    lhsT = mx[t % 3][:, c, :] if c < 6 else mmT[t % 2][:, c - 6, :]
                    P.I("pe", "matmul", psY[nh][:], lhsT=lhsT, rhs=wout[:, c, nh * 512:(nh + 1) * 512], start=(c == 0), stop=(c == 7),
                        reads=[mx[t % 3], mmT[t % 2], wout], writes=[psY[nh]])

        def A7(t):
            x_ = xr[t % 4]
            for nh in range(2):
                P.I("dve", "scalar_tensor_tensor", out=x_[:, nh * 512:(nh + 1) * 512], in0=x_[:, nh * 512:(nh + 1) * 512],
                    scalar=ALPHA, in1=psY[nh][:], op0=ALU.mult, op1=ALU.add, reads=[x_, psY[nh]], writes=[(x_, nh)])
            lns[t] = LN(P, k, x_, xo[t % 2], gam, bet, lnb[t % 2])
            lns[t].stats()
            lns[t].rstd()
            lns[t].nmr_()

        def A8(t):
            lns[t].norm()
            lns[t].affine()
            P.dma("sp", xout[t * 128:(t + 1) * 128, :], xo[t % 2][:], reads=[xo[t % 2]], writes=[(xout, t)])
            del lns[t]

        stages_ = [A1, A2, A3, A4, A5, A6, A7, A8]
        A0(0)
        for it in range(NT + len(stages_) - 1):
            if dw is not None and it % 2 == 0:
                issue_pending(P, dw, 1)
            for si in range(len(stages_) - 1, -1, -1):
                t = it - si
                if 0 <= t < NT:
                    stages_[si](t)
            if it + 1 < NT:
                A0(it + 1)
        if dw is not None:
            issue_pending(P, dw, 100)
        P.barrier()


def phase_D(P, k, L, xin, xout, w):
    with P.scope() as ls:
        WUP = w.WUP
        w.WDN[1] = P.sbuf("D_wdn1", [128, 16, 1024], BF16, ls)
        dsrc = k.w_down[L].rearrange("(c p) n -> p c n", p=128)
        issue_pending(P, w, 100)
        for c in range(0, 16, 4):
            P.dma("pool", w.WDN[1][:, c:c + 4, :], dsrc[:, 16 + c:20 + c, :], writes=[(w.WDN[1], c)])
        gam = P.sbuf("D_gam", [128, 1024], F32, ls)
        bet = P.sbuf("D_bet", [128, 1024], F32, ls)
        P.dma("sp", gam[:], k.ln_ffn_g[L:L + 1, :].partition_broadcast(128), writes=[gam])
        P.dma("sp", bet[:], k.ln_ffn_b[L:L + 1, :].partition_broadcast(128), writes=[bet])
        GT = 4
        NG = NT // GT
        NW = GT * 128
        xb = P.sbuf("D_xb", [128, GT, 1024], BF16, ls)
        xT = P.sbuf("D_xT", [128, 8, NW], BF16, ls)
        hT = P.sbuf("D_hT", [128, 32, NW], BF16, ls)
        rl = [P.sbuf("D_rl%d" % i, [128, NW], F32, ls) for i in range(2)]
        xr = [P.sbuf("D_xr%d" % i, [128, 1024], F32, ls) for i in range(2)]
        lnb = [lnbufs(P, "D_ln%d" % i, ls) for i in range(2)]
        psH = [k.ps[2], k.ps[3]]
        psY = [[k.ps[4], k.ps[5]], [k.ps[6], k.ps[7]]]
        st_ = dict(cp=0)

        def load_x(g):
            P.dma("pool", xb[:], xin[g * NW:(g + 1) * NW, :].rearrange("(j p) d -> p j d", p=128), reads=[xin], writes=[xb])

        def transposes(g):
            for j in range(GT):
                transpose_tile(P, k, xb, (lambda j: lambda blk: xb[:, j, blk * 128:(blk + 1) * 128])(j), 8, k.ps[j % 2],
                               xT, xT[:, :, j * 128:(j + 1) * 128], st_["cp"], dkey=j)
                st_["cp"] += 1

        pend = []

        def tail2():
            while pend:
                ln, tt, x_ = pend.pop(0)
                ln.norm()
                ln.affine()
                P.dma("sp", xout[tt * 128:(tt + 1) * 128, :], x_[:], reads=[x_], writes=[(xout, tt)])

        load_x(0)
        transposes(0)
        tl = 0
        for g in range(NG):
            if g + 1 < NG:
                load_x(g + 1)
            for fc in range(32):
                bank = psH[fc % 2]
                for c in range(8):
                    P.I("pe", "matmul", bank[:], lhsT=WUP[:, c, fc * 128:(fc + 1) * 128], rhs=xT[:, c, :], start=(c == 0), stop=(c == 7),
                        reads=[(WUP, c), xT], writes=[bank])
                r_ = rl[fc % 2]
                P.I("act", "activation", out=r_[:], in_=bank[:], func=AF.Relu, reads=[bank], writes=[r_])
                P.I("dve", "tensor_tensor", out=hT[:, fc, :], in0=r_[:], in1=r_[:], op=ALU.mult, reads=[r_], writes=[(hT, fc)])
            if g + 1 < NG:
                transposes(g + 1)
            for j in range(GT):
                tt = g * GT + j
                b = tl % 2
                tl += 1
                tail2()
                x_ = xr[b]
                P.dma("sp", x_[:], xin[tt * 128:(tt + 1) * 128, :], reads=[xin], writes=[x_])
                py = psY[b]
                for nh in range(2):
                    for fc in range(32):
                        wd = w.WDN[fc // 16]
                        P.I("pe", "matmul", py[nh][:], lhsT=hT[:, fc, j * 128:(j + 1) * 128], rhs=wd[:, fc % 16, nh * 512:(nh + 1) * 512],
                            start=(fc == 0), stop=(fc == 31), reads=[hT, (wd, (fc % 16) // 4 * 4)], writes=[py[nh]])
                for nh in range(2):
                    P.I("dve", "scalar_tensor_tensor", out=x_[:, nh * 512:(nh + 1) * 512], in0=x_[:, nh * 512:(nh + 1) * 512],
                        scalar=ALPHA, in1=py[nh][:], op0=ALU.mult, op1=ALU.add, reads=[x_, py[nh]], writes=[(x_, nh)])
                ln = LN(P, k, x_, x_, gam, bet, lnb[b])
                ln.stats()
                ln.rstd()
                ln.nmr_()
                pend.append((ln, tt, x_))
        tail2()
        P.barrier()


def phase_E(P, k):
    with P.scope() as ls:
        W1 = P.sbuf("E_W1", [128, 8, 3328], BF16, ls)
        wsrc = k.w_in_hg[0].rearrange("(c p) n -> p c n", p=128)
        for c in range(8):
            P.dma("pool", W1[:, c, :], wsrc[:, c, :], writes=[(W1, c)])
        l0 = P.sbuf("E_l0", [128, 6], F32, ls)
        l1 = P.sbuf("E_l1", [128, 6], F32, ls)
        for h in range(6):
            P.dma("sp", l0[:, h:h + 1], k.lower_bounds[0:1, h * 128:(h + 1) * 128].rearrange("o p -> p o"), writes=[(l0, h)])
            P.dma("sp", l1[:, h:h + 1], k.lower_bounds[1:2, h * 128:(h + 1) * 128].rearrange("o p -> p o"), writes=[(l1, h)])
        lb = P.sbuf("E_lb", [128, 6], F32, ls)
        omlb = P.sbuf("E_omlb", [128, 6], F32, ls)
        nomlb = P.sbuf("E_nomlb", [128, 6], F32, ls)
        ltmp = P.sbuf("E_ltmp", [128, 6], F32, ls)
        P.I("dve", "tensor_tensor", out=ltmp[:], in0=l0[:], in1=l1[:], op=ALU.subtract, reads=[l0, l1], writes=[ltmp])
        P.I("act", "activation", out=ltmp[:], in_=ltmp[:], func=AF.Exp, reads=[ltmp], writes=[ltmp])
        P.I("dve", "tensor_single_scalar", out=lb[:], in_=ltmp[:], scalar=1.0, op=ALU.add, reads=[ltmp], writes=[lb])
        P.I("dve", "reciprocal", out=lb[:], in_=lb[:], reads=[lb], writes=[lb])
        P.I("dve", "tensor_tensor", out=omlb[:], in0=ltmp[:], in1=lb[:], op=ALU.mult, reads=[ltmp, lb], writes=[omlb])
        P.I("dve", "tensor_single_scalar", out=nomlb[:], in_=omlb[:], scalar=-1.0, op=ALU.mult, reads=[omlb], writes=[nomlb])
        gn = P.sbuf("E_gn", [128, 768], F32, ls)
        P.dma("sp", gn[:], k.hg_norm_g[0:1, :].partition_broadcast(128), writes=[gn])
        epsr = P.sbuf("E_epsr", [128, 1], F32, ls)
        P.I("dve", "memset", epsr[:], RMS_EPS, writes=[epsr])
        state = P.sbuf("E_state", [128, 6, 128], F32, ls)
        sbf = [P.sbuf("E_sbf%d" % i, [128, 6, 128], BF16, ls) for i in range(2)]
        P.I("dve", "memset", state[:], 0.0, writes=[state])
        P.I("dve", "memset", sbf[0][:], 0.0, writes=[sbf[0]])
        xb = [P.sbuf("E_xb%d" % i, [128, 4, 1024], BF16, ls) for i in range(1)]
        xT = [P.sbuf("E_xT%d" % i, [128, 8, 512], BF16, ls) for i in range(1)]
        vbf = [P.sbuf("E_v%d" % i, [128, 4, 768], BF16, ls) for i in range(2)]
        sg = [P.sbuf("E_sg%d" % i, [128, 4, 768], F32, ls) for i in range(2)]
        qst = [P.sbuf("E_qst%d" % i, [128, 512], BF16, ls) for i in range(1)]
        A = [[P.sbuf("E_A%d_%d" % (s_, i), [128, 512], F32, ls) for i in range(5)] for s_ in range(3)]
        QT = [P.sbuf("E_QT%d" % i, [128, 6, 512], BF16, ls) for i in range(1)]
        KT = [P.sbuf("E_KT%d" % i, [128, 6, 512], BF16, ls) for i in range(1)]
        KD = [P.sbuf("E_KD%d" % i, [128, 6, 512], BF16, ls) for i in range(1)]
        EGL = [P.sbuf("E_EGL%d" % i, [128, 6, 8], F32, ls) for i in range(1)]
        KDA = [P.sbuf("E_KDA%d" % i, [128, 6, 128], BF16, ls) for i in range(4)]
        KDB = [P.sbuf("E_KDB%d" % i, [128, 6, 128], BF16, ls) for i in range(4)]
        for i in range(4):
            P.I("pool", "memset", KDA[i][:], 0.0, writes=[KDA[i]])
            P.I("pool", "memset", KDB[i][:], 0.0, writes=[KDB[i]])
        SCM = [P.sbuf("E_SCM%d" % i, [128, 6, 128], BF16, ls) for i in range(4)]
        junk = P.sbuf("E_junk", [128, 128], F32, ls)
        o32 = [P.sbuf("E_o32_%d" % i, [128, 768], F32, ls) for i in range(2)]
        ss = [P.sbuf("E_ss%d" % i, [128, 6], F32, ls) for i in range(2)]
        rstd = [P.sbuf("E_rstd%d" % i, [128, 6], F32, ls) for i in range(2)]
        t1 = [P.sbuf("E_t1_%d" % i, [128, 768], F32, ls) for i in range(1)]
        mixb = [P.sbuf("E_mixb%d" % i, [128, 768], BF16, ls) for i in range(1)]
        mst = [P.sbuf("E_mst%d" % i, [128, 6, 128], BF16, ls) for i in range(1)]
        psSC = [k.ps[0], k.ps[1]]
        psO = [k.ps[2], k.ps[3]]
        psKV = [k.ps[4], k.ps[5]]
        rotb = [k.ps[6], k.ps[7]]
        st_ = dict(rot=0, cp=0, hs=0, sb=0, tj=0)

        def nbank():
            st_["rot"] += 1
            return rotb[st_["rot"] % 2]

        pend_tail = []

        def flush_tail():
            while pend_tail:
                pend_tail.pop(0)()

        def hsl(bank_list, hd):
            return bank_list[hd // 4][:, (hd % 4) * 128:(hd % 4 + 1) * 128], bank_list[hd // 4]

        CUTE = 9.0
        for g in range(8 if CUTE >= 9 else 1):
            if CUTE < 2:
                break
            gb = g % 2
            xbg, xTg, vg, sgg = xb[0], xT[0], vbf[gb], sg[gb]

            def x_load_T(gn_):
                P.dma("pool", xbg[:], k.xmid[gn_ * 512:(gn_ + 1) * 512, :].rearrange("(j p) d -> p j d", p=128), reads=[k.xmid], writes=[xbg])
                for j in range(4):
                    transpose_tile(P, k, xbg, (lambda j: lambda b: xbg[:, j, b * 128:(b + 1) * 128])(j), 8, nbank(),
                                   xTg, xTg[:, :, j * 128:(j + 1) * 128], st_["cp"], dkey=j)
                    st_["cp"] += 1

            def tok_steps(gn_):
                vn, sn = vbf[gn_ % 2], sg[gn_ % 2]
                steps = []
                for j in range(4):
                    for n in range(3):
                        def st(j=j, n=n):
                            bank = nbank()
                            for c in range(8):
                                P.I("pe", "matmul", bank[:], lhsT=xTg[:, c, j * 128:(j + 1) * 128], rhs=W1[:, c, 1536 + 512 * n:2048 + 512 * n],
                                    start=(c == 0), stop=(c == 7), reads=[W1, xTg], writes=[bank])
                            if n == 0:
                                _copy(P, k, 0, vn[:, j, 0:512], bank[:], reads=[bank], writes=[(vn, (j, 0))])
                            elif n == 1:
                                _copy(P, k, 0, vn[:, j, 512:768], bank[:, 0:256], reads=[bank], writes=[(vn, (j, 1))])
                                _copy(P, k, 0, sn[:, j, 0:256], bank[:, 256:512], reads=[bank], writes=[(sn, (j, 0))])
                            else:
                                _copy(P, k, 0, sn[:, j, 256:768], bank[:], reads=[bank], writes=[(sn, (j, 1))])
                        steps.append(st)
                for fc in (24, 25):
                    def sq(fc=fc):
                        bank = nbank()
                        for c in range(8):
                            P.I("pe", "matmul", bank[:], lhsT=W1[:, c, fc * 128:(fc + 1) * 128], rhs=xTg[:, c, :], start=(c == 0), stop=(c == 7),
                                reads=[W1, xTg], writes=[bank])
                        s_ = qst[0]
                        _copy(P, k, 0, s_[:], bank[:], reads=[bank], writes=[s_])
                        P.dma("sp", k.qmT[(fc - 24) * 128:(fc - 23) * 128, gn_ * 512:(gn_ + 1) * 512], s_[:], reads=[s_],
                              writes=[(k.qmT, (fc, gn_))])
                    steps.append(sq)

                def ssilu():
                    P.I("act", "activation", out=sn[:], in_=sn[:], func=AF.Silu, reads=[sn], writes=[sn])
                    P.I("pool", "tensor_tensor", out=sn[:], in0=sn[:], in1=gn[:].unsqueeze(1).to_broadcast([128, 4, 768]), op=ALU.mult,
                        reads=[sn, gn], writes=[sn])
                steps.append(ssilu)
                return steps

            if g == 0:
                x_load_T(0)
                for st in tok_steps(0):
                    st()
            if CUTE < 3:
                continue
            for trio in range(2):
                hds = [trio * 3 + i for i in range(3)]
                Fb = [k.ps[i] for i in range(3)]
                Qb = [k.ps[3 + i] for i in range(3)]
                for i, hd in enumerate(hds):
                    for c in range(8):
                        P.I("pe", "matmul", Fb[i][:], lhsT=W1[:, c, 768 + hd * 128:768 + (hd + 1) * 128], rhs=xTg[:, c, :], start=(c == 0),
                            stop=(c == 7), reads=[W1, xTg], writes=[Fb[i]])
                for i, hd in enumerate(hds):
                    a = A[i]
                    P.I("act", "activation", out=a[0][:], in_=Fb[i][:], func=AF.Exp, scale=-1.0, reads=[Fb[i]], writes=[a[0]])
                for i, hd in enumerate(hds):
                    for c in range(8):
                        P.I("pe", "matmul", Qb[i][:], lhsT=W1[:, c, hd * 128:(hd + 1) * 128], rhs=xTg[:, c, :], start=(c == 0), stop=(c == 7),
                            reads=[W1, xTg], writes=[Qb[i]])
                for i, hd in enumerate(hds):
                    a = A[i]
                    P.I("act", "activation", out=a[0][:], in_=a[0][:], func=AF.Ln, bias=1.0, reads=[a[0]], writes=[a[0]])
                for i, hd in enumerate(hds):
                    a = A[i]
                    P.I("act", "activation", out=a[1][:], in_=a[0][:], func=AF.Exp, scale=-1.0, reads=[a[0]], writes=[a[1]])
                for i, hd in enumerate(hds):
                    a = A[i]
                    P.I("act", "activation", out=a[2][:], in_=a[1][:], func=AF.Ln, scale=omlb[:, hd:hd + 1], bias=lb[:, hd:hd + 1],
                        reads=[a[1], omlb, lb], writes=[a[2]])
                    P.I("dve", "tensor_scalar", out=a[3][:], in0=a[1][:], scalar1=nomlb[:, hd:hd + 1], scalar2=omlb[:, hd:hd + 1],
                        op0=ALU.mult, op1=ALU.add, reads=[a[1], nomlb, omlb], writes=[a[3]])
                for i, hd in enumerate(hds):
                    a = A[i]
                    P.I("dve", "tensor_tensor_scan", out=a[4][:], data0=k.scanmask[:], data1=a[2][:], initial=0.0, op0=ALU.mult,
                        op1=ALU.add, reads=[k.scanmask, a[2]], writes=[a[4]])
                for i, hd in enumerate(hds):
                    a = A[i]
                    G3 = a[4][:].rearrange("p (c j) -> p c j", j=64)
                    P.I("act", "activation", out=a[0][:], in_=a[4][:], func=AF.Exp, reads=[a[4]], writes=[a[0]])
                    P.I("act", "activation", out=a[1][:], in_=a[4][:], func=AF.Exp, scale=-1.0, reads=[a[4]], writes=[a[1]])
                    P.I("dve", "tensor_tensor", out=a[2][:].rearrange("p (c j) -> p c j", j=64), in0=G3,
                        in1=G3[:, :, 63:64].to_broadcast([128, 8, 64]), op=ALU.subtract, reads=[a[4]], writes=[a[2]])
                for i, hd in enumerate(hds):
                    a = A[i]
                    G3 = a[4][:].rearrange("p (c j) -> p c j", j=64)
                    P.I("dve", "tensor_tensor", out=QT[0][:, hd, :], in0=Qb[i][:], in1=a[0][:], op=ALU.mult, reads=[Qb[i], a[0]],
                        writes=[(QT[0], hd)])
                    P.I("dve", "tensor_tensor", out=KT[0][:, hd, :], in0=a[3][:], in1=a[1][:], op=ALU.mult, reads=[a[3], a[1]],
                        writes=[(KT[0], hd)])
                    P.I("act", "activation", out=a[2][:], in_=a[2][:], func=AF.Exp, scale=-1.0, reads=[a[2]], writes=[a[2]])
                    P.I("act", "activation", out=EGL[0][:, hd, :], in_=G3[:, :, 63], func=AF.Exp, reads=[a[4]], writes=[(EGL[0], hd)])
                for i, hd in enumerate(hds):
                    a = A[i]
                    P.I("dve", "tensor_tensor", out=KD[0][:, hd, :], in0=a[3][:], in1=a[2][:], op=ALU.mult, reads=[a[3], a[2]],
                        writes=[(KD[0], hd)])
            if CUTE < 4:
                continue
            for j in range(4):
                jc = slice(j * 128, (j + 1) * 128)
                bT = nbank()
                bTb = bT[:].bitcast(BF16)
                for hd in range(6):
                    P.I("pe", "transpose", out=bTb[:, hd * 128:(hd + 1) * 128], in_=KD[0][:, hd, jc], identity=k.ident[:],
                        reads=[(KD[0], hd), k.ident], writes=[(bT, hd)])
                P.I("dve", "tensor_copy", out=KDA[j][0:64, :, :], in_=bTb[0:64, 0:768].rearrange("p (h k) -> p h k", k=128),
                    reads=[bT], writes=[KDA[j]])
                P.I("dve", "tensor_copy", out=KDB[j][64:128, :, :], in_=bTb[64:128, 0:768].rearrange("p (h k) -> p h k", k=128),
                    reads=[bT], writes=[KDB[j]])
                for hd in range(6):
                    osl, ob = hsl(psSC, hd)
                    P.I("pe", "matmul", osl, lhsT=KT[0][:, hd, jc], rhs=QT[0][:, hd, jc], start=True, stop=True,
                        reads=[(KT[0], hd), (QT[0], hd)], writes=[(ob, hd)])
                P.I("dve", "tensor_tensor", out=SCM[j][:, 0:4, :], in0=psSC[0][:].rearrange("p (h t) -> p h t", t=128),
                    in1=k.blkmask[:].unsqueeze(1).to_broadcast([128, 4, 128]), op=ALU.mult, reads=[psSC[0], k.blkmask],
                    writes=[(SCM[j], 0)])
                P.I("dve", "tensor_tensor", out=SCM[j][:, 4:6, :], in0=psSC[1][:, 0:256].rearrange("p (h t) -> p h t", t=128),
                    in1=k.blkmask[:].unsqueeze(1).to_broadcast([128, 2, 128]), op=ALU.mult, reads=[psSC[1], k.blkmask],
                    writes=[(SCM[j], 1)])
            tokq = []
            if g + 1 < 8:
                x_load_T(g + 1)
                tokq = tok_steps(g + 1)

            def tok_drain(n, tokq=tokq):
                for _ in range(n):
                    if tokq:
                        tokq.pop(0)()
            for j in range(4):
                tj = st_["tj"]
                st_["tj"] += 1
                jb = tj % 2
                tt = g * 4 + j
                jc = slice(j * 128, (j + 1) * 128)
                if CUTE < 5:
                    continue
                sA, sB = sbf[0], sbf[1]
                for hd in range(6):
                    osl, ob = hsl(psO, hd)
                    vs_ = vg[:, j, hd * 128:(hd + 1) * 128]
                    P.I("pe", "matmul", osl, lhsT=SCM[j][:, hd, :], rhs=vs_, start=(hd % 4 == 0), stop=False,
                        reads=[SCM[j], vg], writes=[(ob, hd)])
                for hd in range(6):
                    osl, ob = hsl(psO, hd)
                    P.I("pe", "matmul", osl[0:64, :], lhsT=QT[0][:, hd, j * 128:j * 128 + 64], rhs=sA[:, hd, :], start=False, stop=False,
                        reads=[(QT[0], hd), (sA, hd)], writes=[(ob, hd)])
                for half, (KDx, s_src, s_dst) in enumerate(((KDA[j], sA, sB), (KDB[j], sB, sA))):
                    ch = j * 2 + half
                    for hd in range(6):
                        ksl, kb = hsl(psKV, hd)
                        P.I("pe", "matmul", ksl, lhsT=KDx[:, hd, :], rhs=vg[:, j, hd * 128:(hd + 1) * 128], start=True, stop=True,
                            reads=[KDx, vg], writes=[(kb, hd)])
                    if half == 0:
                        flush_tail()
                    tok_drain(2 if half == 0 else 1)
                    for hd in range(6):
                        ksl, kb = hsl(psKV, hd)
                        P.I("dve", "scalar_tensor_tensor", out=state[:, hd, :], in0=state[:, hd, :], scalar=EGL[0][:, hd, ch:ch + 1],
                            in1=ksl, op0=ALU.mult, op1=ALU.add, reads=[(state, hd), (EGL[0], hd), (kb, hd)], writes=[(state, hd)])
                        if hd % 3 != 2:
                            P.I("dve", "tensor_copy", out=s_dst[:, hd, :], in_=state[:, hd, :], reads=[(state, hd)], writes=[(s_dst, hd)])
                        else:
                            P.I("pool", "tensor_copy", out=s_dst[:, hd, :], in_=state[:, hd, :], reads=[(state, hd)], writes=[(s_dst, hd)])
                    if half == 0:
                        for hd in range(6):
                            osl, ob = hsl(psO, hd)
                            P.I("pe", "matmul", osl[64:128, :], lhsT=QT[0][:, hd, j * 128 + 64:(j + 1) * 128], rhs=sB[:, hd, :],
                                start=False, stop=True, reads=[(QT[0], hd), (sB, hd)], writes=[(ob, hd)])
                ob_ = o32[tj % 2]
                P.I("act", "activation", out=ob_[:, 0:512], in_=psO[0][:], func=AF.Copy, reads=[psO[0]], writes=[(ob_, 0)])
                P.I("act", "activation", out=ob_[:, 512:768], in_=psO[1][:, 0:256], func=AF.Copy, reads=[psO[1]], writes=[(ob_, 1)])

                def tail(jb=jb, j=j, tt=tt, ob_=ob_, sgg=sgg):
                    for hd in range(6):
                        P.I("act", "activation", out=junk[:], in_=ob_[:, hd * 128:(hd + 1) * 128], func=AF.Square,
                            accum_out=ss[jb][:, hd:hd + 1], reads=[ob_], writes=[junk, (ss[jb], hd)])
                    P.I("act", "activation", out=rstd[jb][:], in_=ss[jb][:], func=AF.Ln, scale=1.0 / 128.0, bias=epsr[:, 0:1],
                        reads=[ss[jb], epsr], writes=[rstd[jb]])
                    P.I("act", "activation", out=rstd[jb][:], in_=rstd[jb][:], func=AF.Exp, scale=-0.5, reads=[rstd[jb]], writes=[rstd[jb]])
                    P.I("dve", "tensor_tensor", out=t1[0][:].rearrange("p (h v) -> p h v", v=128),
                        in0=ob_[:].rearrange("p (h v) -> p h v", v=128), in1=rstd[jb][:].unsqueeze(2).to_broadcast([128, 6, 128]),
                        op=ALU.mult, reads=[ob_, rstd[jb]], writes=[t1[0]])
                    P.I("dve", "tensor_tensor", out=mixb[0][:], in0=t1[0][:], in1=sgg[:, j, :], op=ALU.mult, reads=[t1[0], sgg],
                        writes=[mixb[0]])
                    transpose_tile(P, k, mixb[0], lambda blk: mixb[0][:, blk * 128:(blk + 1) * 128], 6, nbank(), mst[0], mst[0][:], 1)
                    P.dma("sp", k.mixT[0:768, tt * 128:(tt + 1) * 128].rearrange("(c p) t -> p c t", p=128), mst[0][:], reads=[mst[0]],
                          writes=[(k.mixT, tt)])
                pend_tail.append(tail)
            tok_drain(100)
            flush_tail()
        P.barrier()


WSHAPES = dict(w_in_sb=[1, 1024, 2560], w_in_hg=[1, 1024, 3328], w_mem_kv=[2, 1024, 512], lower_bounds=[2, 768],
               hg_norm_g=[1, 768], w_out=[2, 1024, 1024], ln_mix_g=[2, 1024], ln_mix_b=[2, 1024],
               w_up=[2, 1024, 4096], w_down=[2, 4096, 1024], ln_ffn_g=[2, 1024], ln_ffn_b=[2, 1024])


def build(stages="ABCDE", debug=False):
    nc = bass.Bass("TRN2", target_bir_lowering=False)
    with ExitStack() as es:
        P = Prog(nc, es)
        k = K()
        k.x = P.dram("x", [S, D], F32, kind="ExternalInput")
        k.mem = P.dram("mem", [NMEM, D], F32, kind="ExternalInput")
        for name, shp in WSHAPES.items():
            setattr(k, name, P.dram(name, shp, F32, kind="ExternalInput"))
        skind = "ExternalOutput" if debug else "Internal"
        k.qT0 = P.dram("qT0", [768, S], BF16, kind=skind)
        k.kT0 = P.dram("kT0", [768, S], BF16, kind=skind)
        k.v0 = P.dram("v0", [S, 768], BF16, kind=skind)
        k.qmT = P.dram("qmT", [256, S], BF16, kind=skind)
        k.mixT = P.dram("mixT", [768, S], BF16, kind=skind)
        k.x1 = P.dram("x1", [S, D], F32, kind=skind)
        k.xmid = P.dram("xmid", [S, D], F32, kind=skind)
        k.y = P.dram("y", [S, D], F32, kind="ExternalOutput")
        build_consts(P, k)
        P.barrier()
        two = "E" in stages
        if "A" in stages:
            phase_A(P, k)
        scC = P.scope()
        cw = load_C_weights(P, k, 0, scC)
        if "B" in stages:
            phase_B(P, k)
        scD = P.scope()
        dw = alloc_D_weights(P, k, 0, scD)
        if "C" in stages:
            phase_C(P, k, 0, k.x, k.x1, cw, dw)
        scC.close()
        if "D" in stages:
            phase_D(P, k, 0, k.x1, k.xmid if two else k.y, dw)
        scD.close()
        if two:
            phase_E(P, k)
            scC = P.scope()
            cw = load_C_weights(P, k, 1, scC)
            scD = P.scope()
            dw = alloc_D_weights(P, k, 1, scD)
            phase_C(P, k, 1, k.xmid, k.x1, cw, dw)
            scC.close()
            phase_D(P, k, 1, k.x1, k.y, dw)
            scD.close()
        if "e" in stages:
            phase_E(P, k)
        P.finish()
        k.n_ins = P.n_ins
    return nc, k


_CACHE = {}


def kernel(**inputs):
    if "nc" not in _CACHE:
        _CACHE["nc"] = build("ABCDE")[0]
    nc = _CACHE["nc"]
    B = inputs["x"].shape[0]
    shared = {n: np.ascontiguousarray(inputs[n], dtype=np.float32) for n in WSHAPES}
    in_maps = []
    for b in range(B):
        m = dict(shared)
        m["x"] = np.ascontiguousarray(inputs["x"][b], dtype=np.float32)
        m["mem"] = np.ascontiguousarray(inputs["mem"][b], dtype=np.float32)
        in_maps.append(m)
    res = run_bass_kernel_spmd(nc, in_maps, core_ids=list(range(B)))
    return np.stack([np.asarray(r["y"], dtype=np.float32) for r in res.results], axis=0)
```

```python
from contextlib import ExitStack
import numpy as np
import concourse.bass as bass
import concourse.mybir as mybir
from concourse.bass_utils import run_bass_kernel_spmd

F32 = mybir.dt.float32
BF16 = mybir.dt.bfloat16
AF = mybir.ActivationFunctionType
ALU = mybir.AluOpType
AX = mybir.AxisListType

COMPUTE = ("pe", "act", "dve", "pool", "sp")
DMAQ = ("sp", "pool", "act")
NDMA_SEMS = 12


class Buf:
    def __init__(self, t, name, psum=False):
        self.t = t
        self.name = name
        self.psum = psum
        self.state = {}

    def __getitem__(self, idx):
        return self.t[idx]


class Scope:
    def __init__(self, prog):
        self.prog = prog
        self.bufs = []

    def __enter__(self):
        return self

    def __exit__(self, *a):
        self.close()
        return False

    def close(self):
        for b in self.bufs:
            self.prog.free(b)
        self.bufs = []


class Op:
    __slots__ = ("idx", "eng", "fn", "deps", "is_dma", "sig", "dma_slot", "dma_val", "eidx", "vc", "dmaknown")

    def __init__(self, idx, eng, fn, is_dma):
        self.idx = idx
        self.eng = eng
        self.fn = fn
        self.deps = set()
        self.is_dma = is_dma
        self.sig = 0
        self.eidx = 0
        self.vc = None
        self.dmaknown = None


class Prog:
    def __init__(self, nc, es):
        self.nc = nc
        self.es = es
        self.ops = []
        self.bufs = []
        self.engs = {"pe": nc.tensor, "act": nc.scalar, "dve": nc.vector, "pool": nc.gpsimd, "sp": nc.sync}
        self.sems = {e: es.enter_context(nc.semaphore("s_" + e)) for e in COMPUTE}
        self.dsems = {q: [es.enter_context(nc.semaphore("d_%s%d" % (q, i))) for i in range(NDMA_SEMS)]
                      for q in DMAQ}
        self.barrier_deps = {e: set() for e in self.engs}
        self.emitted = 0
        self.phase_start = 0
        NE = len(COMPUTE)
        self.cur_vc = {e: [0] * NE for e in self.engs}
        self.cur_dma = {e: set() for e in self.engs}
        self.cnt = {e: 0 for e in COMPUTE}
        self.scount = {e: 0 for e in COMPUTE}
        self.dcount = {q: 0 for q in DMAQ}
        self.n_ins = 0

    ARENA_BYTES = 212736

    def _arena_init(self):
        self.arena = self.es.enter_context(self.nc.sbuf_tensor("arena", [128, self.ARENA_BYTES // 2], BF16))
        self.free_list = [(0, self.ARENA_BYTES)]

    def scope(self):
        return Scope(self)

    def sbuf(self, name, shape, dt, scope=None):
        if not hasattr(self, "arena"):
            self._arena_init()
        esz = 4 if dt == F32 else 2
        n = 1
        for d in shape[1:]:
            n *= d
        nbytes = (n * esz + 63) // 64 * 64
        for i, (off, sz) in enumerate(self.free_list):
            if sz >= nbytes:
                if sz == nbytes:
                    self.free_list.pop(i)
                else:
                    self.free_list[i] = (off + nbytes, sz - nbytes)
                break
        else:
            raise AssertionError("arena out of SBUF for %s %s; free=%s" % (name, shape, self.free_list))
        ap = self.arena[0:shape[0], off // 2:off // 2 + n * esz // 2]
        if dt == F32:
            ap = ap.bitcast(F32)
        if len(shape) == 3:
            ap = ap.rearrange("p (a b) -> p a b", b=shape[2])
        elif len(shape) != 2:
            raise AssertionError("2-D / 3-D only")
        b = Buf(ap, name)
        b.region = (off, nbytes)
        self.bufs.append(b)
        if scope is not None:
            scope.bufs.append(b)
        return b

    def free(self, b):
        off, nbytes = b.region
        b.region = None
        fl = sorted(self.free_list + [(off, nbytes)])
        merged = []
        for o, sz in fl:
            if merged and merged[-1][0] + merged[-1][1] == o:
                merged[-1] = (merged[-1][0], merged[-1][1] + sz)
            else:
                merged.append((o, sz))
        self.free_list = merged

    def psum(self, name, shape, dt):
        t = self.es.enter_context(self.nc.psum_tensor(name, list(shape), dt))
        b = Buf(t, name, psum=True)
        self.bufs.append(b)
        return b

    def dram(self, name, shape, dt, kind="Internal"):
        t = self.nc.dram_tensor(name, list(shape), dt, kind=kind)
        b = Buf(t, name)
        self.bufs.append(b)
        return b

    @staticmethod
    def _conf(k1, k2):
        return k1 is None or k2 is None or k1 == k2

    def _rec(self, eng, fn, reads, writes, is_dma):
        op = Op(len(self.ops), eng, fn, is_dma)
        self.ops.append(op)
        for r in reads:
            b, key = r if isinstance(r, tuple) else (r, None)
            for k2, st in b.state.items():
                if st[0] is not None and (self._conf(key, k2) or (b.psum and self.ops[st[0]].eng != eng)):
                    op.deps.add(st[0])
                if b.psum:
                    op.deps.update(r2 for r2 in st[1] if self.ops[r2].eng != eng)
        for w in writes:
            b, key = w if isinstance(w, tuple) else (w, None)
            for k2, st in b.state.items():
                if self._conf(key, k2):
                    if st[0] is not None:
                        op.deps.add(st[0])
                    op.deps.update(st[1])
                elif b.psum:
                    if st[0] is not None and self.ops[st[0]].eng != eng:
                        op.deps.add(st[0])
                    op.deps.update(r2 for r2 in st[1] if self.ops[r2].eng != eng)
        op.deps |= self.barrier_deps[eng]
        self.barrier_deps[eng] = set()
        for r in reads:
            b, key = r if isinstance(r, tuple) else (r, None)
            st = b.state.setdefault(key, [None, []])
            st[1].append(op.idx)
        for w in writes:
            b, key = w if isinstance(w, tuple) else (w, None)
            if key is None:
                b.state = {None: [op.idx, []]}
            else:
                b.state[key] = [op.idx, []]
        op.deps.discard(op.idx)
        return op

    def op(self, eng, fn, reads=(), writes=()):
        return self._rec(eng, fn, reads, writes, False)

    def I(self, eng, meth, *args, reads=(), writes=(), **kw):
        return self._rec(eng, lambda e: getattr(e, meth)(*args, **kw), reads, writes, False)

    def dma(self, q, out, in_, reads=(), writes=()):
        return self._rec(q, lambda e: e.dma_start(out=out, in_=in_), reads, writes, True)

    def barrier(self):
        tails = set()
        last = {}
        for op in self.ops[self.phase_start:]:
            if op.is_dma:
                tails.add(op.idx)
            else:
                last[op.eng] = op.idx
        tails |= set(last.values())
        bop = Op(len(self.ops), "sp", lambda e: e.nop(), False)
        bop.deps = tails | self.barrier_deps["sp"]
        self.ops.append(bop)
        bop.sig = 1
        self.flush()
        for e in self.engs:
            self.barrier_deps[e] = {bop.idx}
        for b in self.bufs:
            b.state = {}
        self.phase_start = len(self.ops)

    def flush(self):
        ops = self.ops
        NE = len(COMPUTE)
        eid = {e: i for i, e in enumerate(COMPUTE)}
        new = ops[self.emitted:]
        for op in new:
            if not op.is_dma:
                self.cnt[op.eng] += 1
                op.eidx = self.cnt[op.eng]
        needed = []
        for op in new:
            vc = self.cur_vc[op.eng]
            dk = self.cur_dma[op.eng]
            real = []
            for d in sorted(op.deps, reverse=True):
                dop = ops[d]
                if dop.is_dma:
                    if d in dk:
                        continue
                    real.append(d)
                    dk.add(d)
                else:
                    if dop.eng == "pe" and op.eng == "pe":
                        continue
                    j = eid[dop.eng]
                    if vc[j] >= dop.eidx:
                        continue
                    real.append(d)
                    vc[j] = dop.eidx
                dk |= dop.dmaknown
                dvc = dop.vc
                for i in range(NE):
                    if dvc[i] > vc[i]:
                        vc[i] = dvc[i]
            op.vc = list(vc)
            op.dmaknown = set(dk)
            needed.append(real)
            for d in real:
                if not ops[d].sig:
                    assert d >= self.emitted, "dependency on an already emitted non-signalling op"
                    ops[d].sig = 1
        for op in new:
            if op.is_dma:
                n = self.dcount[op.eng]
                self.dcount[op.eng] += 1
                op.dma_slot = n % NDMA_SEMS
                op.dma_val = 16 * (n // NDMA_SEMS + 1)
            elif op.sig:
                self.scount[op.eng] += 1
                op.sig = self.scount[op.eng]
        for op, real in zip(new, needed):
            e = self.engs[op.eng]
            if op.is_dma and op.dma_val > 16:
                e.wait_ge(self.dsems[op.eng][op.dma_slot], op.dma_val - 16)
                self.n_ins += 1
            for d in real:
                dop = ops[d]
                if dop.is_dma:
                    e.wait_ge(self.dsems[dop.eng][dop.dma_slot], dop.dma_val)
                else:
                    e.wait_ge(self.sems[dop.eng], dop.sig)
                self.n_ins += 1
            ins = op.fn(e)
            self.n_ins += 1
            if op.is_dma:
                ins.then_inc(self.dsems[op.eng][op.dma_slot], 16)
            elif op.sig:
                ins.then_inc(self.sems[op.eng], 1)
            op.fn = None
        self.emitted = len(ops)

    def finish(self):
        self.barrier()


S = 4096
D = 1024
NT = S // 128
DFF = 4096
SB_H = 12
HG_H = 6
NMEM = 256
ALPHA = float(4 ** 0.25)
LN_EPS = 1e-5
RMS_EPS = 1e-6
NEG = -30000.0


class K:
    pass


def _copy(P, k, idx, out, in_, reads, writes, scale=None):
    if idx % 2 == 0:
        if scale is None:
            P.op("act", lambda e: e.activation(out=out, in_=in_, func=AF.Copy), reads=reads, writes=writes)
        else:
            P.op("act", lambda e: e.activation(out=out, in_=in_, func=AF.Copy, scale=scale), reads=reads, writes=writes)
    else:
        if scale is None:
            P.op("dve", lambda e: e.tensor_copy(out=out, in_=in_), reads=reads, writes=writes)
        else:
            P.op("dve", lambda e: e.tensor_single_scalar(out=out, in_=in_, scalar=scale, op=ALU.mult),
                 reads=reads, writes=writes)


def build_consts(P, k):
    k.ident = P.sbuf("ident", [128, 128], BF16)
    k.negtri = P.sbuf("negtri", [128, 128], BF16)
    k.negones = P.sbuf("negones", [128, 128], BF16)
    k.maskneg = P.sbuf("maskneg", [128, 128], BF16)
    k.blkmask = P.sbuf("blkmask", [128, 128], F32)
    k.scanmask = P.sbuf("scanmask", [128, 512], F32)
    tmp = P.sbuf("ctmp", [128, 128], F32)
    tmp1 = P.sbuf("ctmp1", [128, 128], F32)
    P.op("dve", lambda e: e.memset(tmp[:], 0.0), writes=[tmp])
    P.op("pool", lambda e: e.affine_select(out=tmp[:], in_=tmp[:], pattern=[[-1, 128]], compare_op=ALU.not_equal,
                                            fill=1.0, base=0, channel_multiplier=1), reads=[tmp], writes=[tmp])
    P.op("dve", lambda e: e.tensor_copy(out=k.ident[:], in_=tmp[:]), reads=[tmp], writes=[k.ident])
    P.op("dve", lambda e: e.memset(tmp1[:], 0.0), writes=[tmp1])
    P.op("pool", lambda e: e.affine_select(out=tmp1[:], in_=tmp1[:], pattern=[[1, 128]], compare_op=ALU.is_gt,
                                            fill=-1.0, base=0, channel_multiplier=-1), reads=[tmp1], writes=[tmp1])
    P.op("dve", lambda e: e.tensor_copy(out=k.negtri[:], in_=tmp1[:]), reads=[tmp1], writes=[k.negtri])
    P.op("dve", lambda e: e.tensor_single_scalar(out=k.maskneg[:], in_=tmp1[:], scalar=-NEG, op=ALU.mult),
         reads=[tmp1], writes=[k.maskneg])
    P.op("dve", lambda e: e.memset(k.negones[:], -1.0), writes=[k.negones])
    P.op("dve", lambda e: e.memset(k.blkmask[:], 1.0), writes=[k.blkmask])
    P.op("pool", lambda e: e.affine_select(out=k.blkmask[:], in_=k.blkmask[:], pattern=[[1, 128]], compare_op=ALU.is_ge,
                                            fill=0.0, base=0, channel_multiplier=-1), reads=[k.blkmask], writes=[k.blkmask])
    P.op("dve", lambda e: e.memset(k.blkmask[0:64, 64:128], 0.0), reads=[k.blkmask], writes=[k.blkmask])
    P.op("dve", lambda e: e.memset(k.scanmask[:], 1.0), writes=[k.scanmask])
    P.op("dve", lambda e: e.memset(k.scanmask[:].rearrange("p (c j) -> p c j", j=64)[:, :, 0:1], 0.0),
         reads=[k.scanmask], writes=[k.scanmask])
    k.epsln = P.sbuf("epsln", [128, 1], F32)
    P.op("dve", lambda e: e.memset(k.epsln[:], LN_EPS), writes=[k.epsln])
    k.pspair = [P.es.enter_context(P.nc.psum_tensor("pspair%d" % i, [128, 1024], F32)) for i in range(4)]
    k.ps = []
    for i in range(8):
        b = Buf(k.pspair[i // 2][:, (i % 2) * 512:(i % 2 + 1) * 512], "psb%d" % i, psum=True)
        P.bufs.append(b)
        k.ps.append(b)


def transpose_tile(P, k, src, src_ap_fn, nblk, psbank, dst, dst_ap, cidx, extra_reads=(), dkey=None):
    psb = psbank[:].bitcast(BF16)
    for b in range(nblk):
        P.I("pe", "transpose", out=psb[:, b * 128:(b + 1) * 128], in_=src_ap_fn(b), identity=k.ident[:],
            reads=[src, k.ident] + list(extra_reads), writes=[(psbank, b)])
    _copy(P, k, cidx, dst_ap, psb[:, 0:nblk * 128] if dst_ap.ndim == 2 else
          psb[:, 0:nblk * 128].rearrange("p (c t) -> p c t", t=128), reads=[psbank], writes=[(dst, dkey)])


def phase_A(P, k):
    with P.scope() as ls:
        W0 = P.sbuf("W0", [128, 8, 2560], BF16, ls)
        wsrc = k.w_in_sb[0].rearrange("(c p) n -> p c n", p=128)
        for c in range(8):
            P.dma("pool", W0[:, c, :], wsrc[:, c, :], reads=[], writes=[(W0, c)])
        xb = [P.sbuf("A_xb%d" % i, [128, 4, 1024], BF16, ls) for i in range(2)]
        xT = [P.sbuf("A_xT%d" % i, [128, 8, 512], BF16, ls) for i in range(2)]
        st = [P.sbuf("A_st%d" % i, [128, 512], BF16, ls) for i in range(4)]
        vst = [P.sbuf("A_vst%d" % i, [128, 4, 768], BF16, ls) for i in range(2)]
        cp = 0
        sti = 0
        rot = 0
        for g in range(8):
            xbg = xb[g % 2]
            xTg = xT[g % 2]
            P.dma("pool", xbg[:], k.x[g * 512:(g + 1) * 512, :].rearrange("(j p) d -> p j d", p=128),
                  reads=[], writes=[xbg])
            for j in range(4):
                transpose_tile(P, k, xbg, (lambda xbg, j: lambda b: xbg[:, j, b * 128:(b + 1) * 128])(xbg, j), 8, k.ps[j % 2],
                               xTg, xTg[:, :, j * 128:(j + 1) * 128], cp, dkey=j)
                cp += 1
            for fc in list(range(12)) + [18, 19]:
                bank = k.ps[2 + rot % 6]
                rot += 1
                for c in range(8):
                    P.I("pe", "matmul", bank[:], lhsT=W0[:, c, fc * 128:(fc + 1) * 128], rhs=xTg[:, c, :],
                        start=(c == 0), stop=(c == 7), reads=[W0, xTg], writes=[bank])
                s_ = st[sti % 4]
                sti += 1
                _copy(P, k, cp, s_[:], bank[:], reads=[bank], writes=[s_], scale=(0.125 if fc < 6 else None))
                cp += 1
                if fc < 6:
                    dst = k.qT0[fc * 128:(fc + 1) * 128, g * 512:(g + 1) * 512]
                    dbuf = k.qT0
                elif fc < 12:
                    dst = k.kT0[(fc - 6) * 128:(fc - 5) * 128, g * 512:(g + 1) * 512]
                    dbuf = k.kT0
                else:
                    dst = k.qmT[(fc - 18) * 128:(fc - 17) * 128, g * 512:(g + 1) * 512]
                    dbuf = k.qmT
                P.dma("sp", dst, s_[:], reads=[s_], writes=[(dbuf, (fc, g))])
            vs = vst[g % 2]
            for j in range(4):
                for (c0, n) in ((0, 512), (512, 256)):
                    bank = k.ps[2 + rot % 6]
                    rot += 1
                    for c in range(8):
                        P.I("pe", "matmul", bank[:, 0:n], lhsT=xTg[:, c, j * 128:(j + 1) * 128],
                            rhs=W0[:, c, 1536 + c0:1536 + c0 + n], start=(c == 0), stop=(c == 7),
                            reads=[W0, xTg], writes=[bank])
                    _copy(P, k, cp, vs[:, j, c0:c0 + n], bank[:, 0:n], reads=[bank], writes=[(vs, (j, c0))])
                    cp += 1
            P.dma("sp", k.v0[g * 512:(g + 1) * 512, :].rearrange("(j p) f -> p j f", p=128), vs[:],
                  reads=[vs], writes=[(k.v0, g)])
        P.barrier()


KB = 16


def phase_B(P, k):
    with P.scope() as ls:
        kTs = [P.sbuf("B_kT%d" % i, [128, S], BF16, ls) for i in range(2)]
        qTs = [P.sbuf("B_qT%d" % i, [128, S], BF16, ls) for i in range(2)]
        vvs = [P.sbuf("B_v%d" % i, [128, NT, 128], BF16, ls) for i in range(2)]
        NSP = KB + 2
        spb = [P.sbuf("B_sp%d" % i, [128, 2, 512], BF16, ls) for i in range(NSP)]
        Rb = [P.sbuf("B_Rb%d" % i, [128, 2, 512], BF16, ls) for i in range(NSP)]
        wb = [P.sbuf("B_w%d" % i, [128, 2, 512], BF16, ls) for i in range(NSP)]
        Rf = [P.sbuf("B_R%d" % i, [128, 2, 512], F32, ls) for i in range(2)]
        ost = [P.sbuf("B_ost%d" % i, [128, 512], BF16, ls) for i in range(2)]

        def pair(pi):
            return k.pspair[pi][:].rearrange("p (b c) -> p b c", c=512), [k.ps[2 * pi], k.ps[2 * pi + 1]]

        slots = [pair(0), pair(1), pair(2)]
        pso = [k.ps[6], k.ps[7]]
        units = []
        for hp in range(6):
            for g in range(8):
                for n in range(4 * g + 4):
                    i = 4 * g + 3 - n
                    col0 = max(0, (i - 4 * g) * 128)
                    units.append(dict(hp=hp, g=g, n=n, i=i, col0=col0, diag=(i >= 4 * g), first=(n == 0), last=(i == 0),
                                      gi=hp * 8 + g))
        NU = len(units)
        loaded = [-1]
        slot_ctr = [0]
        zslot = {}
        eslot = {}

        def next_slot():
            slot_ctr[0] += 1
            return slots[slot_ctr[0] % 3]

        def load_pair(hp):
            if loaded[0] >= hp or hp >= 6:
                return
            loaded[0] = hp
            kT, qT, vv = kTs[hp % 2], qTs[hp % 2], vvs[hp % 2]
            P.dma("sp", kT[:], k.kT0[hp * 128:(hp + 1) * 128, :], reads=[k.kT0], writes=[kT])
            P.dma("sp", qT[:], k.qT0[hp * 128:(hp + 1) * 128, :], reads=[k.qT0], writes=[qT])
            P.dma("sp", vv[:], k.v0[:, hp * 128:(hp + 1) * 128].rearrange("(i p) f -> p i f", p=128),
                  reads=[k.v0], writes=[vv])

        def qk(u, pz, stop):
            ap, bufs = pz
            c0, q0, i = u["col0"], u["g"] * 512, u["i"]
            kT, qT = kTs[u["hp"] % 2], qTs[u["hp"] % 2]
            for hh in range(2):
                b0 = 64 * hh
                P.I("pe", "matmul", ap[:, hh, c0:512], lhsT=kT[b0:b0 + 64, i * 128:(i + 1) * 128],
                    rhs=qT[b0:b0 + 64, q0 + c0:q0 + 512], start=True, stop=stop, reads=[kT, qT], writes=[bufs[hh]])

        def maskmm(u, pz):
            ap, bufs = pz
            c0 = u["col0"]
            for hh in range(2):
                P.I("pe", "matmul", ap[:, hh, c0:c0 + 128], lhsT=k.ident[:], rhs=k.maskneg[:], start=False, stop=True,
                    reads=[k.ident, k.maskneg], writes=[bufs[hh]])

        def st_z(idx):
            u = units[idx]
            load_pair(u["hp"])
            if idx - u["hp"] * 144 >= 2 * KB:
                load_pair(u["hp"] + 1)
            pz = next_slot()
            zslot[idx] = pz
            qk(u, pz, not u["diag"])
            if u["diag"]:
                maskmm(u, pz)

        def st_SP(idx):
            u = units[idx]
            c0 = u["col0"]
            ap, bufs = zslot.pop(idx)
            sp = spb[idx % NSP]
            P.I("act", "activation", out=sp[:, :, c0:512], in_=ap[:, :, c0:512], func=AF.Softplus, reads=bufs, writes=[sp])

        def st_R(idx):
            u = units[idx]
            if u["last"]:
                return
            c0 = u["col0"]
            sp = spb[idx % NSP]
            R = Rf[u["gi"] % 2]
            if u["first"]:
                P.I("pool", "memset", R[:], 0.0, writes=[R])
            P.I("dve", "tensor_tensor", out=R[:, :, c0:512], in0=R[:, :, c0:512], in1=sp[:, :, c0:512], op=ALU.add,
                reads=[R, sp], writes=[R])
            rb = Rb[idx % NSP]
            P.I("dve", "tensor_copy", out=rb[:], in_=R[:], reads=[R], writes=[rb])

        def st_eg(idx):
            u = units[idx]
            c0 = u["col0"]
            pz = next_slot()
            eslot[idx] = pz
            ap, bufs = pz
            sp = spb[idx % NSP]
            qk(u, pz, False)
            for hh in range(2):
                P.I("pe", "matmul", ap[:, hh, c0:512], lhsT=k.negtri[:], rhs=sp[:, hh, c0:512], start=False,
                    stop=(u["first"] and not u["diag"]), reads=[k.negtri, sp], writes=[bufs[hh]])
            if not u["first"]:
                rb = Rb[(idx - 1) % NSP]
                for hh in range(2):
                    P.I("pe", "matmul", ap[:, hh, c0:512], lhsT=k.negones[:], rhs=rb[:, hh, c0:512], start=False,
                        stop=(not u["diag"]), reads=[k.negones, rb], writes=[bufs[hh]])
            if u["diag"]:
                maskmm(u, pz)

        def st_W(idx):
            u = units[idx]
            c0 = u["col0"]
            ap, bufs = eslot.pop(idx)
            w = wb[idx % NSP]
            P.I("act", "activation", out=w[:, :, c0:512], in_=ap[:, :, c0:512], func=AF.Exp, reads=bufs, writes=[w])

        def st_pv(idx):
            u = units[idx]
            c0, i = u["col0"], u["i"]
            w = wb[idx % NSP]
            bank = pso[u["gi"] % 2]
            vv = vvs[u["hp"] % 2]
            for hh in range(2):
                P.I("pe", "matmul", bank[64 * hh:64 * hh + 64, c0:512], lhsT=vv[:, i, hh * 64:(hh + 1) * 64], rhs=w[:, hh, c0:512],
                    start=u["first"], stop=u["last"], reads=[vv, w], writes=[bank])
            if u["last"]:
                o = ost[u["gi"] % 2]
                hp, g = u["hp"], u["g"]
                P.I("dve", "tensor_copy", out=o[:], in_=bank[:], reads=[bank], writes=[o])
                P.dma("pool", k.mixT[hp * 128:(hp + 1) * 128, g * 512:(g + 1) * 512], o[:], reads=[o], writes=[(k.mixT, (hp, g))])

        st_z(0)
        prev = []
        for b0 in range(0, NU, KB):
            ids = list(range(b0, min(b0 + KB, NU)))
            for n_, idx in enumerate(ids):
                if n_ + 1 < len(ids):
                    st_z(ids[n_ + 1])
                st_SP(idx)
                st_R(idx)
                if prev:
                    st_pv(prev.pop(0))
            while prev:
                st_pv(prev.pop(0))
            st_eg(ids[0])
            for n_, idx in enumerate(ids):
                if n_ + 1 < len(ids):
                    st_eg(ids[n_ + 1])
                elif ids[-1] + 1 < NU:
                    st_z(ids[-1] + 1)
                st_W(idx)
            prev = list(ids)
        while prev:
            st_pv(prev.pop(0))
        P.barrier()


class LN:
    def __init__(self, P, k, r, dst, gam, bet, bufs):
        self.P, self.k, self.r, self.dst, self.gam, self.bet = P, k, r, dst, gam, bet
        self.st, self.mv, self.rs, self.nmr = bufs

    def stats(self):
        P, r, st, mv = self.P, self.r, self.st, self.mv
        P.I("dve", "bn_stats", out=st[:, 0, :], in_=r[:, 0:512], reads=[r], writes=[(st, 0)])
        P.I("dve", "bn_stats", out=st[:, 1, :], in_=r[:, 512:1024], reads=[r], writes=[(st, 1)])
        P.I("dve", "bn_aggr", out=mv[:], in_=st[:].rearrange("p a b -> p (a b)"), reads=[st], writes=[mv])

    def rstd(self):
        P, k, mv, rs = self.P, self.k, self.mv, self.rs
        P.I("act", "activation", out=rs[:], in_=mv[:, 1:2], func=AF.Ln, bias=k.epsln[:, 0:1], reads=[mv, k.epsln], writes=[rs])
        P.I("act", "activation", out=rs[:], in_=rs[:], func=AF.Exp, scale=-0.5, reads=[rs], writes=[rs])

    def nmr_(self):
        P, mv, rs, nmr = self.P, self.mv, self.rs, self.nmr
        P.I("pool", "tensor_tensor", out=nmr[:], in0=mv[:, 0:1], in1=rs[:], op=ALU.mult, reads=[mv, rs], writes=[nmr])
        P.I("pool", "tensor_single_scalar", out=nmr[:], in_=nmr[:], scalar=-1.0, op=ALU.mult, reads=[nmr], writes=[nmr])

    def norm(self):
        P, r, rs, nmr, dst = self.P, self.r, self.rs, self.nmr, self.dst
        P.I("act", "activation", out=dst[:], in_=r[:], func=AF.Identity, scale=rs[:, 0:1], bias=nmr[:, 0:1],
            reads=[r, rs, nmr], writes=[dst])

    def affine(self):
        P, dst, gam, bet = self.P, self.dst, self.gam, self.bet
        P.I("pool", "tensor_tensor", out=dst[:], in0=dst[:], in1=gam[:], op=ALU.mult, reads=[dst, gam], writes=[dst])
        P.I("pool", "tensor_tensor", out=dst[:], in0=dst[:], in1=bet[:], op=ALU.add, reads=[dst, bet], writes=[dst])


def lnbufs(P, name, ls):
    return (P.sbuf(name + "st", [128, 2, 6], F32, ls), P.sbuf(name + "mv", [128, 2], F32, ls),
            P.sbuf(name + "rs", [128, 1], F32, ls), P.sbuf(name + "nm", [128, 1], F32, ls))


def load_C_weights(P, k, L, sc):
    w = K()
    w.wout = P.sbuf("C_wout", [128, 8, 1024], BF16, sc)
    wsrc = k.w_out[L].rearrange("(c p) n -> p c n", p=128)
    for c in range(0, 8, 2):
        P.dma("pool", w.wout[:, c:c + 2, :], wsrc[:, c:c + 2, :], writes=[(w.wout, c)])
    w.wkv = P.sbuf("C_wkv", [128, 8, 512], BF16, sc)
    P.dma("pool", w.wkv[:], k.w_mem_kv[L].rearrange("(c p) n -> p c n", p=128), writes=[w.wkv])
    w.memb = P.sbuf("C_memb", [128, 2, 1024], BF16, sc)
    P.dma("pool", w.memb[:], k.mem[:, :].rearrange("(j p) d -> p j d", p=128), writes=[w.memb])
    w.gam = P.sbuf("C_gam", [128, 1024], F32, sc)
    w.bet = P.sbuf("C_bet", [128, 1024], F32, sc)
    P.dma("sp", w.gam[:], k.ln_mix_g[L:L + 1, :].partition_broadcast(128), writes=[w.gam])
    P.dma("sp", w.bet[:], k.ln_mix_b[L:L + 1, :].partition_broadcast(128), writes=[w.bet])
    return w


def alloc_D_weights(P, k, L, sc):
    w = K()
    w.L = L
    w.WUP = P.sbuf("D_wup", [128, 8, DFF], BF16, sc)
    w.WDN = [P.sbuf("D_wdn0", [128, 16, 1024], BF16, sc), None]
    usrc = k.w_up[L].rearrange("(c p) f -> p c f", p=128)
    dsrc = k.w_down[L].rearrange("(c p) n -> p c n", p=128)
    w.pending = []
    for c in range(8):
        w.pending.append((w.WUP[:, c, :], usrc[:, c, :], (w.WUP, c)))
    for c in range(0, 16, 4):
        w.pending.append((w.WDN[0][:, c:c + 4, :], dsrc[:, c:c + 4, :], (w.WDN[0], c)))
    return w


def issue_pending(P, w, n):
    for _ in range(n):
        if w.pending:
            dst, src, wr = w.pending.pop(0)
            P.dma("pool", dst, src, writes=[wr])


def phase_C(P, k, L, xin, xout, w, dw=None):
    with P.scope() as ls:
        wout, wkv, memb, gam, bet = w.wout, w.wkv, w.memb, w.gam, w.bet
        memT = P.sbuf("C_memT", [128, 8, 256], BF16, ls)
        kmT = P.sbuf("C_kmT", [128, 2, 256], BF16, ls)
        vm = P.sbuf("C_vm", [128, 2, 256], BF16, ls)
        for j in range(2):
            transpose_tile(P, k, memb, (lambda j: lambda b: memb[:, j, b * 128:(b + 1) * 128])(j), 8, k.ps[j],
                           memT, memT[:, :, j * 128:(j + 1) * 128], j, dkey=j)
        for fc in range(2):
            bank = k.ps[2 + fc]
            for c in range(8):
                P.I("pe", "matmul", bank[:, 0:256], lhsT=wkv[:, c, fc * 128:(fc + 1) * 128], rhs=memT[:, c, :],
                    start=(c == 0), stop=(c == 7), reads=[wkv, memT], writes=[bank])
            _copy(P, k, fc, kmT[:, fc, :], bank[:, 0:256], reads=[bank], writes=[(kmT, fc)])
        for mc in range(2):
            bank = k.ps[4 + mc]
            for c in range(8):
                P.I("pe", "matmul", bank[:, 0:256], lhsT=memT[:, c, mc * 128:(mc + 1) * 128], rhs=wkv[:, c, 256:512],
                    start=(c == 0), stop=(c == 7), reads=[wkv, memT], writes=[bank])
            _copy(P, k, mc + 1, vm[:, mc, :], bank[:, 0:256], reads=[bank], writes=[(vm, mc)])

        qm = [P.sbuf("C_qm%d" % i, [128, 2, 128], BF16, ls) for i in range(3)]
        mx = [P.sbuf("C_mx%d" % i, [128, 6, 128], BF16, ls) for i in range(3)]
        xr = [P.sbuf("C_xr%d" % i, [128, 1024], F32, ls) for i in range(4)]
        E = [P.sbuf("C_E%d" % i, [128, 4, 256], F32, ls) for i in range(2)]
        Pb = [P.sbuf("C_P%d" % i, [128, 4, 256], BF16, ls) for i in range(2)]
        PT = [P.sbuf("C_PT%d" % i, [128, 8, 128], BF16, ls) for i in range(2)]
        mmT = [P.sbuf("C_mmT%d" % i, [128, 2, 128], BF16, ls) for i in range(2)]
        xo = [P.sbuf("C_xo%d" % i, [128, 1024], F32, ls) for i in range(2)]
        mxv = [P.sbuf("C_mxv%d" % i, [128, 4], F32, ls) for i in range(2)]
        nb_ = [P.sbuf("C_nb%d" % i, [128, 4], F32, ls) for i in range(2)]
        ssum = [P.sbuf("C_ss%d" % i, [128, 4], F32, ls) for i in range(2)]
        rsum = [P.sbuf("C_rs%d" % i, [128, 4], F32, ls) for i in range(2)]
        lnb = [lnbufs(P, "C_ln%d" % i, ls) for i in range(2)]
        psS = [[k.ps[0], k.ps[1]], [k.ps[2], k.ps[3]]]
        psT = k.ps[4]
        psMO = k.ps[5]
        psY = [k.ps[6], k.ps[7]]
        psTb = psT[:].bitcast(BF16)
        lns = {}

        def A0(t):
            P.dma("sp", qm[t % 3][:], k.qmT[:, t * 128:(t + 1) * 128].rearrange("(c p) t -> p c t", p=128), reads=[k.qmT],
                  writes=[qm[t % 3]])

        def A1(t):
            for h in range(4):
                bank = psS[t % 2][h % 2]
                p0 = 64 * (h % 2)
                P.I("pe", "matmul", bank[:, (h // 2) * 256:(h // 2 + 1) * 256], lhsT=qm[t % 3][p0:p0 + 64, h // 2, :],
                    rhs=kmT[p0:p0 + 64, h // 2, :], start=True, stop=True, reads=[qm[t % 3], kmT], writes=[(bank, h // 2)])

        def A2(t):
            b2 = t % 2
            for hb in range(2):
                P.I("dve", "tensor_reduce", out=mxv[b2][:, 2 * hb:2 * hb + 2], in_=psS[b2][hb][:].rearrange("p (h m) -> p h m", m=256),
                    axis=AX.X, op=ALU.max, reads=[psS[b2][hb]], writes=[(mxv[b2], hb)])
            P.I("dve", "tensor_single_scalar", out=nb_[b2][:], in_=mxv[b2][:], scalar=-0.125, op=ALU.mult,
                reads=[mxv[b2]], writes=[nb_[b2]])
            for h in range(4):
                q = (h % 2) * 2 + h // 2
                P.I("act", "activation", out=E[b2][:, h, :], in_=psS[b2][h % 2][:, (h // 2) * 256:(h // 2 + 1) * 256], func=AF.Exp,
                    scale=0.125, bias=nb_[b2][:, q:q + 1], accum_out=ssum[b2][:, h:h + 1],
                    reads=[psS[b2][h % 2], nb_[b2]], writes=[(E[b2], h), (ssum[b2], h)])

        def A3(t):
            b2 = t % 2
            P.I("dve", "reciprocal", out=rsum[b2][:], in_=ssum[b2][:], reads=[ssum[b2]], writes=[rsum[b2]])
            P.I("dve", "tensor_tensor", out=Pb[b2][:], in0=E[b2][:], in1=rsum[b2][:].unsqueeze(2).to_broadcast([128, 4, 256]),
                op=ALU.mult, reads=[E[b2], rsum[b2]], writes=[Pb[b2]])

        def A4(t):
            b2 = t % 2
            P.dma("sp", mx[t % 3][:], k.mixT[0:768, t * 128:(t + 1) * 128].rearrange("(c p) t -> p c t", p=128), reads=[k.mixT],
                  writes=[mx[t % 3]])
            for blk in range(8):
                P.I("pe", "transpose", out=psTb[:, blk * 128:(blk + 1) * 128], in_=Pb[b2][:, blk // 2, (blk % 2) * 128:(blk % 2 + 1) * 128],
                    identity=k.ident[:], reads=[Pb[b2], k.ident], writes=[(psT, blk)])
            P.I("act", "activation", out=PT[b2][:], in_=psTb[:, 0:1024].rearrange("p (c t) -> p c t", t=128), func=AF.Copy,
                reads=[psT], writes=[PT[b2]])

        def A5(t):
            b2 = t % 2
            P.dma("sp", xr[t % 4][:], xin[t * 128:(t + 1) * 128, :], reads=[xin], writes=[xr[t % 4]])
            for h in range(4):
                p0 = 64 * (h % 2)
                for mc in range(2):
                    P.I("pe", "matmul", psMO[p0:p0 + 64, (h // 2) * 128:(h // 2 + 1) * 128], lhsT=vm[:, mc, h * 64:(h + 1) * 64],
                        rhs=PT[b2][:, h * 2 + mc, :], start=(mc == 0), stop=(mc == 1), reads=[vm, PT[b2]], writes=[(psMO, h)])
            P.I("dve", "tensor_copy", out=mmT[b2][:], in_=psMO[:, 0:256].rearrange("p (c t) -> p c t", t=128), reads=[psMO],
                writes=[mmT[b2]])

        def A6(t):
            for nh in range(2):
                for c in range(8):
                    lhsT = mx[t % 3][:, c, :] if c < 6 else mmT[t % 2][:, c - 6, :]
                    P.I("pe", "matmul", psY[nh][:], lhsT=lhsT, rhs=wout[:, c, nh * 512:(nh + 1) * 512], start=(c == 0), stop=(c == 7),
                        reads=[mx[t % 3], mmT[t % 2], wout], writes=[psY[nh]])

        def A7(t):
            x_ = xr[t % 4]
            for nh in range(2):
                P.I("dve", "scalar_tensor_tensor", out=x_[:, nh * 512:(nh + 1) * 512], in0=x_[:, nh * 512:(nh + 1) * 512],
                    scalar=ALPHA, in1=psY[nh][:], op0=ALU.mult, op1=ALU.add, reads=[x_, psY[nh]], writes=[(x_, nh)])
            lns[t] = LN(P, k, x_, xo[t % 2], gam, bet, lnb[t % 2])
            lns[t].stats()
            lns[t].rstd()
            lns[t].nmr_()

        def A8(t):
            lns[t].norm()
            lns[t].affine()
            P.dma("sp", xout[t * 128:(t + 1) * 128, :], xo[t % 2][:], reads=[xo[t % 2]], writes=[(xout, t)])
            del lns[t]

        stages_ = [A1, A2, A3, A4, A5, A6, A7, A8]
        A0(0)
        for it in range(NT + len(stages_) - 1):
            if dw is not None and it % 2 == 0:
                issue_pending(P, dw, 1)
            for si in range(len(stages_) - 1, -1, -1):
                t = it - si
                if 0 <= t < NT:
                    stages_[si](t)
            if it + 1 < NT:
                A0(it + 1)
        if dw is not None:
            issue_pending(P, dw, 100)
        P.barrier()


def phase_D(P, k, L, xin, xout, w):
    with P.scope() as ls:
        WUP = w.WUP
        w.WDN[1] = P.sbuf("D_wdn1", [128, 16, 1024], BF16, ls)
        dsrc = k.w_down[L].rearrange("(c p) n -> p c n", p=128)
        issue_pending(P, w, 100)
        for c in range(0, 16, 4):
            P.dma("pool", w.WDN[1][:, c:c + 4, :], dsrc[:, 16 + c:20 + c, :], writes=[(w.WDN[1], c)])
        gam = P.sbuf("D_gam", [128, 1024], F32, ls)
        bet = P.sbuf("D_bet", [128, 1024], F32, ls)
        P.dma("sp", gam[:], k.ln_ffn_g[L:L + 1, :].partition_broadcast(128), writes=[gam])
        P.dma("sp", bet[:], k.ln_ffn_b[L:L + 1, :].partition_broadcast(128), writes=[bet])
        GT = 4
        NG = NT // GT
        NW = GT * 128
        xb = P.sbuf("D_xb", [128, GT, 1024], BF16, ls)
        xT = P.sbuf("D_xT", [128, 8, NW], BF16, ls)
        hT = P.sbuf("D_hT", [128, 32, NW], BF16, ls)
        rl = [P.sbuf("D_rl%d" % i, [128, NW], F32, ls) for i in range(2)]
        xr = [P.sbuf("D_xr%d" % i, [128, 1024], F32, ls) for i in range(2)]
        lnb = [lnbufs(P, "D_ln%d" % i, ls) for i in range(2)]
        psH = [k.ps[2], k.ps[3]]
        psY = [[k.ps[4], k.ps[5]], [k.ps[6], k.ps[7]]]
        st_ = dict(cp=0)

        def load_x(g):
            P.dma("pool", xb[:], xin[g * NW:(g + 1) * NW, :].rearrange("(j p) d -> p j d", p=128), reads=[xin], writes=[xb])

        def transposes(g):
            for j in range(GT):
                transpose_tile(P, k, xb, (lambda j: lambda blk: xb[:, j, blk * 128:(blk + 1) * 128])(j), 8, k.ps[j % 2],
                               xT, xT[:, :, j * 128:(j + 1) * 128], st_["cp"], dkey=j)
                st_["cp"] += 1

        pend = []

        def tail2():
            while pend:
                ln, tt, x_ = pend.pop(0)
                ln.norm()
                ln.affine()
                P.dma("sp", xout[tt * 128:(tt + 1) * 128, :], x_[:], reads=[x_], writes=[(xout, tt)])

        load_x(0)
        transposes(0)
        tl = 0
        for g in range(NG):
            if g + 1 < NG:
                load_x(g + 1)
            for fc in range(32):
                bank = psH[fc % 2]
                for c in range(8):
                    P.I("pe", "matmul", bank[:], lhsT=WUP[:, c, fc * 128:(fc + 1) * 128], rhs=xT[:, c, :], start=(c == 0), stop=(c == 7),
                        reads=[(WUP, c), xT], writes=[bank])
                r_ = rl[fc % 2]
                P.I("act", "activation", out=r_[:], in_=bank[:], func=AF.Relu, reads=[bank], writes=[r_])
                P.I("dve", "tensor_tensor", out=hT[:, fc, :], in0=r_[:], in1=r_[:], op=ALU.mult, reads=[r_], writes=[(hT, fc)])
            if g + 1 < NG:
                transposes(g + 1)
            for j in range(GT):
                tt = g * GT + j
                b = tl % 2
                tl += 1
                tail2()
                x_ = xr[b]
                P.dma("sp", x_[:], xin[tt * 128:(tt + 1) * 128, :], reads=[xin], writes=[x_])
                py = psY[b]
                for nh in range(2):
                    for fc in range(32):
                        wd = w.WDN[fc // 16]
                        P.I("pe", "matmul", py[nh][:], lhsT=hT[:, fc, j * 128:(j + 1) * 128], rhs=wd[:, fc % 16, nh * 512:(nh + 1) * 512],
                            start=(fc == 0), stop=(fc == 31), reads=[hT, (wd, (fc % 16) // 4 * 4)], writes=[py[nh]])
                for nh in range(2):
                    P.I("dve", "scalar_tensor_tensor", out=x_[:, nh * 512:(nh + 1) * 512], in0=x_[:, nh * 512:(nh + 1) * 512],
                        scalar=ALPHA, in1=py[nh][:], op0=ALU.mult, op1=ALU.add, reads=[x_, py[nh]], writes=[(x_, nh)])
                ln = LN(P, k, x_, x_, gam, bet, lnb[b])
                ln.stats()
                ln.rstd()
                ln.nmr_()
                pend.append((ln, tt, x_))
        tail2()
        P.barrier()


def phase_E(P, k):
    with P.scope() as ls:
        W1 = P.sbuf("E_W1", [128, 8, 3328], BF16, ls)
        wsrc = k.w_in_hg[0].rearrange("(c p) n -> p c n", p=128)
        for c in range(8):
            P.dma("pool", W1[:, c, :], wsrc[:, c, :], writes=[(W1, c)])
        l0 = P.sbuf("E_l0", [128, 6], F32, ls)
        l1 = P.sbuf("E_l1", [128, 6], F32, ls)
        for h in range(6):
            P.dma("sp", l0[:, h:h + 1], k.lower_bounds[0:1, h * 128:(h + 1) * 128].rearrange("o p -> p o"), writes=[(l0, h)])
            P.dma("sp", l1[:, h:h + 1], k.lower_bounds[1:2, h * 128:(h + 1) * 128].rearrange("o p -> p o"), writes=[(l1, h)])
        lb = P.sbuf("E_lb", [128, 6], F32, ls)
        omlb = P.sbuf("E_omlb", [128, 6], F32, ls)
        nomlb = P.sbuf("E_nomlb", [128, 6], F32, ls)
        ltmp = P.sbuf("E_ltmp", [128, 6], F32, ls)
        P.I("dve", "tensor_tensor", out=ltmp[:], in0=l0[:], in1=l1[:], op=ALU.subtract, reads=[l0, l1], writes=[ltmp])
        P.I("act", "activation", out=ltmp[:], in_=ltmp[:], func=AF.Exp, reads=[ltmp], writes=[ltmp])
        P.I("dve", "tensor_single_scalar", out=lb[:], in_=ltmp[:], scalar=1.0, op=ALU.add, reads=[ltmp], writes=[lb])
        P.I("dve", "reciprocal", out=lb[:], in_=lb[:], reads=[lb], writes=[lb])
        P.I("dve", "tensor_tensor", out=omlb[:], in0=ltmp[:], in1=lb[:], op=ALU.mult, reads=[ltmp, lb], writes=[omlb])
        P.I("dve", "tensor_single_scalar", out=nomlb[:], in_=omlb[:], scalar=-1.0, op=ALU.mult, reads=[omlb], writes=[nomlb])
        gn = P.sbuf("E_gn", [128, 768], F32, ls)
        P.dma("sp", gn[:], k.hg_norm_g[0:1, :].partition_broadcast(128), writes=[gn])
        epsr = P.sbuf("E_epsr", [128, 1], F32, ls)
        P.I("dve", "memset", epsr[:], RMS_EPS, writes=[epsr])
        state = P.sbuf("E_state", [128, 6, 128], F32, ls)
        sbf = [P.sbuf("E_sbf%d" % i, [128, 6, 128], BF16, ls) for i in range(2)]
        P.I("dve", "memset", state[:], 0.0, writes=[state])
        P.I("dve", "memset", sbf[0][:], 0.0, writes=[sbf[0]])
        xb = [P.sbuf("E_xb%d" % i, [128, 4, 1024], BF16, ls) for i in range(1)]
        xT = [P.sbuf("E_xT%d" % i, [128, 8, 512], BF16, ls) for i in range(1)]
        vbf = [P.sbuf("E_v%d" % i, [128, 4, 768], BF16, ls) for i in range(2)]
        sg = [P.sbuf("E_sg%d" % i, [128, 4, 768], F32, ls) for i in range(2)]
        qst = [P.sbuf("E_qst%d" % i, [128, 512], BF16, ls) for i in range(1)]
        A = [[P.sbuf("E_A%d_%d" % (s_, i), [128, 512], F32, ls) for i in range(5)] for s_ in range(3)]
        QT = [P.sbuf("E_QT%d" % i, [128, 6, 512], BF16, ls) for i in range(1)]
        KT = [P.sbuf("E_KT%d" % i, [128, 6, 512], BF16, ls) for i in range(1)]
        KD = [P.sbuf("E_KD%d" % i, [128, 6, 512], BF16, ls) for i in range(1)]
        EGL = [P.sbuf("E_EGL%d" % i, [128, 6, 8], F32, ls) for i in range(1)]
        KDA = [P.sbuf("E_KDA%d" % i, [128, 6, 128], BF16, ls) for i in range(4)]
        KDB = [P.sbuf("E_KDB%d" % i, [128, 6, 128], BF16, ls) for i in range(4)]
        for i in range(4):
            P.I("pool", "memset", KDA[i][:], 0.0, writes=[KDA[i]])
            P.I("pool", "memset", KDB[i][:], 0.0, writes=[KDB[i]])
        SCM = [P.sbuf("E_SCM%d" % i, [128, 6, 128], BF16, ls) for i in range(4)]
        junk = P.sbuf("E_junk", [128, 128], F32, ls)
        o32 = [P.sbuf("E_o32_%d" % i, [128, 768], F32, ls) for i in range(2)]
        ss = [P.sbuf("E_ss%d" % i, [128, 6], F32, ls) for i in range(2)]
        rstd = [P.sbuf("E_rstd%d" % i, [128, 6], F32, ls) for i in range(2)]
        t1 = [P.sbuf("E_t1_%d" % i, [128, 768], F32, ls) for i in range(1)]
        mixb = [P.sbuf("E_mixb%d" % i, [128, 768], BF16, ls) for i in range(1)]
        mst = [P.sbuf("E_mst%d" % i, [128, 6, 128], BF16, ls) for i in range(1)]
        psSC = [k.ps[0], k.ps[1]]
        psO = [k.ps[2], k.ps[3]]
        psKV = [k.ps[4], k.ps[5]]
        rotb = [k.ps[6], k.ps[7]]
        st_ = dict(rot=0, cp=0, hs=0, sb=0, tj=0)

        def nbank():
            st_["rot"] += 1
            return rotb[st_["rot"] % 2]

        pend_tail = []

        def flush_tail():
            while pend_tail:
                pend_tail.pop(0)()

        def hsl(bank_list, hd):
            return bank_list[hd // 4][:, (hd % 4) * 128:(hd % 4 + 1) * 128], bank_list[hd // 4]

        CUTE = 9.0
        for g in range(8 if CUTE >= 9 else 1):
            if CUTE < 2:
                break
            gb = g % 2
            xbg, xTg, vg, sgg = xb[0], xT[0], vbf[gb], sg[gb]

            def x_load_T(gn_):
                P.dma("pool", xbg[:], k.xmid[gn_ * 512:(gn_ + 1) * 512, :].rearrange("(j p) d -> p j d", p=128), reads=[k.xmid], writes=[xbg])
                for j in range(4):
                    transpose_tile(P, k, xbg, (lambda j: lambda b: xbg[:, j, b * 128:(b + 1) * 128])(j), 8, nbank(),
                                   xTg, xTg[:, :, j * 128:(j + 1) * 128], st_["cp"], dkey=j)
                    st_["cp"] += 1

            def tok_steps(gn_):
                vn, sn = vbf[gn_ % 2], sg[gn_ % 2]
                steps = []
                for j in range(4):
                    for n in range(3):
                        def st(j=j, n=n):
                            bank = nbank()
                            for c in range(8):
                                P.I("pe", "matmul", bank[:], lhsT=xTg[:, c, j * 128:(j + 1) * 128], rhs=W1[:, c, 1536 + 512 * n:2048 + 512 * n],
                                    start=(c == 0), stop=(c == 7), reads=[W1, xTg], writes=[bank])
                            if n == 0:
                                _copy(P, k, 0, vn[:, j, 0:512], bank[:], reads=[bank], writes=[(vn, (j, 0))])
                            elif n == 1:
                                _copy(P, k, 0, vn[:, j, 512:768], bank[:, 0:256], reads=[bank], writes=[(vn, (j, 1))])
                                _copy(P, k, 0, sn[:, j, 0:256], bank[:, 256:512], reads=[bank], writes=[(sn, (j, 0))])
                            else:
                                _copy(P, k, 0, sn[:, j, 256:768], bank[:], reads=[bank], writes=[(sn, (j, 1))])
                        steps.append(st)
                for fc in (24, 25):
                    def sq(fc=fc):
                        bank = nbank()
                        for c in range(8):
                            P.I("pe", "matmul", bank[:], lhsT=W1[:, c, fc * 128:(fc + 1) * 128], rhs=xTg[:, c, :], start=(c == 0), stop=(c == 7),
                                reads=[W1, xTg], writes=[bank])
                        s_ = qst[0]
                        _copy(P, k, 0, s_[:], bank[:], reads=[bank], writes=[s_])
                        P.dma("sp", k.qmT[(fc - 24) * 128:(fc - 23) * 128, gn_ * 512:(gn_ + 1) * 512], s_[:], reads=[s_],
                              writes=[(k.qmT, (fc, gn_))])
                    steps.append(sq)

                def ssilu():
                    P.I("act", "activation", out=sn[:], in_=sn[:], func=AF.Silu, reads=[sn], writes=[sn])
                    P.I("pool", "tensor_tensor", out=sn[:], in0=sn[:], in1=gn[:].unsqueeze(1).to_broadcast([128, 4, 768]), op=ALU.mult,
                        reads=[sn, gn], writes=[sn])
                steps.append(ssilu)
                return steps

            if g == 0:
                x_load_T(0)
                for st in tok_steps(0):
                    st()
            if CUTE < 3:
                continue
            for trio in range(2):
                hds = [trio * 3 + i for i in range(3)]
                Fb = [k.ps[i] for i in range(3)]
                Qb = [k.ps[3 + i] for i in range(3)]
                for i, hd in enumerate(hds):
                    for c in range(8):
                        P.I("pe", "matmul", Fb[i][:], lhsT=W1[:, c, 768 + hd * 128:768 + (hd + 1) * 128], rhs=xTg[:, c, :], start=(c == 0),
                            stop=(c == 7), reads=[W1, xTg], writes=[Fb[i]])
                for i, hd in enumerate(hds):
                    a = A[i]
                    P.I("act", "activation", out=a[0][:], in_=Fb[i][:], func=AF.Exp, scale=-1.0, reads=[Fb[i]], writes=[a[0]])
                for i, hd in enumerate(hds):
                    for c in range(8):
                        P.I("pe", "matmul", Qb[i][:], lhsT=W1[:, c, hd * 128:(hd + 1) * 128], rhs=xTg[:, c, :], start=(c == 0), stop=(c == 7),
                            reads=[W1, xTg], writes=[Qb[i]])
                for i, hd in enumerate(hds):
                    a = A[i]
                    P.I("act", "activation", out=a[0][:], in_=a[0][:], func=AF.Ln, bias=1.0, reads=[a[0]], writes=[a[0]])
                for i, hd in enumerate(hds):
                    a = A[i]
                    P.I("act", "activation", out=a[1][:], in_=a[0][:], func=AF.Exp, scale=-1.0, reads=[a[0]], writes=[a[1]])
                for i, hd in enumerate(hds):
                    a = A[i]
                    P.I("act", "activation", out=a[2][:], in_=a[1][:], func=AF.Ln, scale=omlb[:, hd:hd + 1], bias=lb[:, hd:hd + 1],
                        reads=[a[1], omlb, lb], writes=[a[2]])
                    P.I("dve", "tensor_scalar", out=a[3][:], in0=a[1][:], scalar1=nomlb[:, hd:hd + 1], scalar2=omlb[:, hd:hd + 1],
                        op0=ALU.mult, op1=ALU.add, reads=[a[1], nomlb, omlb], writes=[a[3]])
                for i, hd in enumerate(hds):
                    a = A[i]
                    P.I("dve", "tensor_tensor_scan", out=a[4][:], data0=k.scanmask[:], data1=a[2][:], initial=0.0, op0=ALU.mult,
                        op1=ALU.add, reads=[k.scanmask, a[2]], writes=[a[4]])
                for i, hd in enumerate(hds):
                    a = A[i]
                    G3 = a[4][:].rearrange("p (c j) -> p c j", j=64)
                    P.I("act", "activation", out=a[0][:], in_=a[4][:], func=AF.Exp, reads=[a[4]], writes=[a[0]])
                    P.I("act", "activation", out=a[1][:], in_=a[4][:], func=AF.Exp, scale=-1.0, reads=[a[4]], writes=[a[1]])
                    P.I("dve", "tensor_tensor", out=a[2][:].rearrange("p (c j) -> p c j", j=64), in0=G3,
                        in1=G3[:, :, 63:64].to_broadcast([128, 8, 64]), op=ALU.subtract, reads=[a[4]], writes=[a[2]])
                for i, hd in enumerate(hds):
                    a = A[i]
                    G3 = a[4][:].rearrange("p (c j) -> p c j", j=64)
                    P.I("dve", "tensor_tensor", out=QT[0][:, hd, :], in0=Qb[i][:], in1=a[0][:], op=ALU.mult, reads=[Qb[i], a[0]],
                        writes=[(QT[0], hd)])
                    P.I("dve", "tensor_tensor", out=KT[0][:, hd, :], in0=a[3][:], in1=a[1][:], op=ALU.mult, reads=[a[3], a[1]],
                        writes=[(KT[0], hd)])
                    P.I("act", "activation", out=a[2][:], in_=a[2][:], func=AF.Exp, scale=-1.0, reads=[a[2]], writes=[a[2]])
                    P.I("act", "activation", out=EGL[0][:, hd, :], in_=G3[:, :, 63], func=AF.Exp, reads=[a[4]], writes=[(EGL[0], hd)])
                for i, hd in enumerate(hds):
                    a = A[i]
                    P.I("dve", "tensor_tensor", out=KD[0][:, hd, :], in0=a[3][:], in1=a[2][:], op=ALU.mult, reads=[a[3], a[2]],
                        writes=[(KD[0], hd)])
            if CUTE < 4:
                continue
            for j in range(4):
                jc = slice(j * 128, (j + 1) * 128)
                bT = nbank()
                bTb = bT[:].bitcast(BF16)
                for hd in range(6):
                    P.I("pe", "transpose", out=bTb[:, hd * 128:(hd + 1) * 128], in_=KD[0][:, hd, jc], identity=k.ident[:],
                        reads=[(KD[0], hd), k.ident], writes=[(bT, hd)])
                P.I("dve", "tensor_copy", out=KDA[j][0:64, :, :], in_=bTb[0:64, 0:768].rearrange("p (h k) -> p h k", k=128),
                    reads=[bT], writes=[KDA[j]])
                P.I("dve", "tensor_copy", out=KDB[j][64:128, :, :], in_=bTb[64:128, 0:768].rearrange("p (h k) -> p h k", k=128),
                    reads=[bT], writes=[KDB[j]])
                for hd in range(6):
                    osl, ob = hsl(psSC, hd)
                    P.I("pe", "matmul", osl, lhsT=KT[0][:, hd, jc], rhs=QT[0][:, hd, jc], start=True, stop=True,
                        reads=[(KT[0], hd), (QT[0], hd)], writes=[(ob, hd)])
                P.I("dve", "tensor_tensor", out=SCM[j][:, 0:4, :], in0=psSC[0][:].rearrange("p (h t) -> p h t", t=128),
                    in1=k.blkmask[:].unsqueeze(1).to_broadcast([128, 4, 128]), op=ALU.mult, reads=[psSC[0], k.blkmask],
                    writes=[(SCM[j], 0)])
                P.I("dve", "tensor_tensor", out=SCM[j][:, 4:6, :], in0=psSC[1][:, 0:256].rearrange("p (h t) -> p h t", t=128),
                    in1=k.blkmask[:].unsqueeze(1).to_broadcast([128, 2, 128]), op=ALU.mult, reads=[psSC[1], k.blkmask],
                    writes=[(SCM[j], 1)])
            tokq = []
            if g + 1 < 8:
                x_load_T(g + 1)
                tokq = tok_steps(g + 1)

            def tok_drain(n, tokq=tokq):
                for _ in range(n):
                    if tokq:
                        tokq.pop(0)()
            for j in range(4):
                tj = st_["tj"]
                st_["tj"] += 1
                jb = tj % 2
                tt = g * 4 + j
                jc = slice(j * 128, (j + 1) * 128)
                if CUTE < 5:
                    continue
                sA, sB = sbf[0], sbf[1]
                for hd in range(6):
                    osl, ob = hsl(psO, hd)
                    vs_ = vg[:, j, hd * 128:(hd + 1) * 128]
                    P.I("pe", "matmul", osl, lhsT=SCM[j][:, hd, :], rhs=vs_, start=(hd % 4 == 0), stop=False,
                        reads=[SCM[j], vg], writes=[(ob, hd)])
                for hd in range(6):
                    osl, ob = hsl(psO, hd)
                    P.I("pe", "matmul", osl[0:64, :], lhsT=QT[0][:, hd, j * 128:j * 128 + 64], rhs=sA[:, hd, :], start=False, stop=False,
                        reads=[(QT[0], hd), (sA, hd)], writes=[(ob, hd)])
                for half, (KDx, s_src, s_dst) in enumerate(((KDA[j], sA, sB), (KDB[j], sB, sA))):
                    ch = j * 2 + half
                    for hd in range(6):
                        ksl, kb = hsl(psKV, hd)
                        P.I("pe", "matmul", ksl, lhsT=KDx[:, hd, :], rhs=vg[:, j, hd * 128:(hd + 1) * 128], start=True, stop=True,
                            reads=[KDx, vg], writes=[(kb, hd)])
                    if half == 0:
                        flush_tail()
                    tok_drain(2 if half == 0 else 1)
                    for hx in range(6 + 2):
                        if hx < 6:
                            hd = hx
                            ksl, kb = hsl(psKV, hd)
                            P.I("dve", "scalar_tensor_tensor", out=state[:, hd, :], in0=state[:, hd, :], scalar=EGL[0][:, hd, ch:ch + 1],
                                in1=ksl, op0=ALU.mult, op1=ALU.add, reads=[(state, hd), (EGL[0], hd), (kb, hd)], writes=[(state, hd)])
                        if hx >= 2:
                            hd = hx - 2
                            if hd % 3 != 2:
                                P.I("dve", "tensor_copy", out=s_dst[:, hd, :], in_=state[:, hd, :], reads=[(state, hd)], writes=[(s_dst, hd)])
                            else:
                                P.I("pool", "tensor_copy", out=s_dst[:, hd, :], in_=state[:, hd, :], reads=[(state, hd)], writes=[(s_dst, hd)])
                    if half == 0:
                        for hd in range(6):
                            osl, ob = hsl(psO, hd)
                            P.I("pe", "matmul", osl[64:128, :], lhsT=QT[0][:, hd, j * 128 + 64:(j + 1) * 128], rhs=sB[:, hd, :],
                                start=False, stop=True, reads=[(QT[0], hd), (sB, hd)], writes=[(ob, hd)])
                ob_ = o32[tj % 2]
                P.I("act", "activation", out=ob_[:, 0:512], in_=psO[0][:], func=AF.Copy, reads=[psO[0]], writes=[(ob_, 0)])
                P.I("act", "activation", out=ob_[:, 512:768], in_=psO[1][:, 0:256], func=AF.Copy, reads=[psO[1]], writes=[(ob_, 1)])

                def tail(jb=jb, j=j, tt=tt, ob_=ob_, sgg=sgg):
                    for hd in range(6):
                        P.I("act", "activation", out=junk[:], in_=ob_[:, hd * 128:(hd + 1) * 128], func=AF.Square,
                            accum_out=ss[jb][:, hd:hd + 1], reads=[ob_], writes=[junk, (ss[jb], hd)])
                    P.I("act", "activation", out=rstd[jb][:], in_=ss[jb][:], func=AF.Ln, scale=1.0 / 128.0, bias=epsr[:, 0:1],
                        reads=[ss[jb], epsr], writes=[rstd[jb]])
                    P.I("act", "activation", out=rstd[jb][:], in_=rstd[jb][:], func=AF.Exp, scale=-0.5, reads=[rstd[jb]], writes=[rstd[jb]])
                    P.I("dve", "tensor_tensor", out=t1[0][:].rearrange("p (h v) -> p h v", v=128),
                        in0=ob_[:].rearrange("p (h v) -> p h v", v=128), in1=rstd[jb][:].unsqueeze(2).to_broadcast([128, 6, 128]),
                        op=ALU.mult, reads=[ob_, rstd[jb]], writes=[t1[0]])
                    P.I("dve", "tensor_tensor", out=mixb[0][:], in0=t1[0][:], in1=sgg[:, j, :], op=ALU.mult, reads=[t1[0], sgg],
                        writes=[mixb[0]])
                    transpose_tile(P, k, mixb[0], lambda blk: mixb[0][:, blk * 128:(blk + 1) * 128], 6, nbank(), mst[0], mst[0][:], 1)
                    P.dma("sp", k.mixT[0:768, tt * 128:(tt + 1) * 128].rearrange("(c p) t -> p c t", p=128), mst[0][:], reads=[mst[0]],
                          writes=[(k.mixT, tt)])
                pend_tail.append(tail)
            tok_drain(100)
            flush_tail()
        P.barrier()


WSHAPES = dict(w_in_sb=[1, 1024, 2560], w_in_hg=[1, 1024, 3328], w_mem_kv=[2, 1024, 512], lower_bounds=[2, 768],
               hg_norm_g=[1, 768], w_out=[2, 1024, 1024], ln_mix_g=[2, 1024], ln_mix_b=[2, 1024],
               w_up=[2, 1024, 4096], w_down=[2, 4096, 1024], ln_ffn_g=[2, 1024], ln_ffn_b=[2, 1024])


def build(stages="ABCDE", debug=False):
    nc = bass.Bass("TRN2", target_bir_lowering=False)
    with ExitStack() as es:
        P = Prog(nc, es)
        k = K()
        k.x = P.dram("x", [S, D], F32, kind="ExternalInput")
        k.mem = P.dram("mem", [NMEM, D], F32, kind="ExternalInput")
        for name, shp in WSHAPES.items():
            setattr(k, name, P.dram(name, shp, F32, kind="ExternalInput"))
        skind = "ExternalOutput" if debug else "Internal"
        k.qT0 = P.dram("qT0", [768, S], BF16, kind=skind)
        k.kT0 = P.dram("kT0", [768, S], BF16, kind=skind)
        k.v0 = P.dram("v0", [S, 768], BF16, kind=skind)
        k.qmT = P.dram("qmT", [256, S], BF16, kind=skind)
        k.mixT = P.dram("mixT", [768, S], BF16, kind=skind)
        k.x1 = P.dram("x1", [S, D], F32, kind=skind)
        k.xmid = P.dram("xmid", [S, D], F32, kind=skind)
        k.y = P.dram("y", [S, D], F32, kind="ExternalOutput")
        build_consts(P, k)
        P.barrier()
        two = "E" in stages
        if "A" in stages:
            phase_A(P, k)
        scC = P.scope()
        cw = load_C_weights(P, k, 0, scC)
        if "B" in stages:
            phase_B(P, k)
        scD = P.scope()
        dw = alloc_D_weights(P, k, 0, scD)
        if "C" in stages:
            phase_C(P, k, 0, k.x, k.x1, cw, dw)
        scC.close()
        if "D" in stages:
            phase_D(P, k, 0, k.x1, k.xmid if two else k.y, dw)
        scD.close()
        if two:
            phase_E(P, k)
            scC = P.scope()
            cw = load_C_weights(P, k, 1, scC)
            scD = P.scope()
            dw = alloc_D_weights(P, k, 1, scD)
            phase_C(P, k, 1, k.xmid, k.x1, cw, dw)
            scC.close()
            phase_D(P, k, 1, k.x1, k.y, dw)
            scD.close()
        if "e" in stages:
            phase_E(P, k)
        P.finish()
        k.n_ins = P.n_ins
    return nc, k


_CACHE = {}


def kernel(**inputs):
    if "nc" not in _CACHE:
        _CACHE["nc"] = build("ABCDE")[0]
    nc = _CACHE["nc"]
    B = inputs["x"].shape[0]
    shared = {n: np.ascontiguousarray(inputs[n], dtype=np.float32) for n in WSHAPES}
    in_maps = []
    for b in range(B):
        m = dict(shared)
        m["x"] = np.ascontiguousarray(inputs["x"][b], dtype=np.float32)
        m["mem"] = np.ascontiguousarray(inputs["mem"][b], dtype=np.float32)
        in_maps.append(m)
    res = run_bass_kernel_spmd(nc, in_maps, core_ids=list(range(B)))
    return np.stack([np.asarray(r["y"], dtype=np.float32) for r in res.results], axis=0)
```

```python
from contextlib import ExitStack
import numpy as np
import concourse.bass as bass
import concourse.mybir as mybir
from concourse.bass_utils import run_bass_kernel_spmd

F32 = mybir.dt.float32
BF16 = mybir.dt.bfloat16
AF = mybir.ActivationFunctionType
ALU = mybir.AluOpType
AX = mybir.AxisListType

COMPUTE = ("pe", "act", "dve", "pool", "sp")
DMAQ = ("sp", "pool", "act")
NDMA_SEMS = 12


class Buf:
    def __init__(self, t, name, psum=False):
        self.t = t
        self.name = name
        self.psum = psum
        self.state = {}

    def __getitem__(self, idx):
        return self.t[idx]


class Scope:
    def __init__(self, prog):
        self.prog = prog
        self.bufs = []

    def __enter__(self):
        return self

    def __exit__(self, *a):
        self.close()
        return False

    def close(self):
        for b in self.bufs:
            self.prog.free(b)
        self.bufs = []


class Op:
    __slots__ = ("idx", "eng", "fn", "deps", "is_dma", "sig", "dma_slot", "dma_val", "eidx", "vc", "dmaknown")

    def __init__(self, idx, eng, fn, is_dma):
        self.idx = idx
        self.eng = eng
        self.fn = fn
        self.deps = set()
        self.is_dma = is_dma
        self.sig = 0
        self.eidx = 0
        self.vc = None
        self.dmaknown = None


class Prog:
    def __init__(self, nc, es):
        self.nc = nc
        self.es = es
        self.ops = []
        self.bufs = []
        self.engs = {"pe": nc.tensor, "act": nc.scalar, "dve": nc.vector, "pool": nc.gpsimd, "sp": nc.sync}
        self.sems = {e: es.enter_context(nc.semaphore("s_" + e)) for e in COMPUTE}
        self.dsems = {q: [es.enter_context(nc.semaphore("d_%s%d" % (q, i))) for i in range(NDMA_SEMS)]
                      for q in DMAQ}
        self.barrier_deps = {e: set() for e in self.engs}
        self.emitted = 0
        self.phase_start = 0
        NE = len(COMPUTE)
        self.cur_vc = {e: [0] * NE for e in self.engs}
        self.cur_dma = {e: set() for e in self.engs}
        self.cnt = {e: 0 for e in COMPUTE}
        self.scount = {e: 0 for e in COMPUTE}
        self.dcount = {q: 0 for q in DMAQ}
        self.n_ins = 0

    ARENA_BYTES = 212736

    def _arena_init(self):
        self.arena = self.es.enter_context(self.nc.sbuf_tensor("arena", [128, self.ARENA_BYTES // 2], BF16))
        self.free_list = [(0, self.ARENA_BYTES)]

    def scope(self):
        return Scope(self)

    def sbuf(self, name, shape, dt, scope=None):
        if not hasattr(self, "arena"):
            self._arena_init()
        esz = 4 if dt == F32 else 2
        n = 1
        for d in shape[1:]:
            n *= d
        nbytes = (n * esz + 63) // 64 * 64
        for i, (off, sz) in enumerate(self.free_list):
            if sz >= nbytes:
                if sz == nbytes:
                    self.free_list.pop(i)
                else:
                    self.free_list[i] = (off + nbytes, sz - nbytes)
                break
        else:
            raise AssertionError("arena out of SBUF for %s %s; free=%s" % (name, shape, self.free_list))
        ap = self.arena[0:shape[0], off // 2:off // 2 + n * esz // 2]
        if dt == F32:
            ap = ap.bitcast(F32)
        if len(shape) == 3:
            ap = ap.rearrange("p (a b) -> p a b", b=shape[2])
        elif len(shape) != 2:
            raise AssertionError("2-D / 3-D only")
        b = Buf(ap, name)
        b.region = (off, nbytes)
        self.bufs.append(b)
        if scope is not None:
            scope.bufs.append(b)
        return b

    def free(self, b):
        off, nbytes = b.region
        b.region = None
        fl = sorted(self.free_list + [(off, nbytes)])
        merged = []
        for o, sz in fl:
            if merged and merged[-1][0] + merged[-1][1] == o:
                merged[-1] = (merged[-1][0], merged[-1][1] + sz)
            else:
                merged.append((o, sz))
        self.free_list = merged

    def psum(self, name, shape, dt):
        t = self.es.enter_context(self.nc.psum_tensor(name, list(shape), dt))
        b = Buf(t, name, psum=True)
        self.bufs.append(b)
        return b

    def dram(self, name, shape, dt, kind="Internal"):
        t = self.nc.dram_tensor(name, list(shape), dt, kind=kind)
        b = Buf(t, name)
        self.bufs.append(b)
        return b

    @staticmethod
    def _conf(k1, k2):
        return k1 is None or k2 is None or k1 == k2

    def _rec(self, eng, fn, reads, writes, is_dma):
        op = Op(len(self.ops), eng, fn, is_dma)
        self.ops.append(op)
        for r in reads:
            b, key = r if isinstance(r, tuple) else (r, None)
            for k2, st in b.state.items():
                if st[0] is not None and (self._conf(key, k2) or (b.psum and self.ops[st[0]].eng != eng)):
                    op.deps.add(st[0])
                if b.psum:
                    op.deps.update(r2 for r2 in st[1] if self.ops[r2].eng != eng)
        for w in writes:
            b, key = w if isinstance(w, tuple) else (w, None)
            for k2, st in b.state.items():
                if self._conf(key, k2):
                    if st[0] is not None:
                        op.deps.add(st[0])
                    op.deps.update(st[1])
                elif b.psum:
                    if st[0] is not None and self.ops[st[0]].eng != eng:
                        op.deps.add(st[0])
                    op.deps.update(r2 for r2 in st[1] if self.ops[r2].eng != eng)
        op.deps |= self.barrier_deps[eng]
        self.barrier_deps[eng] = set()
        for r in reads:
            b, key = r if isinstance(r, tuple) else (r, None)
            st = b.state.setdefault(key, [None, []])
            st[1].append(op.idx)
        for w in writes:
            b, key = w if isinstance(w, tuple) else (w, None)
            if key is None:
                b.state = {None: [op.idx, []]}
            else:
                b.state[key] = [op.idx, []]
        op.deps.discard(op.idx)
        return op

    def op(self, eng, fn, reads=(), writes=()):
        return self._rec(eng, fn, reads, writes, False)

    def I(self, eng, meth, *args, reads=(), writes=(), **kw):
        return self._rec(eng, lambda e: getattr(e, meth)(*args, **kw), reads, writes, False)

    def dma(self, q, out, in_, reads=(), writes=()):
        return self._rec(q, lambda e: e.dma_start(out=out, in_=in_), reads, writes, True)

    def barrier(self):
        tails = set()
        last = {}
        for op in self.ops[self.phase_start:]:
            if op.is_dma:
                tails.add(op.idx)
            else:
                last[op.eng] = op.idx
        tails |= set(last.values())
        bop = Op(len(self.ops), "sp", lambda e: e.nop(), False)
        bop.deps = tails | self.barrier_deps["sp"]
        self.ops.append(bop)
        bop.sig = 1
        self.flush()
        for e in self.engs:
            self.barrier_deps[e] = {bop.idx}
        for b in self.bufs:
            b.state = {}
        self.phase_start = len(self.ops)

    def flush(self):
        ops = self.ops
        NE = len(COMPUTE)
        eid = {e: i for i, e in enumerate(COMPUTE)}
        new = ops[self.emitted:]
        for op in new:
            if not op.is_dma:
                self.cnt[op.eng] += 1
                op.eidx = self.cnt[op.eng]
        needed = []
        for op in new:
            vc = self.cur_vc[op.eng]
            dk = self.cur_dma[op.eng]
            real = []
            for d in sorted(op.deps, reverse=True):
                dop = ops[d]
                if dop.is_dma:
                    if d in dk:
                        continue
                    real.append(d)
                    dk.add(d)
                else:
                    if dop.eng == "pe" and op.eng == "pe":
                        continue
                    j = eid[dop.eng]
                    if vc[j] >= dop.eidx:
                        continue
                    real.append(d)
                    vc[j] = dop.eidx
                dk |= dop.dmaknown
                dvc = dop.vc
                for i in range(NE):
                    if dvc[i] > vc[i]:
                        vc[i] = dvc[i]
            op.vc = list(vc)
            op.dmaknown = set(dk)
            needed.append(real)
            for d in real:
                if not ops[d].sig:
                    assert d >= self.emitted, "dependency on an already emitted non-signalling op"
                    ops[d].sig = 1
        for op in new:
            if op.is_dma:
                n = self.dcount[op.eng]
                self.dcount[op.eng] += 1
                op.dma_slot = n % NDMA_SEMS
                op.dma_val = 16 * (n // NDMA_SEMS + 1)
            elif op.sig:
                self.scount[op.eng] += 1
                op.sig = self.scount[op.eng]
        for op, real in zip(new, needed):
            e = self.engs[op.eng]
            if op.is_dma and op.dma_val > 16:
                e.wait_ge(self.dsems[op.eng][op.dma_slot], op.dma_val - 16)
                self.n_ins += 1
            for d in real:
                dop = ops[d]
                if dop.is_dma:
                    e.wait_ge(self.dsems[dop.eng][dop.dma_slot], dop.dma_val)
                else:
                    e.wait_ge(self.sems[dop.eng], dop.sig)
                self.n_ins += 1
            ins = op.fn(e)
            self.n_ins += 1
            if op.is_dma:
                ins.then_inc(self.dsems[op.eng][op.dma_slot], 16)
            elif op.sig:
                ins.then_inc(self.sems[op.eng], 1)
            op.fn = None
        self.emitted = len(ops)

    def finish(self):
        self.barrier()


S = 4096
D = 1024
NT = S // 128
DFF = 4096
SB_H = 12
HG_H = 6
NMEM = 256
ALPHA = float(4 ** 0.25)
LN_EPS = 1e-5
RMS_EPS = 1e-6
NEG = -30000.0


class K:
    pass


def _copy(P, k, idx, out, in_, reads, writes, scale=None):
    if idx % 2 == 0:
        if scale is None:
            P.op("act", lambda e: e.activation(out=out, in_=in_, func=AF.Copy), reads=reads, writes=writes)
        else:
            P.op("act", lambda e: e.activation(out=out, in_=in_, func=AF.Copy, scale=scale), reads=reads, writes=writes)
    else:
        if scale is None:
            P.op("dve", lambda e: e.tensor_copy(out=out, in_=in_), reads=reads, writes=writes)
        else:
            P.op("dve", lambda e: e.tensor_single_scalar(out=out, in_=in_, scalar=scale, op=ALU.mult),
                 reads=reads, writes=writes)


def build_consts(P, k):
    k.ident = P.sbuf("ident", [128, 128], BF16)
    k.negtri = P.sbuf("negtri", [128, 128], BF16)
    k.negones = P.sbuf("negones", [128, 128], BF16)
    k.maskneg = P.sbuf("maskneg", [128, 128], BF16)
    k.blkmask = P.sbuf("blkmask", [128, 128], F32)
    k.scanmask = P.sbuf("scanmask", [128, 512], F32)
    tmp = P.sbuf("ctmp", [128, 128], F32)
    tmp1 = P.sbuf("ctmp1", [128, 128], F32)
    P.op("dve", lambda e: e.memset(tmp[:], 0.0), writes=[tmp])
    P.op("pool", lambda e: e.affine_select(out=tmp[:], in_=tmp[:], pattern=[[-1, 128]], compare_op=ALU.not_equal,
                                            fill=1.0, base=0, channel_multiplier=1), reads=[tmp], writes=[tmp])
    P.op("dve", lambda e: e.tensor_copy(out=k.ident[:], in_=tmp[:]), reads=[tmp], writes=[k.ident])
    P.op("dve", lambda e: e.memset(tmp1[:], 0.0), writes=[tmp1])
    P.op("pool", lambda e: e.affine_select(out=tmp1[:], in_=tmp1[:], pattern=[[1, 128]], compare_op=ALU.is_gt,
                                            fill=-1.0, base=0, channel_multiplier=-1), reads=[tmp1], writes=[tmp1])
    P.op("dve", lambda e: e.tensor_copy(out=k.negtri[:], in_=tmp1[:]), reads=[tmp1], writes=[k.negtri])
    P.op("dve", lambda e: e.tensor_single_scalar(out=k.maskneg[:], in_=tmp1[:], scalar=-NEG, op=ALU.mult),
         reads=[tmp1], writes=[k.maskneg])
    P.op("dve", lambda e: e.memset(k.negones[:], -1.0), writes=[k.negones])
    P.op("dve", lambda e: e.memset(k.blkmask[:], 1.0), writes=[k.blkmask])
    P.op("pool", lambda e: e.affine_select(out=k.blkmask[:], in_=k.blkmask[:], pattern=[[1, 128]], compare_op=ALU.is_ge,
                                            fill=0.0, base=0, channel_multiplier=-1), reads=[k.blkmask], writes=[k.blkmask])
    P.op("dve", lambda e: e.memset(k.blkmask[0:64, 64:128], 0.0), reads=[k.blkmask], writes=[k.blkmask])
    P.op("dve", lambda e: e.memset(k.scanmask[:], 1.0), writes=[k.scanmask])
    P.op("dve", lambda e: e.memset(k.scanmask[:].rearrange("p (c j) -> p c j", j=64)[:, :, 0:1], 0.0),
         reads=[k.scanmask], writes=[k.scanmask])
    k.epsln = P.sbuf("epsln", [128, 1], F32)
    P.op("dve", lambda e: e.memset(k.epsln[:], LN_EPS), writes=[k.epsln])
    k.pspair = [P.es.enter_context(P.nc.psum_tensor("pspair%d" % i, [128, 1024], F32)) for i in range(4)]
    k.ps = []
    for i in range(8):
        b = Buf(k.pspair[i // 2][:, (i % 2) * 512:(i % 2 + 1) * 512], "psb%d" % i, psum=True)
        P.bufs.append(b)
        k.ps.append(b)


def transpose_tile(P, k, src, src_ap_fn, nblk, psbank, dst, dst_ap, cidx, extra_reads=(), dkey=None):
    psb = psbank[:].bitcast(BF16)
    for b in range(nblk):
        P.I("pe", "transpose", out=psb[:, b * 128:(b + 1) * 128], in_=src_ap_fn(b), identity=k.ident[:],
            reads=[src, k.ident] + list(extra_reads), writes=[(psbank, b)])
    _copy(P, k, cidx, dst_ap, psb[:, 0:nblk * 128] if dst_ap.ndim == 2 else
          psb[:, 0:nblk * 128].rearrange("p (c t) -> p c t", t=128), reads=[psbank], writes=[(dst, dkey)])


def phase_A(P, k):
    with P.scope() as ls:
        W0 = P.sbuf("W0", [128, 8, 2560], BF16, ls)
        wsrc = k.w_in_sb[0].rearrange("(c p) n -> p c n", p=128)
        for c in range(8):
            P.dma("pool", W0[:, c, :], wsrc[:, c, :], reads=[], writes=[(W0, c)])
        xb = [P.sbuf("A_xb%d" % i, [128, 4, 1024], BF16, ls) for i in range(2)]
        xT = [P.sbuf("A_xT%d" % i, [128, 8, 512], BF16, ls) for i in range(2)]
        st = [P.sbuf("A_st%d" % i, [128, 512], BF16, ls) for i in range(4)]
        vst = [P.sbuf("A_vst%d" % i, [128, 4, 768], BF16, ls) for i in range(2)]
        cp = 0
        sti = 0
        rot = 0
        for g in range(8):
            xbg = xb[g % 2]
            xTg = xT[g % 2]
            P.dma("pool", xbg[:], k.x[g * 512:(g + 1) * 512, :].rearrange("(j p) d -> p j d", p=128),
                  reads=[], writes=[xbg])
            for j in range(4):
                transpose_tile(P, k, xbg, (lambda xbg, j: lambda b: xbg[:, j, b * 128:(b + 1) * 128])(xbg, j), 8, k.ps[j % 2],
                               xTg, xTg[:, :, j * 128:(j + 1) * 128], cp, dkey=j)
                cp += 1
            for fc in list(range(12)) + [18, 19]:
                bank = k.ps[2 + rot % 6]
                rot += 1
                for c in range(8):
                    P.I("pe", "matmul", bank[:], lhsT=W0[:, c, fc * 128:(fc + 1) * 128], rhs=xTg[:, c, :],
                        start=(c == 0), stop=(c == 7), reads=[W0, xTg], writes=[bank])
                s_ = st[sti % 4]
                sti += 1
                _copy(P, k, cp, s_[:], bank[:], reads=[bank], writes=[s_], scale=(0.125 if fc < 6 else None))
                cp += 1
                if fc < 6:
                    dst = k.qT0[fc * 128:(fc + 1) * 128, g * 512:(g + 1) * 512]
                    dbuf = k.qT0
                elif fc < 12:
                    dst = k.kT0[(fc - 6) * 128:(fc - 5) * 128, g * 512:(g + 1) * 512]
                    dbuf = k.kT0
                else:
                    dst = k.qmT[(fc - 18) * 128:(fc - 17) * 128, g * 512:(g + 1) * 512]
                    dbuf = k.qmT
                P.dma("sp", dst, s_[:], reads=[s_], writes=[(dbuf, (fc, g))])
            vs = vst[g % 2]
            for j in range(4):
                for (c0, n) in ((0, 512), (512, 256)):
                    bank = k.ps[2 + rot % 6]
                    rot += 1
                    for c in range(8):
                        P.I("pe", "matmul", bank[:, 0:n], lhsT=xTg[:, c, j * 128:(j + 1) * 128],
                            rhs=W0[:, c, 1536 + c0:1536 + c0 + n], start=(c == 0), stop=(c == 7),
                            reads=[W0, xTg], writes=[bank])
                    _copy(P, k, cp, vs[:, j, c0:c0 + n], bank[:, 0:n], reads=[bank], writes=[(vs, (j, c0))])
                    cp += 1
            P.dma("sp", k.v0[g * 512:(g + 1) * 512, :].rearrange("(j p) f -> p j f", p=128), vs[:],
                  reads=[vs], writes=[(k.v0, g)])
        P.barrier()


KB = 16


def phase_B(P, k):
    with P.scope() as ls:
        kTs = [P.sbuf("B_kT%d" % i, [128, S], BF16, ls) for i in range(2)]
        qTs = [P.sbuf("B_qT%d" % i, [128, S], BF16, ls) for i in range(2)]
        vvs = [P.sbuf("B_v%d" % i, [128, NT, 128], BF16, ls) for i in range(2)]
        NSP = KB + 2
        spb = [P.sbuf("B_sp%d" % i, [128, 2, 512], BF16, ls) for i in range(NSP)]
        Rb = [P.sbuf("B_Rb%d" % i, [128, 2, 512], BF16, ls) for i in range(NSP)]
        wb = [P.sbuf("B_w%d" % i, [128, 2, 512], BF16, ls) for i in range(NSP)]
        Rf = [P.sbuf("B_R%d" % i, [128, 2, 512], F32, ls) for i in range(2)]
        ost = [P.sbuf("B_ost%d" % i, [128, 512], BF16, ls) for i in range(2)]

        def pair(pi):
            return k.pspair[pi][:].rearrange("p (b c) -> p b c", c=512), [k.ps[2 * pi], k.ps[2 * pi + 1]]

        slots = [pair(0), pair(1), pair(2)]
        pso = [k.ps[6], k.ps[7]]
        units = []
        for hp in range(6):
            for g in range(8):
                for n in range(4 * g + 4):
                    i = 4 * g + 3 - n
                    col0 = max(0, (i - 4 * g) * 128)
                    units.append(dict(hp=hp, g=g, n=n, i=i, col0=col0, diag=(i >= 4 * g), first=(n == 0), last=(i == 0),
                                      gi=hp * 8 + g))
        NU = len(units)
        loaded = [-1]
        slot_ctr = [0]
        zslot = {}
        eslot = {}

        def next_slot():
            slot_ctr[0] += 1
            return slots[slot_ctr[0] % 3]

        def load_pair(hp):
            if loaded[0] >= hp or hp >= 6:
                return
            loaded[0] = hp
            kT, qT, vv = kTs[hp % 2], qTs[hp % 2], vvs[hp % 2]
            P.dma("sp", kT[:], k.kT0[hp * 128:(hp + 1) * 128, :], reads=[k.kT0], writes=[kT])
            P.dma("sp", qT[:], k.qT0[hp * 128:(hp + 1) * 128, :], reads=[k.qT0], writes=[qT])
            P.dma("sp", vv[:], k.v0[:, hp * 128:(hp + 1) * 128].rearrange("(i p) f -> p i f", p=128),
                  reads=[k.v0], writes=[vv])

        def qk(u, pz, stop):
            ap, bufs = pz
            c0, q0, i = u["col0"], u["g"] * 512, u["i"]
            kT, qT = kTs[u["hp"] % 2], qTs[u["hp"] % 2]
            for hh in range(2):
                b0 = 64 * hh
                P.I("pe", "matmul", ap[:, hh, c0:512], lhsT=kT[b0:b0 + 64, i * 128:(i + 1) * 128],
                    rhs=qT[b0:b0 + 64, q0 + c0:q0 + 512], start=True, stop=stop, reads=[kT, qT], writes=[bufs[hh]])

        def maskmm(u, pz):
            ap, bufs = pz
            c0 = u["col0"]
            for hh in range(2):
                P.I("pe", "matmul", ap[:, hh, c0:c0 + 128], lhsT=k.ident[:], rhs=k.maskneg[:], start=False, stop=True,
                    reads=[k.ident, k.maskneg], writes=[bufs[hh]])

        def st_z(idx):
            u = units[idx]
            load_pair(u["hp"])
            if idx - u["hp"] * 144 >= 2 * KB:
                load_pair(u["hp"] + 1)
            pz = next_slot()
            zslot[idx] = pz
            qk(u, pz, not u["diag"])
            if u["diag"]:
                maskmm(u, pz)

        def st_SP(idx):
            u = units[idx]
            c0 = u["col0"]
            ap, bufs = zslot.pop(idx)
            sp = spb[idx % NSP]
            P.I("act", "activation", out=sp[:, :, c0:512], in_=ap[:, :, c0:512], func=AF.Softplus, reads=bufs, writes=[sp])

        def st_R(idx):
            u = units[idx]
            if u["last"]:
                return
            c0 = u["col0"]
            sp = spb[idx % NSP]
            R = Rf[u["gi"] % 2]
            if u["first"]:
                P.I("pool", "memset", R[:], 0.0, writes=[R])
            P.I("dve", "tensor_tensor", out=R[:, :, c0:512], in0=R[:, :, c0:512], in1=sp[:, :, c0:512], op=ALU.add,
                reads=[R, sp], writes=[R])
            rb = Rb[idx % NSP]
            P.I("dve", "tensor_copy", out=rb[:], in_=R[:], reads=[R], writes=[rb])

        def st_eg(idx):
            u = units[idx]
            c0 = u["col0"]
            pz = next_slot()
            eslot[idx] = pz
            ap, bufs = pz
            sp = spb[idx % NSP]
            qk(u, pz, False)
            for hh in range(2):
                P.I("pe", "matmul", ap[:, hh, c0:512], lhsT=k.negtri[:], rhs=sp[:, hh, c0:512], start=False,
                    stop=(u["first"] and not u["diag"]), reads=[k.negtri, sp], writes=[bufs[hh]])
            if not u["first"]:
                rb = Rb[(idx - 1) % NSP]
                for hh in range(2):
                    P.I("pe", "matmul", ap[:, hh, c0:512], lhsT=k.negones[:], rhs=rb[:, hh, c0:512], start=False,
                        stop=(not u["diag"]), reads=[k.negones, rb], writes=[bufs[hh]])
            if u["diag"]:
                maskmm(u, pz)

        def st_W(idx):
            u = units[idx]
            c0 = u["col0"]
            ap, bufs = eslot.pop(idx)
            w = wb[idx % NSP]
            P.I("act", "activation", out=w[:, :, c0:512], in_=ap[:, :, c0:512], func=AF.Exp, reads=bufs, writes=[w])

        def st_pv(idx):
            u = units[idx]
            c0, i = u["col0"], u["i"]
            w = wb[idx % NSP]
            bank = pso[u["gi"] % 2]
            vv = vvs[u["hp"] % 2]
            for hh in range(2):
                P.I("pe", "matmul", bank[64 * hh:64 * hh + 64, c0:512], lhsT=vv[:, i, hh * 64:(hh + 1) * 64], rhs=w[:, hh, c0:512],
                    start=u["first"], stop=u["last"], reads=[vv, w], writes=[bank])
            if u["last"]:
                o = ost[u["gi"] % 2]
                hp, g = u["hp"], u["g"]
                P.I("dve", "tensor_copy", out=o[:], in_=bank[:], reads=[bank], writes=[o])
                P.dma("pool", k.mixT[hp * 128:(hp + 1) * 128, g * 512:(g + 1) * 512], o[:], reads=[o], writes=[(k.mixT, (hp, g))])

        st_z(0)
        prev = []
        for b0 in range(0, NU, KB):
            ids = list(range(b0, min(b0 + KB, NU)))
            for n_, idx in enumerate(ids):
                if n_ + 1 < len(ids):
                    st_z(ids[n_ + 1])
                st_SP(idx)
                st_R(idx)
                if prev:
                    st_pv(prev.pop(0))
            while prev:
                st_pv(prev.pop(0))
            st_eg(ids[0])
            for n_, idx in enumerate(ids):
                if n_ + 1 < len(ids):
                    st_eg(ids[n_ + 1])
                elif ids[-1] + 1 < NU:
                    st_z(ids[-1] + 1)
                st_W(idx)
            prev = list(ids)
        while prev:
            st_pv(prev.pop(0))
        P.barrier()


class LN:
    def __init__(self, P, k, r, dst, gam, bet, bufs):
        self.P, self.k, self.r, self.dst, self.gam, self.bet = P, k, r, dst, gam, bet
        self.st, self.mv, self.rs, self.nmr = bufs

    def stats(self):
        P, r, st, mv = self.P, self.r, self.st, self.mv
        P.I("dve", "bn_stats", out=st[:, 0, :], in_=r[:, 0:512], reads=[r], writes=[(st, 0)])
        P.I("dve", "bn_stats", out=st[:, 1, :], in_=r[:, 512:1024], reads=[r], writes=[(st, 1)])
        P.I("dve", "bn_aggr", out=mv[:], in_=st[:].rearrange("p a b -> p (a b)"), reads=[st], writes=[mv])

    def rstd(self):
        P, k, mv, rs = self.P, self.k, self.mv, self.rs
        P.I("act", "activation", out=rs[:], in_=mv[:, 1:2], func=AF.Ln, bias=k.epsln[:, 0:1], reads=[mv, k.epsln], writes=[rs])
        P.I("act", "activation", out=rs[:], in_=rs[:], func=AF.Exp, scale=-0.5, reads=[rs], writes=[rs])

    def nmr_(self):
        P, mv, rs, nmr = self.P, self.mv, self.rs, self.nmr
        P.I("pool", "tensor_tensor", out=nmr[:], in0=mv[:, 0:1], in1=rs[:], op=ALU.mult, reads=[mv, rs], writes=[nmr])
        P.I("pool", "tensor_single_scalar", out=nmr[:], in_=nmr[:], scalar=-1.0, op=ALU.mult, reads=[nmr], writes=[nmr])

    def norm(self):
        P, r, rs, nmr, dst = self.P, self.r, self.rs, self.nmr, self.dst
        P.I("act", "activation", out=dst[:], in_=r[:], func=AF.Identity, scale=rs[:, 0:1], bias=nmr[:, 0:1],
            reads=[r, rs, nmr], writes=[dst])

    def affine(self):
        P, dst, gam, bet = self.P, self.dst, self.gam, self.bet
        P.I("pool", "tensor_tensor", out=dst[:], in0=dst[:], in1=gam[:], op=ALU.mult, reads=[dst, gam], writes=[dst])
        P.I("pool", "tensor_tensor", out=dst[:], in0=dst[:], in1=bet[:], op=ALU.add, reads=[dst, bet], writes=[dst])


def lnbufs(P, name, ls):
    return (P.sbuf(name + "st", [128, 2, 6], F32, ls), P.sbuf(name + "mv", [128, 2], F32, ls),
            P.sbuf(name + "rs", [128, 1], F32, ls), P.sbuf(name + "nm", [128, 1], F32, ls))


def load_C_weights(P, k, L, sc):
    w = K()
    w.wout = P.sbuf("C_wout", [128, 8, 1024], BF16, sc)
    wsrc = k.w_out[L].rearrange("(c p) n -> p c n", p=128)
    for c in range(0, 8, 2):
        P.dma("pool", w.wout[:, c:c + 2, :], wsrc[:, c:c + 2, :], writes=[(w.wout, c)])
    w.wkv = P.sbuf("C_wkv", [128, 8, 512], BF16, sc)
    P.dma("pool", w.wkv[:], k.w_mem_kv[L].rearrange("(c p) n -> p c n", p=128), writes=[w.wkv])
    w.memb = P.sbuf("C_memb", [128, 2, 1024], BF16, sc)
    P.dma("pool", w.memb[:], k.mem[:, :].rearrange("(j p) d -> p j d", p=128), writes=[w.memb])
    w.gam = P.sbuf("C_gam", [128, 1024], F32, sc)
    w.bet = P.sbuf("C_bet", [128, 1024], F32, sc)
    P.dma("sp", w.gam[:], k.ln_mix_g[L:L + 1, :].partition_broadcast(128), writes=[w.gam])
    P.dma("sp", w.bet[:], k.ln_mix_b[L:L + 1, :].partition_broadcast(128), writes=[w.bet])
    return w


def alloc_D_weights(P, k, L, sc):
    w = K()
    w.L = L
    w.WUP = P.sbuf("D_wup", [128, 8, DFF], BF16, sc)
    w.WDN = [P.sbuf("D_wdn0", [128, 16, 1024], BF16, sc), None]
    usrc = k.w_up[L].rearrange("(c p) f -> p c f", p=128)
    dsrc = k.w_down[L].rearrange("(c p) n -> p c n", p=128)
    w.pending = []
    for c in range(8):
        w.pending.append((w.WUP[:, c, :], usrc[:, c, :], (w.WUP, c)))
    for c in range(0, 16, 4):
        w.pending.append((w.WDN[0][:, c:c + 4, :], dsrc[:, c:c + 4, :], (w.WDN[0], c)))
    return w


def issue_pending(P, w, n):
    for _ in range(n):
        if w.pending:
            dst, src, wr = w.pending.pop(0)
            P.dma("pool", dst, src, writes=[wr])


def phase_C(P, k, L, xin, xout, w, dw=None):
    with P.scope() as ls:
        wout, wkv, memb, gam, bet = w.wout, w.wkv, w.memb, w.gam, w.bet
        memT = P.sbuf("C_memT", [128, 8, 256], BF16, ls)
        kmT = P.sbuf("C_kmT", [128, 2, 256], BF16, ls)
        vm = P.sbuf("C_vm", [128, 2, 256], BF16, ls)
        for j in range(2):
            transpose_tile(P, k, memb, (lambda j: lambda b: memb[:, j, b * 128:(b + 1) * 128])(j), 8, k.ps[j],
                           memT, memT[:, :, j * 128:(j + 1) * 128], j, dkey=j)
        for fc in range(2):
            bank = k.ps[2 + fc]
            for c in range(8):
                P.I("pe", "matmul", bank[:, 0:256], lhsT=wkv[:, c, fc * 128:(fc + 1) * 128], rhs=memT[:, c, :],
                    start=(c == 0), stop=(c == 7), reads=[wkv, memT], writes=[bank])
            _copy(P, k, fc, kmT[:, fc, :], bank[:, 0:256], reads=[bank], writes=[(kmT, fc)])
        for mc in range(2):
            bank = k.ps[4 + mc]
            for c in range(8):
                P.I("pe", "matmul", bank[:, 0:256], lhsT=memT[:, c, mc * 128:(mc + 1) * 128], rhs=wkv[:, c, 256:512],
                    start=(c == 0), stop=(c == 7), reads=[wkv, memT], writes=[bank])
            _copy(P, k, mc + 1, vm[:, mc, :], bank[:, 0:256], reads=[bank], writes=[(vm, mc)])

        qm = [P.sbuf("C_qm%d" % i, [128, 2, 128], BF16, ls) for i in range(3)]
        mx = [P.sbuf("C_mx%d" % i, [128, 6, 128], BF16, ls) for i in range(3)]
        xr = [P.sbuf("C_xr%d" % i, [128, 1024], F32, ls) for i in range(4)]
        E = [P.sbuf("C_E%d" % i, [128, 4, 256], F32, ls) for i in range(2)]
        Pb = [P.sbuf("C_P%d" % i, [128, 4, 256], BF16, ls) for i in range(2)]
        PT = [P.sbuf("C_PT%d" % i, [128, 8, 128], BF16, ls) for i in range(2)]
        mmT = [P.sbuf("C_mmT%d" % i, [128, 2, 128], BF16, ls) for i in range(2)]
        xo = [P.sbuf("C_xo%d" % i, [128, 1024], F32, ls) for i in range(2)]
        mxv = [P.sbuf("C_mxv%d" % i, [128, 4], F32, ls) for i in range(2)]
        nb_ = [P.sbuf("C_nb%d" % i, [128, 4], F32, ls) for i in range(2)]
        ssum = [P.sbuf("C_ss%d" % i, [128, 4], F32, ls) for i in range(2)]
        rsum = [P.sbuf("C_rs%d" % i, [128, 4], F32, ls) for i in range(2)]
        lnb = [lnbufs(P, "C_ln%d" % i, ls) for i in range(2)]
        psS = [[k.ps[0], k.ps[1]], [k.ps[2], k.ps[3]]]
        psT = k.ps[4]
        psMO = k.ps[5]
        psY = [k.ps[6], k.ps[7]]
        psTb = psT[:].bitcast(BF16)
        lns = {}

        def A0(t):
            P.dma("sp", qm[t % 3][:], k.qmT[:, t * 128:(t + 1) * 128].rearrange("(c p) t -> p c t", p=128), reads=[k.qmT],
                  writes=[qm[t % 3]])

        def A1(t):
            for h in range(4):
                bank = psS[t % 2][h % 2]
                p0 = 64 * (h % 2)
                P.I("pe", "matmul", bank[:, (h // 2) * 256:(h // 2 + 1) * 256], lhsT=qm[t % 3][p0:p0 + 64, h // 2, :],
                    rhs=kmT[p0:p0 + 64, h // 2, :], start=True, stop=True, reads=[qm[t % 3], kmT], writes=[(bank, h // 2)])

        def A2(t):
            b2 = t % 2
            for hb in range(2):
                P.I("dve", "tensor_reduce", out=mxv[b2][:, 2 * hb:2 * hb + 2], in_=psS[b2][hb][:].rearrange("p (h m) -> p h m", m=256),
                    axis=AX.X, op=ALU.max, reads=[psS[b2][hb]], writes=[(mxv[b2], hb)])
            P.I("dve", "tensor_single_scalar", out=nb_[b2][:], in_=mxv[b2][:], scalar=-0.125, op=ALU.mult,
                reads=[mxv[b2]], writes=[nb_[b2]])
            for h in range(4):
                q = (h % 2) * 2 + h // 2
                P.I("act", "activation", out=E[b2][:, h, :], in_=psS[b2][h % 2][:, (h // 2) * 256:(h // 2 + 1) * 256], func=AF.Exp,
                    scale=0.125, bias=nb_[b2][:, q:q + 1], accum_out=ssum[b2][:, h:h + 1],
                    reads=[psS[b2][h % 2], nb_[b2]], writes=[(E[b2], h), (ssum[b2], h)])

        def A3(t):
            b2 = t % 2
            P.I("dve", "reciprocal", out=rsum[b2][:], in_=ssum[b2][:], reads=[ssum[b2]], writes=[rsum[b2]])
            P.I("dve", "tensor_tensor", out=Pb[b2][:], in0=E[b2][:], in1=rsum[b2][:].unsqueeze(2).to_broadcast([128, 4, 256]),
                op=ALU.mult, reads=[E[b2], rsum[b2]], writes=[Pb[b2]])

        def A4(t):
            b2 = t % 2
            P.dma("sp", mx[t % 3][:], k.mixT[0:768, t * 128:(t + 1) * 128].rearrange("(c p) t -> p c t", p=128), reads=[k.mixT],
                  writes=[mx[t % 3]])
            for blk in range(8):
                P.I("pe", "transpose", out=psTb[:, blk * 128:(blk + 1) * 128], in_=Pb[b2][:, blk // 2, (blk % 2) * 128:(blk % 2 + 1) * 128],
                    identity=k.ident[:], reads=[Pb[b2], k.ident], writes=[(psT, blk)])
            P.I("act", "activation", out=PT[b2][:], in_=psTb[:, 0:1024].rearrange("p (c t) -> p c t", t=128), func=AF.Copy,
                reads=[psT], writes=[PT[b2]])

        def A5(t):
            b2 = t % 2
            P.dma("sp", xr[t % 4][:], xin[t * 128:(t + 1) * 128, :], reads=[xin], writes=[xr[t % 4]])
            for h in range(4):
                p0 = 64 * (h % 2)
                for mc in range(2):
                    P.I("pe", "matmul", psMO[p0:p0 + 64, (h // 2) * 128:(h // 2 + 1) * 128], lhsT=vm[:, mc, h * 64:(h + 1) * 64],
                        rhs=PT[b2][:, h * 2 + mc, :], start=(mc == 0), stop=(mc == 1), reads=[vm, PT[b2]], writes=[(psMO, h)])
            P.I("dve", "tensor_copy", out=mmT[b2][:], in_=psMO[:, 0:256].rearrange("p (c t) -> p c t", t=128), reads=[psMO],
                writes=[mmT[b2]])

        def A6(t):
            for nh in range(2):
                for c in range(8):
                    lhsT = mx[t % 3][:, c, :] if c < 6 else mmT[t % 2][:, c - 6, :]
                    P.I("pe", "matmul", psY[nh][:], lhsT=lhsT, rhs=wout[:, c, nh * 512:(nh + 1) * 512], start=(c == 0), stop=(c == 7),
                        reads=[mx[t % 3], mmT[t % 2], wout], writes=[psY[nh]])

        def A7(t):
            x_ = xr[t % 4]
            for nh in range(2):
                P.I("dve", "scalar_tensor_tensor", out=x_[:, nh * 512:(nh + 1) * 512], in0=x_[:, nh * 512:(nh + 1) * 512],
                    scalar=ALPHA, in1=psY[nh][:], op0=ALU.mult, op1=ALU.add, reads=[x_, psY[nh]], writes=[(x_, nh)])
            lns[t] = LN(P, k, x_, xo[t % 2], gam, bet, lnb[t % 2])
            lns[t].stats()
            lns[t].rstd()
            lns[t].nmr_()

        def A8(t):
            lns[t].norm()
            lns[t].affine()
            P.dma("sp", xout[t * 128:(t + 1) * 128, :], xo[t % 2][:], reads=[xo[t % 2]], writes=[(xout, t)])
            del lns[t]

        stages_ = [A1, A2, A3, A4, A5, A6, A7, A8]
        A0(0)
        for it in range(NT + len(stages_) - 1):
            if dw is not None and it % 2 == 0:
                issue_pending(P, dw, 1)
            for si in range(len(stages_) - 1, -1, -1):
                t = it - si
                if 0 <= t < NT:
                    stages_[si](t)
            if it + 1 < NT:
                A0(it + 1)
        if dw is not None:
            issue_pending(P, dw, 100)
        P.barrier()


def phase_D(P, k, L, xin, xout, w):
    with P.scope() as ls:
        WUP = w.WUP
        w.WDN[1] = P.sbuf("D_wdn1", [128, 16, 1024], BF16, ls)
        dsrc = k.w_down[L].rearrange("(c p) n -> p c n", p=128)
        issue_pending(P, w, 100)
        for c in range(0, 16, 4):
            P.dma("pool", w.WDN[1][:, c:c + 4, :], dsrc[:, 16 + c:20 + c, :], writes=[(w.WDN[1], c)])
        gam = P.sbuf("D_gam", [128, 1024], F32, ls)
        bet = P.sbuf("D_bet", [128, 1024], F32, ls)
        P.dma("sp", gam[:], k.ln_ffn_g[L:L + 1, :].partition_broadcast(128), writes=[gam])
        P.dma("sp", bet[:], k.ln_ffn_b[L:L + 1, :].partition_broadcast(128), writes=[bet])
        GT = 4
        NG = NT // GT
        NW = GT * 128
        xb = P.sbuf("D_xb", [128, GT, 1024], BF16, ls)
        xT = P.sbuf("D_xT", [128, 8, NW], BF16, ls)
        hT = P.sbuf("D_hT", [128, 32, NW], BF16, ls)
        rl = [P.sbuf("D_rl%d" % i, [128, NW], F32, ls) for i in range(2)]
        xr = [P.sbuf("D_xr%d" % i, [128, 1024], F32, ls) for i in range(2)]
        lnb = [lnbufs(P, "D_ln%d" % i, ls) for i in range(2)]
        psH = [k.ps[2], k.ps[3]]
        psY = [[k.ps[4], k.ps[5]], [k.ps[6], k.ps[7]]]
        st_ = dict(cp=0)

        def load_x(g):
            P.dma("pool", xb[:], xin[g * NW:(g + 1) * NW, :].rearrange("(j p) d -> p j d", p=128), reads=[xin], writes=[xb])

        def transposes(g):
            for j in range(GT):
                transpose_tile(P, k, xb, (lambda j: lambda blk: xb[:, j, blk * 128:(blk + 1) * 128])(j), 8, k.ps[j % 2],
                               xT, xT[:, :, j * 128:(j + 1) * 128], st_["cp"], dkey=j)
                st_["cp"] += 1

        pend = []

        def tail2():
            while pend:
                ln, tt, x_ = pend.pop(0)
                ln.norm()
                ln.affine()
                P.dma("sp", xout[tt * 128:(tt + 1) * 128, :], x_[:], reads=[x_], writes=[(xout, tt)])

        load_x(0)
        transposes(0)
        tl = 0
        for g in range(NG):
            if g + 1 < NG:
                load_x(g + 1)
            for fc in range(32):
                bank = psH[fc % 2]
                for c in range(8):
                    P.I("pe", "matmul", bank[:], lhsT=WUP[:, c, fc * 128:(fc + 1) * 128], rhs=xT[:, c, :], start=(c == 0), stop=(c == 7),
                        reads=[(WUP, c), xT], writes=[bank])
                r_ = rl[fc % 2]
                P.I("act", "activation", out=r_[:], in_=bank[:], func=AF.Relu, reads=[bank], writes=[r_])
                P.I("dve", "tensor_tensor", out=hT[:, fc, :], in0=r_[:], in1=r_[:], op=ALU.mult, reads=[r_], writes=[(hT, fc)])
            if g + 1 < NG:
                transposes(g + 1)
            for j in range(GT):
                tt = g * GT + j
                b = tl % 2
                tl += 1
                tail2()
                x_ = xr[b]
                P.dma("sp", x_[:], xin[tt * 128:(tt + 1) * 128, :], reads=[xin], writes=[x_])
                py = psY[b]
                for nh in range(2):
                    for fc in range(32):
                        wd = w.WDN[fc // 16]
                        P.I("pe", "matmul", py[nh][:], lhsT=hT[:, fc, j * 128:(j + 1) * 128], rhs=wd[:, fc % 16, nh * 512:(nh + 1) * 512],
                            start=(fc == 0), stop=(fc == 31), reads=[hT, (wd, (fc % 16) // 4 * 4)], writes=[py[nh]])
                for nh in range(2):
                    P.I("dve", "scalar_tensor_tensor", out=x_[:, nh * 512:(nh + 1) * 512], in0=x_[:, nh * 512:(nh + 1) * 512],
                        scalar=ALPHA, in1=py[nh][:], op0=ALU.mult, op1=ALU.add, reads=[x_, py[nh]], writes=[(x_, nh)])
                ln = LN(P, k, x_, x_, gam, bet, lnb[b])
                ln.stats()
                ln.rstd()
                ln.nmr_()
                pend.append((ln, tt, x_))
        tail2()
        P.barrier()


def phase_E(P, k):
    with P.scope() as ls:
        W1 = P.sbuf("E_W1", [128, 8, 3328], BF16, ls)
        wsrc = k.w_in_hg[0].rearrange("(c p) n -> p c n", p=128)
        for c in range(8):
            P.dma("pool", W1[:, c, :], wsrc[:, c, :], writes=[(W1, c)])
        l0 = P.sbuf("E_l0", [128, 6], F32, ls)
        l1 = P.sbuf("E_l1", [128, 6], F32, ls)
        for h in range(6):
            P.dma("sp", l0[:, h:h + 1], k.lower_bounds[0:1, h * 128:(h + 1) * 128].rearrange("o p -> p o"), writes=[(l0, h)])
            P.dma("sp", l1[:, h:h + 1], k.lower_bounds[1:2, h * 128:(h + 1) * 128].rearrange("o p -> p o"), writes=[(l1, h)])
        lb = P.sbuf("E_lb", [128, 6], F32, ls)
        omlb = P.sbuf("E_omlb", [128, 6], F32, ls)
        nomlb = P.sbuf("E_nomlb", [128, 6], F32, ls)
        ltmp = P.sbuf("E_ltmp", [128, 6], F32, ls)
        P.I("dve", "tensor_tensor", out=ltmp[:], in0=l0[:], in1=l1[:], op=ALU.subtract, reads=[l0, l1], writes=[ltmp])
        P.I("act", "activation", out=ltmp[:], in_=ltmp[:], func=AF.Exp, reads=[ltmp], writes=[ltmp])
        P.I("dve", "tensor_single_scalar", out=lb[:], in_=ltmp[:], scalar=1.0, op=ALU.add, reads=[ltmp], writes=[lb])
        P.I("dve", "reciprocal", out=lb[:], in_=lb[:], reads=[lb], writes=[lb])
        P.I("dve", "tensor_tensor", out=omlb[:], in0=ltmp[:], in1=lb[:], op=ALU.mult, reads=[ltmp, lb], writes=[omlb])
        P.I("dve", "tensor_single_scalar", out=nomlb[:], in_=omlb[:], scalar=-1.0, op=ALU.mult, reads=[omlb], writes=[nomlb])
        gn = P.sbuf("E_gn", [128, 768], F32, ls)
        P.dma("sp", gn[:], k.hg_norm_g[0:1, :].partition_broadcast(128), writes=[gn])
        epsr = P.sbuf("E_epsr", [128, 1], F32, ls)
        P.I("dve", "memset", epsr[:], RMS_EPS, writes=[epsr])
        state = P.sbuf("E_state", [128, 6, 128], F32, ls)
        sbf = [P.sbuf("E_sbf%d" % i, [128, 6, 128], BF16, ls) for i in range(2)]
        P.I("dve", "memset", state[:], 0.0, writes=[state])
        P.I("dve", "memset", sbf[0][:], 0.0, writes=[sbf[0]])
        xb = [P.sbuf("E_xb%d" % i, [128, 4, 1024], BF16, ls) for i in range(1)]
        xT = [P.sbuf("E_xT%d" % i, [128, 8, 512], BF16, ls) for i in range(1)]
        vbf = [P.sbuf("E_v%d" % i, [128, 4, 768], BF16, ls) for i in range(2)]
        sg = [P.sbuf("E_sg%d" % i, [128, 4, 768], F32, ls) for i in range(2)]
        qst = [P.sbuf("E_qst%d" % i, [128, 512], BF16, ls) for i in range(1)]
        A = [[P.sbuf("E_A%d_%d" % (s_, i), [128, 512], F32, ls) for i in range(5)] for s_ in range(3)]
        QT = [P.sbuf("E_QT%d" % i, [128, 6, 512], BF16, ls) for i in range(1)]
        KT = [P.sbuf("E_KT%d" % i, [128, 6, 512], BF16, ls) for i in range(1)]
        KD = [P.sbuf("E_KD%d" % i, [128, 6, 512], BF16, ls) for i in range(1)]
        EGL = [P.sbuf("E_EGL%d" % i, [128, 6, 8], F32, ls) for i in range(1)]
        KDA = [P.sbuf("E_KDA%d" % i, [128, 6, 128], BF16, ls) for i in range(4)]
        KDB = [P.sbuf("E_KDB%d" % i, [128, 6, 128], BF16, ls) for i in range(4)]
        for i in range(4):
            P.I("pool", "memset", KDA[i][:], 0.0, writes=[KDA[i]])
            P.I("pool", "memset", KDB[i][:], 0.0, writes=[KDB[i]])
        SCM = [P.sbuf("E_SCM%d" % i, [128, 6, 128], BF16, ls) for i in range(4)]
        junk = P.sbuf("E_junk", [128, 128], F32, ls)
        o32 = [P.sbuf("E_o32_%d" % i, [128, 768], F32, ls) for i in range(2)]
        ss = [P.sbuf("E_ss%d" % i, [128, 6], F32, ls) for i in range(2)]
        rstd = [P.sbuf("E_rstd%d" % i, [128, 6], F32, ls) for i in range(2)]
        t1 = [P.sbuf("E_t1_%d" % i, [128, 768], F32, ls) for i in range(1)]
        mixb = [P.sbuf("E_mixb%d" % i, [128, 768], BF16, ls) for i in range(1)]
        mst = [P.sbuf("E_mst%d" % i, [128, 6, 128], BF16, ls) for i in range(1)]
        psSC = [k.ps[0], k.ps[1]]
        psO = [k.ps[2], k.ps[3]]
        psKV = [k.ps[4], k.ps[5]]
        rotb = [k.ps[6], k.ps[7]]
        st_ = dict(rot=0, cp=0, hs=0, sb=0, tj=0)

        def nbank():
            st_["rot"] += 1
            return rotb[st_["rot"] % 2]

        pend_tail = []

        def flush_tail():
            while pend_tail:
                pend_tail.pop(0)()

        def hsl(bank_list, hd):
            return bank_list[hd // 4][:, (hd % 4) * 128:(hd % 4 + 1) * 128], bank_list[hd // 4]

        CUTE = 9.0
        for g in range(8 if CUTE >= 9 else 1):
            if CUTE < 2:
                break
            gb = g % 2
            xbg, xTg, vg, sgg = xb[0], xT[0], vbf[gb], sg[gb]

            def x_load_T(gn_):
                P.dma("pool", xbg[:], k.xmid[gn_ * 512:(gn_ + 1) * 512, :].rearrange("(j p) d -> p j d", p=128), reads=[k.xmid], writes=[xbg])
                for j in range(4):
                    transpose_tile(P, k, xbg, (lambda j: lambda b: xbg[:, j, b * 128:(b + 1) * 128])(j), 8, nbank(),
                                   xTg, xTg[:, :, j * 128:(j + 1) * 128], st_["cp"], dkey=j)
                    st_["cp"] += 1

            def tok_steps(gn_):
                vn, sn = vbf[gn_ % 2], sg[gn_ % 2]
                steps = []
                for j in range(4):
                    for n in range(3):
                        def st(j=j, n=n):
                            bank = nbank()
                            for c in range(8):
                                P.I("pe", "matmul", bank[:], lhsT=xTg[:, c, j * 128:(j + 1) * 128], rhs=W1[:, c, 1536 + 512 * n:2048 + 512 * n],
                                    start=(c == 0), stop=(c == 7), reads=[W1, xTg], writes=[bank])
                            if n == 0:
                                _copy(P, k, 0, vn[:, j, 0:512], bank[:], reads=[bank], writes=[(vn, (j, 0))])
                            elif n == 1:
                                _copy(P, k, 0, vn[:, j, 512:768], bank[:, 0:256], reads=[bank], writes=[(vn, (j, 1))])
                                _copy(P, k, 0, sn[:, j, 0:256], bank[:, 256:512], reads=[bank], writes=[(sn, (j, 0))])
                            else:
                                _copy(P, k, 0, sn[:, j, 256:768], bank[:], reads=[bank], writes=[(sn, (j, 1))])
                        steps.append(st)
                for fc in (24, 25):
                    def sq(fc=fc):
                        bank = nbank()
                        for c in range(8):
                            P.I("pe", "matmul", bank[:], lhsT=W1[:, c, fc * 128:(fc + 1) * 128], rhs=xTg[:, c, :], start=(c == 0), stop=(c == 7),
                                reads=[W1, xTg], writes=[bank])
                        s_ = qst[0]
                        _copy(P, k, 0, s_[:], bank[:], reads=[bank], writes=[s_])
                        P.dma("sp", k.qmT[(fc - 24) * 128:(fc - 23) * 128, gn_ * 512:(gn_ + 1) * 512], s_[:], reads=[s_],
                              writes=[(k.qmT, (fc, gn_))])
                    steps.append(sq)

                def ssilu():
                    P.I("act", "activation", out=sn[:], in_=sn[:], func=AF.Silu, reads=[sn], writes=[sn])
                    P.I("pool", "tensor_tensor", out=sn[:], in0=sn[:], in1=gn[:].unsqueeze(1).to_broadcast([128, 4, 768]), op=ALU.mult,
                        reads=[sn, gn], writes=[sn])
                steps.append(ssilu)
                return steps

            if g == 0:
                x_load_T(0)
                for st in tok_steps(0):
                    st()
            if CUTE < 3:
                continue
            for trio in range(2):
                hds = [trio * 3 + i for i in range(3)]
                Fb = [k.ps[i] for i in range(3)]
                Qb = [k.ps[3 + i] for i in range(3)]
                for i, hd in enumerate(hds):
                    for c in range(8):
                        P.I("pe", "matmul", Fb[i][:], lhsT=W1[:, c, 768 + hd * 128:768 + (hd + 1) * 128], rhs=xTg[:, c, :], start=(c == 0),
                            stop=(c == 7), reads=[W1, xTg], writes=[Fb[i]])
                for i, hd in enumerate(hds):
                    a = A[i]
                    P.I("act", "activation", out=a[0][:], in_=Fb[i][:], func=AF.Exp, scale=-1.0, reads=[Fb[i]], writes=[a[0]])
                for i, hd in enumerate(hds):
                    for c in range(8):
                        P.I("pe", "matmul", Qb[i][:], lhsT=W1[:, c, hd * 128:(hd + 1) * 128], rhs=xTg[:, c, :], start=(c == 0), stop=(c == 7),
                            reads=[W1, xTg], writes=[Qb[i]])
                for i, hd in enumerate(hds):
                    a = A[i]
                    P.I("act", "activation", out=a[0][:], in_=a[0][:], func=AF.Ln, bias=1.0, reads=[a[0]], writes=[a[0]])
                for i, hd in enumerate(hds):
                    a = A[i]
                    P.I("act", "activation", out=a[1][:], in_=a[0][:], func=AF.Exp, scale=-1.0, reads=[a[0]], writes=[a[1]])
                for i, hd in enumerate(hds):
                    a = A[i]
                    P.I("act", "activation", out=a[2][:], in_=a[1][:], func=AF.Ln, scale=omlb[:, hd:hd + 1], bias=lb[:, hd:hd + 1],
                        reads=[a[1], omlb, lb], writes=[a[2]])
                    P.I("dve", "tensor_scalar", out=a[3][:], in0=a[1][:], scalar1=nomlb[:, hd:hd + 1], scalar2=omlb[:, hd:hd + 1],
                        op0=ALU.mult, op1=ALU.add, reads=[a[1], nomlb, omlb], writes=[a[3]])
                for i, hd in enumerate(hds):
                    a = A[i]
                    P.I("dve", "tensor_tensor_scan", out=a[4][:], data0=k.scanmask[:], data1=a[2][:], initial=0.0, op0=ALU.mult,
                        op1=ALU.add, reads=[k.scanmask, a[2]], writes=[a[4]])
                for i, hd in enumerate(hds):
                    a = A[i]
                    G3 = a[4][:].rearrange("p (c j) -> p c j", j=64)
                    P.I("act", "activation", out=a[0][:], in_=a[4][:], func=AF.Exp, reads=[a[4]], writes=[a[0]])
                    P.I("act", "activation", out=a[1][:], in_=a[4][:], func=AF.Exp, scale=-1.0, reads=[a[4]], writes=[a[1]])
                    P.I("dve", "tensor_tensor", out=a[2][:].rearrange("p (c j) -> p c j", j=64), in0=G3,
                        in1=G3[:, :, 63:64].to_broadcast([128, 8, 64]), op=ALU.subtract, reads=[a[4]], writes=[a[2]])
                for i, hd in enumerate(hds):
                    a = A[i]
                    G3 = a[4][:].rearrange("p (c j) -> p c j", j=64)
                    P.I("dve", "tensor_tensor", out=QT[0][:, hd, :], in0=Qb[i][:], in1=a[0][:], op=ALU.mult, reads=[Qb[i], a[0]],
                        writes=[(QT[0], hd)])
                    P.I("dve", "tensor_tensor", out=KT[0][:, hd, :], in0=a[3][:], in1=a[1][:], op=ALU.mult, reads=[a[3], a[1]],
                        writes=[(KT[0], hd)])
                    P.I("act", "activation", out=a[2][:], in_=a[2][:], func=AF.Exp, scale=-1.0, reads=[a[2]], writes=[a[2]])
                    P.I("act", "activation", out=EGL[0][:, hd, :], in_=G3[:, :, 63], func=AF.Exp, reads=[a[4]], writes=[(EGL[0], hd)])
                for i, hd in enumerate(hds):
                    a = A[i]
                    P.I("dve", "tensor_tensor", out=KD[0][:, hd, :], in0=a[3][:], in1=a[2][:], op=ALU.mult, reads=[a[3], a[2]],
                        writes=[(KD[0], hd)])
            if CUTE < 4:
                continue
            for j in range(4):
                jc = slice(j * 128, (j + 1) * 128)
                bT = nbank()
                bTb = bT[:].bitcast(BF16)
                for hd in range(6):
                    P.I("pe", "transpose", out=bTb[:, hd * 128:(hd + 1) * 128], in_=KD[0][:, hd, jc], identity=k.ident[:],
                        reads=[(KD[0], hd), k.ident], writes=[(bT, hd)])
                P.I("dve", "tensor_copy", out=KDA[j][0:64, :, :], in_=bTb[0:64, 0:768].rearrange("p (h k) -> p h k", k=128),
                    reads=[bT], writes=[KDA[j]])
                P.I("dve", "tensor_copy", out=KDB[j][64:128, :, :], in_=bTb[64:128, 0:768].rearrange("p (h k) -> p h k", k=128),
                    reads=[bT], writes=[KDB[j]])
                for hd in range(6):
                    osl, ob = hsl(psSC, hd)
                    P.I("pe", "matmul", osl, lhsT=KT[0][:, hd, jc], rhs=QT[0][:, hd, jc], start=True, stop=True,
                        reads=[(KT[0], hd), (QT[0], hd)], writes=[(ob, hd)])
                P.I("dve", "tensor_tensor", out=SCM[j][:, 0:4, :], in0=psSC[0][:].rearrange("p (h t) -> p h t", t=128),
                    in1=k.blkmask[:].unsqueeze(1).to_broadcast([128, 4, 128]), op=ALU.mult, reads=[psSC[0], k.blkmask],
                    writes=[(SCM[j], 0)])
                P.I("dve", "tensor_tensor", out=SCM[j][:, 4:6, :], in0=psSC[1][:, 0:256].rearrange("p (h t) -> p h t", t=128),
                    in1=k.blkmask[:].unsqueeze(1).to_broadcast([128, 2, 128]), op=ALU.mult, reads=[psSC[1], k.blkmask],
                    writes=[(SCM[j], 1)])
            tokq = []
            if g + 1 < 8:
                x_load_T(g + 1)
                tokq = tok_steps(g + 1)

            def tok_drain(n, tokq=tokq):
                for _ in range(n):
                    if tokq:
                        tokq.pop(0)()
            for j in range(4):
                tj = st_["tj"]
                st_["tj"] += 1
                jb = tj % 2
                tt = g * 4 + j
                jc = slice(j * 128, (j + 1) * 128)
                if CUTE < 5:
                    continue
                sA, sB = sbf[0], sbf[1]
                for half, (KDx, s_src, s_dst) in enumerate(((KDA[j], sA, sB), (KDB[j], sB, sA))):
                    ch = j * 2 + half
                    for hd in range(6):
                        ksl, kb = hsl(psKV, hd)
                        P.I("pe", "matmul", ksl, lhsT=KDx[:, hd, :], rhs=vg[:, j, hd * 128:(hd + 1) * 128], start=True, stop=True,
                            reads=[KDx, vg], writes=[(kb, hd)])
                    if half == 0:
                        for hd in range(6):
                            osl, ob = hsl(psO, hd)
                            vs_ = vg[:, j, hd * 128:(hd + 1) * 128]
                            P.I("pe", "matmul", osl, lhsT=SCM[j][:, hd, :], rhs=vs_, start=(hd % 4 == 0), stop=False,
                                reads=[SCM[j], vg], writes=[(ob, hd)])
                        for hd in range(6):
                            osl, ob = hsl(psO, hd)
                            P.I("pe", "matmul", osl[0:64, :], lhsT=QT[0][:, hd, j * 128:j * 128 + 64], rhs=sA[:, hd, :], start=False, stop=False,
                                reads=[(QT[0], hd), (sA, hd)], writes=[(ob, hd)])
                    else:
                        for hd in range(6):
                            osl, ob = hsl(psO, hd)
                            P.I("pe", "matmul", osl[64:128, :], lhsT=QT[0][:, hd, j * 128 + 64:(j + 1) * 128], rhs=sB[:, hd, :],
                                start=False, stop=True, reads=[(QT[0], hd), (sB, hd)], writes=[(ob, hd)])
                    if half == 0:
                        flush_tail()
                    tok_drain(2 if half == 0 else 1)
                    for hx in range(6 + 2):
                        if hx < 6:
                            hd = hx
                            ksl, kb = hsl(psKV, hd)
                            P.I("dve", "scalar_tensor_tensor", out=state[:, hd, :], in0=state[:, hd, :], scalar=EGL[0][:, hd, ch:ch + 1],
                                in1=ksl, op0=ALU.mult, op1=ALU.add, reads=[(state, hd), (EGL[0], hd), (kb, hd)], writes=[(state, hd)])
                        if hx >= 2:
                            hd = hx - 2
                            if hd % 3 != 2:
                                P.I("dve", "tensor_copy", out=s_dst[:, hd, :], in_=state[:, hd, :], reads=[(state, hd)], writes=[(s_dst, hd)])
                            else:
                                P.I("pool", "tensor_copy", out=s_dst[:, hd, :], in_=state[:, hd, :], reads=[(state, hd)], writes=[(s_dst, hd)])
                ob_ = o32[tj % 2]
                P.I("act", "activation", out=ob_[:, 0:512], in_=psO[0][:], func=AF.Copy, reads=[psO[0]], writes=[(ob_, 0)])
                P.I("act", "activation", out=ob_[:, 512:768], in_=psO[1][:, 0:256], func=AF.Copy, reads=[psO[1]], writes=[(ob_, 1)])

                def tail(jb=jb, j=j, tt=tt, ob_=ob_, sgg=sgg):
                    for hd in range(6):
                        P.I("act", "activation", out=junk[:], in_=ob_[:, hd * 128:(hd + 1) * 128], func=AF.Square,
                            accum_out=ss[jb][:, hd:hd + 1], reads=[ob_], writes=[junk, (ss[jb], hd)])
                    P.I("act", "activation", out=rstd[jb][:], in_=ss[jb][:], func=AF.Ln, scale=1.0 / 128.0, bias=epsr[:, 0:1],
                        reads=[ss[jb], epsr], writes=[rstd[jb]])
                    P.I("act", "activation", out=rstd[jb][:], in_=rstd[jb][:], func=AF.Exp, scale=-0.5, reads=[rstd[jb]], writes=[rstd[jb]])
                    P.I("dve", "tensor_tensor", out=t1[0][:].rearrange("p (h v) -> p h v", v=128),
                        in0=ob_[:].rearrange("p (h v) -> p h v", v=128), in1=rstd[jb][:].unsqueeze(2).to_broadcast([128, 6, 128]),
                        op=ALU.mult, reads=[ob_, rstd[jb]], writes=[t1[0]])
                    P.I("dve", "tensor_tensor", out=mixb[0][:], in0=t1[0][:], in1=sgg[:, j, :], op=ALU.mult, reads=[t1[0], sgg],
                        writes=[mixb[0]])
                    transpose_tile(P, k, mixb[0], lambda blk: mixb[0][:, blk * 128:(blk + 1) * 128], 6, nbank(), mst[0], mst[0][:], 1)
                    P.dma("sp", k.mixT[0:768, tt * 128:(tt + 1) * 128].rearrange("(c p) t -> p c t", p=128), mst[0][:], reads=[mst[0]],
                          writes=[(k.mixT, tt)])
                pend_tail.append(tail)
            tok_drain(100)
            flush_tail()
        P.barrier()


WSHAPES = dict(w_in_sb=[1, 1024, 2560], w_in_hg=[1, 1024, 3328], w_mem_kv=[2, 1024, 512], lower_bounds=[2, 768],
               hg_norm_g=[1, 768], w_out=[2, 1024, 1024], ln_mix_g=[2, 1024], ln_mix_b=[2, 1024],
               w_up=[2, 1024, 4096], w_down=[2, 4096, 1024], ln_ffn_g=[2, 1024], ln_ffn_b=[2, 1024])


def build(stages="ABCDE", debug=False):
    nc = bass.Bass("TRN2", target_bir_lowering=False)
    with ExitStack() as es:
        P = Prog(nc, es)
        k = K()
        k.x = P.dram("x", [S, D], F32, kind="ExternalInput")
        k.mem = P.dram("mem", [NMEM, D], F32, kind="ExternalInput")
        for name, shp in WSHAPES.items():
            setattr(k, name, P.dram(name, shp, F32, kind="ExternalInput"))
        skind = "ExternalOutput" if debug else "Internal"
        k.qT0 = P.dram("qT0", [768, S], BF16, kind=skind)
        k.kT0 = P.dram("kT0", [768, S], BF16, kind=skind)
        k.v0 = P.dram("v0", [S, 768], BF16, kind=skind)
        k.qmT = P.dram("qmT", [256, S], BF16, kind=skind)
        k.mixT = P.dram("mixT", [768, S], BF16, kind=skind)
        k.x1 = P.dram("x1", [S, D], F32, kind=skind)
        k.xmid = P.dram("xmid", [S, D], F32, kind=skind)
        k.y = P.dram("y", [S, D], F32, kind="ExternalOutput")
        build_consts(P, k)
        P.barrier()
        two = "E" in stages
        if "A" in stages:
            phase_A(P, k)
        scC = P.scope()
        cw = load_C_weights(P, k, 0, scC)
        if "B" in stages:
            phase_B(P, k)
        scD = P.scope()
        dw = alloc_D_weights(P, k, 0, scD)
        if "C" in stages:
            phase_C(P, k, 0, k.x, k.x1, cw, dw)
        scC.close()
        if "D" in stages:
            phase_D(P, k, 0, k.x1, k.xmid if two else k.y, dw)
        scD.close()
        if two:
            phase_E(P, k)
            scC = P.scope()
            cw = load_C_weights(P, k, 1, scC)
            scD = P.scope()
            dw = alloc_D_weights(P, k, 1, scD)
            phase_C(P, k, 1, k.xmid, k.x1, cw, dw)
            scC.close()
            phase_D(P, k, 1, k.x1, k.y, dw)
            scD.close()
        if "e" in stages:
            phase_E(P, k)
        P.finish()
        k.n_ins = P.n_ins
    return nc, k


_CACHE = {}


def kernel(**inputs):
    if "nc" not in _CACHE:
        _CACHE["nc"] = build("ABCDE")[0]
    nc = _CACHE["nc"]
    B = inputs["x"].shape[0]
    shared = {n: np.ascontiguousarray(inputs[n], dtype=np.float32) for n in WSHAPES}
    in_maps = []
    for b in range(B):
        m = dict(shared)
        m["x"] = np.ascontiguousarray(inputs["x"][b], dtype=np.float32)
        m["mem"] = np.ascontiguousarray(inputs["mem"][b], dtype=np.float32)
        in_maps.append(m)
    res = run_bass_kernel_spmd(nc, in_maps, core_ids=list(range(B)))
    return np.stack([np.asarray(r["y"], dtype=np.float32) for r in res.results], axis=0)
```
